# Optimizing a Trainium2 kernel written in Bass

```python
import math
import jax, jax.numpy as jnp
from jax import lax
import numpy as np

D_MODEL = 4096
BATCH = 4
SEQ = 2048
DEPTH = 2
DEC_BATCH = 128
DEC_SEQ = 8
PAST_LEN = 16384
PAGE_SIZE = 128

N_MIXERS = 2
N_SSM_LAYERS = (DEPTH + 1) // 2
N_POOL_LAYERS = DEPTH // 2
SSM_GROUP = 16
SSM_GROUPS = D_MODEL // SSM_GROUP
SSM_STATE = 64
SSM_BLOCK = 128
DT_MIN = 1e-3
DT_MAX = 1e-1
POOL_WINDOWS = (2, 4, 8, 16)
POOL_GROUPS = len(POOL_WINDOWS)
POOL_WIDTH = D_MODEL // POOL_GROUPS
POOL_BUF = max(POOL_WINDOWS) - 1
D_FF = -(-(8 * D_MODEL) // (3 * 256)) * 256
RMS_EPS = 1e-6

kernel_name = "s5_pool_hybrid_decode_step"


def rmsnorm(x, g):
    xf = x.astype(jnp.float32)
    y = xf * lax.rsqrt(jnp.mean(xf * xf, axis=-1, keepdims=True) + RMS_EPS) * g.astype(jnp.float32)
    return y.astype(x.dtype)


def _ssm_combine(e1, e2):
    a1r, a1i, b1r, b1i = e1
    a2r, a2i, b2r, b2i = e2
    return (a1r * a2r - a1i * a2i,
            a1r * a2i + a1i * a2r,
            a2r * b1r - a2i * b1i + b2r,
            a2r * b1i + a2i * b1r + b2i)


def s5_mixer(h, h0_re, h0_im, lam_re, lam_im, log_step, b_re, b_im, c_re, c_im, d_skip, w_glu):
    bt, t, _ = h.shape
    f32 = jnp.float32
    lr = lam_re.astype(f32)
    li = lam_im.astype(f32)
    dt = jnp.exp(log_step.astype(f32))[:, None]
    mag = jnp.exp(lr * dt)
    ar = mag * jnp.cos(li * dt)
    ai = mag * jnp.sin(li * dt)
    den = lr * lr + li * li
    fr = ((ar - 1.0) * lr + ai * li) / den
    fi = (ai * lr - (ar - 1.0) * li) / den
    br = b_re.astype(f32)
    bi = b_im.astype(f32)
    bbr = fr[..., None] * br - fi[..., None] * bi
    bbi = fr[..., None] * bi + fi[..., None] * br
    cr = c_re.astype(f32)
    ci = c_im.astype(f32)

    blk = math.gcd(t, SSM_BLOCK)
    nb = t // blk
    u = h.astype(f32)
    ub = jnp.swapaxes(u, 0, 1).reshape(nb, blk, bt, SSM_GROUPS, SSM_GROUP)
    a_r = jnp.broadcast_to(ar, (blk, 1, SSM_GROUPS, SSM_STATE))
    a_i = jnp.broadcast_to(ai, (blk, 1, SSM_GROUPS, SSM_STATE))

    def block(carry, u_blk):
        sr, si = carry
        bur = jnp.einsum('tbgc,gpc->tbgp', u_blk, bbr)
        bui = jnp.einsum('tbgc,gpc->tbgp', u_blk, bbi)
        bur = bur.at[0].add(ar * sr - ai * si)
        bui = bui.at[0].add(ar * si + ai * sr)
        _, _, xr, xi = lax.associative_scan(_ssm_combine, (a_r, a_i, bur, bui), axis=0)
        y = jnp.einsum('tbgp,gcp->tbgc', xr, cr) - jnp.einsum('tbgp,gcp->tbgc', xi, ci)
        return (xr[-1], xi[-1]), y

    if h0_re is None:
        s0 = (jnp.zeros((bt, SSM_GROUPS, SSM_STATE), f32), jnp.zeros((bt, SSM_GROUPS, SSM_STATE), f32))
    else:
        s0 = (h0_re.astype(f32), h0_im.astype(f32))
    (sr, si), yb = lax.scan(block, s0, ub)
    y = jnp.swapaxes(yb.reshape(t, bt, D_MODEL), 0, 1) + d_skip.astype(f32) * u
    y = jax.nn.gelu(y).astype(h.dtype)
    g = y @ w_glu
    out = g[..., :D_MODEL] * jax.nn.sigmoid(g[..., D_MODEL:])
    return out, sr.astype(h.dtype), si.astype(h.dtype)


def pool_mixer(h, buf, start_pos, w_pool, scale):
    bt, t, d = h.shape
    f32 = jnp.float32
    hf = h.astype(f32)
    past = jnp.zeros((bt, POOL_BUF, d), f32) if buf is None else buf.astype(f32)
    ext = jnp.concatenate([past, hf], axis=1)
    cs = jnp.concatenate([jnp.zeros((bt, 1, d), f32), jnp.cumsum(ext, axis=1)], axis=1)
    pos = jnp.arange(t, dtype=jnp.int32) + start_pos
    parts = []
    for gi, w in enumerate(POOL_WINDOWS):
        sl = slice(gi * POOL_WIDTH, (gi + 1) * POOL_WIDTH)
        lo = POOL_BUF + 1 - w
        wsum = cs[:, POOL_BUF + 1:POOL_BUF + 1 + t, sl] - cs[:, lo:lo + t, sl]
        cnt = jnp.minimum(pos + 1, w).astype(f32)[:, None]
        parts.append(wsum / cnt - hf[:, :, sl])
    p = jnp.stack(parts, axis=2).astype(h.dtype)
    z = jnp.einsum('btgc,gcd->btgd', p, w_pool).reshape(bt, t, d)
    out = z * scale
    new_buf = ext[:, -POOL_BUF:, :].astype(h.dtype)
    return out, new_buf


def swiglu(h, w_gate_up, w_down):
    gu = h @ w_gate_up
    return (jax.nn.silu(gu[..., :D_FF]) * gu[..., D_FF:]) @ w_down


def trunk(x, ssm_re, ssm_im, pool_buf, start_pos, norm_mix, norm_ffn, ssm_lambda_re, ssm_lambda_im,
          ssm_log_step, ssm_b_re, ssm_b_im, ssm_c_re, ssm_c_im, ssm_d, ssm_w_glu, pool_w, pool_scale,
          ffn_w_gate_up, ffn_w_down, norm_final):
    new_re, new_im, new_pool = [], [], []
    for i in range(DEPTH):
        j = i // N_MIXERS
        h = rmsnorm(x, norm_mix[i])
        if i % N_MIXERS == 0:
            h0r = None if ssm_re is None else ssm_re[j]
            h0i = None if ssm_im is None else ssm_im[j]
            out, sr, si = s5_mixer(h, h0r, h0i, ssm_lambda_re[j], ssm_lambda_im[j], ssm_log_step[j],
                                   ssm_b_re[j], ssm_b_im[j], ssm_c_re[j], ssm_c_im[j], ssm_d[j], ssm_w_glu[j])
            new_re.append(sr)
            new_im.append(si)
        else:
            pb = None if pool_buf is None else pool_buf[j]
            out, nbuf = pool_mixer(h, pb, start_pos, pool_w[j], pool_scale[j])
            new_pool.append(nbuf)
        x = x + out
        x = x + swiglu(rmsnorm(x, norm_ffn[i]), ffn_w_gate_up[i], ffn_w_down[i])
    y = rmsnorm(x, norm_final)
    return y, jnp.stack(new_re), jnp.stack(new_im), jnp.stack(new_pool)


def setup_inputs(seed: int = 0) -> dict:
    key = jax.random.key(seed)
    k = jax.random.split(key, 24)
    f32 = jnp.float32
    nrm = lambda kk, shape, s: jax.random.normal(kk, shape, f32) * s
    G, P, C = SSM_GROUPS, SSM_STATE, SSM_GROUP
    lam_im0 = jnp.broadcast_to(math.pi * jnp.arange(P, dtype=f32), (N_SSM_LAYERS, G, P))
    return {
        "x_prompt": nrm(k[0], (BATCH, SEQ, D_MODEL), 1.0),
        "x_sample": nrm(k[1], (DEC_BATCH, DEC_SEQ, D_MODEL), 1.0),
        "state_ssm_re": nrm(k[2], (N_SSM_LAYERS, DEC_BATCH, G, P), 0.1),
        "state_ssm_im": nrm(k[3], (N_SSM_LAYERS, DEC_BATCH, G, P), 0.1),
        "state_pool": nrm(k[4], (N_POOL_LAYERS, DEC_BATCH, POOL_BUF, D_MODEL), 1.0),
        "norm_mix": 1.0 + nrm(k[5], (DEPTH, D_MODEL), 0.02),
        "norm_ffn": 1.0 + nrm(k[6], (DEPTH, D_MODEL), 0.02),
        "ssm_lambda_re": -0.5 + nrm(k[7], (N_SSM_LAYERS, G, P), 0.01),
        "ssm_lambda_im": lam_im0 + nrm(k[8], (N_SSM_LAYERS, G, P), 0.01),
        "ssm_log_step": jax.random.uniform(k[9], (N_SSM_LAYERS, G), f32, math.log(DT_MIN), math.log(DT_MAX)),
        "ssm_b_re": nrm(k[10], (N_SSM_LAYERS, G, P, C), (2 * C) ** -0.5),
        "ssm_b_im": nrm(k[11], (N_SSM_LAYERS, G, P, C), (2 * C) ** -0.5),
        "ssm_c_re": nrm(k[12], (N_SSM_LAYERS, G, C, P), P ** -0.5),
        "ssm_c_im": nrm(k[13], (N_SSM_LAYERS, G, C, P), P ** -0.5),
        "ssm_d": nrm(k[14], (N_SSM_LAYERS, D_MODEL), 1.0),
        "ssm_w_glu": nrm(k[15], (N_SSM_LAYERS, D_MODEL, 2 * D_MODEL), D_MODEL ** -0.5),
        "pool_w": nrm(k[16], (N_POOL_LAYERS, POOL_GROUPS, POOL_WIDTH, POOL_WIDTH), POOL_WIDTH ** -0.5),
        "pool_scale": 1.0 + nrm(k[17], (N_POOL_LAYERS, D_MODEL), 0.02),
        "ffn_w_gate_up": nrm(k[18], (DEPTH, D_MODEL, 2 * D_FF), D_MODEL ** -0.5),
        "ffn_w_down": nrm(k[19], (DEPTH, D_FF, D_MODEL), D_FF ** -0.5),
        "norm_final": 1.0 + nrm(k[20], (D_MODEL,), 0.02),
    }


def reference(x_prompt, x_sample, state_ssm_re, state_ssm_im, state_pool, norm_mix, norm_ffn,
              ssm_lambda_re, ssm_lambda_im, ssm_log_step, ssm_b_re, ssm_b_im, ssm_c_re, ssm_c_im,
              ssm_d, ssm_w_glu, pool_w, pool_scale, ffn_w_gate_up, ffn_w_down, norm_final):
    y_prompt, ssm_re_p, ssm_im_p, pool_p = trunk(
        x_prompt, None, None, None, 0, norm_mix, norm_ffn, ssm_lambda_re, ssm_lambda_im, ssm_log_step,
        ssm_b_re, ssm_b_im, ssm_c_re, ssm_c_im, ssm_d, ssm_w_glu, pool_w, pool_scale,
        ffn_w_gate_up, ffn_w_down, norm_final)
    y_sample, ssm_re_s, ssm_im_s, pool_s = trunk(
        x_sample, state_ssm_re, state_ssm_im, state_pool, PAST_LEN, norm_mix, norm_ffn, ssm_lambda_re,
        ssm_lambda_im, ssm_log_step, ssm_b_re, ssm_b_im, ssm_c_re, ssm_c_im, ssm_d, ssm_w_glu, pool_w,
        pool_scale, ffn_w_gate_up, ffn_w_down, norm_final)
    return (y_prompt, y_sample, ssm_re_p, ssm_im_p, pool_p, ssm_re_s, ssm_im_s, pool_s)
```

```python
import math
import numpy as np
import concourse.bass as bass
import concourse.mybir as mybir
from concourse.bass_utils import run_bass_kernel_spmd
from contextlib import ExitStack

F32 = mybir.dt.float32
F32R = mybir.dt.float32r
BF16 = mybir.dt.bfloat16
AF = mybir.ActivationFunctionType
ALU = mybir.AluOpType

D = 4096
NCH = 32
DFF = 11008
NF = 86
FG = 4
NGRP = 22
NSLOT = 4
TPA, TPB, TS = 527, 512, 64
TA, TB = 592, 576
TPRE = 506
TM = 592
MAGIC = 12582912.0
TWO_PI = 2.0 * math.pi
EPS = 1e-6
GC0 = math.sqrt(2.0 / math.pi)
NT_GLU, NT_FFN, NT_POOL = 128, 21 * 24 + 16, 16
NTILE = NT_GLU + NT_FFN + NT_POOL + NT_FFN
NTOK_IN = 2 * TPRE + TA + TB
WIN = (2, 4, 8, 16)


class Prog:
    def __init__(self):
        self.ops = []
        self.lastw = {}
        self.readers = {}
        self.lastdma = {}

    def add(self, eng, fn, R=(), W=(), dsem=None):
        i = len(self.ops)
        deps = {}
        for r in R:
            j = self.lastw.get(r)
            if j is not None:
                deps[j] = 'raw'
        for w in W:
            j = self.lastw.get(w)
            if j is not None and j not in deps:
                deps[j] = 'waw'
            for j in self.readers.get(w, {}).values():
                if j not in deps:
                    deps[j] = 'war'
        if dsem is not None:
            j = self.lastdma.get(dsem)
            if j is not None:
                deps[j] = 'raw'
            self.lastdma[dsem] = i
        rk = eng if dsem is None else ('dma', i)
        for r in R:
            self.readers.setdefault(r, {})[rk] = i
        for w in W:
            self.lastw[w] = i
            self.readers[w] = {}
        self.ops.append(dict(eng=eng, fn=fn, deps=deps, dsem=dsem, sig=False, force=False))
        return i

    def emit(self, nc, block, es):
        ops = self.ops
        engs = ('pe', 'act', 'dve', 'pool', 'sp')
        for i, o in enumerate(ops):
            keep = []
            for j, kind in o['deps'].items():
                d = ops[j]
                if d['dsem'] is None and d['eng'] == o['eng'] and not o['force']:
                    if kind != 'raw' or o['eng'] == 'pe':
                        continue
                keep.append(j)
                if d['dsem'] is None:
                    d['sig'] = True
            o['keep'] = keep
        cnt = {e: 0 for e in engs}
        dcnt = {}
        for o in ops:
            if o['dsem'] is not None:
                dcnt[o['dsem']] = dcnt.get(o['dsem'], 0) + 16
                o['sv'] = ('d_' + o['dsem'], dcnt[o['dsem']])
            elif o['sig']:
                cnt[o['eng']] += 1
                o['sv'] = ('e_' + o['eng'], cnt[o['eng']])
        self.cnt, self.dcnt = cnt, dcnt
        names = ['e_' + e for e in engs] + ['d_' + k for k in dcnt]
        sems = {n: es.enter_context(nc.semaphore(n)) for n in names}

        def run(engname, e):
            waited = {}
            for o in ops:
                if o['eng'] != engname:
                    continue
                need = {}
                for j in o['keep']:
                    s, v = ops[j]['sv']
                    if need.get(s, 0) < v:
                        need[s] = v
                for s, v in need.items():
                    if waited.get(s, 0) < v:
                        e.wait_ge(sems[s], v)
                        waited[s] = v
                if o['fn'] is None:
                    continue
                ins = o['fn'](e)
                if o['dsem'] is not None:
                    ins.then_inc(sems['d_' + o['dsem']], 16)
                elif o['sig']:
                    ins.then_inc(sems['e_' + engname], 1)

        @block.tensor
        def _(e):
            run('pe', e)

        @block.scalar
        def _(e):
            run('act', e)

        @block.vector
        def _(e):
            run('dve', e)

        @block.gpsimd
        def _(e):
            run('pool', e)

        @block.sync
        def _(e):
            run('sp', e)


DBG = []
STAGE = 0


def build_program(stage=0):
    nc = bass.Bass("TRN2", target_bir_lowering=False)
    del DBG[:]
    dt_in = lambda n, s: nc.dram_tensor(n, s, F32, kind="ExternalInput").ap()
    dt_out = lambda n, s: nc.dram_tensor(n, s, F32, kind="ExternalOutput").ap()
    xin = dt_in("xin", [NTOK_IN, D])
    NW = {0: NTILE, 1: 1, 2: 1, 3: NT_GLU, 4: NT_GLU + NT_FFN, 5: NT_GLU + NT_FFN + NT_POOL, 6: NTILE}[stage]
    wst = dt_in("wst", [NW, 128, 2048])
    vecs_d = dt_in("vecs", [128, 7 * 32])
    ident_d = dt_in("ident", [128, 128])
    kr_d = dt_in("kr", [2, 128, 2 * TM])
    krp_d = dt_in("krp", [2, 128, TM])
    pos_d = dt_in("pos", [128, 2 * TM])
    lam_d = dt_in("lam", [3, 128, 128])
    bt_d = dt_in("bt", [128, 128, 256])
    ct_d = dt_in("ct", [128, 128, 64])
    sst_d = dt_in("sst", [2, 16, 128, 128])
    spool_d = dt_in("spool", [16, 15, D])
    y_d = dt_out("y", [1152, D])
    sp_d = dt_out("sp", [2, 128, 128])
    ss_d = dt_out("ss", [2, 16, 128, 128])
    psn_d = dt_out("psn", [2, 32, 64, 128])
    pso_d = dt_out("pso", [16, 7, D])
    ppn_d = dt_out("ppn", [32, 15, 128])

    es = ExitStack()
    dbg_d = dt_out("dbg", [128, 65536]) if stage else None
    dcur = dict(o=0)
    sb = lambda n, s, d=F32: es.enter_context(nc.sbuf_tensor(n, s, d))
    x = sb("x", [128, NCH, TM])
    xn = sb("xn", [128, NCH, TM], BF16)
    ring = sb("ring", [128, NSLOT, 2048], BF16)
    ident = sb("ident_s", [128, 128])
    onesr = sb("onesr", [128, 128], F32R)
    vecs = sb("vecs_s", [128, 7 * 32])
    kidx = sb("kidx", [128, TM])
    rmask = sb("rmask", [128, TM])
    rstd = sb("rstd", [128, TM])
    prm = sb("prm", [128, 10, 128])
    St = sb("St", [128, 128, 9, 2])
    Zi = sb("Zi", [128, 128, 9, 2])
    hist = sb("hist", [128, NCH, 15])
    ucr = sb("ucr", [128, TM], F32R)
    xr = sb("xr", [128, TM], F32R)
    xi = sb("xi", [128, TM], F32R)
    bt = sb("bt_s", [128, 2, 256], F32R)
    cp = sb("cp", [128, 4, 2, 128], F32R)
    ctraw = sb("ctraw", [128, 1, 64])
    sq = ucr
    h1 = sb("h1", [128, FG, TM], BF16)
    scr = sb("scr", [128, 10 * TM])
    sphc = sb("sphc", [120, 2, 128])
    hst = sb("hst", [64, 4, 128])
    hpt = sb("hpt", [15, 4, 128])
    pst = [es.enter_context(nc.psum_tensor(f"ps{i}", [128, 512], F32)) for i in range(8)]
    block = es.enter_context(nc.Block())

    P = Prog()
    V = lambda c: vecs[:, c:c + 1]
    vcol = lambda k, c: vecs[:, k * 32 + c:k * 32 + c + 1]
    S = lambda i: scr[:, i * TM:(i + 1) * TM]

    P.add('sp', lambda e: e.dma_start(out=ident[:], in_=ident_d), W=['ident'], dsem='c0')
    P.add('sp', lambda e: e.dma_start(out=vecs[:], in_=vecs_d), W=['vecs'], dsem='c1')
    P.add('sp', lambda e: e.dma_start(out=prm[:, 0:3, :], in_=lam_d.rearrange("a p q -> p a q")), W=['prm'], dsem='c2')
    P.add('dve', lambda e: e.memset(scr[:, 0:128], 1.0), W=['scrp'])
    P.add('dve', lambda e: e.tensor_copy(out=onesr[:], in_=scr[:, 0:128]), R=['scrp'], W=['onesr'])
    P.add('dve', lambda e: e.memset(scr[:, 1024:2048], 0.0), W=['scrz'])
    P.add('dve', lambda e: e.tensor_copy(out=cp[:, :, :, :].rearrange("p a b c -> p (a b c)"), in_=scr[:, 1024:2048]), R=['scrz'], W=['cp0', 'cp1', 'cp2', 'cp3'])
    P.add('dve', lambda e: e.memset(St[:], 0.0), W=['St'])
    P.add('dve', lambda e: e.memset(hist[:], 0.0), W=['hist'])

    pr = lambda i: prm[:, i, :]
    T0, T1 = S(0)[:, 0:128], S(1)[:, 0:128]
    T2, T3 = S(2)[:, 0:128], S(3)[:, 0:128]
    R_, W_ = ['prm'], ['prm']
    a = lambda eng, fn, R=(), W=(): P.add(eng, fn, R=list(R) + ['prm', 'scrp'], W=list(W) + ['prm', 'scrp'])
    def ts(out, in0, s1, s2=None, op0=ALU.mult, op1=ALU.add):
        if s2 is None:
            a('dve', lambda e: e.tensor_scalar(out=out, in0=in0, scalar1=s1, scalar2=None, op0=op0))
        else:
            a('dve', lambda e: e.tensor_scalar(out=out, in0=in0, scalar1=s1, scalar2=s2, op0=op0, op1=op1))

    def tt(out, in0, in1, op):
        a('dve', lambda e: e.tensor_tensor(out=out, in0=in0, in1=in1, op=op))

    def stt(out, in0, sc, in1, op0, op1):
        a('dve', lambda e: e.scalar_tensor_tensor(out=out, in0=in0, scalar=sc, in1=in1, op0=op0, op1=op1))

    T4, T5, T6, T7 = S(4)[:, 0:128], S(5)[:, 0:128], S(6)[:, 0:128], S(7)[:, 0:128]
    a('act', lambda e: e.activation(out=T0, in_=pr(2), func=AF.Exp))
    tt(T1, pr(0), T0, ALU.mult)
    tt(T2, pr(1), T0, ALU.mult)
    ts(pr(8), T2, 1.0 / TWO_PI)
    ts(T0, pr(8), MAGIC, None, op0=ALU.add)
    ts(T0, T0, -MAGIC, None, op0=ALU.add)
    tt(T0, pr(8), T0, ALU.subtract)
    ts(T0, T0, math.pi / 2.0)
    tt(T2, T0, T0, ALU.mult)
    ts(T3, T2, 1.0 / 362880.0)
    stt(T3, T3, -1.0 / 5040.0, T2, ALU.add, ALU.mult)
    stt(T3, T3, 1.0 / 120.0, T2, ALU.add, ALU.mult)
    stt(T3, T3, -1.0 / 6.0, T2, ALU.add, ALU.mult)
    stt(T3, T3, 1.0, T0, ALU.add, ALU.mult)
    ts(T4, T2, -1.0 / 3628800.0)
    stt(T4, T4, 1.0 / 40320.0, T2, ALU.add, ALU.mult)
    stt(T4, T4, -1.0 / 720.0, T2, ALU.add, ALU.mult)
    stt(T4, T4, 1.0 / 24.0, T2, ALU.add, ALU.mult)
    stt(T4, T4, -0.5, T2, ALU.add, ALU.mult)
    ts(T4, T4, 1.0, None, op0=ALU.add)
    stt(T5, T3, 2.0, T4, ALU.mult, ALU.mult)
    tt(T6, T3, T3, ALU.mult)
    ts(T6, T6, -2.0, 1.0)
    stt(T3, T5, 2.0, T6, ALU.mult, ALU.mult)
    tt(T4, T5, T5, ALU.mult)
    ts(T4, T4, -2.0)
    ts(T5, T1, 1.0 / 6.0, 1.0)
    tt(T5, T5, T1, ALU.mult)
    ts(T5, T5, 1.0 / 5.0, 1.0)
    tt(T5, T5, T1, ALU.mult)
    ts(T5, T5, 1.0 / 4.0, 1.0)
    tt(T5, T5, T1, ALU.mult)
    ts(T5, T5, 1.0 / 3.0, 1.0)
    tt(T5, T5, T1, ALU.mult)
    ts(T5, T5, 1.0 / 2.0, 1.0)
    tt(T5, T5, T1, ALU.mult)
    ts(pr(3), T5, 1.0, None, op0=ALU.add)
    tt(pr(9), pr(3), T3, ALU.mult)
    tt(T1, pr(3), T4, ALU.mult)
    tt(T1, T1, T5, ALU.add)
    ts(T3, T1, 1.0, None, op0=ALU.add)
    tt(T0, pr(0), pr(0), ALU.mult)
    tt(T2, pr(1), pr(1), ALU.mult)
    tt(T0, T0, T2, ALU.add)
    a('dve', lambda e: e.reciprocal(out=T0, in_=T0))
    tt(T2, T1, pr(0), ALU.mult)
    tt(pr(4), pr(9), pr(1), ALU.mult)
    tt(T2, T2, pr(4), ALU.add)
    tt(pr(4), T2, T0, ALU.mult)
    tt(T2, pr(9), pr(0), ALU.mult)
    tt(T1, T1, pr(1), ALU.mult)
    tt(T2, T2, T1, ALU.subtract)
    tt(pr(5), T2, T0, ALU.mult)
    a('dve', lambda e: e.tensor_copy(out=pr(0), in_=pr(8)))
    a('dve', lambda e: e.tensor_copy(out=pr(1), in_=pr(3)))
    a('dve', lambda e: e.tensor_copy(out=pr(2), in_=T3))
    a('dve', lambda e: e.tensor_copy(out=pr(3), in_=pr(9)))
    a('dve', lambda e: e.tensor_tensor(out=T0, in0=pr(4), in1=pr(4), op=ALU.mult))
    a('dve', lambda e: e.tensor_tensor(out=T1, in0=pr(5), in1=pr(5), op=ALU.mult))
    a('dve', lambda e: e.tensor_tensor(out=T0, in0=T0, in1=T1, op=ALU.add))
    a('dve', lambda e: e.reciprocal(out=T0, in_=T0))
    a('dve', lambda e: e.tensor_tensor(out=pr(6), in0=pr(4), in1=T0, op=ALU.mult))
    a('dve', lambda e: e.tensor_tensor(out=T1, in0=pr(5), in1=T0, op=ALU.mult))
    a('dve', lambda e: e.tensor_scalar(out=pr(7), in0=T1, scalar1=-1.0, scalar2=None, op0=ALU.mult))

    wstate = dict(next_dma=0, next_use=0)
    TOTAL_TILES = 2 * NTILE

    def wtile():
        i = wstate['next_use']
        wstate['next_use'] += 1
        while wstate['next_dma'] < min(TOTAL_TILES, i + NSLOT):
            j = wstate['next_dma']
            wstate['next_dma'] += 1
            sl = j % NSLOT
            P.add('pool', (lambda e, j=j, sl=sl: e.dma_start(out=ring[:, sl, :], in_=wst[(j % NTILE) % NW])),
                  W=[f'ring{sl}'], dsem=f'w{sl}')
        return i % NSLOT

    def nts_of(T):
        h = T // 2
        if h % 2:
            h += 1
        return [(0, h), (h, T)]

    def barrier():
        regs = list(P.lastw.keys())
        i = P.add('sp', (lambda e: e.nop()), R=[], W=regs)
        P.ops[i]['force'] = True
        for k, j in P.lastdma.items():
            P.ops[i]['deps'].setdefault(j, 'raw')

    bank = dict(i=0)

    def nbank(nb=8):
        b = bank['i'] % nb
        bank['i'] += 1
        return b

    def load_x(row0, T):
        stage = scr[:, 0:D]
        t0 = 0
        while t0 < T:
            n = min(128, T - t0)
            P.add('sp', (lambda e, t0=t0, n=n: e.dma_start(out=stage[0:n, :], in_=xin[row0 + t0:row0 + t0 + n, :])),
                  W=['stage'], dsem='ld')
            for c4 in range(8):
                b = nbank()
                for cc in range(4):
                    c = c4 * 4 + cc
                    P.add('pe', (lambda e, b=b, cc=cc, c=c, n=n: e.transpose(
                        out=pst[b][:, cc * 128:cc * 128 + n], in_=stage[0:n, c * 128:(c + 1) * 128],
                        identity=ident[0:n, 0:n])), R=['stage', 'ident'], W=[f'ps{b}'])
                eng = 'act' if c4 % 2 else 'dve'
                if eng == 'act':
                    P.add('act', (lambda e, b=b, c4=c4, t0=t0, n=n: e.activation(
                        out=x[:, c4 * 4:c4 * 4 + 4, t0:t0 + n],
                        in_=pst[b][:, :].rearrange("p (c t) -> p c t", t=128)[:, :, 0:n], func=AF.Copy)),
                        R=[f'ps{b}'], W=[f'x{c}' for c in range(c4 * 4, c4 * 4 + 4)])
                else:
                    P.add('dve', (lambda e, b=b, c4=c4, t0=t0, n=n: e.tensor_copy(
                        out=x[:, c4 * 4:c4 * 4 + 4, t0:t0 + n],
                        in_=pst[b][:, :].rearrange("p (c t) -> p c t", t=128)[:, :, 0:n])),
                        R=[f'ps{b}'], W=[f'x{c}' for c in range(c4 * 4, c4 * 4 + 4)])
            t0 += n

    def rms(T):
        nts = nts_of(T)
        bs = [nbank() for _ in nts]
        for c in range(NCH):
            P.add('act', (lambda e, c=c: e.activation(out=sq[:, 0:T], in_=x[:, c, 0:T], func=AF.Square)),
                  R=[f'x{c}'], W=['ucr'])
            for (lo, hi), b in zip(nts, bs):
                P.add('pe', (lambda e, c=c, lo=lo, hi=hi, b=b: e.matmul(
                    pst[b][:, 0:hi - lo], onesr[:], sq[:, lo:hi], start=(c == 0), stop=(c == NCH - 1))),
                    R=['ucr', 'onesr'], W=[f'ps{b}'])
        for (lo, hi), b in zip(nts, bs):
            P.add('dve', (lambda e, lo=lo, hi=hi, b=b: e.tensor_scalar(
                out=rstd[:, lo:hi], in0=pst[b][:, 0:hi - lo], scalar1=1.0 / D, scalar2=EPS, op0=ALU.mult, op1=ALU.add)),
                R=[f'ps{b}'], W=['rstd'])
        P.add('act', lambda e: e.activation(out=rstd[:, 0:T], in_=rstd[:, 0:T], func=AF.Sqrt), R=['rstd'], W=['rstd'])
        P.add('dve', lambda e: e.reciprocal(out=rstd[:, 0:T], in_=rstd[:, 0:T]), R=['rstd'], W=['rstd'])

    def norm_to_xn(T, gk):
        for c in range(NCH):
            P.add('dve', (lambda e, c=c: e.scalar_tensor_tensor(
                out=xn[:, c, 0:T], in0=x[:, c, 0:T], scalar=vcol(gk, c), in1=rstd[:, 0:T], op0=ALU.mult, op1=ALU.mult)),
                R=[f'x{c}', 'rstd', 'vecs'], W=[f'xn{c}'])

    qdma = dict(n=0)

    def ssm(T, Tp, nseg_s, kcol0, state_only, seq0):
        nts = nts_of(T)
        A1, A2, TSb, TCb, RM, W1, W2, W3 = [S(i)[:, 0:T] for i in range(8)]
        G1, G2 = S(8)[:, 0:T], S(9)[:, 0:T]
        nseg = 1 + nseg_s
        sview = lambda ap: ap[:, Tp:Tp + 8 * nseg_s].rearrange("p (s k) -> p s k", k=8)
        for comp, (c0, s0, c1, s1) in enumerate([(2, 0, 3, 1), (3, 0, 2, 1)]):
            P.add('dve', (lambda e, comp=comp, c0=c0: e.tensor_tensor(
                out=Zi[:, :, 0:nseg, comp], in0=St[:, :, 0:nseg, 0],
                in1=prm[:, c0, :].unsqueeze(2).to_broadcast([128, 128, nseg]), op=ALU.mult)),
                R=['St', 'prm'], W=['Zi'])
            P.add('dve', (lambda e, comp=comp, c1=c1: e.tensor_tensor(
                out=scr[:, 8 * TM:8 * TM + 128 * nseg].rearrange("p (q s) -> p q s", s=nseg), in0=St[:, :, 0:nseg, 1],
                in1=prm[:, c1, :].unsqueeze(2).to_broadcast([128, 128, nseg]), op=ALU.mult)),
                R=['St', 'prm'], W=['G1', 'G2'])
            P.add('dve', (lambda e, comp=comp: e.tensor_tensor(
                out=Zi[:, :, 0:nseg, comp], in0=Zi[:, :, 0:nseg, comp],
                in1=scr[:, 8 * TM:8 * TM + 128 * nseg].rearrange("p (q s) -> p q s", s=nseg),
                op=(ALU.subtract if comp == 0 else ALU.add))), R=['Zi', 'G1', 'G2'], W=['Zi'])
        for c in range(NCH):
            P.add('dve', (lambda e, c=c: e.scalar_tensor_tensor(
                out=ucr[:, 0:T], in0=x[:, c, 0:T], scalar=vcol(0, c), in1=rstd[:, 0:T], op0=ALU.mult, op1=ALU.mult)),
                R=[f'x{c}', 'rstd', 'vecs'], W=['ucr'])
            ybs = [6, 7] if not state_only else []
            for ql in range(4):
                q = 4 * c + ql
                bsl = qdma['n'] % 2
                qdma['n'] += 1
                P.add('pool', (lambda e, q=q, bsl=bsl: e.dma_start(out=bt[:, bsl, :], in_=bt_d[q])),
                      W=[f'bt{bsl}'], dsem=f'bt{bsl}')
                if not state_only:
                    P.add('sp', (lambda e, q=q: e.dma_start(out=ctraw[:, 0, :], in_=ct_d[q])),
                          W=['ctraw'], dsem='ct')
                thq, rq = prm[:, 0, q:q + 1], prm[:, 1, q:q + 1]
                P.add('pool', (lambda e, thq=thq: e.tensor_scalar(out=A1, in0=kidx[:, 0:T], scalar1=thq, scalar2=MAGIC, op0=ALU.mult, op1=ALU.add)),
                      R=['kidx', 'prm'], W=['A1'])
                P.add('pool', (lambda e: e.tensor_scalar(out=A1, in0=A1, scalar1=-MAGIC, scalar2=None, op0=ALU.add)), R=['A1'], W=['A1'])
                P.add('pool', (lambda e, thq=thq: e.tensor_scalar(out=A2, in0=kidx[:, 0:T], scalar1=thq, scalar2=None, op0=ALU.mult)),
                      R=['kidx', 'prm'], W=['A2'])
                P.add('pool', (lambda e: e.tensor_tensor(out=A2, in0=A2, in1=A1, op=ALU.subtract)), R=['A1', 'A2'], W=['A2'])
                P.add('act', (lambda e: e.activation(out=TSb, in_=A2, func=AF.Sin, scale=TWO_PI * 0.999999)), R=['A2'], W=['TS'])
                P.add('act', (lambda e: e.activation(out=TCb, in_=A2, func=AF.Sin, scale=math.pi * 0.999999)), R=['A2'], W=['TC'])
                P.add('pool', (lambda e: e.tensor_tensor(out=TCb, in0=TCb, in1=TCb, op=ALU.mult)), R=['TC'], W=['TC'])
                P.add('pool', (lambda e: e.tensor_scalar(out=TCb, in0=TCb, scalar1=-2.0, scalar2=1.0, op0=ALU.mult, op1=ALU.add)), R=['TC'], W=['TC'])
                P.add('pool', (lambda e, rq=rq: e.tensor_scalar(out=RM, in0=rmask[:, 0:T], scalar1=rq, scalar2=None, op0=ALU.mult)),
                      R=['rmask', 'prm'], W=['RM'])
                bb = [(nbank(6), nbank(6)) for _ in nts]
                for (lo, hi), (b0, b1) in zip(nts, bb):
                    for comp, b in ((0, b0), (1, b1)):
                        P.add('pe', (lambda e, comp=comp, b=b, lo=lo, hi=hi, bsl=bsl: e.matmul(
                            pst[b][:, 0:hi - lo], bt[:, bsl, comp * 128:(comp + 1) * 128], ucr[:, lo:hi], start=True, stop=True)),
                            R=[f'bt{bsl}', 'ucr'], W=[f'ps{b}'])
                for (lo, hi), (b0, b1) in zip(nts, bb):
                    n = hi - lo
                    P.add('dve', (lambda e, lo=lo, hi=hi, b0=b0, n=n: e.tensor_tensor(out=W1[:, lo:hi], in0=pst[b0][:, 0:n], in1=TCb[:, lo:hi], op=ALU.mult)),
                          R=[f'ps{b0}', 'TC'], W=['W1'])
                    P.add('dve', (lambda e, lo=lo, hi=hi, b1=b1, n=n: e.tensor_tensor(out=W2[:, lo:hi], in0=pst[b1][:, 0:n], in1=TSb[:, lo:hi], op=ALU.mult)),
                          R=[f'ps{b1}', 'TS'], W=['W2'])
                    P.add('dve', (lambda e, lo=lo, hi=hi, b1=b1, n=n: e.tensor_tensor(out=W3[:, lo:hi], in0=pst[b1][:, 0:n], in1=TCb[:, lo:hi], op=ALU.mult)),
                          R=[f'ps{b1}', 'TC'], W=['W3'])
                    P.add('dve', (lambda e, lo=lo, hi=hi, b0=b0, n=n: e.tensor_tensor(out=G2[:, lo:hi], in0=pst[b0][:, 0:n], in1=TSb[:, lo:hi], op=ALU.mult)),
                          R=[f'ps{b0}', 'TS'], W=['G2'])
                P.add('dve', (lambda e: e.tensor_tensor(out=W1, in0=W1, in1=W2, op=ALU.add)), R=['W1', 'W2'], W=['W1'])
                P.add('dve', (lambda e: e.tensor_tensor(out=W3, in0=W3, in1=G2, op=ALU.subtract)), R=['W3', 'G2'], W=['W3'])
                for comp, Wb, nm in ((0, W1, 'W1'), (1, W3, 'W3')):
                    P.add('dve', (lambda e, comp=comp, Wb=Wb, q=q: e.tensor_tensor(
                        out=Wb[:, 0:1], in0=Wb[:, 0:1], in1=Zi[:, q, 0:1, comp], op=ALU.add)), R=[nm, 'Zi'], W=[nm])
                    if nseg_s:
                        P.add('dve', (lambda e, comp=comp, Wb=Wb, q=q: e.tensor_tensor(
                            out=sview(Wb)[:, :, 0], in0=sview(Wb)[:, :, 0], in1=Zi[:, q, 1:nseg, comp], op=ALU.add)), R=[nm, 'Zi'], W=[nm])
                P.add('dve', (lambda e: e.tensor_tensor_scan(out=W2, data0=RM, data1=W1, initial=0.0, op0=ALU.mult, op1=ALU.add)),
                      R=['RM', 'W1'], W=['W2'])
                P.add('dve', (lambda e: e.tensor_tensor_scan(out=W1, data0=RM, data1=W3, initial=0.0, op0=ALU.mult, op1=ALU.add)),
                      R=['RM', 'W3', 'W2'], W=['W1'])
                if state_only:
                    cols = [(Tp - 1, Tp, 0)]
                    for (lo, hi, seg) in cols:
                        P.add('dve', (lambda e, lo=lo, hi=hi: e.tensor_tensor(out=W3[:, lo:hi], in0=W2[:, lo:hi], in1=TCb[:, lo:hi], op=ALU.mult)), R=['W2', 'TC'], W=['W3'])
                        P.add('dve', (lambda e, lo=lo, hi=hi: e.tensor_tensor(out=G2[:, lo:hi], in0=W1[:, lo:hi], in1=TSb[:, lo:hi], op=ALU.mult)), R=['W1', 'TS'], W=['G2'])
                        P.add('dve', (lambda e, lo=lo, hi=hi, q=q: e.tensor_tensor(out=St[:, q, 0:1, 0], in0=W3[:, lo:hi], in1=G2[:, lo:hi], op=ALU.subtract)), R=['W3', 'G2'], W=['St'])
                        P.add('dve', (lambda e, lo=lo, hi=hi: e.tensor_tensor(out=W3[:, lo:hi], in0=W2[:, lo:hi], in1=TSb[:, lo:hi], op=ALU.mult)), R=['W2', 'TS'], W=['W3'])
                        P.add('dve', (lambda e, lo=lo, hi=hi: e.tensor_tensor(out=G2[:, lo:hi], in0=W1[:, lo:hi], in1=TCb[:, lo:hi], op=ALU.mult)), R=['W1', 'TC'], W=['G2'])
                        P.add('dve', (lambda e, lo=lo, hi=hi, q=q: e.tensor_tensor(out=St[:, q, 0:1, 1], in0=W3[:, lo:hi], in1=G2[:, lo:hi], op=ALU.add)), R=['W3', 'G2'], W=['St'])
                    continue
                P.add('dve', (lambda e: e.tensor_tensor(out=W3, in0=W2, in1=TCb, op=ALU.mult)), R=['W2', 'TC'], W=['W3'])
                P.add('dve', (lambda e: e.tensor_tensor(out=G2, in0=W1, in1=TSb, op=ALU.mult)), R=['W1', 'TS'], W=['G2'])
                P.add('dve', (lambda e: e.tensor_tensor(out=xr[:, 0:T], in0=W3, in1=G2, op=ALU.subtract)), R=['W3', 'G2'], W=['xr'])
                P.add('dve', (lambda e: e.tensor_tensor(out=W3, in0=W2, in1=TSb, op=ALU.mult)), R=['W2', 'TS'], W=['W3'])
                P.add('dve', (lambda e: e.tensor_tensor(out=G2, in0=W1, in1=TCb, op=ALU.mult)), R=['W1', 'TC'], W=['G2'])
                P.add('dve', (lambda e: e.tensor_tensor(out=xi[:, 0:T], in0=W3, in1=G2, op=ALU.add)), R=['W3', 'G2'], W=['xi'])
                for comp, src, nm in ((0, xr, 'xr'), (1, xi, 'xi')):
                    P.add('act', (lambda e, comp=comp, src=src, q=q: e.activation(out=St[:, q, 0:1, comp], in_=src[:, Tp - 1:Tp], func=AF.Copy)), R=[nm], W=['St'])
                    if nseg_s:
                        P.add('act', (lambda e, comp=comp, src=src, q=q: e.activation(
                            out=St[:, q, 1:nseg, comp], in_=sview(src)[:, :, 7], func=AF.Copy)), R=[nm], W=['St'])
                frq, fiq = prm[:, 4, q:q + 1], prm[:, 5, q:q + 1]
                cpr = cp[:, ql, 0, 32 * ql:32 * ql + 32]
                cpi = cp[:, ql, 1, 32 * ql:32 * ql + 32]
                t32a, t32b = G1[:, 0:32], G1[:, 32:64]
                P.add('dve', (lambda e, frq=frq: e.tensor_scalar(out=t32a, in0=ctraw[:, 0, 0:32], scalar1=frq, scalar2=None, op0=ALU.mult)), R=['ctraw', 'prm'], W=['G1'])
                P.add('dve', (lambda e, fiq=fiq: e.tensor_scalar(out=t32b, in0=ctraw[:, 0, 32:64], scalar1=fiq, scalar2=None, op0=ALU.mult)), R=['ctraw', 'prm'], W=['G1'])
                P.add('dve', (lambda e, cpr=cpr: e.tensor_tensor(out=cpr, in0=t32a, in1=t32b, op=ALU.subtract)), R=['G1'], W=[f'cp{ql}'])
                P.add('dve', (lambda e, fiq=fiq: e.tensor_scalar(out=t32a, in0=ctraw[:, 0, 0:32], scalar1=fiq, scalar2=-1.0, op0=ALU.mult, op1=ALU.mult)), R=['ctraw', 'prm'], W=['G1'])
                P.add('dve', (lambda e, frq=frq: e.tensor_scalar(out=t32b, in0=ctraw[:, 0, 32:64], scalar1=frq, scalar2=None, op0=ALU.mult)), R=['ctraw', 'prm'], W=['G1'])
                P.add('dve', (lambda e, cpi=cpi: e.tensor_tensor(out=cpi, in0=t32a, in1=t32b, op=ALU.subtract)), R=['G1'], W=[f'cp{ql}'])
                for (lo, hi), yb in zip(nts, ybs):
                    P.add('pe', (lambda e, lo=lo, hi=hi, yb=yb, ql=ql: e.matmul(
                        pst[yb][:, 0:hi - lo], cp[:, ql, 0, :], xr[:, lo:hi], start=(ql == 0), stop=False)),
                        R=[f'cp{ql}', 'xr'], W=[f'ps{yb}'])
                    P.add('pe', (lambda e, lo=lo, hi=hi, yb=yb, ql=ql: e.matmul(
                        pst[yb][:, 0:hi - lo], cp[:, ql, 1, :], xi[:, lo:hi], start=False, stop=(ql == 3))),
                        R=[f'cp{ql}', 'xi'], W=[f'ps{yb}'])
            if state_only:
                continue
            for (lo, hi), yb in zip(nts, ybs):
                n = hi - lo
                P.add('dve', (lambda e, lo=lo, hi=hi, yb=yb, n=n, c=c: e.scalar_tensor_tensor(
                    out=W1[:, lo:hi], in0=ucr[:, lo:hi], scalar=vcol(5, c), in1=pst[yb][:, 0:n], op0=ALU.mult, op1=ALU.add)),
                    R=['ucr', 'vecs', f'ps{yb}'], W=['W1'])
            P.add('act', (lambda e: e.activation(out=W2, in_=W1, func=AF.Square)), R=['W1'], W=['W2'])
            P.add('dve', (lambda e: e.tensor_scalar(out=W2, in0=W2, scalar1=0.044715 * 2 * GC0, scalar2=2 * GC0, op0=ALU.mult, op1=ALU.add)), R=['W2'], W=['W2'])
            P.add('dve', (lambda e: e.tensor_tensor(out=W2, in0=W2, in1=W1, op=ALU.mult)), R=['W2', 'W1'], W=['W2'])
            P.add('act', (lambda e: e.activation(out=W3, in_=W2, func=AF.Sigmoid)), R=['W2'], W=['W3'])
            P.add('dve', (lambda e, c=c: e.tensor_tensor(out=xn[:, c, 0:T], in0=W3, in1=W1, op=ALU.mult)), R=['W3', 'W1'], W=[f'xn{c}'])

    def glu(T):
        nts = nts_of(T)
        for m in range(NCH):
            bs = [[nbank() for _ in nts] for _ in range(2)]
            for part in range(2):
                for half in range(2):
                    sl = wtile()
                    for kl in range(16):
                        k = half * 16 + kl
                        for (lo, hi), b in zip(nts, bs[part]):
                            P.add('pe', (lambda e, sl=sl, kl=kl, k=k, lo=lo, hi=hi, b=b: e.matmul(
                                pst[b][:, 0:hi - lo], ring[:, sl, kl * 128:(kl + 1) * 128], xn[:, k, lo:hi],
                                start=(k == 0), stop=(k == 31))), R=[f'ring{sl}', f'xn{k}'], W=[f'ps{b}'])
            for i, (lo, hi) in enumerate(nts):
                n = hi - lo
                b1, b2 = bs[0][i], bs[1][i]
                G = S(8 + i % 2)[:, 0:n]
                gn = f'G{1 + i % 2}'
                P.add('act', (lambda e, b2=b2, n=n, G=G: e.activation(out=G, in_=pst[b2][:, 0:n], func=AF.Sigmoid)), R=[f'ps{b2}'], W=[gn])
                P.add('dve', (lambda e, b1=b1, n=n, G=G: e.tensor_tensor(out=G, in0=pst[b1][:, 0:n], in1=G, op=ALU.mult)), R=[f'ps{b1}', gn], W=[gn])
                P.add('dve', (lambda e, m=m, lo=lo, hi=hi, G=G: e.tensor_tensor(out=x[:, m, lo:hi], in0=x[:, m, lo:hi], in1=G, op=ALU.add)), R=[gn, f'x{m}'], W=[f'x{m}'])

    def ffn(T):
        nts = nts_of(T)
        for g in range(NGRP):
            nf = FG if g < NGRP - 1 else NF - FG * (NGRP - 1)
            for fl in range(nf):
                bs = [[nbank() for _ in nts] for _ in range(2)]
                for part in range(2):
                    for half in range(2):
                        sl = wtile()
                        for kl in range(16):
                            k = half * 16 + kl
                            for (lo, hi), b in zip(nts, bs[part]):
                                P.add('pe', (lambda e, sl=sl, kl=kl, k=k, lo=lo, hi=hi, b=b: e.matmul(
                                    pst[b][:, 0:hi - lo], ring[:, sl, kl * 128:(kl + 1) * 128], xn[:, k, lo:hi],
                                    start=(k == 0), stop=(k == 31))), R=[f'ring{sl}', f'xn{k}'], W=[f'ps{b}'])
                for i, (lo, hi) in enumerate(nts):
                    n = hi - lo
                    bg, bu = bs[0][i], bs[1][i]
                    G = S(8 + i % 2)[:, 0:n]
                    gn = f'G{1 + i % 2}'
                    P.add('act', (lambda e, bg=bg, n=n, G=G: e.activation(out=G, in_=pst[bg][:, 0:n], func=AF.Silu)), R=[f'ps{bg}'], W=[gn])
                    P.add('dve', (lambda e, bu=bu, n=n, G=G, fl=fl, lo=lo, hi=hi: e.tensor_tensor(
                        out=h1[:, fl, lo:hi], in0=pst[bu][:, 0:n], in1=G, op=ALU.mult)), R=[f'ps{bu}', gn], W=[f'h1_{fl}'])
            for mp in range(8):
                sl = wtile()
                for ml in range(4):
                    m = mp * 4 + ml
                    for i, (lo, hi) in enumerate(nts):
                        n = hi - lo
                        b = nbank()
                        for fl in range(nf):
                            P.add('pe', (lambda e, sl=sl, fl=fl, ml=ml, lo=lo, hi=hi, b=b, nf=nf: e.matmul(
                                pst[b][:, 0:hi - lo], ring[:, sl, fl * 512 + ml * 128:fl * 512 + (ml + 1) * 128], h1[:, fl, lo:hi],
                                start=(fl == 0), stop=(fl == nf - 1))), R=[f'ring{sl}', f'h1_{fl}'], W=[f'ps{b}'])
                        P.add('dve', (lambda e, m=m, lo=lo, hi=hi, b=b, n=n: e.tensor_tensor(
                            out=x[:, m, lo:hi], in0=pst[b][:, 0:n], in1=x[:, m, lo:hi], op=ALU.add)), R=[f'ps{b}', f'x{m}'], W=[f'x{m}'])

    def pool_layer(T, Tp, poscol0, first, st_idx):
        nts = nts_of(T)
        E = 15 + Tp
        ES = 8 * 23
        icnt = scr[:, 10 * TM - 4 * TM:10 * TM]
        posr = S(5)
        P.add('sp', (lambda e: e.dma_start(out=posr[:, 0:T], in_=pos_d[:, poscol0:poscol0 + T])), W=['A1p'], dsem='pos')
        for wi, w in enumerate(WIN):
            ic = icnt[:, wi * TM:wi * TM + T]
            P.add('dve', (lambda e, ic=ic, w=w: e.tensor_scalar(out=ic, in0=posr[:, 0:T], scalar1=1.0, scalar2=float(w), op0=ALU.add, op1=ALU.min)), R=['A1p'], W=[f'ic{wi}'])
            P.add('dve', (lambda e, ic=ic: e.reciprocal(out=ic, in_=ic)), R=[f'ic{wi}'], W=[f'ic{wi}'])
        hc = S(0)
        ext = scr[:, TM:TM + E + ES]
        s2 = scr[:, 3 * TM:3 * TM + E + ES]
        stg = scr[:, 5 * TM:5 * TM + 128]
        for c in range(NCH):
            gi = c // 8
            w = WIN[gi]
            P.add('dve', (lambda e, c=c: e.scalar_tensor_tensor(
                out=hc[:, 0:T], in0=x[:, c, 0:T], scalar=vcol(1, c), in1=rstd[:, 0:T], op0=ALU.mult, op1=ALU.mult)),
                R=[f'x{c}', 'rstd', 'vecs'], W=['hc'])
            P.add('act', (lambda e, c=c: e.activation(out=ext[:, 0:15], in_=hist[:, c, :], func=AF.Copy)), R=['hist'], W=['ext'])
            P.add('act', (lambda e: e.activation(out=ext[:, 15:15 + Tp], in_=hc[:, 0:Tp], func=AF.Copy)), R=['hc'], W=['ext'])
            exs = ext[:, E:E + ES].rearrange("p (s k) -> p s k", k=23)
            P.add('act', (lambda e, exs=exs: e.activation(out=exs[:, :, 15:23], in_=hc[:, Tp:Tp + 64].rearrange("p (s k) -> p s k", k=8), func=AF.Copy)), R=['hc'], W=['ext'])
            b = nbank()
            P.add('sp', (lambda e, c=c: e.dma_start(out=sphc[:, c % 2, :], in_=spool_d[st_idx * 8:st_idx * 8 + 8, :, c * 128:(c + 1) * 128].rearrange("s k d -> (s k) d"))),
                  W=[f'sphc{c % 2}'], dsem=f'sph{c % 2}')
            P.add('pe', (lambda e, c=c, b=b: e.transpose(out=pst[b][:, 0:120], in_=sphc[0:120, c % 2, :], identity=ident[0:120, 0:120])),
                  R=[f'sphc{c % 2}', 'ident'], W=[f'ps{b}'])
            P.add('dve', (lambda e, b=b, exs=exs: e.tensor_copy(out=exs[:, :, 0:15], in_=pst[b][:, 0:120].rearrange("p (s k) -> p s k", k=15))), R=[f'ps{b}'], W=['ext'])
            P.add('act', (lambda e, c=c: e.activation(out=hist[:, c, :], in_=hc[:, Tp - 15:Tp], func=AF.Copy)), R=['hc', 'ext'], W=['hist'])
            L = E + ES
            cur, other = ext, s2
            sh = 1
            while sh < w:
                P.add('dve', (lambda e, cur=cur, other=other, sh=sh, L=L: e.tensor_tensor(out=other[:, sh:L], in0=cur[:, sh:L], in1=cur[:, 0:L - sh], op=ALU.add)),
                      R=['ext', 's2'], W=['ext', 's2'])
                if sh > 1 or True:
                    P.add('act', (lambda e, cur=cur, other=other, sh=sh: e.activation(out=other[:, 0:sh], in_=cur[:, 0:sh], func=AF.Copy)), R=['ext', 's2'], W=['ext', 's2'])
                cur, other = other, cur
                sh *= 2
            ic = icnt[:, gi * TM:gi * TM + T]
            pb = S(5)
            P.add('dve', (lambda e, cur=cur, ic=ic: e.tensor_tensor(out=pb[:, 0:Tp], in0=cur[:, 15:15 + Tp], in1=ic[:, 0:Tp], op=ALU.mult)), R=['ext', 's2', f'ic{gi}'], W=['pb'])
            curs = cur[:, E:E + ES].rearrange("p (s k) -> p s k", k=23)
            P.add('dve', (lambda e, curs=curs, ic=ic: e.tensor_tensor(
                out=pb[:, Tp:Tp + 64].rearrange("p (s k) -> p s k", k=8), in0=curs[:, :, 15:23],
                in1=ic[:, Tp:Tp + 64].rearrange("p (s k) -> p s k", k=8), op=ALU.mult)), R=['ext', 's2', f'ic{gi}'], W=['pb'])
            if T > Tp + 64:
                P.add('dve', (lambda e: e.memset(pb[:, Tp + 64:T], 0.0)), W=['pb'])
            P.add('dve', (lambda e, c=c: e.tensor_tensor(out=xn[:, c, 0:T], in0=pb[:, 0:T], in1=hc[:, 0:T], op=ALU.subtract)), R=['pb', 'hc'], W=[f'xn{c}'])
            b = nbank()
            P.add('pe', (lambda e, b=b: e.transpose(out=pst[b][0:64, 0:128], in_=hc[:, Tp:Tp + 64], identity=ident[:, :])), R=['hc', 'ident'], W=[f'ps{b}'])
            P.add('act', (lambda e, b=b, c=c: e.activation(out=hst[0:64, c % 4, :], in_=pst[b][0:64, 0:128], func=AF.Copy)), R=[f'ps{b}'], W=[f'hst{c % 4}'])
            P.add('sp', (lambda e, c=c: e.dma_start(out=psn_d[st_idx, c, :, :], in_=hst[0:64, c % 4, :])), R=[f'hst{c % 4}'], dsem=f'o_psn{c % 4}')
            if not first:
                b = nbank()
                P.add('pe', (lambda e, b=b: e.transpose(out=pst[b][0:15, 0:128], in_=hc[:, Tp - 15:Tp], identity=ident[:, :])), R=['hc', 'ident'], W=[f'ps{b}'])
                P.add('act', (lambda e, b=b, c=c: e.activation(out=hpt[0:15, c % 4, :], in_=pst[b][0:15, 0:128], func=AF.Copy)), R=[f'ps{b}'], W=[f'hpt{c % 4}'])
                P.add('sp', (lambda e, c=c: e.dma_start(out=ppn_d[c, :, :], in_=hpt[0:15, c % 4, :])), R=[f'hpt{c % 4}'], dsem=f'o_ppn{c % 4}')
        for gi in range(4):
            for mpair in range(4):
                sl = wtile()
                for ml in range(2):
                    m = gi * 8 + mpair * 2 + ml
                    for (lo, hi) in nts:
                        n = hi - lo
                        b = nbank()
                        for k in range(8):
                            P.add('pe', (lambda e, sl=sl, ml=ml, k=k, gi=gi, lo=lo, hi=hi, b=b: e.matmul(
                                pst[b][:, 0:hi - lo], ring[:, sl, (ml * 8 + k) * 128:(ml * 8 + k + 1) * 128], xn[:, gi * 8 + k, lo:hi],
                                start=(k == 0), stop=(k == 7))), R=[f'ring{sl}', f'xn{gi * 8 + k}'], W=[f'ps{b}'])
                        P.add('dve', (lambda e, m=m, lo=lo, hi=hi, b=b, n=n: e.scalar_tensor_tensor(
                            out=x[:, m, lo:hi], in0=pst[b][:, 0:n], scalar=vcol(6, m), in1=x[:, m, lo:hi], op0=ALU.mult, op1=ALU.add)),
                            R=[f'ps{b}', f'x{m}', 'vecs'], W=[f'x{m}'])


    def out_y(T, Tp, prow0, srow0, halo):
        ost = scr[:, 0:D]
        segs = []
        t = halo
        while t < Tp:
            n = min(128, Tp - t)
            segs.append((t, n, prow0 + (t - halo)))
            t += n
        segs.append((Tp, 64, srow0))
        for (t0, n, r0) in segs:
            for c in range(NCH):
                hcf = S(8 + c % 2)
                gn = f'G{1 + c % 2}'
                P.add('dve', (lambda e, c=c, t0=t0, n=n, hcf=hcf: e.scalar_tensor_tensor(
                    out=hcf[:, 0:n], in0=x[:, c, t0:t0 + n], scalar=vcol(4, c), in1=rstd[:, t0:t0 + n], op0=ALU.mult, op1=ALU.mult)),
                    R=[f'x{c}', 'rstd', 'vecs'], W=[gn])
                b = nbank()
                P.add('pe', (lambda e, b=b, n=n, hcf=hcf: e.transpose(out=pst[b][0:n, 0:128], in_=hcf[:, 0:n], identity=ident[:, :])), R=[gn, 'ident'], W=[f'ps{b}'])
                P.add('act', (lambda e, b=b, n=n, c=c: e.activation(out=ost[0:n, c * 128:(c + 1) * 128], in_=pst[b][0:n, 0:128], func=AF.Copy)), R=[f'ps{b}'], W=['stage'])
            P.add('sp', (lambda e, n=n, r0=r0: e.dma_start(out=y_d[r0:r0 + n, :], in_=ost[0:n, :])), R=['stage'], dsem='o_y')

    def load_sample_states(seq0):
        tmp = scr[:, 0:2 * 8 * 128].rearrange("p (a s q) -> p a s q", a=2, s=8)
        for a_ in range(2):
            P.add('sp', (lambda e, a_=a_: e.dma_start(out=tmp[:, a_, :, :], in_=sst_d[a_, seq0:seq0 + 8, :, :].rearrange("s q p -> q s p"))), W=['stage'], dsem='ld')
        xs = scr[:, 2048:2048 + 2048].rearrange("p (a s q) -> p a s q", a=2, s=8)
        for a_ in range(2):
            for s in range(8):
                b = nbank()
                P.add('pe', (lambda e, a_=a_, s=s, b=b: e.transpose(out=pst[b][:, 0:128], in_=tmp[:, a_, s, :], identity=ident[:, :])), R=['stage', 'ident'], W=[f'ps{b}'])
                P.add('act', (lambda e, a_=a_, s=s, b=b: e.activation(out=xs[:, a_, s, :], in_=pst[b][:, 0:128], func=AF.Copy)), R=[f'ps{b}'], W=['xs'])
        for s in range(8):
            t_ = scr[:, 4096:4224]
            P.add('dve', (lambda e, s=s: e.tensor_tensor(out=t_, in0=xs[:, 1, s, :], in1=prm[:, 7, :], op=ALU.mult)), R=['xs', 'prm'], W=['A1p'])
            P.add('dve', (lambda e, s=s: e.tensor_tensor(out=St[:, :, 1 + s, 0], in0=xs[:, 0, s, :], in1=prm[:, 6, :], op=ALU.mult)), R=['xs', 'prm'], W=['St'])
            P.add('dve', (lambda e, s=s: e.tensor_tensor(out=St[:, :, 1 + s, 0], in0=St[:, :, 1 + s, 0], in1=t_, op=ALU.subtract)), R=['St', 'A1p'], W=['St'])
            P.add('dve', (lambda e, s=s: e.tensor_tensor(out=t_, in0=xs[:, 1, s, :], in1=prm[:, 6, :], op=ALU.mult)), R=['xs', 'prm', 'St'], W=['A1p'])
            P.add('dve', (lambda e, s=s: e.tensor_tensor(out=St[:, :, 1 + s, 1], in0=xs[:, 0, s, :], in1=prm[:, 7, :], op=ALU.mult)), R=['xs', 'prm'], W=['St'])
            P.add('dve', (lambda e, s=s: e.tensor_tensor(out=St[:, :, 1 + s, 1], in0=St[:, :, 1 + s, 1], in1=t_, op=ALU.add)), R=['St', 'A1p'], W=['St'])

    def store_states(segs, dst_fn, dsem):
        ob = scr[:, 0:2 * 9 * 128].rearrange("p (a s q) -> p a s q", a=2, s=9)
        for seg in segs:
            t_ = scr[:, 2304:2432]
            u_ = scr[:, 2432:2560]
            P.add('dve', (lambda e, seg=seg: e.tensor_tensor(out=t_, in0=St[:, :, seg, 1], in1=prm[:, 5, :], op=ALU.mult)), R=['St', 'prm'], W=['A1p'])
            P.add('dve', (lambda e, seg=seg: e.tensor_tensor(out=u_, in0=St[:, :, seg, 0], in1=prm[:, 4, :], op=ALU.mult)), R=['St', 'prm'], W=['A1q'])
            P.add('dve', (lambda e: e.tensor_tensor(out=u_, in0=u_, in1=t_, op=ALU.subtract)), R=['A1p', 'A1q'], W=['A1q'])
            b = nbank()
            P.add('pe', (lambda e, b=b: e.transpose(out=pst[b][:, 0:128], in_=u_, identity=ident[:, :])), R=['A1q', 'ident'], W=[f'ps{b}'])
            P.add('act', (lambda e, b=b, seg=seg: e.activation(out=ob[:, 0, seg, :], in_=pst[b][:, 0:128], func=AF.Copy)), R=[f'ps{b}'], W=['ob'])
            P.add('dve', (lambda e, seg=seg: e.tensor_tensor(out=t_, in0=St[:, :, seg, 1], in1=prm[:, 4, :], op=ALU.mult)), R=['St', 'prm', 'A1q'], W=['A1p'])
            P.add('dve', (lambda e, seg=seg: e.tensor_tensor(out=u_, in0=St[:, :, seg, 0], in1=prm[:, 5, :], op=ALU.mult)), R=['St', 'prm'], W=['A1q'])
            P.add('dve', (lambda e: e.tensor_tensor(out=u_, in0=u_, in1=t_, op=ALU.add)), R=['A1p', 'A1q'], W=['A1q'])
            b = nbank()
            P.add('pe', (lambda e, b=b: e.transpose(out=pst[b][:, 0:128], in_=u_, identity=ident[:, :])), R=['A1q', 'ident'], W=[f'ps{b}'])
            P.add('act', (lambda e, b=b, seg=seg: e.activation(out=ob[:, 1, seg, :], in_=pst[b][:, 0:128], func=AF.Copy)), R=[f'ps{b}'], W=['ob'])
            for a_ in range(2):
                P.add('sp', (lambda e, a_=a_, seg=seg: e.dma_start(out=dst_fn(a_, seg), in_=ob[:, a_, seg, :])), R=['ob'], dsem=dsem)

    def load_kr(src, col0, T):
        P.add('sp', (lambda e: e.dma_start(out=kidx[:, 0:T], in_=src[0, :, col0:col0 + T])), W=['kidx'], dsem='k0')
        P.add('sp', (lambda e: e.dma_start(out=rmask[:, 0:T], in_=src[1, :, col0:col0 + T])), W=['rmask'], dsem='k1')

    def dump(name, ap, regs, n, b3=None):
        if not stage:
            return
        off = dcur['o']
        dcur['o'] += n
        DBG.append((name, off, n))
        o_ap = dbg_d[:, off:off + n]
        if b3:
            o_ap = o_ap.rearrange("p (a b) -> p a b", b=b3)
        P.add('sp', (lambda e: e.dma_start(out=o_ap, in_=ap)), R=regs, dsem='o_dbg')

    def finish():
        P.add('sp', (lambda e: e.nop()), R=[], W=[], dsem=None)
        last = P.ops[-1]
        for k, j in P.lastdma.items():
            if k.startswith('o_'):
                last['deps'][j] = 'raw'
        P.emit(nc, block, es)
        es.close()
        _CACHE['P'] = P
        return nc

    barrier()
    dump('prm', prm[:, :, :].rearrange("p a b -> p (a b)"), ['prm', 'scrp'], 1280)
    load_kr(krp_d, 0, TPRE)
    for seg in range(2):
        load_x(seg * TPRE, TPRE)
        barrier()
        if seg == 1:
            dump('x_c0', x[:, 0, 0:TPRE], ['x0'], TPRE)
            dump('x_c31', x[:, 31, 0:TPRE], ['x31'], TPRE)
        rms(TPRE)
        if seg == 1:
            dump('rstd', rstd[:, 0:TPRE], ['rstd'], TPRE)
        ssm(TPRE, TPRE, 0, 0, True, 0)
        barrier()
        dump(f'St_pre{seg}', St[:, :, 0, :], ['St'], 256, b3=2)
        if stage == 1 and seg == 1:
            return finish()
    row0 = 2 * TPRE
    for st_idx, (T, Tp, halo) in enumerate(((TA, TPA, 15), (TB, TPB, 0))):
        first = st_idx == 0
        load_kr(kr_d, st_idx * TM, T)
        load_sample_states(st_idx * 8)
        barrier()
        load_x(row0, T)
        barrier()
        rms(T)
        ssm(T, Tp, 8, 0, False, st_idx * 8)
        barrier()
        if stage and first:
            dump('StA', St[:, :, 0, :], ['St'], 256, b3=2)
            for cc in (0, 17, 31):
                P.add('dve', (lambda e, cc=cc: e.tensor_copy(out=S(9), in_=xn[:, cc, :])), R=[f'xn{cc}'], W=['G2'])
                dump(f'xnA_{cc}', S(9), ['G2'], TM)
            P.add('dve', (lambda e: e.tensor_copy(out=S(8), in_=xr[:, :])), R=['xr'], W=['G1'])
            dump('xrA', S(8), ['G1'], TM)
            dump('TSA', S(2), ['TS'], TM)
            dump('TCA', S(3), ['TC'], TM)
            dump('RMA', S(4), ['RM'], TM)
            dump('kidxA', kidx[:, :], ['kidx'], TM)
            dump('ZiA', Zi[:, 127, :, :].rearrange("p a b -> p (a b)"), ['Zi'], 18)
        store_states(range(1, 9), (lambda a_, seg, st_idx=st_idx: ss_d[a_, st_idx * 8 + seg - 1, :, :]), 'o_ss')
        if stage == 2:
            barrier()
            return finish()
        if not first:
            store_states([0], (lambda a_, seg: sp_d[a_, :, :]), 'o_sp')
        barrier()
        def dumpx(tag):
            if stage and first:
                barrier()
                for cc in (0, 17, 31):
                    dump(f'{tag}_{cc}', x[:, cc, :], [f'x{cc}'], TM)
                dump(f'{tag}_rstd', rstd[:, :], ['rstd'], TM)
        glu(T)
        dumpx('x1')
        if stage == 3:
            barrier()
            return finish()
        rms(T)
        norm_to_xn(T, 2)
        ffn(T)
        dumpx('x2')
        if stage == 4:
            barrier()
            return finish()
        rms(T)
        barrier()
        P.add('sp', (lambda e, st_idx=st_idx: e.dma_start(out=pso_d[st_idx * 8:st_idx * 8 + 8, :, :], in_=spool_d[st_idx * 8:st_idx * 8 + 8, 8:15, :])), dsem='o_pso')
        pool_layer(T, Tp, st_idx * TM, first, st_idx)
        barrier()
        dumpx('x3')
        if stage == 5:
            barrier()
            return finish()
        rms(T)
        norm_to_xn(T, 3)
        ffn(T)
        dumpx('x4')
        rms(T)
        barrier()
        out_y(T, Tp, st_idx * 512, 1024 + st_idx * 64, halo)
        barrier()
        if stage == 6:
            return finish()
        row0 += T
    return finish()


_CACHE = {}


def _prep_weights(ssm_w_glu, pool_w, ffn_w_gate_up, ffn_w_down):
    tiles = np.zeros((NTILE, 128, 2048), np.float32)
    W = ssm_w_glu[0].reshape(32, 128, 2, 32, 128)
    W = W.reshape(2, 16, 128, 2, 32, 128)
    tiles[0:NT_GLU] = W.transpose(4, 3, 0, 2, 1, 5).reshape(NT_GLU, 128, 2048)
    base = NT_GLU
    for L in range(2):
        GU = ffn_w_gate_up[L].reshape(2, 16, 128, 2, NF, 128)
        GUt = GU.transpose(4, 3, 0, 2, 1, 5).reshape(NF, 4, 128, 2048)
        DN = ffn_w_down[L].reshape(NF, 128, 8, 512)
        t = base
        for g in range(NGRP):
            nf = FG if g < NGRP - 1 else NF - FG * (NGRP - 1)
            for fl in range(nf):
                tiles[t:t + 4] = GUt[g * FG + fl]
                t += 4
            blk = DN[g * FG:g * FG + nf]
            tiles[t:t + 8, :, 0:nf * 512] = blk.transpose(2, 1, 0, 3).reshape(8, 128, nf * 512)
            t += 8
        assert t == base + NT_FFN
        base = t
        if L == 0:
            PW = pool_w[0].reshape(4, 8, 128, 4, 2, 128)
            tiles[base:base + NT_POOL] = PW.transpose(0, 3, 2, 4, 1, 5).reshape(NT_POOL, 128, 2048)
            base += NT_POOL
    assert base == NTILE
    return tiles


def kernel(x_prompt, x_sample, state_ssm_re, state_ssm_im, state_pool, norm_mix, norm_ffn,
           ssm_lambda_re, ssm_lambda_im, ssm_log_step, ssm_b_re, ssm_b_im, ssm_c_re, ssm_c_im,
           ssm_d, ssm_w_glu, pool_w, pool_scale, ffn_w_gate_up, ffn_w_down, norm_final):
    f = lambda a: np.ascontiguousarray(np.asarray(a, dtype=np.float32))
    x_prompt, x_sample = f(x_prompt), f(x_sample)
    stage = STAGE
    if 'nc' not in _CACHE:
        _CACHE['nc'] = build_program(stage)
    nc = _CACHE['nc']
    wst = _prep_weights(f(ssm_w_glu), f(pool_w), f(ffn_w_gate_up), f(ffn_w_down))
    if stage:
        wst = np.ascontiguousarray(wst[0:{1: 1, 2: 1, 3: NT_GLU, 4: NT_GLU + NT_FFN, 5: NT_GLU + NT_FFN + NT_POOL, 6: NTILE}[stage]])
    fm = lambda v: f(v).reshape(32, 128).T
    vecs = np.concatenate([fm(norm_mix[0]), fm(norm_mix[1]), fm(norm_ffn[0]), fm(norm_ffn[1]), fm(norm_final),
                           fm(ssm_d[0]), fm(pool_scale[0])], axis=1)
    ident = np.eye(128, dtype=np.float32)
    lq = lambda a: f(a).reshape(128, 2, 64).transpose(1, 2, 0).reshape(128, 128)
    lam = np.stack([lq(ssm_lambda_re[0]), lq(ssm_lambda_im[0]),
                    lq(np.repeat(f(ssm_log_step[0])[:, None], 64, axis=1))])
    bt = np.zeros((128, 128, 256), np.float32)
    ct = np.zeros((128, 128, 64), np.float32)
    for comp, (B, C) in enumerate(((f(ssm_b_re[0]), f(ssm_c_re[0])), (f(ssm_b_im[0]), f(ssm_c_im[0])))):
        Bq = B.reshape(128, 2, 64, 16)
        Cq = C.reshape(128, 2, 16, 64)
        for gl in range(2):
            for q4 in range(4):
                qs = np.arange(q4, 128, 4)
                r0 = q4 * 32 + gl * 16
                bt[qs, r0:r0 + 16, comp * 128 + gl * 64:comp * 128 + gl * 64 + 64] = Bq[qs, gl].transpose(0, 2, 1)
            ct[:, gl * 64:gl * 64 + 64, comp * 32 + gl * 16:comp * 32 + gl * 16 + 16] = Cq[:, gl].transpose(0, 2, 1)
    def krow(Tp, T):
        k = np.zeros(TM, np.float32)
        m = np.ones(TM, np.float32)
        k[0:Tp] = np.arange(Tp)
        m[0] = 0.0
        for s in range(8):
            if Tp + 8 * s + 8 <= T:
                k[Tp + 8 * s:Tp + 8 * s + 8] = np.arange(8)
                m[Tp + 8 * s] = 0.0
        return k, m
    kA, mA = krow(TPA, TA)
    kB, mB = krow(TPB, TB)
    kr = np.stack([np.concatenate([kA, kB]), np.concatenate([mA, mB])])
    kr = np.ascontiguousarray(np.broadcast_to(kr[:, None, :], (2, 128, 2 * TM)))
    kP = np.zeros(TM, np.float32); kP[0:TPRE] = np.arange(TPRE)
    mP = np.ones(TM, np.float32); mP[0] = 0.0
    krp = np.ascontiguousarray(np.broadcast_to(np.stack([kP, mP])[:, None, :], (2, 128, TM)))
    in_maps = []
    for core in range(8):
        b, hf = core // 2, core % 2
        xin = np.zeros((NTOK_IN, D), np.float32)
        pos = np.full((2 * TM,), 1.0e4, np.float32)
        if hf == 1:
            xin[3:3 + 1009] = x_prompt[b, 0:1009]
            xin[2 * TPRE:2 * TPRE + 15] = x_prompt[b, 1009:1024]
        p0 = hf * 1024
        a0 = 2 * TPRE
        xin[a0 + 15:a0 + 15 + 512] = x_prompt[b, p0:p0 + 512]
        xin[a0 + TPA:a0 + TPA + 64] = x_sample[core * 16:core * 16 + 8].reshape(64, D)
        b0 = a0 + TA
        xin[b0:b0 + 512] = x_prompt[b, p0 + 512:p0 + 1024]
        xin[b0 + TPB:b0 + TPB + 64] = x_sample[core * 16 + 8:core * 16 + 16].reshape(64, D)
        pos[15:15 + 512] = p0 + np.arange(512)
        pos[TM:TM + 512] = p0 + 512 + np.arange(512)
        sst = np.stack([f(state_ssm_re[0])[core * 16:core * 16 + 16].reshape(16, 128, 128),
                        f(state_ssm_im[0])[core * 16:core * 16 + 16].reshape(16, 128, 128)])
        in_maps.append(dict(xin=xin, wst=wst, vecs=vecs, ident=ident, kr=kr, krp=krp,
                            pos=np.ascontiguousarray(np.broadcast_to(pos[None, :], (128, 2 * TM))),
                            lam=lam, bt=bt, ct=ct, sst=sst,
                            spool=f(state_pool[0])[core * 16:core * 16 + 16]))
    res = run_bass_kernel_spmd(nc, in_maps, core_ids=list(range(8)))
    R = res.results
    if stage:
        _CACHE['dbg'] = (list(DBG), R, dict(xin1=in_maps[1]['xin'], lam=lam, bt=bt, ct=ct, vecs=vecs))
    y_prompt = np.zeros((4, 2048, D), np.float32)
    y_sample = np.zeros((128, 8, D), np.float32)
    sre_p = np.zeros((1, 4, 256, 64), np.float32)
    sim_p = np.zeros((1, 4, 256, 64), np.float32)
    pool_p = np.zeros((1, 4, 15, D), np.float32)
    sre_s = np.zeros((1, 128, 256, 64), np.float32)
    sim_s = np.zeros((1, 128, 256, 64), np.float32)
    pool_s = np.zeros((1, 128, 15, D), np.float32)
    for core in range(8):
        b, hf = core // 2, core % 2
        r = R[core]
        y_prompt[b, hf * 1024:(hf + 1) * 1024] = r["y"][0:1024]
        y_sample[core * 16:(core + 1) * 16] = r["y"][1024:1152].reshape(16, 8, D)
        if hf == 1:
            sre_p[0, b] = r["sp"][0].reshape(256, 64)
            sim_p[0, b] = r["sp"][1].reshape(256, 64)
            pool_p[0, b] = r["ppn"].transpose(1, 0, 2).reshape(15, D)
        sre_s[0, core * 16:(core + 1) * 16] = r["ss"][0].reshape(16, 256, 64)
        sim_s[0, core * 16:(core + 1) * 16] = r["ss"][1].reshape(16, 256, 64)
        pool_s[0, core * 16:(core + 1) * 16, 0:7] = r["pso"]
        pool_s[0, core * 16:(core + 1) * 16, 7:15] = r["psn"].reshape(2, 32, 8, 8, 128).transpose(0, 2, 3, 1, 4).reshape(16, 8, D)
    return (y_prompt, y_sample, sre_p, sim_p, pool_p, sre_s, sim_s, pool_s)
```

```python
import math
import numpy as np
import concourse.bass as bass
import concourse.mybir as mybir
from concourse.bass_utils import run_bass_kernel_spmd
from contextlib import ExitStack

F32 = mybir.dt.float32
F32R = mybir.dt.float32r
BF16 = mybir.dt.bfloat16
AF = mybir.ActivationFunctionType
ALU = mybir.AluOpType

D = 4096
NCH = 32
DFF = 11008
NF = 86
FG = 4
NGRP = 22
NSLOT = 4
TPA, TPB, TS = 527, 512, 64
TA, TB = 592, 576
TPRE = 506
TM = 592
MAGIC = 12582912.0
TWO_PI = 2.0 * math.pi
EPS = 1e-6
GC0 = math.sqrt(2.0 / math.pi)
NT_GLU, NT_FFN, NT_POOL = 128, 21 * 24 + 16, 16
NTILE = NT_GLU + NT_FFN + NT_POOL + NT_FFN
NTOK_IN = 2 * TPRE + TA + TB
WIN = (2, 4, 8, 16)


class Prog:
    def __init__(self):
        self.ops = []
        self.lastw = {}
        self.readers = {}
        self.lastdma = {}

    def add(self, eng, fn, R=(), W=(), dsem=None):
        i = len(self.ops)
        deps = {}
        for r in R:
            j = self.lastw.get(r)
            if j is not None:
                deps[j] = 'raw'
        for w in W:
            j = self.lastw.get(w)
            if j is not None and j not in deps:
                deps[j] = 'waw'
            for j in self.readers.get(w, {}).values():
                if j not in deps:
                    deps[j] = 'war'
        if dsem is not None:
            j = self.lastdma.get(dsem)
            if j is not None:
                deps[j] = 'raw'
            self.lastdma[dsem] = i
        rk = eng if dsem is None else ('dma', i)
        for r in R:
            self.readers.setdefault(r, {})[rk] = i
        for w in W:
            self.lastw[w] = i
            self.readers[w] = {}
        self.ops.append(dict(eng=eng, fn=fn, deps=deps, dsem=dsem, sig=False, force=False))
        return i

    def emit(self, nc, block, es):
        ops = self.ops
        engs = ('pe', 'act', 'dve', 'pool', 'sp')
        for i, o in enumerate(ops):
            keep = []
            for j, kind in o['deps'].items():
                d = ops[j]
                if d['dsem'] is None and d['eng'] == o['eng'] and not o['force']:
                    if kind != 'raw' or o['eng'] == 'pe':
                        continue
                keep.append(j)
                if d['dsem'] is None:
                    d['sig'] = True
            o['keep'] = keep
        cnt = {e: 0 for e in engs}
        dcnt = {}
        for o in ops:
            if o['dsem'] is not None:
                dcnt[o['dsem']] = dcnt.get(o['dsem'], 0) + 16
                o['sv'] = ('d_' + o['dsem'], dcnt[o['dsem']])
            elif o['sig']:
                cnt[o['eng']] += 1
                o['sv'] = ('e_' + o['eng'], cnt[o['eng']])
        self.cnt, self.dcnt = cnt, dcnt
        names = ['e_' + e for e in engs] + ['d_' + k for k in dcnt]
        sems = {n: es.enter_context(nc.semaphore(n)) for n in names}

        def run(engname, e):
            waited = {}
            for o in ops:
                if o['eng'] != engname:
                    continue
                need = {}
                for j in o['keep']:
                    s, v = ops[j]['sv']
                    if need.get(s, 0) < v:
                        need[s] = v
                for s, v in need.items():
                    if waited.get(s, 0) < v:
                        e.wait_ge(sems[s], v)
                        waited[s] = v
                if o['fn'] is None:
                    continue
                ins = o['fn'](e)
                if o['dsem'] is not None:
                    ins.then_inc(sems['d_' + o['dsem']], 16)
                elif o['sig']:
                    ins.then_inc(sems['e_' + engname], 1)

        @block.tensor
        def _(e):
            run('pe', e)

        @block.scalar
        def _(e):
            run('act', e)

        @block.vector
        def _(e):
            run('dve', e)

        @block.gpsimd
        def _(e):
            run('pool', e)

        @block.sync
        def _(e):
            run('sp', e)


DBG = []
STAGE = 0


def build_program(stage=0):
    nc = bass.Bass("TRN2", target_bir_lowering=False)
    del DBG[:]
    dt_in = lambda n, s: nc.dram_tensor(n, s, F32, kind="ExternalInput").ap()
    dt_out = lambda n, s: nc.dram_tensor(n, s, F32, kind="ExternalOutput").ap()
    xin = dt_in("xin", [NTOK_IN, D])
    NW = {0: NTILE, 1: 1, 2: 1, 3: NT_GLU, 4: NT_GLU + NT_FFN, 5: NT_GLU + NT_FFN + NT_POOL, 6: NTILE}[stage]
    wst = dt_in("wst", [NW, 128, 2048])
    vecs_d = dt_in("vecs", [128, 7 * 32])
    ident_d = dt_in("ident", [128, 128])
    kr_d = dt_in("kr", [2, 128, 2 * TM])
    krp_d = dt_in("krp", [2, 128, TM])
    pos_d = dt_in("pos", [128, 2 * TM])
    lam_d = dt_in("lam", [3, 128, 128])
    bt_d = dt_in("bt", [128, 128, 256])
    ct_d = dt_in("ct", [128, 128, 64])
    sst_d = dt_in("sst", [2, 16, 128, 128])
    spool_d = dt_in("spool", [16, 15, D])
    y_d = dt_out("y", [1152, D])
    sp_d = dt_out("sp", [2, 128, 128])
    ss_d = dt_out("ss", [2, 16, 128, 128])
    psn_d = dt_out("psn", [2, 32, 64, 128])
    pso_d = dt_out("pso", [16, 7, D])
    ppn_d = dt_out("ppn", [32, 15, 128])

    es = ExitStack()
    dbg_d = dt_out("dbg", [128, 65536]) if stage else None
    dcur = dict(o=0)
    sb = lambda n, s, d=F32: es.enter_context(nc.sbuf_tensor(n, s, d))
    x = sb("x", [128, NCH, TM])
    xn = sb("xn", [128, NCH, TM], BF16)
    ring = sb("ring", [128, NSLOT, 2048], BF16)
    ident = sb("ident_s", [128, 128])
    onesr = sb("onesr", [128, 128], F32R)
    vecs = sb("vecs_s", [128, 7 * 32])
    kidx = sb("kidx", [128, TM])
    rmask = sb("rmask", [128, TM])
    rstd = sb("rstd", [128, TM])
    prm = sb("prm", [128, 10, 128])
    St = sb("St", [128, 128, 9, 2])
    zq = sb("zq", [128, 9, 2])
    zt = sb("zt", [128, 16])
    hist = sb("hist", [128, NCH, 15])
    ucr = sb("ucr", [128, TM], F32R)
    xr = sb("xr", [128, TM], F32R)
    xi = sb("xi", [128, TM], F32R)
    bt = sb("bt_s", [128, 2, 256], F32R)
    cp = sb("cp", [128, 4, 2, 128], F32R)
    ctraw = sb("ctraw", [128, 1, 64])
    sq = ucr
    h1 = sb("h1", [128, FG, TM], BF16)
    scr = sb("scr", [128, 13 * TM])
    sphc = sb("sphc", [120, 2, 128])
    hst = sb("hst", [64, 4, 128])
    hpt = sb("hpt", [15, 4, 128])
    pst = [es.enter_context(nc.psum_tensor(f"ps{i}", [128, 512], F32)) for i in range(8)]
    block = es.enter_context(nc.Block())

    P = Prog()
    V = lambda c: vecs[:, c:c + 1]
    vcol = lambda k, c: vecs[:, k * 32 + c:k * 32 + c + 1]
    S = lambda i: scr[:, i * TM:(i + 1) * TM]

    P.add('sp', lambda e: e.dma_start(out=ident[:], in_=ident_d), W=['ident'], dsem='c0')
    P.add('sp', lambda e: e.dma_start(out=vecs[:], in_=vecs_d), W=['vecs'], dsem='c1')
    P.add('sp', lambda e: e.dma_start(out=prm[:, 0:3, :], in_=lam_d.rearrange("a p q -> p a q")), W=['prm'], dsem='c2')
    P.add('dve', lambda e: e.memset(scr[:, 0:128], 1.0), W=['scrp'])
    P.add('dve', lambda e: e.tensor_copy(out=onesr[:], in_=scr[:, 0:128]), R=['scrp'], W=['onesr'])
    P.add('dve', lambda e: e.memset(scr[:, 1024:2048], 0.0), W=['scrz'])
    P.add('dve', lambda e: e.tensor_copy(out=cp[:, :, :, :].rearrange("p a b c -> p (a b c)"), in_=scr[:, 1024:2048]), R=['scrz'], W=['cp0', 'cp1', 'cp2', 'cp3'])
    P.add('dve', lambda e: e.memset(St[:], 0.0), W=['St'])
    P.add('dve', lambda e: e.memset(hist[:], 0.0), W=['hist'])

    pr = lambda i: prm[:, i, :]
    T0, T1 = S(0)[:, 0:128], S(1)[:, 0:128]
    T2, T3 = S(2)[:, 0:128], S(3)[:, 0:128]
    R_, W_ = ['prm'], ['prm']
    a = lambda eng, fn, R=(), W=(): P.add(eng, fn, R=list(R) + ['prm', 'scrp'], W=list(W) + ['prm', 'scrp'])
    def ts(out, in0, s1, s2=None, op0=ALU.mult, op1=ALU.add):
        if s2 is None:
            a('dve', lambda e: e.tensor_scalar(out=out, in0=in0, scalar1=s1, scalar2=None, op0=op0))
        else:
            a('dve', lambda e: e.tensor_scalar(out=out, in0=in0, scalar1=s1, scalar2=s2, op0=op0, op1=op1))

    def tt(out, in0, in1, op):
        a('dve', lambda e: e.tensor_tensor(out=out, in0=in0, in1=in1, op=op))

    def stt(out, in0, sc, in1, op0, op1):
        a('dve', lambda e: e.scalar_tensor_tensor(out=out, in0=in0, scalar=sc, in1=in1, op0=op0, op1=op1))

    T4, T5, T6, T7 = S(4)[:, 0:128], S(5)[:, 0:128], S(6)[:, 0:128], S(7)[:, 0:128]
    a('act', lambda e: e.activation(out=T0, in_=pr(2), func=AF.Exp))
    tt(T1, pr(0), T0, ALU.mult)
    tt(T2, pr(1), T0, ALU.mult)
    ts(pr(8), T2, 1.0 / TWO_PI)
    ts(T0, pr(8), MAGIC, None, op0=ALU.add)
    ts(T0, T0, -MAGIC, None, op0=ALU.add)
    tt(T0, pr(8), T0, ALU.subtract)
    ts(T0, T0, math.pi / 2.0)
    tt(T2, T0, T0, ALU.mult)
    ts(T3, T2, 1.0 / 362880.0)
    stt(T3, T3, -1.0 / 5040.0, T2, ALU.add, ALU.mult)
    stt(T3, T3, 1.0 / 120.0, T2, ALU.add, ALU.mult)
    stt(T3, T3, -1.0 / 6.0, T2, ALU.add, ALU.mult)
    stt(T3, T3, 1.0, T0, ALU.add, ALU.mult)
    ts(T4, T2, -1.0 / 3628800.0)
    stt(T4, T4, 1.0 / 40320.0, T2, ALU.add, ALU.mult)
    stt(T4, T4, -1.0 / 720.0, T2, ALU.add, ALU.mult)
    stt(T4, T4, 1.0 / 24.0, T2, ALU.add, ALU.mult)
    stt(T4, T4, -0.5, T2, ALU.add, ALU.mult)
    ts(T4, T4, 1.0, None, op0=ALU.add)
    stt(T5, T3, 2.0, T4, ALU.mult, ALU.mult)
    tt(T6, T3, T3, ALU.mult)
    ts(T6, T6, -2.0, 1.0)
    stt(T3, T5, 2.0, T6, ALU.mult, ALU.mult)
    tt(T4, T5, T5, ALU.mult)
    ts(T4, T4, -2.0)
    ts(T5, T1, 1.0 / 6.0, 1.0)
    tt(T5, T5, T1, ALU.mult)
    ts(T5, T5, 1.0 / 5.0, 1.0)
    tt(T5, T5, T1, ALU.mult)
    ts(T5, T5, 1.0 / 4.0, 1.0)
    tt(T5, T5, T1, ALU.mult)
    ts(T5, T5, 1.0 / 3.0, 1.0)
    tt(T5, T5, T1, ALU.mult)
    ts(T5, T5, 1.0 / 2.0, 1.0)
    tt(T5, T5, T1, ALU.mult)
    ts(pr(3), T5, 1.0, None, op0=ALU.add)
    tt(pr(9), pr(3), T3, ALU.mult)
    tt(T1, pr(3), T4, ALU.mult)
    tt(T1, T1, T5, ALU.add)
    ts(T3, T1, 1.0, None, op0=ALU.add)
    tt(T0, pr(0), pr(0), ALU.mult)
    tt(T2, pr(1), pr(1), ALU.mult)
    tt(T0, T0, T2, ALU.add)
    a('dve', lambda e: e.reciprocal(out=T0, in_=T0))
    tt(T2, T1, pr(0), ALU.mult)
    tt(pr(4), pr(9), pr(1), ALU.mult)
    tt(T2, T2, pr(4), ALU.add)
    tt(pr(4), T2, T0, ALU.mult)
    tt(T2, pr(9), pr(0), ALU.mult)
    tt(T1, T1, pr(1), ALU.mult)
    tt(T2, T2, T1, ALU.subtract)
    tt(pr(5), T2, T0, ALU.mult)
    a('dve', lambda e: e.tensor_copy(out=pr(0), in_=pr(8)))
    a('dve', lambda e: e.tensor_copy(out=pr(1), in_=pr(3)))
    a('dve', lambda e: e.tensor_copy(out=pr(2), in_=T3))
    a('dve', lambda e: e.tensor_copy(out=pr(3), in_=pr(9)))
    a('dve', lambda e: e.tensor_tensor(out=T0, in0=pr(4), in1=pr(4), op=ALU.mult))
    a('dve', lambda e: e.tensor_tensor(out=T1, in0=pr(5), in1=pr(5), op=ALU.mult))
    a('dve', lambda e: e.tensor_tensor(out=T0, in0=T0, in1=T1, op=ALU.add))
    a('dve', lambda e: e.reciprocal(out=T0, in_=T0))
    a('dve', lambda e: e.tensor_tensor(out=pr(6), in0=pr(4), in1=T0, op=ALU.mult))
    a('dve', lambda e: e.tensor_tensor(out=T1, in0=pr(5), in1=T0, op=ALU.mult))
    a('dve', lambda e: e.tensor_scalar(out=pr(7), in0=T1, scalar1=-1.0, scalar2=None, op0=ALU.mult))

    wstate = dict(next_dma=0, next_use=0)
    TOTAL_TILES = 2 * NTILE

    def wtile():
        i = wstate['next_use']
        wstate['next_use'] += 1
        while wstate['next_dma'] < min(TOTAL_TILES, i + NSLOT):
            j = wstate['next_dma']
            wstate['next_dma'] += 1
            sl = j % NSLOT
            P.add('pool', (lambda e, j=j, sl=sl: e.dma_start(out=ring[:, sl, :], in_=wst[(j % NTILE) % NW])),
                  W=[f'ring{sl}'], dsem=f'w{sl}')
        return i % NSLOT

    def nts_of(T):
        h = T // 2
        if h % 2:
            h += 1
        return [(0, h), (h, T)]

    def barrier():
        regs = list(P.lastw.keys())
        i = P.add('sp', (lambda e: e.nop()), R=[], W=regs)
        P.ops[i]['force'] = True
        for k, j in P.lastdma.items():
            P.ops[i]['deps'].setdefault(j, 'raw')

    bank = dict(i=0)

    def nbank(nb=8):
        b = bank['i'] % nb
        bank['i'] += 1
        return b

    def load_x(row0, T):
        stage = scr[:, 0:D]
        t0 = 0
        while t0 < T:
            n = min(128, T - t0)
            P.add('sp', (lambda e, t0=t0, n=n: e.dma_start(out=stage[0:n, :], in_=xin[row0 + t0:row0 + t0 + n, :])),
                  W=['stage'], dsem='ld')
            for c4 in range(8):
                b = nbank()
                for cc in range(4):
                    c = c4 * 4 + cc
                    P.add('pe', (lambda e, b=b, cc=cc, c=c, n=n: e.transpose(
                        out=pst[b][:, cc * 128:cc * 128 + n], in_=stage[0:n, c * 128:(c + 1) * 128],
                        identity=ident[0:n, 0:n])), R=['stage', 'ident'], W=[f'ps{b}'])
                eng = 'act' if c4 % 2 else 'dve'
                if eng == 'act':
                    P.add('act', (lambda e, b=b, c4=c4, t0=t0, n=n: e.activation(
                        out=x[:, c4 * 4:c4 * 4 + 4, t0:t0 + n],
                        in_=pst[b][:, :].rearrange("p (c t) -> p c t", t=128)[:, :, 0:n], func=AF.Copy)),
                        R=[f'ps{b}'], W=[f'x{c}' for c in range(c4 * 4, c4 * 4 + 4)])
                else:
                    P.add('dve', (lambda e, b=b, c4=c4, t0=t0, n=n: e.tensor_copy(
                        out=x[:, c4 * 4:c4 * 4 + 4, t0:t0 + n],
                        in_=pst[b][:, :].rearrange("p (c t) -> p c t", t=128)[:, :, 0:n])),
                        R=[f'ps{b}'], W=[f'x{c}' for c in range(c4 * 4, c4 * 4 + 4)])
            t0 += n

    def rms(T):
        nts = nts_of(T)
        bs = [nbank() for _ in nts]
        for c in range(NCH):
            P.add('act', (lambda e, c=c: e.activation(out=sq[:, 0:T], in_=x[:, c, 0:T], func=AF.Square)),
                  R=[f'x{c}'], W=['ucr'])
            for (lo, hi), b in zip(nts, bs):
                P.add('pe', (lambda e, c=c, lo=lo, hi=hi, b=b: e.matmul(
                    pst[b][:, 0:hi - lo], onesr[:], sq[:, lo:hi], start=(c == 0), stop=(c == NCH - 1))),
                    R=['ucr', 'onesr'], W=[f'ps{b}'])
        for (lo, hi), b in zip(nts, bs):
            P.add('dve', (lambda e, lo=lo, hi=hi, b=b: e.tensor_scalar(
                out=rstd[:, lo:hi], in0=pst[b][:, 0:hi - lo], scalar1=1.0 / D, scalar2=EPS, op0=ALU.mult, op1=ALU.add)),
                R=[f'ps{b}'], W=['rstd'])
        P.add('act', lambda e: e.activation(out=rstd[:, 0:T], in_=rstd[:, 0:T], func=AF.Sqrt), R=['rstd'], W=['rstd'])
        P.add('dve', lambda e: e.reciprocal(out=rstd[:, 0:T], in_=rstd[:, 0:T]), R=['rstd'], W=['rstd'])

    def norm_to_xn(T, gk):
        for c in range(NCH):
            P.add('dve', (lambda e, c=c: e.scalar_tensor_tensor(
                out=xn[:, c, 0:T], in0=x[:, c, 0:T], scalar=vcol(gk, c), in1=rstd[:, 0:T], op0=ALU.mult, op1=ALU.mult)),
                R=[f'x{c}', 'rstd', 'vecs'], W=[f'xn{c}'])

    qdma = dict(n=0)

    def ssm(T, Tp, nseg_s, kcol0, state_only, seq0):
        nts = nts_of(T)
        A1, A2 = S(0)[:, 0:T], S(1)[:, 0:T]
        TSs = [S(2)[:, 0:T], S(10)[:, 0:T]]
        TCs = [S(3)[:, 0:T], S(11)[:, 0:T]]
        RMs = [S(4)[:, 0:T], S(12)[:, 0:T]]
        W1, W2, W3 = S(5)[:, 0:T], S(6)[:, 0:T], S(7)[:, 0:T]
        G1, G2 = S(8)[:, 0:T], S(9)[:, 0:T]
        nseg = 1 + nseg_s
        sview = lambda ap: ap[:, Tp:Tp + 8 * nseg_s].rearrange("p (s k) -> p s k", k=8)

        def tablegen(q):
            p = q % 2
            TSb, TCb, RM = TSs[p], TCs[p], RMs[p]
            thq, rq = prm[:, 0, q:q + 1], prm[:, 1, q:q + 1]
            P.add('pool', (lambda e: e.tensor_scalar(out=A1, in0=kidx[:, 0:T], scalar1=thq, scalar2=MAGIC, op0=ALU.mult, op1=ALU.add)),
                  R=['kidx', 'prm'], W=['A1'])
            P.add('pool', (lambda e: e.tensor_scalar(out=A1, in0=A1, scalar1=-MAGIC, scalar2=1.0, op0=ALU.add, op1=ALU.mult)), R=['A1'], W=['A1'])
            P.add('pool', (lambda e: e.tensor_scalar(out=A2, in0=kidx[:, 0:T], scalar1=thq, scalar2=0.0, op0=ALU.mult, op1=ALU.add)),
                  R=['kidx', 'prm'], W=['A2'])
            P.add('pool', (lambda e: e.tensor_tensor(out=A2, in0=A2, in1=A1, op=ALU.subtract)), R=['A1', 'A2'], W=['A2'])
            P.add('act', (lambda e: e.activation(out=TSb, in_=A2, func=AF.Sin, scale=TWO_PI * 0.999999)), R=['A2'], W=[f'TS{p}'])
            P.add('act', (lambda e: e.activation(out=TCb, in_=A2, func=AF.Sin, scale=math.pi * 0.999999)), R=['A2'], W=[f'TC{p}'])
            P.add('pool', (lambda e: e.tensor_tensor(out=TCb, in0=TCb, in1=TCb, op=ALU.mult)), R=[f'TC{p}'], W=[f'TC{p}'])
            P.add('pool', (lambda e: e.tensor_scalar(out=TCb, in0=TCb, scalar1=-2.0, scalar2=1.0, op0=ALU.mult, op1=ALU.add)), R=[f'TC{p}'], W=[f'TC{p}'])
            P.add('pool', (lambda e: e.tensor_scalar(out=RM, in0=rmask[:, 0:T], scalar1=rq, scalar2=0.0, op0=ALU.mult, op1=ALU.add)),
                  R=['rmask', 'prm'], W=[f'RM{p}'])

        tablegen(0)
        for c in range(NCH):
            P.add('dve', (lambda e, c=c: e.scalar_tensor_tensor(
                out=ucr[:, 0:T], in0=x[:, c, 0:T], scalar=vcol(0, c), in1=rstd[:, 0:T], op0=ALU.mult, op1=ALU.mult)),
                R=[f'x{c}', 'rstd', 'vecs'], W=['ucr'])
            ybs = [6, 7] if not state_only else []
            for ql in range(4):
                q = 4 * c + ql
                p = q % 2
                TSb, TCb, RM = TSs[p], TCs[p], RMs[p]
                tsn, tcn, rmn = f'TS{p}', f'TC{p}', f'RM{p}'
                bsl = qdma['n'] % 2
                qdma['n'] += 1
                P.add('pool', (lambda e, q=q, bsl=bsl: e.dma_start(out=bt[:, bsl, :], in_=bt_d[q])),
                      W=[f'bt{bsl}'], dsem=f'bt{bsl}')
                if not state_only:
                    P.add('sp', (lambda e, q=q: e.dma_start(out=ctraw[:, 0, :], in_=ct_d[q])),
                          W=['ctraw'], dsem='ct')
                if q + 1 < 128:
                    tablegen(q + 1)
                arq, aiq = prm[:, 2, q:q + 1], prm[:, 3, q:q + 1]
                P.add('dve', (lambda e, q=q, aiq=aiq: e.tensor_scalar(out=zt[:, 0:nseg], in0=St[:, q, 0:nseg, 1], scalar1=aiq, scalar2=None, op0=ALU.mult)), R=['St', 'prm'], W=['zt'])
                P.add('dve', (lambda e, q=q, arq=arq: e.scalar_tensor_tensor(out=zq[:, 0:nseg, 0], in0=St[:, q, 0:nseg, 0], scalar=arq, in1=zt[:, 0:nseg], op0=ALU.mult, op1=ALU.subtract)), R=['St', 'prm', 'zt'], W=['zq'])
                P.add('dve', (lambda e, q=q, arq=arq: e.tensor_scalar(out=zt[:, 0:nseg], in0=St[:, q, 0:nseg, 1], scalar1=arq, scalar2=None, op0=ALU.mult)), R=['St', 'prm', 'zq'], W=['zt'])
                P.add('dve', (lambda e, q=q, aiq=aiq: e.scalar_tensor_tensor(out=zq[:, 0:nseg, 1], in0=St[:, q, 0:nseg, 0], scalar=aiq, in1=zt[:, 0:nseg], op0=ALU.mult, op1=ALU.add)), R=['St', 'prm', 'zt'], W=['zq'])
                bb = [(nbank(6), nbank(6)) for _ in nts]
                for (lo, hi), (b0, b1) in zip(nts, bb):
                    for comp, b in ((0, b0), (1, b1)):
                        P.add('pe', (lambda e, comp=comp, b=b, lo=lo, hi=hi, bsl=bsl: e.matmul(
                            pst[b][:, 0:hi - lo], bt[:, bsl, comp * 128:(comp + 1) * 128], ucr[:, lo:hi], start=True, stop=True)),
                            R=[f'bt{bsl}', 'ucr'], W=[f'ps{b}'])
                for (lo, hi), (b0, b1) in zip(nts, bb):
                    n = hi - lo
                    P.add('dve', (lambda e, lo=lo, hi=hi, b0=b0, n=n, TCb=TCb: e.tensor_tensor(out=W1[:, lo:hi], in0=pst[b0][:, 0:n], in1=TCb[:, lo:hi], op=ALU.mult)),
                          R=[f'ps{b0}', tcn], W=['W1'])
                    P.add('dve', (lambda e, lo=lo, hi=hi, b1=b1, n=n, TSb=TSb: e.tensor_tensor(out=W2[:, lo:hi], in0=pst[b1][:, 0:n], in1=TSb[:, lo:hi], op=ALU.mult)),
                          R=[f'ps{b1}', tsn], W=['W2'])
                    P.add('dve', (lambda e, lo=lo, hi=hi, b1=b1, n=n, TCb=TCb: e.tensor_tensor(out=W3[:, lo:hi], in0=pst[b1][:, 0:n], in1=TCb[:, lo:hi], op=ALU.mult)),
                          R=[f'ps{b1}', tcn], W=['W3'])
                    P.add('dve', (lambda e, lo=lo, hi=hi, b0=b0, n=n, TSb=TSb: e.tensor_tensor(out=G2[:, lo:hi], in0=pst[b0][:, 0:n], in1=TSb[:, lo:hi], op=ALU.mult)),
                          R=[f'ps{b0}', tsn], W=['G2'])
                P.add('dve', (lambda e: e.tensor_tensor(out=W1, in0=W1, in1=W2, op=ALU.add)), R=['W1', 'W2'], W=['W1'])
                P.add('dve', (lambda e: e.tensor_tensor(out=W3, in0=W3, in1=G2, op=ALU.subtract)), R=['W3', 'G2'], W=['W3'])
                for comp, Wb, nm in ((0, W1, 'W1'), (1, W3, 'W3')):
                    P.add('dve', (lambda e, comp=comp, Wb=Wb: e.tensor_tensor(
                        out=Wb[:, 0:1], in0=Wb[:, 0:1], in1=zq[:, 0:1, comp], op=ALU.add)), R=[nm, 'zq'], W=[nm])
                    if nseg_s:
                        P.add('dve', (lambda e, comp=comp, Wb=Wb: e.tensor_tensor(
                            out=sview(Wb)[:, :, 0], in0=sview(Wb)[:, :, 0], in1=zq[:, 1:nseg, comp], op=ALU.add)), R=[nm, 'zq'], W=[nm])
                P.add('dve', (lambda e, RM=RM: e.tensor_tensor_scan(out=W2, data0=RM, data1=W1, initial=0.0, op0=ALU.mult, op1=ALU.add)),
                      R=[rmn, 'W1'], W=['W2'])
                P.add('dve', (lambda e, RM=RM: e.tensor_tensor_scan(out=W1, data0=RM, data1=W3, initial=0.0, op0=ALU.mult, op1=ALU.add)),
                      R=[rmn, 'W3', 'W2'], W=['W1'])
                if state_only:
                    lo, hi = Tp - 1, Tp
                    P.add('dve', (lambda e, TCb=TCb: e.tensor_tensor(out=W3[:, lo:hi], in0=W2[:, lo:hi], in1=TCb[:, lo:hi], op=ALU.mult)), R=['W2', tcn], W=['W3'])
                    P.add('dve', (lambda e, TSb=TSb: e.tensor_tensor(out=G2[:, lo:hi], in0=W1[:, lo:hi], in1=TSb[:, lo:hi], op=ALU.mult)), R=['W1', tsn], W=['G2'])
                    P.add('dve', (lambda e, q=q: e.tensor_tensor(out=St[:, q, 0:1, 0], in0=W3[:, lo:hi], in1=G2[:, lo:hi], op=ALU.subtract)), R=['W3', 'G2'], W=['St'])
                    P.add('dve', (lambda e, TSb=TSb: e.tensor_tensor(out=W3[:, lo:hi], in0=W2[:, lo:hi], in1=TSb[:, lo:hi], op=ALU.mult)), R=['W2', tsn], W=['W3'])
                    P.add('dve', (lambda e, TCb=TCb: e.tensor_tensor(out=G2[:, lo:hi], in0=W1[:, lo:hi], in1=TCb[:, lo:hi], op=ALU.mult)), R=['W1', tcn], W=['G2'])
                    P.add('dve', (lambda e, q=q: e.tensor_tensor(out=St[:, q, 0:1, 1], in0=W3[:, lo:hi], in1=G2[:, lo:hi], op=ALU.add)), R=['W3', 'G2'], W=['St'])
                    continue
                P.add('pool', (lambda e, TCb=TCb: e.tensor_tensor(out=A1, in0=W2, in1=TCb, op=ALU.mult)), R=['W2', tcn], W=['A1'])
                P.add('pool', (lambda e, TSb=TSb: e.tensor_tensor(out=A2, in0=W1, in1=TSb, op=ALU.mult)), R=['W1', tsn], W=['A2'])
                P.add('pool', (lambda e: e.tensor_tensor(out=xr[:, 0:T], in0=A1, in1=A2, op=ALU.subtract)), R=['A1', 'A2'], W=['xr'])
                P.add('pool', (lambda e, TSb=TSb: e.tensor_tensor(out=A1, in0=W2, in1=TSb, op=ALU.mult)), R=['W2', tsn], W=['A1'])
                P.add('pool', (lambda e, TCb=TCb: e.tensor_tensor(out=A2, in0=W1, in1=TCb, op=ALU.mult)), R=['W1', tcn], W=['A2'])
                P.add('pool', (lambda e: e.tensor_tensor(out=xi[:, 0:T], in0=A1, in1=A2, op=ALU.add)), R=['A1', 'A2'], W=['xi'])
                for comp, src, nm in ((0, xr, 'xr'), (1, xi, 'xi')):
                    P.add('act', (lambda e, comp=comp, src=src, q=q: e.activation(out=St[:, q, 0:1, comp], in_=src[:, Tp - 1:Tp], func=AF.Copy)), R=[nm], W=['St'])
                    if nseg_s:
                        P.add('act', (lambda e, comp=comp, src=src, q=q: e.activation(
                            out=St[:, q, 1:nseg, comp], in_=sview(src)[:, :, 7], func=AF.Copy)), R=[nm], W=['St'])
                frq, fiq = prm[:, 4, q:q + 1], prm[:, 5, q:q + 1]
                cpr = cp[:, ql, 0, 32 * ql:32 * ql + 32]
                cpi = cp[:, ql, 1, 32 * ql:32 * ql + 32]
                t32a, t32b = G1[:, 0:32], G1[:, 32:64]
                P.add('dve', (lambda e, frq=frq: e.tensor_scalar(out=t32a, in0=ctraw[:, 0, 0:32], scalar1=frq, scalar2=None, op0=ALU.mult)), R=['ctraw', 'prm'], W=['G1'])
                P.add('dve', (lambda e, fiq=fiq: e.tensor_scalar(out=t32b, in0=ctraw[:, 0, 32:64], scalar1=fiq, scalar2=None, op0=ALU.mult)), R=['ctraw', 'prm'], W=['G1'])
                P.add('dve', (lambda e, cpr=cpr: e.tensor_tensor(out=cpr, in0=t32a, in1=t32b, op=ALU.subtract)), R=['G1'], W=[f'cp{ql}'])
                P.add('dve', (lambda e, fiq=fiq: e.tensor_scalar(out=t32a, in0=ctraw[:, 0, 0:32], scalar1=fiq, scalar2=-1.0, op0=ALU.mult, op1=ALU.mult)), R=['ctraw', 'prm'], W=['G1'])
                P.add('dve', (lambda e, frq=frq: e.tensor_scalar(out=t32b, in0=ctraw[:, 0, 32:64], scalar1=frq, scalar2=None, op0=ALU.mult)), R=['ctraw', 'prm'], W=['G1'])
                P.add('dve', (lambda e, cpi=cpi: e.tensor_tensor(out=cpi, in0=t32a, in1=t32b, op=ALU.subtract)), R=['G1'], W=[f'cp{ql}'])
                for (lo, hi), yb in zip(nts, ybs):
                    P.add('pe', (lambda e, lo=lo, hi=hi, yb=yb, ql=ql: e.matmul(
                        pst[yb][:, 0:hi - lo], cp[:, ql, 0, :], xr[:, lo:hi], start=(ql == 0), stop=False)),
                        R=[f'cp{ql}', 'xr'], W=[f'ps{yb}'])
                    P.add('pe', (lambda e, lo=lo, hi=hi, yb=yb, ql=ql: e.matmul(
                        pst[yb][:, 0:hi - lo], cp[:, ql, 1, :], xi[:, lo:hi], start=False, stop=(ql == 3))),
                        R=[f'cp{ql}', 'xi'], W=[f'ps{yb}'])
            if state_only:
                continue
            for (lo, hi), yb in zip(nts, ybs):
                n = hi - lo
                P.add('dve', (lambda e, lo=lo, hi=hi, yb=yb, n=n, c=c: e.scalar_tensor_tensor(
                    out=W1[:, lo:hi], in0=ucr[:, lo:hi], scalar=vcol(5, c), in1=pst[yb][:, 0:n], op0=ALU.mult, op1=ALU.add)),
                    R=['ucr', 'vecs', f'ps{yb}'], W=['W1'])
            P.add('act', (lambda e: e.activation(out=W2, in_=W1, func=AF.Square)), R=['W1'], W=['W2'])
            P.add('dve', (lambda e: e.tensor_scalar(out=W2, in0=W2, scalar1=0.044715 * 2 * GC0, scalar2=2 * GC0, op0=ALU.mult, op1=ALU.add)), R=['W2'], W=['W2'])
            P.add('dve', (lambda e: e.tensor_tensor(out=W2, in0=W2, in1=W1, op=ALU.mult)), R=['W2', 'W1'], W=['W2'])
            P.add('act', (lambda e: e.activation(out=W3, in_=W2, func=AF.Sigmoid)), R=['W2'], W=['W3'])
            P.add('dve', (lambda e, c=c: e.tensor_tensor(out=xn[:, c, 0:T], in0=W3, in1=W1, op=ALU.mult)), R=['W3', 'W1'], W=[f'xn{c}'])

    def glu(T):
        nts = nts_of(T)
        for m in range(NCH):
            bs = [[nbank() for _ in nts] for _ in range(2)]
            for part in range(2):
                for half in range(2):
                    sl = wtile()
                    for kl in range(16):
                        k = half * 16 + kl
                        for (lo, hi), b in zip(nts, bs[part]):
                            P.add('pe', (lambda e, sl=sl, kl=kl, k=k, lo=lo, hi=hi, b=b: e.matmul(
                                pst[b][:, 0:hi - lo], ring[:, sl, kl * 128:(kl + 1) * 128], xn[:, k, lo:hi],
                                start=(k == 0), stop=(k == 31))), R=[f'ring{sl}', f'xn{k}'], W=[f'ps{b}'])
            for i, (lo, hi) in enumerate(nts):
                n = hi - lo
                b1, b2 = bs[0][i], bs[1][i]
                G = S(8 + i % 2)[:, 0:n]
                gn = f'G{1 + i % 2}'
                P.add('act', (lambda e, b2=b2, n=n, G=G: e.activation(out=G, in_=pst[b2][:, 0:n], func=AF.Sigmoid)), R=[f'ps{b2}'], W=[gn])
                P.add('dve', (lambda e, b1=b1, n=n, G=G: e.tensor_tensor(out=G, in0=pst[b1][:, 0:n], in1=G, op=ALU.mult)), R=[f'ps{b1}', gn], W=[gn])
                P.add('dve', (lambda e, m=m, lo=lo, hi=hi, G=G: e.tensor_tensor(out=x[:, m, lo:hi], in0=x[:, m, lo:hi], in1=G, op=ALU.add)), R=[gn, f'x{m}'], W=[f'x{m}'])

    def ffn(T):
        nts = nts_of(T)
        for g in range(NGRP):
            nf = FG if g < NGRP - 1 else NF - FG * (NGRP - 1)
            for fl in range(nf):
                bs = [[nbank() for _ in nts] for _ in range(2)]
                for part in range(2):
                    for half in range(2):
                        sl = wtile()
                        for kl in range(16):
                            k = half * 16 + kl
                            for (lo, hi), b in zip(nts, bs[part]):
                                P.add('pe', (lambda e, sl=sl, kl=kl, k=k, lo=lo, hi=hi, b=b: e.matmul(
                                    pst[b][:, 0:hi - lo], ring[:, sl, kl * 128:(kl + 1) * 128], xn[:, k, lo:hi],
                                    start=(k == 0), stop=(k == 31))), R=[f'ring{sl}', f'xn{k}'], W=[f'ps{b}'])
                for i, (lo, hi) in enumerate(nts):
                    n = hi - lo
                    bg, bu = bs[0][i], bs[1][i]
                    G = S(8 + i % 2)[:, 0:n]
                    gn = f'G{1 + i % 2}'
                    P.add('act', (lambda e, bg=bg, n=n, G=G: e.activation(out=G, in_=pst[bg][:, 0:n], func=AF.Silu)), R=[f'ps{bg}'], W=[gn])
                    P.add('dve', (lambda e, bu=bu, n=n, G=G, fl=fl, lo=lo, hi=hi: e.tensor_tensor(
                        out=h1[:, fl, lo:hi], in0=pst[bu][:, 0:n], in1=G, op=ALU.mult)), R=[f'ps{bu}', gn], W=[f'h1_{fl}'])
            for mp in range(8):
                sl = wtile()
                for ml in range(4):
                    m = mp * 4 + ml
                    for i, (lo, hi) in enumerate(nts):
                        n = hi - lo
                        b = nbank()
                        for fl in range(nf):
                            P.add('pe', (lambda e, sl=sl, fl=fl, ml=ml, lo=lo, hi=hi, b=b, nf=nf: e.matmul(
                                pst[b][:, 0:hi - lo], ring[:, sl, fl * 512 + ml * 128:fl * 512 + (ml + 1) * 128], h1[:, fl, lo:hi],
                                start=(fl == 0), stop=(fl == nf - 1))), R=[f'ring{sl}', f'h1_{fl}'], W=[f'ps{b}'])
                        P.add('dve', (lambda e, m=m, lo=lo, hi=hi, b=b, n=n: e.tensor_tensor(
                            out=x[:, m, lo:hi], in0=pst[b][:, 0:n], in1=x[:, m, lo:hi], op=ALU.add)), R=[f'ps{b}', f'x{m}'], W=[f'x{m}'])

    def pool_layer(T, Tp, poscol0, first, st_idx):
        nts = nts_of(T)
        E = 15 + Tp
        ES = 8 * 23
        icnt = scr[:, 10 * TM - 4 * TM:10 * TM]
        posr = S(5)
        P.add('sp', (lambda e: e.dma_start(out=posr[:, 0:T], in_=pos_d[:, poscol0:poscol0 + T])), W=['A1p'], dsem='pos')
        for wi, w in enumerate(WIN):
            ic = icnt[:, wi * TM:wi * TM + T]
            P.add('dve', (lambda e, ic=ic, w=w: e.tensor_scalar(out=ic, in0=posr[:, 0:T], scalar1=1.0, scalar2=float(w), op0=ALU.add, op1=ALU.min)), R=['A1p'], W=[f'ic{wi}'])
            P.add('dve', (lambda e, ic=ic: e.reciprocal(out=ic, in_=ic)), R=[f'ic{wi}'], W=[f'ic{wi}'])
        hc = S(0)
        ext = scr[:, TM:TM + E + ES]
        s2 = scr[:, 3 * TM:3 * TM + E + ES]
        stg = scr[:, 5 * TM:5 * TM + 128]
        for c in range(NCH):
            gi = c // 8
            w = WIN[gi]
            P.add('dve', (lambda e, c=c: e.scalar_tensor_tensor(
                out=hc[:, 0:T], in0=x[:, c, 0:T], scalar=vcol(1, c), in1=rstd[:, 0:T], op0=ALU.mult, op1=ALU.mult)),
                R=[f'x{c}', 'rstd', 'vecs'], W=['hc'])
            P.add('act', (lambda e, c=c: e.activation(out=ext[:, 0:15], in_=hist[:, c, :], func=AF.Copy)), R=['hist'], W=['ext'])
            P.add('act', (lambda e: e.activation(out=ext[:, 15:15 + Tp], in_=hc[:, 0:Tp], func=AF.Copy)), R=['hc'], W=['ext'])
            exs = ext[:, E:E + ES].rearrange("p (s k) -> p s k", k=23)
            P.add('act', (lambda e, exs=exs: e.activation(out=exs[:, :, 15:23], in_=hc[:, Tp:Tp + 64].rearrange("p (s k) -> p s k", k=8), func=AF.Copy)), R=['hc'], W=['ext'])
            b = nbank()
            P.add('sp', (lambda e, c=c: e.dma_start(out=sphc[:, c % 2, :], in_=spool_d[st_idx * 8:st_idx * 8 + 8, :, c * 128:(c + 1) * 128].rearrange("s k d -> (s k) d"))),
                  W=[f'sphc{c % 2}'], dsem=f'sph{c % 2}')
            P.add('pe', (lambda e, c=c, b=b: e.transpose(out=pst[b][:, 0:120], in_=sphc[0:120, c % 2, :], identity=ident[0:120, 0:120])),
                  R=[f'sphc{c % 2}', 'ident'], W=[f'ps{b}'])
            P.add('dve', (lambda e, b=b, exs=exs: e.tensor_copy(out=exs[:, :, 0:15], in_=pst[b][:, 0:120].rearrange("p (s k) -> p s k", k=15))), R=[f'ps{b}'], W=['ext'])
            P.add('act', (lambda e, c=c: e.activation(out=hist[:, c, :], in_=hc[:, Tp - 15:Tp], func=AF.Copy)), R=['hc', 'ext'], W=['hist'])
            L = E + ES
            cur, other = ext, s2
            sh = 1
            while sh < w:
                P.add('dve', (lambda e, cur=cur, other=other, sh=sh, L=L: e.tensor_tensor(out=other[:, sh:L], in0=cur[:, sh:L], in1=cur[:, 0:L - sh], op=ALU.add)),
                      R=['ext', 's2'], W=['ext', 's2'])
                if sh > 1 or True:
                    P.add('act', (lambda e, cur=cur, other=other, sh=sh: e.activation(out=other[:, 0:sh], in_=cur[:, 0:sh], func=AF.Copy)), R=['ext', 's2'], W=['ext', 's2'])
                cur, other = other, cur
                sh *= 2
            ic = icnt[:, gi * TM:gi * TM + T]
            pb = S(5)
            P.add('dve', (lambda e, cur=cur, ic=ic: e.tensor_tensor(out=pb[:, 0:Tp], in0=cur[:, 15:15 + Tp], in1=ic[:, 0:Tp], op=ALU.mult)), R=['ext', 's2', f'ic{gi}'], W=['pb'])
            curs = cur[:, E:E + ES].rearrange("p (s k) -> p s k", k=23)
            P.add('dve', (lambda e, curs=curs, ic=ic: e.tensor_tensor(
                out=pb[:, Tp:Tp + 64].rearrange("p (s k) -> p s k", k=8), in0=curs[:, :, 15:23],
                in1=ic[:, Tp:Tp + 64].rearrange("p (s k) -> p s k", k=8), op=ALU.mult)), R=['ext', 's2', f'ic{gi}'], W=['pb'])
            if T > Tp + 64:
                P.add('dve', (lambda e: e.memset(pb[:, Tp + 64:T], 0.0)), W=['pb'])
            P.add('dve', (lambda e, c=c: e.tensor_tensor(out=xn[:, c, 0:T], in0=pb[:, 0:T], in1=hc[:, 0:T], op=ALU.subtract)), R=['pb', 'hc'], W=[f'xn{c}'])
            b = nbank()
            P.add('pe', (lambda e, b=b: e.transpose(out=pst[b][0:64, 0:128], in_=hc[:, Tp:Tp + 64], identity=ident[:, :])), R=['hc', 'ident'], W=[f'ps{b}'])
            P.add('act', (lambda e, b=b, c=c: e.activation(out=hst[0:64, c % 4, :], in_=pst[b][0:64, 0:128], func=AF.Copy)), R=[f'ps{b}'], W=[f'hst{c % 4}'])
            P.add('sp', (lambda e, c=c: e.dma_start(out=psn_d[st_idx, c, :, :], in_=hst[0:64, c % 4, :])), R=[f'hst{c % 4}'], dsem=f'o_psn{c % 4}')
            if not first:
                b = nbank()
                P.add('pe', (lambda e, b=b: e.transpose(out=pst[b][0:15, 0:128], in_=hc[:, Tp - 15:Tp], identity=ident[:, :])), R=['hc', 'ident'], W=[f'ps{b}'])
                P.add('act', (lambda e, b=b, c=c: e.activation(out=hpt[0:15, c % 4, :], in_=pst[b][0:15, 0:128], func=AF.Copy)), R=[f'ps{b}'], W=[f'hpt{c % 4}'])
                P.add('sp', (lambda e, c=c: e.dma_start(out=ppn_d[c, :, :], in_=hpt[0:15, c % 4, :])), R=[f'hpt{c % 4}'], dsem=f'o_ppn{c % 4}')
        for gi in range(4):
            for mpair in range(4):
                sl = wtile()
                for ml in range(2):
                    m = gi * 8 + mpair * 2 + ml
                    for (lo, hi) in nts:
                        n = hi - lo
                        b = nbank()
                        for k in range(8):
                            P.add('pe', (lambda e, sl=sl, ml=ml, k=k, gi=gi, lo=lo, hi=hi, b=b: e.matmul(
                                pst[b][:, 0:hi - lo], ring[:, sl, (ml * 8 + k) * 128:(ml * 8 + k + 1) * 128], xn[:, gi * 8 + k, lo:hi],
                                start=(k == 0), stop=(k == 7))), R=[f'ring{sl}', f'xn{gi * 8 + k}'], W=[f'ps{b}'])
                        P.add('dve', (lambda e, m=m, lo=lo, hi=hi, b=b, n=n: e.scalar_tensor_tensor(
                            out=x[:, m, lo:hi], in0=pst[b][:, 0:n], scalar=vcol(6, m), in1=x[:, m, lo:hi], op0=ALU.mult, op1=ALU.add)),
                            R=[f'ps{b}', f'x{m}', 'vecs'], W=[f'x{m}'])


    def out_y(T, Tp, prow0, srow0, halo):
        ost = scr[:, 0:D]
        segs = []
        t = halo
        while t < Tp:
            n = min(128, Tp - t)
            segs.append((t, n, prow0 + (t - halo)))
            t += n
        segs.append((Tp, 64, srow0))
        for (t0, n, r0) in segs:
            for c in range(NCH):
                hcf = S(8 + c % 2)
                gn = f'G{1 + c % 2}'
                P.add('dve', (lambda e, c=c, t0=t0, n=n, hcf=hcf: e.scalar_tensor_tensor(
                    out=hcf[:, 0:n], in0=x[:, c, t0:t0 + n], scalar=vcol(4, c), in1=rstd[:, t0:t0 + n], op0=ALU.mult, op1=ALU.mult)),
                    R=[f'x{c}', 'rstd', 'vecs'], W=[gn])
                b = nbank()
                P.add('pe', (lambda e, b=b, n=n, hcf=hcf: e.transpose(out=pst[b][0:n, 0:128], in_=hcf[:, 0:n], identity=ident[:, :])), R=[gn, 'ident'], W=[f'ps{b}'])
                P.add('act', (lambda e, b=b, n=n, c=c: e.activation(out=ost[0:n, c * 128:(c + 1) * 128], in_=pst[b][0:n, 0:128], func=AF.Copy)), R=[f'ps{b}'], W=['stage'])
            P.add('sp', (lambda e, n=n, r0=r0: e.dma_start(out=y_d[r0:r0 + n, :], in_=ost[0:n, :])), R=['stage'], dsem='o_y')

    def load_sample_states(seq0):
        tmp = scr[:, 0:2 * 8 * 128].rearrange("p (a s q) -> p a s q", a=2, s=8)
        for a_ in range(2):
            P.add('sp', (lambda e, a_=a_: e.dma_start(out=tmp[:, a_, :, :], in_=sst_d[a_, seq0:seq0 + 8, :, :].rearrange("s q p -> q s p"))), W=['stage'], dsem='ld')
        xs = scr[:, 2048:2048 + 2048].rearrange("p (a s q) -> p a s q", a=2, s=8)
        for a_ in range(2):
            for s in range(8):
                b = nbank()
                P.add('pe', (lambda e, a_=a_, s=s, b=b: e.transpose(out=pst[b][:, 0:128], in_=tmp[:, a_, s, :], identity=ident[:, :])), R=['stage', 'ident'], W=[f'ps{b}'])
                P.add('act', (lambda e, a_=a_, s=s, b=b: e.activation(out=xs[:, a_, s, :], in_=pst[b][:, 0:128], func=AF.Copy)), R=[f'ps{b}'], W=['xs'])
        for s in range(8):
            t_ = scr[:, 4096:4224]
            P.add('dve', (lambda e, s=s: e.tensor_tensor(out=t_, in0=xs[:, 1, s, :], in1=prm[:, 7, :], op=ALU.mult)), R=['xs', 'prm'], W=['A1p'])
            P.add('dve', (lambda e, s=s: e.tensor_tensor(out=St[:, :, 1 + s, 0], in0=xs[:, 0, s, :], in1=prm[:, 6, :], op=ALU.mult)), R=['xs', 'prm'], W=['St'])
            P.add('dve', (lambda e, s=s: e.tensor_tensor(out=St[:, :, 1 + s, 0], in0=St[:, :, 1 + s, 0], in1=t_, op=ALU.subtract)), R=['St', 'A1p'], W=['St'])
            P.add('dve', (lambda e, s=s: e.tensor_tensor(out=t_, in0=xs[:, 1, s, :], in1=prm[:, 6, :], op=ALU.mult)), R=['xs', 'prm', 'St'], W=['A1p'])
            P.add('dve', (lambda e, s=s: e.tensor_tensor(out=St[:, :, 1 + s, 1], in0=xs[:, 0, s, :], in1=prm[:, 7, :], op=ALU.mult)), R=['xs', 'prm'], W=['St'])
            P.add('dve', (lambda e, s=s: e.tensor_tensor(out=St[:, :, 1 + s, 1], in0=St[:, :, 1 + s, 1], in1=t_, op=ALU.add)), R=['St', 'A1p'], W=['St'])

    def store_states(segs, dst_fn, dsem):
        ob = scr[:, 0:2 * 9 * 128].rearrange("p (a s q) -> p a s q", a=2, s=9)
        for seg in segs:
            t_ = scr[:, 2304:2432]
            u_ = scr[:, 2432:2560]
            P.add('dve', (lambda e, seg=seg: e.tensor_tensor(out=t_, in0=St[:, :, seg, 1], in1=prm[:, 5, :], op=ALU.mult)), R=['St', 'prm'], W=['A1p'])
            P.add('dve', (lambda e, seg=seg: e.tensor_tensor(out=u_, in0=St[:, :, seg, 0], in1=prm[:, 4, :], op=ALU.mult)), R=['St', 'prm'], W=['A1q'])
            P.add('dve', (lambda e: e.tensor_tensor(out=u_, in0=u_, in1=t_, op=ALU.subtract)), R=['A1p', 'A1q'], W=['A1q'])
            b = nbank()
            P.add('pe', (lambda e, b=b: e.transpose(out=pst[b][:, 0:128], in_=u_, identity=ident[:, :])), R=['A1q', 'ident'], W=[f'ps{b}'])
            P.add('act', (lambda e, b=b, seg=seg: e.activation(out=ob[:, 0, seg, :], in_=pst[b][:, 0:128], func=AF.Copy)), R=[f'ps{b}'], W=['ob'])
            P.add('dve', (lambda e, seg=seg: e.tensor_tensor(out=t_, in0=St[:, :, seg, 1], in1=prm[:, 4, :], op=ALU.mult)), R=['St', 'prm', 'A1q'], W=['A1p'])
            P.add('dve', (lambda e, seg=seg: e.tensor_tensor(out=u_, in0=St[:, :, seg, 0], in1=prm[:, 5, :], op=ALU.mult)), R=['St', 'prm'], W=['A1q'])
            P.add('dve', (lambda e: e.tensor_tensor(out=u_, in0=u_, in1=t_, op=ALU.add)), R=['A1p', 'A1q'], W=['A1q'])
            b = nbank()
            P.add('pe', (lambda e, b=b: e.transpose(out=pst[b][:, 0:128], in_=u_, identity=ident[:, :])), R=['A1q', 'ident'], W=[f'ps{b}'])
            P.add('act', (lambda e, b=b, seg=seg: e.activation(out=ob[:, 1, seg, :], in_=pst[b][:, 0:128], func=AF.Copy)), R=[f'ps{b}'], W=['ob'])
            for a_ in range(2):
                P.add('sp', (lambda e, a_=a_, seg=seg: e.dma_start(out=dst_fn(a_, seg), in_=ob[:, a_, seg, :])), R=['ob'], dsem=dsem)

    def load_kr(src, col0, T):
        P.add('sp', (lambda e: e.dma_start(out=kidx[:, 0:T], in_=src[0, :, col0:col0 + T])), W=['kidx'], dsem='k0')
        P.add('sp', (lambda e: e.dma_start(out=rmask[:, 0:T], in_=src[1, :, col0:col0 + T])), W=['rmask'], dsem='k1')

    def dump(name, ap, regs, n, b3=None):
        if not stage:
            return
        off = dcur['o']
        dcur['o'] += n
        DBG.append((name, off, n))
        o_ap = dbg_d[:, off:off + n]
        if b3:
            o_ap = o_ap.rearrange("p (a b) -> p a b", b=b3)
        P.add('sp', (lambda e: e.dma_start(out=o_ap, in_=ap)), R=regs, dsem='o_dbg')

    def finish():
        P.add('sp', (lambda e: e.nop()), R=[], W=[], dsem=None)
        last = P.ops[-1]
        for k, j in P.lastdma.items():
            if k.startswith('o_'):
                last['deps'][j] = 'raw'
        P.emit(nc, block, es)
        es.close()
        _CACHE['P'] = P
        return nc

    barrier()
    dump('prm', prm[:, :, :].rearrange("p a b -> p (a b)"), ['prm', 'scrp'], 1280)
    load_kr(krp_d, 0, TPRE)
    for seg in range(2):
        load_x(seg * TPRE, TPRE)
        barrier()
        if seg == 1:
            dump('x_c0', x[:, 0, 0:TPRE], ['x0'], TPRE)
            dump('x_c31', x[:, 31, 0:TPRE], ['x31'], TPRE)
        rms(TPRE)
        if seg == 1:
            dump('rstd', rstd[:, 0:TPRE], ['rstd'], TPRE)
        ssm(TPRE, TPRE, 0, 0, True, 0)
        barrier()
        dump(f'St_pre{seg}', St[:, :, 0, :], ['St'], 256, b3=2)
        if stage == 1 and seg == 1:
            return finish()
    row0 = 2 * TPRE
    for st_idx, (T, Tp, halo) in enumerate(((TA, TPA, 15), (TB, TPB, 0))):
        first = st_idx == 0
        load_kr(kr_d, st_idx * TM, T)
        load_sample_states(st_idx * 8)
        barrier()
        load_x(row0, T)
        barrier()
        rms(T)
        ssm(T, Tp, 8, 0, False, st_idx * 8)
        barrier()
        if stage and first:
            dump('StA', St[:, :, 0, :], ['St'], 256, b3=2)
            for cc in (0, 17, 31):
                P.add('dve', (lambda e, cc=cc: e.tensor_copy(out=S(9), in_=xn[:, cc, :])), R=[f'xn{cc}'], W=['G2'])
                dump(f'xnA_{cc}', S(9), ['G2'], TM)
            P.add('dve', (lambda e: e.tensor_copy(out=S(8), in_=xr[:, :])), R=['xr'], W=['G1'])
            dump('xrA', S(8), ['G1'], TM)
            dump('TSA', S(2), ['TS0'], TM)
            dump('TCA', S(3), ['TC0'], TM)
            dump('RMA', S(4), ['RM0'], TM)
            dump('kidxA', kidx[:, :], ['kidx'], TM)
        store_states(range(1, 9), (lambda a_, seg, st_idx=st_idx: ss_d[a_, st_idx * 8 + seg - 1, :, :]), 'o_ss')
        if stage == 2:
            barrier()
            return finish()
        if not first:
            store_states([0], (lambda a_, seg: sp_d[a_, :, :]), 'o_sp')
        barrier()
        def dumpx(tag):
            if stage and first:
                barrier()
                for cc in (0, 17, 31):
                    dump(f'{tag}_{cc}', x[:, cc, :], [f'x{cc}'], TM)
                dump(f'{tag}_rstd', rstd[:, :], ['rstd'], TM)
        glu(T)
        dumpx('x1')
        if stage == 3:
            barrier()
            return finish()
        rms(T)
        norm_to_xn(T, 2)
        ffn(T)
        dumpx('x2')
        if stage == 4:
            barrier()
            return finish()
        rms(T)
        barrier()
        P.add('sp', (lambda e, st_idx=st_idx: e.dma_start(out=pso_d[st_idx * 8:st_idx * 8 + 8, :, :], in_=spool_d[st_idx * 8:st_idx * 8 + 8, 8:15, :])), dsem='o_pso')
        pool_layer(T, Tp, st_idx * TM, first, st_idx)
        barrier()
        dumpx('x3')
        if stage == 5:
            barrier()
            return finish()
        rms(T)
        norm_to_xn(T, 3)
        ffn(T)
        dumpx('x4')
        rms(T)
        barrier()
        out_y(T, Tp, st_idx * 512, 1024 + st_idx * 64, halo)
        barrier()
        if stage == 6:
            return finish()
        row0 += T
    return finish()


_CACHE = {}


def _prep_weights(ssm_w_glu, pool_w, ffn_w_gate_up, ffn_w_down):
    tiles = np.zeros((NTILE, 128, 2048), np.float32)
    W = ssm_w_glu[0].reshape(32, 128, 2, 32, 128)
    W = W.reshape(2, 16, 128, 2, 32, 128)
    tiles[0:NT_GLU] = W.transpose(4, 3, 0, 2, 1, 5).reshape(NT_GLU, 128, 2048)
    base = NT_GLU
    for L in range(2):
        GU = ffn_w_gate_up[L].reshape(2, 16, 128, 2, NF, 128)
        GUt = GU.transpose(4, 3, 0, 2, 1, 5).reshape(NF, 4, 128, 2048)
        DN = ffn_w_down[L].reshape(NF, 128, 8, 512)
        t = base
        for g in range(NGRP):
            nf = FG if g < NGRP - 1 else NF - FG * (NGRP - 1)
            for fl in range(nf):
                tiles[t:t + 4] = GUt[g * FG + fl]
                t += 4
            blk = DN[g * FG:g * FG + nf]
            tiles[t:t + 8, :, 0:nf * 512] = blk.transpose(2, 1, 0, 3).reshape(8, 128, nf * 512)
            t += 8
        assert t == base + NT_FFN
        base = t
        if L == 0:
            PW = pool_w[0].reshape(4, 8, 128, 4, 2, 128)
            tiles[base:base + NT_POOL] = PW.transpose(0, 3, 2, 4, 1, 5).reshape(NT_POOL, 128, 2048)
            base += NT_POOL
    assert base == NTILE
    return tiles


def kernel(x_prompt, x_sample, state_ssm_re, state_ssm_im, state_pool, norm_mix, norm_ffn,
           ssm_lambda_re, ssm_lambda_im, ssm_log_step, ssm_b_re, ssm_b_im, ssm_c_re, ssm_c_im,
           ssm_d, ssm_w_glu, pool_w, pool_scale, ffn_w_gate_up, ffn_w_down, norm_final):
    f = lambda a: np.ascontiguousarray(np.asarray(a, dtype=np.float32))
    x_prompt, x_sample = f(x_prompt), f(x_sample)
    stage = STAGE
    if 'nc' not in _CACHE:
        _CACHE['nc'] = build_program(stage)
    nc = _CACHE['nc']
    wst = _prep_weights(f(ssm_w_glu), f(pool_w), f(ffn_w_gate_up), f(ffn_w_down))
    if stage:
        wst = np.ascontiguousarray(wst[0:{1: 1, 2: 1, 3: NT_GLU, 4: NT_GLU + NT_FFN, 5: NT_GLU + NT_FFN + NT_POOL, 6: NTILE}[stage]])
    fm = lambda v: f(v).reshape(32, 128).T
    vecs = np.concatenate([fm(norm_mix[0]), fm(norm_mix[1]), fm(norm_ffn[0]), fm(norm_ffn[1]), fm(norm_final),
                           fm(ssm_d[0]), fm(pool_scale[0])], axis=1)
    ident = np.eye(128, dtype=np.float32)
    lq = lambda a: f(a).reshape(128, 2, 64).transpose(1, 2, 0).reshape(128, 128)
    lam = np.stack([lq(ssm_lambda_re[0]), lq(ssm_lambda_im[0]),
                    lq(np.repeat(f(ssm_log_step[0])[:, None], 64, axis=1))])
    bt = np.zeros((128, 128, 256), np.float32)
    ct = np.zeros((128, 128, 64), np.float32)
    for comp, (B, C) in enumerate(((f(ssm_b_re[0]), f(ssm_c_re[0])), (f(ssm_b_im[0]), f(ssm_c_im[0])))):
        Bq = B.reshape(128, 2, 64, 16)
        Cq = C.reshape(128, 2, 16, 64)
        for gl in range(2):
            for q4 in range(4):
                qs = np.arange(q4, 128, 4)
                r0 = q4 * 32 + gl * 16
                bt[qs, r0:r0 + 16, comp * 128 + gl * 64:comp * 128 + gl * 64 + 64] = Bq[qs, gl].transpose(0, 2, 1)
            ct[:, gl * 64:gl * 64 + 64, comp * 32 + gl * 16:comp * 32 + gl * 16 + 16] = Cq[:, gl].transpose(0, 2, 1)
    def krow(Tp, T):
        k = np.zeros(TM, np.float32)
        m = np.ones(TM, np.float32)
        k[0:Tp] = np.arange(Tp)
        m[0] = 0.0
        for s in range(8):
            if Tp + 8 * s + 8 <= T:
                k[Tp + 8 * s:Tp + 8 * s + 8] = np.arange(8)
                m[Tp + 8 * s] = 0.0
        return k, m
    kA, mA = krow(TPA, TA)
    kB, mB = krow(TPB, TB)
    kr = np.stack([np.concatenate([kA, kB]), np.concatenate([mA, mB])])
    kr = np.ascontiguousarray(np.broadcast_to(kr[:, None, :], (2, 128, 2 * TM)))
    kP = np.zeros(TM, np.float32); kP[0:TPRE] = np.arange(TPRE)
    mP = np.ones(TM, np.float32); mP[0] = 0.0
    krp = np.ascontiguousarray(np.broadcast_to(np.stack([kP, mP])[:, None, :], (2, 128, TM)))
    in_maps = []
    for core in range(8):
        b, hf = core // 2, core % 2
        xin = np.zeros((NTOK_IN, D), np.float32)
        pos = np.full((2 * TM,), 1.0e4, np.float32)
        if hf == 1:
            xin[3:3 + 1009] = x_prompt[b, 0:1009]
            xin[2 * TPRE:2 * TPRE + 15] = x_prompt[b, 1009:1024]
        p0 = hf * 1024
        a0 = 2 * TPRE
        xin[a0 + 15:a0 + 15 + 512] = x_prompt[b, p0:p0 + 512]
        xin[a0 + TPA:a0 + TPA + 64] = x_sample[core * 16:core * 16 + 8].reshape(64, D)
        b0 = a0 + TA
        xin[b0:b0 + 512] = x_prompt[b, p0 + 512:p0 + 1024]
        xin[b0 + TPB:b0 + TPB + 64] = x_sample[core * 16 + 8:core * 16 + 16].reshape(64, D)
        pos[15:15 + 512] = p0 + np.arange(512)
        pos[TM:TM + 512] = p0 + 512 + np.arange(512)
        sst = np.stack([f(state_ssm_re[0])[core * 16:core * 16 + 16].reshape(16, 128, 128),
                        f(state_ssm_im[0])[core * 16:core * 16 + 16].reshape(16, 128, 128)])
        in_maps.append(dict(xin=xin, wst=wst, vecs=vecs, ident=ident, kr=kr, krp=krp,
                            pos=np.ascontiguousarray(np.broadcast_to(pos[None, :], (128, 2 * TM))),
                            lam=lam, bt=bt, ct=ct, sst=sst,
                            spool=f(state_pool[0])[core * 16:core * 16 + 16]))
    res = run_bass_kernel_spmd(nc, in_maps, core_ids=list(range(8)))
    R = res.results
    if stage:
        _CACHE['dbg'] = (list(DBG), R, dict(xin1=in_maps[1]['xin'], lam=lam, bt=bt, ct=ct, vecs=vecs))
    y_prompt = np.zeros((4, 2048, D), np.float32)
    y_sample = np.zeros((128, 8, D), np.float32)
    sre_p = np.zeros((1, 4, 256, 64), np.float32)
    sim_p = np.zeros((1, 4, 256, 64), np.float32)
    pool_p = np.zeros((1, 4, 15, D), np.float32)
    sre_s = np.zeros((1, 128, 256, 64), np.float32)
    sim_s = np.zeros((1, 128, 256, 64), np.float32)
    pool_s = np.zeros((1, 128, 15, D), np.float32)
    for core in range(8):
        b, hf = core // 2, core % 2
        r = R[core]
        y_prompt[b, hf * 1024:(hf + 1) * 1024] = r["y"][0:1024]
        y_sample[core * 16:(core + 1) * 16] = r["y"][1024:1152].reshape(16, 8, D)
        if hf == 1:
            sre_p[0, b] = r["sp"][0].reshape(256, 64)
            sim_p[0, b] = r["sp"][1].reshape(256, 64)
            pool_p[0, b] = r["ppn"].transpose(1, 0, 2).reshape(15, D)
        sre_s[0, core * 16:(core + 1) * 16] = r["ss"][0].reshape(16, 256, 64)
        sim_s[0, core * 16:(core + 1) * 16] = r["ss"][1].reshape(16, 256, 64)
        pool_s[0, core * 16:(core + 1) * 16, 0:7] = r["pso"]
        pool_s[0, core * 16:(core + 1) * 16, 7:15] = r["psn"].reshape(2, 32, 8, 8, 128).transpose(0, 2, 3, 1, 4).reshape(16, 8, D)
    return (y_prompt, y_sample, sre_p, sim_p, pool_p, sre_s, sim_s, pool_s)
```

```python
import math
import numpy as np
import concourse.bass as bass
import concourse.mybir as mybir
from concourse.bass_utils import run_bass_kernel_spmd
from contextlib import ExitStack

F32 = mybir.dt.float32
F32R = mybir.dt.float32r
BF16 = mybir.dt.bfloat16
AF = mybir.ActivationFunctionType
ALU = mybir.AluOpType

D = 4096
NCH = 32
DFF = 11008
NF = 86
FG = 4
NGRP = 22
NSLOT = 4
TPA, TPB, TS = 527, 512, 64
TA, TB = 592, 576
TPRE = 506
TM = 592
MAGIC = 12582912.0
TWO_PI = 2.0 * math.pi
EPS = 1e-6
GC0 = math.sqrt(2.0 / math.pi)
NT_GLU, NT_FFN, NT_POOL = 128, 21 * 24 + 16, 16
NTILE = NT_GLU + NT_FFN + NT_POOL + NT_FFN
NTOK_IN = 2 * TPRE + TA + TB
WIN = (2, 4, 8, 16)


class Prog:
    def __init__(self):
        self.ops = []
        self.lastw = {}
        self.readers = {}
        self.lastdma = {}

    def add(self, eng, fn, R=(), W=(), dsem=None):
        i = len(self.ops)
        deps = {}
        for r in R:
            j = self.lastw.get(r)
            if j is not None:
                deps[j] = 'raw'
        for w in W:
            j = self.lastw.get(w)
            if j is not None and j not in deps:
                deps[j] = 'waw'
            for j in self.readers.get(w, {}).values():
                if j not in deps:
                    deps[j] = 'war'
        if dsem is not None:
            j = self.lastdma.get(dsem)
            if j is not None:
                deps[j] = 'raw'
            self.lastdma[dsem] = i
        rk = eng if dsem is None else ('dma', i)
        for r in R:
            self.readers.setdefault(r, {})[rk] = i
        for w in W:
            self.lastw[w] = i
            self.readers[w] = {}
        self.ops.append(dict(eng=eng, fn=fn, deps=deps, dsem=dsem, sig=False, force=False))
        return i

    def emit(self, nc, block, es):
        ops = self.ops
        engs = ('pe', 'act', 'dve', 'pool', 'sp')
        for i, o in enumerate(ops):
            keep = []
            for j, kind in o['deps'].items():
                d = ops[j]
                if d['dsem'] is None and d['eng'] == o['eng'] and not o['force']:
                    if kind != 'raw' or o['eng'] == 'pe':
                        continue
                keep.append(j)
                if d['dsem'] is None:
                    d['sig'] = True
            o['keep'] = keep
        cnt = {e: 0 for e in engs}
        dcnt = {}
        for o in ops:
            if o['dsem'] is not None:
                dcnt[o['dsem']] = dcnt.get(o['dsem'], 0) + 16
                o['sv'] = ('d_' + o['dsem'], dcnt[o['dsem']])
            elif o['sig']:
                cnt[o['eng']] += 1
                o['sv'] = ('e_' + o['eng'], cnt[o['eng']])
        self.cnt, self.dcnt = cnt, dcnt
        names = ['e_' + e for e in engs] + ['d_' + k for k in dcnt]
        sems = {n: es.enter_context(nc.semaphore(n)) for n in names}

        def run(engname, e):
            waited = {}
            for o in ops:
                if o['eng'] != engname:
                    continue
                need = {}
                for j in o['keep']:
                    s, v = ops[j]['sv']
                    if need.get(s, 0) < v:
                        need[s] = v
                for s, v in need.items():
                    if waited.get(s, 0) < v:
                        e.wait_ge(sems[s], v)
                        waited[s] = v
                if o['fn'] is None:
                    continue
                ins = o['fn'](e)
                if o['dsem'] is not None:
                    ins.then_inc(sems['d_' + o['dsem']], 16)
                elif o['sig']:
                    ins.then_inc(sems['e_' + engname], 1)

        @block.tensor
        def _(e):
            run('pe', e)

        @block.scalar
        def _(e):
            run('act', e)

        @block.vector
        def _(e):
            run('dve', e)

        @block.gpsimd
        def _(e):
            run('pool', e)

        @block.sync
        def _(e):
            run('sp', e)


DBG = []
STAGE = 0


def build_program(stage=0):
    nc = bass.Bass("TRN2", target_bir_lowering=False)
    del DBG[:]
    dt_in = lambda n, s: nc.dram_tensor(n, s, F32, kind="ExternalInput").ap()
    dt_out = lambda n, s: nc.dram_tensor(n, s, F32, kind="ExternalOutput").ap()
    xin = dt_in("xin", [NTOK_IN, D])
    NW = {0: NTILE, 1: 1, 2: 1, 3: NT_GLU, 4: NT_GLU + NT_FFN, 5: NT_GLU + NT_FFN + NT_POOL, 6: NTILE}[stage]
    wst = dt_in("wst", [NW, 128, 2048])
    vecs_d = dt_in("vecs", [128, 7 * 32])
    ident_d = dt_in("ident", [128, 128])
    kr_d = dt_in("kr", [2, 128, 2 * TM])
    krp_d = dt_in("krp", [2, 128, TM])
    pos_d = dt_in("pos", [128, 2 * TM])
    lam_d = dt_in("lam", [3, 128, 128])
    bt_d = dt_in("bt", [128, 128, 256])
    ct_d = dt_in("ct", [128, 128, 64])
    sst_d = dt_in("sst", [2, 16, 128, 128])
    spool_d = dt_in("spool", [16, 15, D])
    y_d = dt_out("y", [1152, D])
    sp_d = dt_out("sp", [2, 128, 128])
    ss_d = dt_out("ss", [2, 16, 128, 128])
    psn_d = dt_out("psn", [2, 32, 64, 128])
    pso_d = dt_out("pso", [16, 7, D])
    ppn_d = dt_out("ppn", [32, 15, 128])

    es = ExitStack()
    dbg_d = dt_out("dbg", [128, 65536]) if stage else None
    dcur = dict(o=0)
    sb = lambda n, s, d=F32: es.enter_context(nc.sbuf_tensor(n, s, d))
    x = sb("x", [128, NCH, TM])
    xn = sb("xn", [128, NCH, TM], BF16)
    ring = sb("ring", [128, NSLOT, 2048], BF16)
    ident = sb("ident_s", [128, 128])
    onesr = sb("onesr", [128, 128], F32R)
    vecs = sb("vecs_s", [128, 7 * 32])
    kidx = sb("kidx", [128, TM])
    rmask = sb("rmask", [128, TM])
    rstd = sb("rstd", [128, TM])
    prm = sb("prm", [128, 10, 128])
    St = sb("St", [128, 128, 9, 2])
    zq = sb("zq", [128, 9, 2])
    zt = sb("zt", [128, 16])
    magic_c = sb("magic_c", [128, 2])
    hist = sb("hist", [128, NCH, 15])
    ucr = sb("ucr", [128, TM], F32R)
    xr = sb("xr", [128, TM], F32R)
    xi = sb("xi", [128, TM], F32R)
    bt = sb("bt_s", [128, 2, 256], F32R)
    cp = sb("cp", [128, 4, 2, 128], F32R)
    ctraw = sb("ctraw", [128, 1, 64])
    sq = ucr
    h1 = sb("h1", [128, FG, TM], BF16)
    scr = sb("scr", [128, 13 * TM])
    sphc = sb("sphc", [120, 2, 128])
    hst = sb("hst", [64, 4, 128])
    hpt = sb("hpt", [15, 4, 128])
    pst = [es.enter_context(nc.psum_tensor(f"ps{i}", [128, 512], F32)) for i in range(8)]
    block = es.enter_context(nc.Block())

    P = Prog()
    V = lambda c: vecs[:, c:c + 1]
    vcol = lambda k, c: vecs[:, k * 32 + c:k * 32 + c + 1]
    S = lambda i: scr[:, i * TM:(i + 1) * TM]

    P.add('sp', lambda e: e.dma_start(out=ident[:], in_=ident_d), W=['ident'], dsem='c0')
    P.add('sp', lambda e: e.dma_start(out=vecs[:], in_=vecs_d), W=['vecs'], dsem='c1')
    P.add('sp', lambda e: e.dma_start(out=prm[:, 0:3, :], in_=lam_d.rearrange("a p q -> p a q")), W=['prm'], dsem='c2')
    P.add('dve', lambda e: e.memset(scr[:, 0:128], 1.0), W=['scrp'])
    P.add('dve', lambda e: e.tensor_copy(out=onesr[:], in_=scr[:, 0:128]), R=['scrp'], W=['onesr'])
    P.add('dve', lambda e: e.memset(scr[:, 1024:2048], 0.0), W=['scrz'])
    P.add('dve', lambda e: e.tensor_copy(out=cp[:, :, :, :].rearrange("p a b c -> p (a b c)"), in_=scr[:, 1024:2048]), R=['scrz'], W=['cp0', 'cp1', 'cp2', 'cp3'])
    P.add('dve', lambda e: e.memset(St[:], 0.0), W=['St'])
    P.add('dve', lambda e: e.memset(magic_c[:, 0:1], MAGIC), W=['magic'])
    P.add('dve', lambda e: e.memset(magic_c[:, 1:2], -MAGIC), R=['magic'], W=['magic'])
    P.add('dve', lambda e: e.memset(hist[:], 0.0), W=['hist'])

    pr = lambda i: prm[:, i, :]
    T0, T1 = S(0)[:, 0:128], S(1)[:, 0:128]
    T2, T3 = S(2)[:, 0:128], S(3)[:, 0:128]
    R_, W_ = ['prm'], ['prm']
    a = lambda eng, fn, R=(), W=(): P.add(eng, fn, R=list(R) + ['prm', 'scrp'], W=list(W) + ['prm', 'scrp'])
    def ts(out, in0, s1, s2=None, op0=ALU.mult, op1=ALU.add):
        if s2 is None:
            a('dve', lambda e: e.tensor_scalar(out=out, in0=in0, scalar1=s1, scalar2=None, op0=op0))
        else:
            a('dve', lambda e: e.tensor_scalar(out=out, in0=in0, scalar1=s1, scalar2=s2, op0=op0, op1=op1))

    def tt(out, in0, in1, op):
        a('dve', lambda e: e.tensor_tensor(out=out, in0=in0, in1=in1, op=op))

    def stt(out, in0, sc, in1, op0, op1):
        a('dve', lambda e: e.scalar_tensor_tensor(out=out, in0=in0, scalar=sc, in1=in1, op0=op0, op1=op1))

    T4, T5, T6, T7 = S(4)[:, 0:128], S(5)[:, 0:128], S(6)[:, 0:128], S(7)[:, 0:128]
    a('act', lambda e: e.activation(out=T0, in_=pr(2), func=AF.Exp))
    tt(T1, pr(0), T0, ALU.mult)
    tt(T2, pr(1), T0, ALU.mult)
    ts(pr(8), T2, 1.0 / TWO_PI)
    ts(T0, pr(8), MAGIC, None, op0=ALU.add)
    ts(T0, T0, -MAGIC, None, op0=ALU.add)
    tt(T0, pr(8), T0, ALU.subtract)
    ts(T0, T0, math.pi / 2.0)
    tt(T2, T0, T0, ALU.mult)
    ts(T3, T2, 1.0 / 362880.0)
    stt(T3, T3, -1.0 / 5040.0, T2, ALU.add, ALU.mult)
    stt(T3, T3, 1.0 / 120.0, T2, ALU.add, ALU.mult)
    stt(T3, T3, -1.0 / 6.0, T2, ALU.add, ALU.mult)
    stt(T3, T3, 1.0, T0, ALU.add, ALU.mult)
    ts(T4, T2, -1.0 / 3628800.0)
    stt(T4, T4, 1.0 / 40320.0, T2, ALU.add, ALU.mult)
    stt(T4, T4, -1.0 / 720.0, T2, ALU.add, ALU.mult)
    stt(T4, T4, 1.0 / 24.0, T2, ALU.add, ALU.mult)
    stt(T4, T4, -0.5, T2, ALU.add, ALU.mult)
    ts(T4, T4, 1.0, None, op0=ALU.add)
    stt(T5, T3, 2.0, T4, ALU.mult, ALU.mult)
    tt(T6, T3, T3, ALU.mult)
    ts(T6, T6, -2.0, 1.0)
    stt(T3, T5, 2.0, T6, ALU.mult, ALU.mult)
    tt(T4, T5, T5, ALU.mult)
    ts(T4, T4, -2.0)
    ts(T5, T1, 1.0 / 6.0, 1.0)
    tt(T5, T5, T1, ALU.mult)
    ts(T5, T5, 1.0 / 5.0, 1.0)
    tt(T5, T5, T1, ALU.mult)
    ts(T5, T5, 1.0 / 4.0, 1.0)
    tt(T5, T5, T1, ALU.mult)
    ts(T5, T5, 1.0 / 3.0, 1.0)
    tt(T5, T5, T1, ALU.mult)
    ts(T5, T5, 1.0 / 2.0, 1.0)
    tt(T5, T5, T1, ALU.mult)
    ts(pr(3), T5, 1.0, None, op0=ALU.add)
    tt(pr(9), pr(3), T3, ALU.mult)
    tt(T1, pr(3), T4, ALU.mult)
    tt(T1, T1, T5, ALU.add)
    ts(T3, T1, 1.0, None, op0=ALU.add)
    tt(T0, pr(0), pr(0), ALU.mult)
    tt(T2, pr(1), pr(1), ALU.mult)
    tt(T0, T0, T2, ALU.add)
    a('dve', lambda e: e.reciprocal(out=T0, in_=T0))
    tt(T2, T1, pr(0), ALU.mult)
    tt(pr(4), pr(9), pr(1), ALU.mult)
    tt(T2, T2, pr(4), ALU.add)
    tt(pr(4), T2, T0, ALU.mult)
    tt(T2, pr(9), pr(0), ALU.mult)
    tt(T1, T1, pr(1), ALU.mult)
    tt(T2, T2, T1, ALU.subtract)
    tt(pr(5), T2, T0, ALU.mult)
    a('dve', lambda e: e.tensor_copy(out=pr(0), in_=pr(8)))
    a('dve', lambda e: e.tensor_copy(out=pr(1), in_=pr(3)))
    a('dve', lambda e: e.tensor_copy(out=pr(2), in_=T3))
    a('dve', lambda e: e.tensor_copy(out=pr(3), in_=pr(9)))
    a('dve', lambda e: e.tensor_tensor(out=T0, in0=pr(4), in1=pr(4), op=ALU.mult))
    a('dve', lambda e: e.tensor_tensor(out=T1, in0=pr(5), in1=pr(5), op=ALU.mult))
    a('dve', lambda e: e.tensor_tensor(out=T0, in0=T0, in1=T1, op=ALU.add))
    a('dve', lambda e: e.reciprocal(out=T0, in_=T0))
    a('dve', lambda e: e.tensor_tensor(out=pr(6), in0=pr(4), in1=T0, op=ALU.mult))
    a('dve', lambda e: e.tensor_tensor(out=T1, in0=pr(5), in1=T0, op=ALU.mult))
    a('dve', lambda e: e.tensor_scalar(out=pr(7), in0=T1, scalar1=-1.0, scalar2=None, op0=ALU.mult))

    wstate = dict(next_dma=0, next_use=0)
    TOTAL_TILES = 2 * NTILE

    def wtile():
        i = wstate['next_use']
        wstate['next_use'] += 1
        while wstate['next_dma'] < min(TOTAL_TILES, i + NSLOT):
            j = wstate['next_dma']
            wstate['next_dma'] += 1
            sl = j % NSLOT
            P.add('pool', (lambda e, j=j, sl=sl: e.dma_start(out=ring[:, sl, :], in_=wst[(j % NTILE) % NW])),
                  W=[f'ring{sl}'], dsem=f'w{sl}')
        return i % NSLOT

    def nts_of(T):
        h = T // 2
        if h % 2:
            h += 1
        return [(0, h), (h, T)]

    def barrier():
        regs = list(P.lastw.keys())
        i = P.add('sp', (lambda e: e.nop()), R=[], W=regs)
        P.ops[i]['force'] = True
        for k, j in P.lastdma.items():
            P.ops[i]['deps'].setdefault(j, 'raw')

    bank = dict(i=0)

    def nbank(nb=8):
        b = bank['i'] % nb
        bank['i'] += 1
        return b

    def load_x(row0, T):
        stage = scr[:, 0:D]
        t0 = 0
        while t0 < T:
            n = min(128, T - t0)
            P.add('sp', (lambda e, t0=t0, n=n: e.dma_start(out=stage[0:n, :], in_=xin[row0 + t0:row0 + t0 + n, :])),
                  W=['stage'], dsem='ld')
            for c4 in range(8):
                b = nbank()
                for cc in range(4):
                    c = c4 * 4 + cc
                    P.add('pe', (lambda e, b=b, cc=cc, c=c, n=n: e.transpose(
                        out=pst[b][:, cc * 128:cc * 128 + n], in_=stage[0:n, c * 128:(c + 1) * 128],
                        identity=ident[0:n, 0:n])), R=['stage', 'ident'], W=[f'ps{b}'])
                eng = 'act' if c4 % 2 else 'dve'
                if eng == 'act':
                    P.add('act', (lambda e, b=b, c4=c4, t0=t0, n=n: e.activation(
                        out=x[:, c4 * 4:c4 * 4 + 4, t0:t0 + n],
                        in_=pst[b][:, :].rearrange("p (c t) -> p c t", t=128)[:, :, 0:n], func=AF.Copy)),
                        R=[f'ps{b}'], W=[f'x{c}' for c in range(c4 * 4, c4 * 4 + 4)])
                else:
                    P.add('dve', (lambda e, b=b, c4=c4, t0=t0, n=n: e.tensor_copy(
                        out=x[:, c4 * 4:c4 * 4 + 4, t0:t0 + n],
                        in_=pst[b][:, :].rearrange("p (c t) -> p c t", t=128)[:, :, 0:n])),
                        R=[f'ps{b}'], W=[f'x{c}' for c in range(c4 * 4, c4 * 4 + 4)])
            t0 += n

    def rms(T):
        nts = nts_of(T)
        bs = [nbank() for _ in nts]
        for c in range(NCH):
            P.add('act', (lambda e, c=c: e.activation(out=sq[:, 0:T], in_=x[:, c, 0:T], func=AF.Square)),
                  R=[f'x{c}'], W=['ucr'])
            for (lo, hi), b in zip(nts, bs):
                P.add('pe', (lambda e, c=c, lo=lo, hi=hi, b=b: e.matmul(
                    pst[b][:, 0:hi - lo], onesr[:], sq[:, lo:hi], start=(c == 0), stop=(c == NCH - 1))),
                    R=['ucr', 'onesr'], W=[f'ps{b}'])
        for (lo, hi), b in zip(nts, bs):
            P.add('dve', (lambda e, lo=lo, hi=hi, b=b: e.tensor_scalar(
                out=rstd[:, lo:hi], in0=pst[b][:, 0:hi - lo], scalar1=1.0 / D, scalar2=EPS, op0=ALU.mult, op1=ALU.add)),
                R=[f'ps{b}'], W=['rstd'])
        P.add('act', lambda e: e.activation(out=rstd[:, 0:T], in_=rstd[:, 0:T], func=AF.Sqrt), R=['rstd'], W=['rstd'])
        P.add('dve', lambda e: e.reciprocal(out=rstd[:, 0:T], in_=rstd[:, 0:T]), R=['rstd'], W=['rstd'])

    def norm_to_xn(T, gk):
        for c in range(NCH):
            P.add('dve', (lambda e, c=c: e.scalar_tensor_tensor(
                out=xn[:, c, 0:T], in0=x[:, c, 0:T], scalar=vcol(gk, c), in1=rstd[:, 0:T], op0=ALU.mult, op1=ALU.mult)),
                R=[f'x{c}', 'rstd', 'vecs'], W=[f'xn{c}'])

    qdma = dict(n=0)

    def ssm(T, Tp, nseg_s, kcol0, state_only, seq0):
        nts = nts_of(T)
        A1, A2 = S(0)[:, 0:T], S(1)[:, 0:T]
        TSs = [S(2)[:, 0:T], S(10)[:, 0:T]]
        TCs = [S(3)[:, 0:T], S(11)[:, 0:T]]
        RMs = [S(4)[:, 0:T], S(12)[:, 0:T]]
        W1, W2, W3 = S(5)[:, 0:T], S(6)[:, 0:T], S(7)[:, 0:T]
        G1, G2 = S(8)[:, 0:T], S(9)[:, 0:T]
        nseg = 1 + nseg_s
        sview = lambda ap: ap[:, Tp:Tp + 8 * nseg_s].rearrange("p (s k) -> p s k", k=8)

        def tablegen(q):
            p = q % 2
            TSb, TCb, RM = TSs[p], TCs[p], RMs[p]
            thq, rq = prm[:, 0, q:q + 1], prm[:, 1, q:q + 1]
            P.add('act', (lambda e: e.activation(out=A1, in_=kidx[:, 0:T], func=AF.Identity, scale=thq, bias=magic_c[:, 0:1])),
                  R=['kidx', 'prm', 'magic'], W=['A1'])
            P.add('act', (lambda e: e.activation(out=A1, in_=A1, func=AF.Identity, scale=1.0, bias=magic_c[:, 1:2])), R=['A1', 'magic'], W=['A1'])
            P.add('act', (lambda e: e.activation(out=A2, in_=kidx[:, 0:T], func=AF.Copy, scale=thq)),
                  R=['kidx', 'prm'], W=['A2'])
            P.add('pool', (lambda e: e.tensor_tensor(out=A2, in0=A2, in1=A1, op=ALU.subtract)), R=['A1', 'A2'], W=['A2'])
            P.add('act', (lambda e: e.activation(out=TSb, in_=A2, func=AF.Sin, scale=TWO_PI * 0.999999)), R=['A2'], W=[f'TS{p}'])
            P.add('act', (lambda e: e.activation(out=TCb, in_=A2, func=AF.Sin, scale=math.pi * 0.999999)), R=['A2'], W=[f'TC{p}'])
            P.add('pool', (lambda e: e.tensor_tensor(out=TCb, in0=TCb, in1=TCb, op=ALU.mult)), R=[f'TC{p}'], W=[f'TC{p}'])
            P.add('pool', (lambda e: e.tensor_scalar(out=TCb, in0=TCb, scalar1=-2.0, scalar2=1.0, op0=ALU.mult, op1=ALU.add)), R=[f'TC{p}'], W=[f'TC{p}'])
            P.add('act', (lambda e: e.activation(out=RM, in_=rmask[:, 0:T], func=AF.Copy, scale=rq)),
                  R=['rmask', 'prm'], W=[f'RM{p}'])

        tablegen(0)
        for c in range(NCH):
            P.add('dve', (lambda e, c=c: e.scalar_tensor_tensor(
                out=ucr[:, 0:T], in0=x[:, c, 0:T], scalar=vcol(0, c), in1=rstd[:, 0:T], op0=ALU.mult, op1=ALU.mult)),
                R=[f'x{c}', 'rstd', 'vecs'], W=['ucr'])
            ybs = [6, 7] if not state_only else []
            for ql in range(4):
                q = 4 * c + ql
                p = q % 2
                TSb, TCb, RM = TSs[p], TCs[p], RMs[p]
                tsn, tcn, rmn = f'TS{p}', f'TC{p}', f'RM{p}'
                bsl = qdma['n'] % 2
                qdma['n'] += 1
                P.add('pool', (lambda e, q=q, bsl=bsl: e.dma_start(out=bt[:, bsl, :], in_=bt_d[q])),
                      W=[f'bt{bsl}'], dsem=f'bt{bsl}')
                if not state_only:
                    P.add('sp', (lambda e, q=q: e.dma_start(out=ctraw[:, 0, :], in_=ct_d[q])),
                          W=['ctraw'], dsem='ct')
                if q + 1 < 128:
                    tablegen(q + 1)
                arq, aiq = prm[:, 2, q:q + 1], prm[:, 3, q:q + 1]
                P.add('dve', (lambda e, q=q, aiq=aiq: e.tensor_scalar(out=zt[:, 0:nseg], in0=St[:, q, 0:nseg, 1], scalar1=aiq, scalar2=None, op0=ALU.mult)), R=['St', 'prm'], W=['zt'])
                P.add('dve', (lambda e, q=q, arq=arq: e.scalar_tensor_tensor(out=zq[:, 0:nseg, 0], in0=St[:, q, 0:nseg, 0], scalar=arq, in1=zt[:, 0:nseg], op0=ALU.mult, op1=ALU.subtract)), R=['St', 'prm', 'zt'], W=['zq'])
                P.add('dve', (lambda e, q=q, arq=arq: e.tensor_scalar(out=zt[:, 0:nseg], in0=St[:, q, 0:nseg, 1], scalar1=arq, scalar2=None, op0=ALU.mult)), R=['St', 'prm', 'zq'], W=['zt'])
                P.add('dve', (lambda e, q=q, aiq=aiq: e.scalar_tensor_tensor(out=zq[:, 0:nseg, 1], in0=St[:, q, 0:nseg, 0], scalar=aiq, in1=zt[:, 0:nseg], op0=ALU.mult, op1=ALU.add)), R=['St', 'prm', 'zt'], W=['zq'])
                bb = [(nbank(6), nbank(6)) for _ in nts]
                for (lo, hi), (b0, b1) in zip(nts, bb):
                    for comp, b in ((0, b0), (1, b1)):
                        P.add('pe', (lambda e, comp=comp, b=b, lo=lo, hi=hi, bsl=bsl: e.matmul(
                            pst[b][:, 0:hi - lo], bt[:, bsl, comp * 128:(comp + 1) * 128], ucr[:, lo:hi], start=True, stop=True)),
                            R=[f'bt{bsl}', 'ucr'], W=[f'ps{b}'])
                for (lo, hi), (b0, b1) in zip(nts, bb):
                    n = hi - lo
                    P.add('dve', (lambda e, lo=lo, hi=hi, b0=b0, n=n, TCb=TCb: e.tensor_tensor(out=W1[:, lo:hi], in0=pst[b0][:, 0:n], in1=TCb[:, lo:hi], op=ALU.mult)),
                          R=[f'ps{b0}', tcn], W=['W1'])
                    P.add('dve', (lambda e, lo=lo, hi=hi, b1=b1, n=n, TSb=TSb: e.tensor_tensor(out=W2[:, lo:hi], in0=pst[b1][:, 0:n], in1=TSb[:, lo:hi], op=ALU.mult)),
                          R=[f'ps{b1}', tsn], W=['W2'])
                    P.add('dve', (lambda e, lo=lo, hi=hi, b1=b1, n=n, TCb=TCb: e.tensor_tensor(out=W3[:, lo:hi], in0=pst[b1][:, 0:n], in1=TCb[:, lo:hi], op=ALU.mult)),
                          R=[f'ps{b1}', tcn], W=['W3'])
                    P.add('dve', (lambda e, lo=lo, hi=hi, b0=b0, n=n, TSb=TSb: e.tensor_tensor(out=G2[:, lo:hi], in0=pst[b0][:, 0:n], in1=TSb[:, lo:hi], op=ALU.mult)),
                          R=[f'ps{b0}', tsn], W=['G2'])
                P.add('dve', (lambda e: e.tensor_tensor(out=W1, in0=W1, in1=W2, op=ALU.add)), R=['W1', 'W2'], W=['W1'])
                P.add('dve', (lambda e: e.tensor_tensor(out=W3, in0=W3, in1=G2, op=ALU.subtract)), R=['W3', 'G2'], W=['W3'])
                for comp, Wb, nm in ((0, W1, 'W1'), (1, W3, 'W3')):
                    P.add('dve', (lambda e, comp=comp, Wb=Wb: e.tensor_tensor(
                        out=Wb[:, 0:1], in0=Wb[:, 0:1], in1=zq[:, 0:1, comp], op=ALU.add)), R=[nm, 'zq'], W=[nm])
                    if nseg_s:
                        P.add('dve', (lambda e, comp=comp, Wb=Wb: e.tensor_tensor(
                            out=sview(Wb)[:, :, 0], in0=sview(Wb)[:, :, 0], in1=zq[:, 1:nseg, comp], op=ALU.add)), R=[nm, 'zq'], W=[nm])
                P.add('dve', (lambda e, RM=RM: e.tensor_tensor_scan(out=W2, data0=RM, data1=W1, initial=0.0, op0=ALU.mult, op1=ALU.add)),
                      R=[rmn, 'W1'], W=['W2'])
                P.add('dve', (lambda e, RM=RM: e.tensor_tensor_scan(out=W1, data0=RM, data1=W3, initial=0.0, op0=ALU.mult, op1=ALU.add)),
                      R=[rmn, 'W3', 'W2'], W=['W1'])
                if state_only:
                    lo, hi = Tp - 1, Tp
                    P.add('dve', (lambda e, TCb=TCb: e.tensor_tensor(out=W3[:, lo:hi], in0=W2[:, lo:hi], in1=TCb[:, lo:hi], op=ALU.mult)), R=['W2', tcn], W=['W3'])
                    P.add('dve', (lambda e, TSb=TSb: e.tensor_tensor(out=G2[:, lo:hi], in0=W1[:, lo:hi], in1=TSb[:, lo:hi], op=ALU.mult)), R=['W1', tsn], W=['G2'])
                    P.add('dve', (lambda e, q=q: e.tensor_tensor(out=St[:, q, 0:1, 0], in0=W3[:, lo:hi], in1=G2[:, lo:hi], op=ALU.subtract)), R=['W3', 'G2'], W=['St'])
                    P.add('dve', (lambda e, TSb=TSb: e.tensor_tensor(out=W3[:, lo:hi], in0=W2[:, lo:hi], in1=TSb[:, lo:hi], op=ALU.mult)), R=['W2', tsn], W=['W3'])
                    P.add('dve', (lambda e, TCb=TCb: e.tensor_tensor(out=G2[:, lo:hi], in0=W1[:, lo:hi], in1=TCb[:, lo:hi], op=ALU.mult)), R=['W1', tcn], W=['G2'])
                    P.add('dve', (lambda e, q=q: e.tensor_tensor(out=St[:, q, 0:1, 1], in0=W3[:, lo:hi], in1=G2[:, lo:hi], op=ALU.add)), R=['W3', 'G2'], W=['St'])
                    continue
                P.add('pool', (lambda e, TCb=TCb: e.tensor_tensor(out=A1, in0=W2, in1=TCb, op=ALU.mult)), R=['W2', tcn], W=['A1'])
                P.add('pool', (lambda e, TSb=TSb: e.tensor_tensor(out=A2, in0=W1, in1=TSb, op=ALU.mult)), R=['W1', tsn], W=['A2'])
                P.add('pool', (lambda e: e.tensor_tensor(out=xr[:, 0:T], in0=A1, in1=A2, op=ALU.subtract)), R=['A1', 'A2'], W=['xr'])
                P.add('pool', (lambda e, TSb=TSb: e.tensor_tensor(out=A1, in0=W2, in1=TSb, op=ALU.mult)), R=['W2', tsn], W=['A1'])
                P.add('pool', (lambda e, TCb=TCb: e.tensor_tensor(out=A2, in0=W1, in1=TCb, op=ALU.mult)), R=['W1', tcn], W=['A2'])
                P.add('pool', (lambda e: e.tensor_tensor(out=xi[:, 0:T], in0=A1, in1=A2, op=ALU.add)), R=['A1', 'A2'], W=['xi'])
                for comp, src, nm in ((0, xr, 'xr'), (1, xi, 'xi')):
                    P.add('act', (lambda e, comp=comp, src=src, q=q: e.activation(out=St[:, q, 0:1, comp], in_=src[:, Tp - 1:Tp], func=AF.Copy)), R=[nm], W=['St'])
                    if nseg_s:
                        P.add('act', (lambda e, comp=comp, src=src, q=q: e.activation(
                            out=St[:, q, 1:nseg, comp], in_=sview(src)[:, :, 7], func=AF.Copy)), R=[nm], W=['St'])
                frq, fiq = prm[:, 4, q:q + 1], prm[:, 5, q:q + 1]
                cpr = cp[:, ql, 0, 32 * ql:32 * ql + 32]
                cpi = cp[:, ql, 1, 32 * ql:32 * ql + 32]
                t32a, t32b = G1[:, 0:32], G1[:, 32:64]
                P.add('dve', (lambda e, frq=frq: e.tensor_scalar(out=t32a, in0=ctraw[:, 0, 0:32], scalar1=frq, scalar2=None, op0=ALU.mult)), R=['ctraw', 'prm'], W=['G1'])
                P.add('dve', (lambda e, fiq=fiq: e.tensor_scalar(out=t32b, in0=ctraw[:, 0, 32:64], scalar1=fiq, scalar2=None, op0=ALU.mult)), R=['ctraw', 'prm'], W=['G1'])
                P.add('dve', (lambda e, cpr=cpr: e.tensor_tensor(out=cpr, in0=t32a, in1=t32b, op=ALU.subtract)), R=['G1'], W=[f'cp{ql}'])
                P.add('dve', (lambda e, fiq=fiq: e.tensor_scalar(out=t32a, in0=ctraw[:, 0, 0:32], scalar1=fiq, scalar2=-1.0, op0=ALU.mult, op1=ALU.mult)), R=['ctraw', 'prm'], W=['G1'])
                P.add('dve', (lambda e, frq=frq: e.tensor_scalar(out=t32b, in0=ctraw[:, 0, 32:64], scalar1=frq, scalar2=None, op0=ALU.mult)), R=['ctraw', 'prm'], W=['G1'])
                P.add('dve', (lambda e, cpi=cpi: e.tensor_tensor(out=cpi, in0=t32a, in1=t32b, op=ALU.subtract)), R=['G1'], W=[f'cp{ql}'])
                for (lo, hi), yb in zip(nts, ybs):
                    P.add('pe', (lambda e, lo=lo, hi=hi, yb=yb, ql=ql: e.matmul(
                        pst[yb][:, 0:hi - lo], cp[:, ql, 0, :], xr[:, lo:hi], start=(ql == 0), stop=False)),
                        R=[f'cp{ql}', 'xr'], W=[f'ps{yb}'])
                    P.add('pe', (lambda e, lo=lo, hi=hi, yb=yb, ql=ql: e.matmul(
                        pst[yb][:, 0:hi - lo], cp[:, ql, 1, :], xi[:, lo:hi], start=False, stop=(ql == 3))),
                        R=[f'cp{ql}', 'xi'], W=[f'ps{yb}'])
            if state_only:
                continue
            for (lo, hi), yb in zip(nts, ybs):
                n = hi - lo
                P.add('dve', (lambda e, lo=lo, hi=hi, yb=yb, n=n, c=c: e.scalar_tensor_tensor(
                    out=W1[:, lo:hi], in0=ucr[:, lo:hi], scalar=vcol(5, c), in1=pst[yb][:, 0:n], op0=ALU.mult, op1=ALU.add)),
                    R=['ucr', 'vecs', f'ps{yb}'], W=['W1'])
            P.add('act', (lambda e: e.activation(out=W2, in_=W1, func=AF.Square)), R=['W1'], W=['W2'])
            P.add('dve', (lambda e: e.tensor_scalar(out=W2, in0=W2, scalar1=0.044715 * 2 * GC0, scalar2=2 * GC0, op0=ALU.mult, op1=ALU.add)), R=['W2'], W=['W2'])
            P.add('dve', (lambda e: e.tensor_tensor(out=W2, in0=W2, in1=W1, op=ALU.mult)), R=['W2', 'W1'], W=['W2'])
            P.add('act', (lambda e: e.activation(out=W3, in_=W2, func=AF.Sigmoid)), R=['W2'], W=['W3'])
            P.add('dve', (lambda e, c=c: e.tensor_tensor(out=xn[:, c, 0:T], in0=W3, in1=W1, op=ALU.mult)), R=['W3', 'W1'], W=[f'xn{c}'])

    def glu(T):
        nts = nts_of(T)
        for m in range(NCH):
            bs = [[nbank() for _ in nts] for _ in range(2)]
            for part in range(2):
                for half in range(2):
                    sl = wtile()
                    for kl in range(16):
                        k = half * 16 + kl
                        for (lo, hi), b in zip(nts, bs[part]):
                            P.add('pe', (lambda e, sl=sl, kl=kl, k=k, lo=lo, hi=hi, b=b: e.matmul(
                                pst[b][:, 0:hi - lo], ring[:, sl, kl * 128:(kl + 1) * 128], xn[:, k, lo:hi],
                                start=(k == 0), stop=(k == 31))), R=[f'ring{sl}', f'xn{k}'], W=[f'ps{b}'])
            for i, (lo, hi) in enumerate(nts):
                n = hi - lo
                b1, b2 = bs[0][i], bs[1][i]
                G = S(8 + i % 2)[:, 0:n]
                gn = f'G{1 + i % 2}'
                P.add('act', (lambda e, b2=b2, n=n, G=G: e.activation(out=G, in_=pst[b2][:, 0:n], func=AF.Sigmoid)), R=[f'ps{b2}'], W=[gn])
                P.add('dve', (lambda e, b1=b1, n=n, G=G: e.tensor_tensor(out=G, in0=pst[b1][:, 0:n], in1=G, op=ALU.mult)), R=[f'ps{b1}', gn], W=[gn])
                P.add('dve', (lambda e, m=m, lo=lo, hi=hi, G=G: e.tensor_tensor(out=x[:, m, lo:hi], in0=x[:, m, lo:hi], in1=G, op=ALU.add)), R=[gn, f'x{m}'], W=[f'x{m}'])

    def ffn(T):
        nts = nts_of(T)
        for g in range(NGRP):
            nf = FG if g < NGRP - 1 else NF - FG * (NGRP - 1)
            for fl in range(nf):
                bs = [[nbank() for _ in nts] for _ in range(2)]
                for part in range(2):
                    for half in range(2):
                        sl = wtile()
                        for kl in range(16):
                            k = half * 16 + kl
                            for (lo, hi), b in zip(nts, bs[part]):
                                P.add('pe', (lambda e, sl=sl, kl=kl, k=k, lo=lo, hi=hi, b=b: e.matmul(
                                    pst[b][:, 0:hi - lo], ring[:, sl, kl * 128:(kl + 1) * 128], xn[:, k, lo:hi],
                                    start=(k == 0), stop=(k == 31))), R=[f'ring{sl}', f'xn{k}'], W=[f'ps{b}'])
                for i, (lo, hi) in enumerate(nts):
                    n = hi - lo
                    bg, bu = bs[0][i], bs[1][i]
                    G = S(8 + i % 2)[:, 0:n]
                    gn = f'G{1 + i % 2}'
                    P.add('act', (lambda e, bg=bg, n=n, G=G: e.activation(out=G, in_=pst[bg][:, 0:n], func=AF.Silu)), R=[f'ps{bg}'], W=[gn])
                    P.add('dve', (lambda e, bu=bu, n=n, G=G, fl=fl, lo=lo, hi=hi: e.tensor_tensor(
                        out=h1[:, fl, lo:hi], in0=pst[bu][:, 0:n], in1=G, op=ALU.mult)), R=[f'ps{bu}', gn], W=[f'h1_{fl}'])
            for mp in range(8):
                sl = wtile()
                for ml in range(4):
                    m = mp * 4 + ml
                    for i, (lo, hi) in enumerate(nts):
                        n = hi - lo
                        b = nbank()
                        for fl in range(nf):
                            P.add('pe', (lambda e, sl=sl, fl=fl, ml=ml, lo=lo, hi=hi, b=b, nf=nf: e.matmul(
                                pst[b][:, 0:hi - lo], ring[:, sl, fl * 512 + ml * 128:fl * 512 + (ml + 1) * 128], h1[:, fl, lo:hi],
                                start=(fl == 0), stop=(fl == nf - 1))), R=[f'ring{sl}', f'h1_{fl}'], W=[f'ps{b}'])
                        P.add('dve', (lambda e, m=m, lo=lo, hi=hi, b=b, n=n: e.tensor_tensor(
                            out=x[:, m, lo:hi], in0=pst[b][:, 0:n], in1=x[:, m, lo:hi], op=ALU.add)), R=[f'ps{b}', f'x{m}'], W=[f'x{m}'])

    def pool_layer(T, Tp, poscol0, first, st_idx):
        nts = nts_of(T)
        E = 15 + Tp
        ES = 8 * 23
        icnt = scr[:, 10 * TM - 4 * TM:10 * TM]
        posr = S(5)
        P.add('sp', (lambda e: e.dma_start(out=posr[:, 0:T], in_=pos_d[:, poscol0:poscol0 + T])), W=['A1p'], dsem='pos')
        for wi, w in enumerate(WIN):
            ic = icnt[:, wi * TM:wi * TM + T]
            P.add('dve', (lambda e, ic=ic, w=w: e.tensor_scalar(out=ic, in0=posr[:, 0:T], scalar1=1.0, scalar2=float(w), op0=ALU.add, op1=ALU.min)), R=['A1p'], W=[f'ic{wi}'])
            P.add('dve', (lambda e, ic=ic: e.reciprocal(out=ic, in_=ic)), R=[f'ic{wi}'], W=[f'ic{wi}'])
        hc = S(0)
        ext = scr[:, TM:TM + E + ES]
        s2 = scr[:, 3 * TM:3 * TM + E + ES]
        stg = scr[:, 5 * TM:5 * TM + 128]
        for c in range(NCH):
            gi = c // 8
            w = WIN[gi]
            P.add('dve', (lambda e, c=c: e.scalar_tensor_tensor(
                out=hc[:, 0:T], in0=x[:, c, 0:T], scalar=vcol(1, c), in1=rstd[:, 0:T], op0=ALU.mult, op1=ALU.mult)),
                R=[f'x{c}', 'rstd', 'vecs'], W=['hc'])
            P.add('act', (lambda e, c=c: e.activation(out=ext[:, 0:15], in_=hist[:, c, :], func=AF.Copy)), R=['hist'], W=['ext'])
            P.add('act', (lambda e: e.activation(out=ext[:, 15:15 + Tp], in_=hc[:, 0:Tp], func=AF.Copy)), R=['hc'], W=['ext'])
            exs = ext[:, E:E + ES].rearrange("p (s k) -> p s k", k=23)
            P.add('act', (lambda e, exs=exs: e.activation(out=exs[:, :, 15:23], in_=hc[:, Tp:Tp + 64].rearrange("p (s k) -> p s k", k=8), func=AF.Copy)), R=['hc'], W=['ext'])
            b = nbank()
            P.add('sp', (lambda e, c=c: e.dma_start(out=sphc[:, c % 2, :], in_=spool_d[st_idx * 8:st_idx * 8 + 8, :, c * 128:(c + 1) * 128].rearrange("s k d -> (s k) d"))),
                  W=[f'sphc{c % 2}'], dsem=f'sph{c % 2}')
            P.add('pe', (lambda e, c=c, b=b: e.transpose(out=pst[b][:, 0:120], in_=sphc[0:120, c % 2, :], identity=ident[0:120, 0:120])),
                  R=[f'sphc{c % 2}', 'ident'], W=[f'ps{b}'])
            P.add('dve', (lambda e, b=b, exs=exs: e.tensor_copy(out=exs[:, :, 0:15], in_=pst[b][:, 0:120].rearrange("p (s k) -> p s k", k=15))), R=[f'ps{b}'], W=['ext'])
            P.add('act', (lambda e, c=c: e.activation(out=hist[:, c, :], in_=hc[:, Tp - 15:Tp], func=AF.Copy)), R=['hc', 'ext'], W=['hist'])
            L = E + ES
            cur, other = ext, s2
            sh = 1
            while sh < w:
                P.add('dve', (lambda e, cur=cur, other=other, sh=sh, L=L: e.tensor_tensor(out=other[:, sh:L], in0=cur[:, sh:L], in1=cur[:, 0:L - sh], op=ALU.add)),
                      R=['ext', 's2'], W=['ext', 's2'])
                if sh > 1 or True:
                    P.add('act', (lambda e, cur=cur, other=other, sh=sh: e.activation(out=other[:, 0:sh], in_=cur[:, 0:sh], func=AF.Copy)), R=['ext', 's2'], W=['ext', 's2'])
                cur, other = other, cur
                sh *= 2
            ic = icnt[:, gi * TM:gi * TM + T]
            pb = S(5)
            P.add('dve', (lambda e, cur=cur, ic=ic: e.tensor_tensor(out=pb[:, 0:Tp], in0=cur[:, 15:15 + Tp], in1=ic[:, 0:Tp], op=ALU.mult)), R=['ext', 's2', f'ic{gi}'], W=['pb'])
            curs = cur[:, E:E + ES].rearrange("p (s k) -> p s k", k=23)
            P.add('dve', (lambda e, curs=curs, ic=ic: e.tensor_tensor(
                out=pb[:, Tp:Tp + 64].rearrange("p (s k) -> p s k", k=8), in0=curs[:, :, 15:23],
                in1=ic[:, Tp:Tp + 64].rearrange("p (s k) -> p s k", k=8), op=ALU.mult)), R=['ext', 's2', f'ic{gi}'], W=['pb'])
            if T > Tp + 64:
                P.add('dve', (lambda e: e.memset(pb[:, Tp + 64:T], 0.0)), W=['pb'])
            P.add('dve', (lambda e, c=c: e.tensor_tensor(out=xn[:, c, 0:T], in0=pb[:, 0:T], in1=hc[:, 0:T], op=ALU.subtract)), R=['pb', 'hc'], W=[f'xn{c}'])
            b = nbank()
            P.add('pe', (lambda e, b=b: e.transpose(out=pst[b][0:64, 0:128], in_=hc[:, Tp:Tp + 64], identity=ident[:, :])), R=['hc', 'ident'], W=[f'ps{b}'])
            P.add('act', (lambda e, b=b, c=c: e.activation(out=hst[0:64, c % 4, :], in_=pst[b][0:64, 0:128], func=AF.Copy)), R=[f'ps{b}'], W=[f'hst{c % 4}'])
            P.add('sp', (lambda e, c=c: e.dma_start(out=psn_d[st_idx, c, :, :], in_=hst[0:64, c % 4, :])), R=[f'hst{c % 4}'], dsem=f'o_psn{c % 4}')
            if not first:
                b = nbank()
                P.add('pe', (lambda e, b=b: e.transpose(out=pst[b][0:15, 0:128], in_=hc[:, Tp - 15:Tp], identity=ident[:, :])), R=['hc', 'ident'], W=[f'ps{b}'])
                P.add('act', (lambda e, b=b, c=c: e.activation(out=hpt[0:15, c % 4, :], in_=pst[b][0:15, 0:128], func=AF.Copy)), R=[f'ps{b}'], W=[f'hpt{c % 4}'])
                P.add('sp', (lambda e, c=c: e.dma_start(out=ppn_d[c, :, :], in_=hpt[0:15, c % 4, :])), R=[f'hpt{c % 4}'], dsem=f'o_ppn{c % 4}')
        for gi in range(4):
            for mpair in range(4):
                sl = wtile()
                for ml in range(2):
                    m = gi * 8 + mpair * 2 + ml
                    for (lo, hi) in nts:
                        n = hi - lo
                        b = nbank()
                        for k in range(8):
                            P.add('pe', (lambda e, sl=sl, ml=ml, k=k, gi=gi, lo=lo, hi=hi, b=b: e.matmul(
                                pst[b][:, 0:hi - lo], ring[:, sl, (ml * 8 + k) * 128:(ml * 8 + k + 1) * 128], xn[:, gi * 8 + k, lo:hi],
                                start=(k == 0), stop=(k == 7))), R=[f'ring{sl}', f'xn{gi * 8 + k}'], W=[f'ps{b}'])
                        P.add('dve', (lambda e, m=m, lo=lo, hi=hi, b=b, n=n: e.scalar_tensor_tensor(
                            out=x[:, m, lo:hi], in0=pst[b][:, 0:n], scalar=vcol(6, m), in1=x[:, m, lo:hi], op0=ALU.mult, op1=ALU.add)),
                            R=[f'ps{b}', f'x{m}', 'vecs'], W=[f'x{m}'])


    def out_y(T, Tp, prow0, srow0, halo):
        ost = scr[:, 0:D]
        segs = []
        t = halo
        while t < Tp:
            n = min(128, Tp - t)
            segs.append((t, n, prow0 + (t - halo)))
            t += n
        segs.append((Tp, 64, srow0))
        for (t0, n, r0) in segs:
            for c in range(NCH):
                hcf = S(8 + c % 2)
                gn = f'G{1 + c % 2}'
                P.add('dve', (lambda e, c=c, t0=t0, n=n, hcf=hcf: e.scalar_tensor_tensor(
                    out=hcf[:, 0:n], in0=x[:, c, t0:t0 + n], scalar=vcol(4, c), in1=rstd[:, t0:t0 + n], op0=ALU.mult, op1=ALU.mult)),
                    R=[f'x{c}', 'rstd', 'vecs'], W=[gn])
                b = nbank()
                P.add('pe', (lambda e, b=b, n=n, hcf=hcf: e.transpose(out=pst[b][0:n, 0:128], in_=hcf[:, 0:n], identity=ident[:, :])), R=[gn, 'ident'], W=[f'ps{b}'])
                P.add('act', (lambda e, b=b, n=n, c=c: e.activation(out=ost[0:n, c * 128:(c + 1) * 128], in_=pst[b][0:n, 0:128], func=AF.Copy)), R=[f'ps{b}'], W=['stage'])
            P.add('sp', (lambda e, n=n, r0=r0: e.dma_start(out=y_d[r0:r0 + n, :], in_=ost[0:n, :])), R=['stage'], dsem='o_y')

    def load_sample_states(seq0):
        tmp = scr[:, 0:2 * 8 * 128].rearrange("p (a s q) -> p a s q", a=2, s=8)
        for a_ in range(2):
            P.add('sp', (lambda e, a_=a_: e.dma_start(out=tmp[:, a_, :, :], in_=sst_d[a_, seq0:seq0 + 8, :, :].rearrange("s q p -> q s p"))), W=['stage'], dsem='ld')
        xs = scr[:, 2048:2048 + 2048].rearrange("p (a s q) -> p a s q", a=2, s=8)
        for a_ in range(2):
            for s in range(8):
                b = nbank()
                P.add('pe', (lambda e, a_=a_, s=s, b=b: e.transpose(out=pst[b][:, 0:128], in_=tmp[:, a_, s, :], identity=ident[:, :])), R=['stage', 'ident'], W=[f'ps{b}'])
                P.add('act', (lambda e, a_=a_, s=s, b=b: e.activation(out=xs[:, a_, s, :], in_=pst[b][:, 0:128], func=AF.Copy)), R=[f'ps{b}'], W=['xs'])
        for s in range(8):
            t_ = scr[:, 4096:4224]
            P.add('dve', (lambda e, s=s: e.tensor_tensor(out=t_, in0=xs[:, 1, s, :], in1=prm[:, 7, :], op=ALU.mult)), R=['xs', 'prm'], W=['A1p'])
            P.add('dve', (lambda e, s=s: e.tensor_tensor(out=St[:, :, 1 + s, 0], in0=xs[:, 0, s, :], in1=prm[:, 6, :], op=ALU.mult)), R=['xs', 'prm'], W=['St'])
            P.add('dve', (lambda e, s=s: e.tensor_tensor(out=St[:, :, 1 + s, 0], in0=St[:, :, 1 + s, 0], in1=t_, op=ALU.subtract)), R=['St', 'A1p'], W=['St'])
            P.add('dve', (lambda e, s=s: e.tensor_tensor(out=t_, in0=xs[:, 1, s, :], in1=prm[:, 6, :], op=ALU.mult)), R=['xs', 'prm', 'St'], W=['A1p'])
            P.add('dve', (lambda e, s=s: e.tensor_tensor(out=St[:, :, 1 + s, 1], in0=xs[:, 0, s, :], in1=prm[:, 7, :], op=ALU.mult)), R=['xs', 'prm'], W=['St'])
            P.add('dve', (lambda e, s=s: e.tensor_tensor(out=St[:, :, 1 + s, 1], in0=St[:, :, 1 + s, 1], in1=t_, op=ALU.add)), R=['St', 'A1p'], W=['St'])

    def store_states(segs, dst_fn, dsem):
        ob = scr[:, 0:2 * 9 * 128].rearrange("p (a s q) -> p a s q", a=2, s=9)
        for seg in segs:
            t_ = scr[:, 2304:2432]
            u_ = scr[:, 2432:2560]
            P.add('dve', (lambda e, seg=seg: e.tensor_tensor(out=t_, in0=St[:, :, seg, 1], in1=prm[:, 5, :], op=ALU.mult)), R=['St', 'prm'], W=['A1p'])
            P.add('dve', (lambda e, seg=seg: e.tensor_tensor(out=u_, in0=St[:, :, seg, 0], in1=prm[:, 4, :], op=ALU.mult)), R=['St', 'prm'], W=['A1q'])
            P.add('dve', (lambda e: e.tensor_tensor(out=u_, in0=u_, in1=t_, op=ALU.subtract)), R=['A1p', 'A1q'], W=['A1q'])
            b = nbank()
            P.add('pe', (lambda e, b=b: e.transpose(out=pst[b][:, 0:128], in_=u_, identity=ident[:, :])), R=['A1q', 'ident'], W=[f'ps{b}'])
            P.add('act', (lambda e, b=b, seg=seg: e.activation(out=ob[:, 0, seg, :], in_=pst[b][:, 0:128], func=AF.Copy)), R=[f'ps{b}'], W=['ob'])
            P.add('dve', (lambda e, seg=seg: e.tensor_tensor(out=t_, in0=St[:, :, seg, 1], in1=prm[:, 4, :], op=ALU.mult)), R=['St', 'prm', 'A1q'], W=['A1p'])
            P.add('dve', (lambda e, seg=seg: e.tensor_tensor(out=u_, in0=St[:, :, seg, 0], in1=prm[:, 5, :], op=ALU.mult)), R=['St', 'prm'], W=['A1q'])
            P.add('dve', (lambda e: e.tensor_tensor(out=u_, in0=u_, in1=t_, op=ALU.add)), R=['A1p', 'A1q'], W=['A1q'])
            b = nbank()
            P.add('pe', (lambda e, b=b: e.transpose(out=pst[b][:, 0:128], in_=u_, identity=ident[:, :])), R=['A1q', 'ident'], W=[f'ps{b}'])
            P.add('act', (lambda e, b=b, seg=seg: e.activation(out=ob[:, 1, seg, :], in_=pst[b][:, 0:128], func=AF.Copy)), R=[f'ps{b}'], W=['ob'])
            for a_ in range(2):
                P.add('sp', (lambda e, a_=a_, seg=seg: e.dma_start(out=dst_fn(a_, seg), in_=ob[:, a_, seg, :])), R=['ob'], dsem=dsem)

    def load_kr(src, col0, T):
        P.add('sp', (lambda e: e.dma_start(out=kidx[:, 0:T], in_=src[0, :, col0:col0 + T])), W=['kidx'], dsem='k0')
        P.add('sp', (lambda e: e.dma_start(out=rmask[:, 0:T], in_=src[1, :, col0:col0 + T])), W=['rmask'], dsem='k1')

    def dump(name, ap, regs, n, b3=None):
        if not stage:
            return
        off = dcur['o']
        dcur['o'] += n
        DBG.append((name, off, n))
        o_ap = dbg_d[:, off:off + n]
        if b3:
            o_ap = o_ap.rearrange("p (a b) -> p a b", b=b3)
        P.add('sp', (lambda e: e.dma_start(out=o_ap, in_=ap)), R=regs, dsem='o_dbg')

    def finish():
        P.add('sp', (lambda e: e.nop()), R=[], W=[], dsem=None)
        last = P.ops[-1]
        for k, j in P.lastdma.items():
            if k.startswith('o_'):
                last['deps'][j] = 'raw'
        P.emit(nc, block, es)
        es.close()
        _CACHE['P'] = P
        return nc

    barrier()
    dump('prm', prm[:, :, :].rearrange("p a b -> p (a b)"), ['prm', 'scrp'], 1280)
    load_kr(krp_d, 0, TPRE)
    for seg in range(2):
        load_x(seg * TPRE, TPRE)
        barrier()
        if seg == 1:
            dump('x_c0', x[:, 0, 0:TPRE], ['x0'], TPRE)
            dump('x_c31', x[:, 31, 0:TPRE], ['x31'], TPRE)
        rms(TPRE)
        if seg == 1:
            dump('rstd', rstd[:, 0:TPRE], ['rstd'], TPRE)
        ssm(TPRE, TPRE, 0, 0, True, 0)
        barrier()
        dump(f'St_pre{seg}', St[:, :, 0, :], ['St'], 256, b3=2)
        if stage == 1 and seg == 1:
            return finish()
    row0 = 2 * TPRE
    for st_idx, (T, Tp, halo) in enumerate(((TA, TPA, 15), (TB, TPB, 0))):
        first = st_idx == 0
        load_kr(kr_d, st_idx * TM, T)
        load_sample_states(st_idx * 8)
        barrier()
        load_x(row0, T)
        barrier()
        rms(T)
        ssm(T, Tp, 8, 0, False, st_idx * 8)
        barrier()
        if stage and first:
            dump('StA', St[:, :, 0, :], ['St'], 256, b3=2)
            for cc in (0, 17, 31):
                P.add('dve', (lambda e, cc=cc: e.tensor_copy(out=S(9), in_=xn[:, cc, :])), R=[f'xn{cc}'], W=['G2'])
                dump(f'xnA_{cc}', S(9), ['G2'], TM)
            P.add('dve', (lambda e: e.tensor_copy(out=S(8), in_=xr[:, :])), R=['xr'], W=['G1'])
            dump('xrA', S(8), ['G1'], TM)
            dump('TSA', S(2), ['TS0'], TM)
            dump('TCA', S(3), ['TC0'], TM)
            dump('RMA', S(4), ['RM0'], TM)
            dump('kidxA', kidx[:, :], ['kidx'], TM)
        store_states(range(1, 9), (lambda a_, seg, st_idx=st_idx: ss_d[a_, st_idx * 8 + seg - 1, :, :]), 'o_ss')
        if stage == 2:
            barrier()
            return finish()
        if not first:
            store_states([0], (lambda a_, seg: sp_d[a_, :, :]), 'o_sp')
        barrier()
        def dumpx(tag):
            if stage and first:
                barrier()
                for cc in (0, 17, 31):
                    dump(f'{tag}_{cc}', x[:, cc, :], [f'x{cc}'], TM)
                dump(f'{tag}_rstd', rstd[:, :], ['rstd'], TM)
        glu(T)
        dumpx('x1')
        if stage == 3:
            barrier()
            return finish()
        rms(T)
        norm_to_xn(T, 2)
        ffn(T)
        dumpx('x2')
        if stage == 4:
            barrier()
            return finish()
        rms(T)
        barrier()
        P.add('sp', (lambda e, st_idx=st_idx: e.dma_start(out=pso_d[st_idx * 8:st_idx * 8 + 8, :, :], in_=spool_d[st_idx * 8:st_idx * 8 + 8, 8:15, :])), dsem='o_pso')
        pool_layer(T, Tp, st_idx * TM, first, st_idx)
        barrier()
        dumpx('x3')
        if stage == 5:
            barrier()
            return finish()
        rms(T)
        norm_to_xn(T, 3)
        ffn(T)
        dumpx('x4')
        rms(T)
        barrier()
        out_y(T, Tp, st_idx * 512, 1024 + st_idx * 64, halo)
        barrier()
        if stage == 6:
            return finish()
        row0 += T
    return finish()


_CACHE = {}


def _prep_weights(ssm_w_glu, pool_w, ffn_w_gate_up, ffn_w_down):
    tiles = np.zeros((NTILE, 128, 2048), np.float32)
    W = ssm_w_glu[0].reshape(32, 128, 2, 32, 128)
    W = W.reshape(2, 16, 128, 2, 32, 128)
    tiles[0:NT_GLU] = W.transpose(4, 3, 0, 2, 1, 5).reshape(NT_GLU, 128, 2048)
    base = NT_GLU
    for L in range(2):
        GU = ffn_w_gate_up[L].reshape(2, 16, 128, 2, NF, 128)
        GUt = GU.transpose(4, 3, 0, 2, 1, 5).reshape(NF, 4, 128, 2048)
        DN = ffn_w_down[L].reshape(NF, 128, 8, 512)
        t = base
        for g in range(NGRP):
            nf = FG if g < NGRP - 1 else NF - FG * (NGRP - 1)
            for fl in range(nf):
                tiles[t:t + 4] = GUt[g * FG + fl]
                t += 4
            blk = DN[g * FG:g * FG + nf]
            tiles[t:t + 8, :, 0:nf * 512] = blk.transpose(2, 1, 0, 3).reshape(8, 128, nf * 512)
            t += 8
        assert t == base + NT_FFN
        base = t
        if L == 0:
            PW = pool_w[0].reshape(4, 8, 128, 4, 2, 128)
            tiles[base:base + NT_POOL] = PW.transpose(0, 3, 2, 4, 1, 5).reshape(NT_POOL, 128, 2048)
            base += NT_POOL
    assert base == NTILE
    return tiles


def kernel(x_prompt, x_sample, state_ssm_re, state_ssm_im, state_pool, norm_mix, norm_ffn,
           ssm_lambda_re, ssm_lambda_im, ssm_log_step, ssm_b_re, ssm_b_im, ssm_c_re, ssm_c_im,
           ssm_d, ssm_w_glu, pool_w, pool_scale, ffn_w_gate_up, ffn_w_down, norm_final):
    f = lambda a: np.ascontiguousarray(np.asarray(a, dtype=np.float32))
    x_prompt, x_sample = f(x_prompt), f(x_sample)
    stage = STAGE
    if 'nc' not in _CACHE:
        _CACHE['nc'] = build_program(stage)
    nc = _CACHE['nc']
    wst = _prep_weights(f(ssm_w_glu), f(pool_w), f(ffn_w_gate_up), f(ffn_w_down))
    if stage:
        wst = np.ascontiguousarray(wst[0:{1: 1, 2: 1, 3: NT_GLU, 4: NT_GLU + NT_FFN, 5: NT_GLU + NT_FFN + NT_POOL, 6: NTILE}[stage]])
    fm = lambda v: f(v).reshape(32, 128).T
    vecs = np.concatenate([fm(norm_mix[0]), fm(norm_mix[1]), fm(norm_ffn[0]), fm(norm_ffn[1]), fm(norm_final),
                           fm(ssm_d[0]), fm(pool_scale[0])], axis=1)
    ident = np.eye(128, dtype=np.float32)
    lq = lambda a: f(a).reshape(128, 2, 64).transpose(1, 2, 0).reshape(128, 128)
    lam = np.stack([lq(ssm_lambda_re[0]), lq(ssm_lambda_im[0]),
                    lq(np.repeat(f(ssm_log_step[0])[:, None], 64, axis=1))])
    bt = np.zeros((128, 128, 256), np.float32)
    ct = np.zeros((128, 128, 64), np.float32)
    for comp, (B, C) in enumerate(((f(ssm_b_re[0]), f(ssm_c_re[0])), (f(ssm_b_im[0]), f(ssm_c_im[0])))):
        Bq = B.reshape(128, 2, 64, 16)
        Cq = C.reshape(128, 2, 16, 64)
        for gl in range(2):
            for q4 in range(4):
                qs = np.arange(q4, 128, 4)
                r0 = q4 * 32 + gl * 16
                bt[qs, r0:r0 + 16, comp * 128 + gl * 64:comp * 128 + gl * 64 + 64] = Bq[qs, gl].transpose(0, 2, 1)
            ct[:, gl * 64:gl * 64 + 64, comp * 32 + gl * 16:comp * 32 + gl * 16 + 16] = Cq[:, gl].transpose(0, 2, 1)
    def krow(Tp, T):
        k = np.zeros(TM, np.float32)
        m = np.ones(TM, np.float32)
        k[0:Tp] = np.arange(Tp)
        m[0] = 0.0
        for s in range(8):
            if Tp + 8 * s + 8 <= T:
                k[Tp + 8 * s:Tp + 8 * s + 8] = np.arange(8)
                m[Tp + 8 * s] = 0.0
        return k, m
    kA, mA = krow(TPA, TA)
    kB, mB = krow(TPB, TB)
    kr = np.stack([np.concatenate([kA, kB]), np.concatenate([mA, mB])])
    kr = np.ascontiguousarray(np.broadcast_to(kr[:, None, :], (2, 128, 2 * TM)))
    kP = np.zeros(TM, np.float32); kP[0:TPRE] = np.arange(TPRE)
    mP = np.ones(TM, np.float32); mP[0] = 0.0
    krp = np.ascontiguousarray(np.broadcast_to(np.stack([kP, mP])[:, None, :], (2, 128, TM)))
    in_maps = []
    for core in range(8):
        b, hf = core // 2, core % 2
        xin = np.zeros((NTOK_IN, D), np.float32)
        pos = np.full((2 * TM,), 1.0e4, np.float32)
        if hf == 1:
            xin[3:3 + 1009] = x_prompt[b, 0:1009]
            xin[2 * TPRE:2 * TPRE + 15] = x_prompt[b, 1009:1024]
        p0 = hf * 1024
        a0 = 2 * TPRE
        xin[a0 + 15:a0 + 15 + 512] = x_prompt[b, p0:p0 + 512]
        xin[a0 + TPA:a0 + TPA + 64] = x_sample[core * 16:core * 16 + 8].reshape(64, D)
        b0 = a0 + TA
        xin[b0:b0 + 512] = x_prompt[b, p0 + 512:p0 + 1024]
        xin[b0 + TPB:b0 + TPB + 64] = x_sample[core * 16 + 8:core * 16 + 16].reshape(64, D)
        pos[15:15 + 512] = p0 + np.arange(512)
        pos[TM:TM + 512] = p0 + 512 + np.arange(512)
        sst = np.stack([f(state_ssm_re[0])[core * 16:core * 16 + 16].reshape(16, 128, 128),
                        f(state_ssm_im[0])[core * 16:core * 16 + 16].reshape(16, 128, 128)])
        in_maps.append(dict(xin=xin, wst=wst, vecs=vecs, ident=ident, kr=kr, krp=krp,
                            pos=np.ascontiguousarray(np.broadcast_to(pos[None, :], (128, 2 * TM))),
                            lam=lam, bt=bt, ct=ct, sst=sst,
                            spool=f(state_pool[0])[core * 16:core * 16 + 16]))
    res = run_bass_kernel_spmd(nc, in_maps, core_ids=list(range(8)))
    R = res.results
    if stage:
        _CACHE['dbg'] = (list(DBG), R, dict(xin1=in_maps[1]['xin'], lam=lam, bt=bt, ct=ct, vecs=vecs))
    y_prompt = np.zeros((4, 2048, D), np.float32)
    y_sample = np.zeros((128, 8, D), np.float32)
    sre_p = np.zeros((1, 4, 256, 64), np.float32)
    sim_p = np.zeros((1, 4, 256, 64), np.float32)
    pool_p = np.zeros((1, 4, 15, D), np.float32)
    sre_s = np.zeros((1, 128, 256, 64), np.float32)
    sim_s = np.zeros((1, 128, 256, 64), np.float32)
    pool_s = np.zeros((1, 128, 15, D), np.float32)
    for core in range(8):
        b, hf = core // 2, core % 2
        r = R[core]
        y_prompt[b, hf * 1024:(hf + 1) * 1024] = r["y"][0:1024]
        y_sample[core * 16:(core + 1) * 16] = r["y"][1024:1152].reshape(16, 8, D)
        if hf == 1:
            sre_p[0, b] = r["sp"][0].reshape(256, 64)
            sim_p[0, b] = r["sp"][1].reshape(256, 64)
            pool_p[0, b] = r["ppn"].transpose(1, 0, 2).reshape(15, D)
        sre_s[0, core * 16:(core + 1) * 16] = r["ss"][0].reshape(16, 256, 64)
        sim_s[0, core * 16:(core + 1) * 16] = r["ss"][1].reshape(16, 256, 64)
        pool_s[0, core * 16:(core + 1) * 16, 0:7] = r["pso"]
        pool_s[0, core * 16:(core + 1) * 16, 7:15] = r["psn"].reshape(2, 32, 8, 8, 128).transpose(0, 2, 3, 1, 4).reshape(16, 8, D)
    return (y_prompt, y_sample, sre_p, sim_p, pool_p, sre_s, sim_s, pool_s)
```

```python
import math
import numpy as np
import concourse.bass as bass
import concourse.mybir as mybir
from concourse.bass_utils import run_bass_kernel_spmd
from contextlib import ExitStack

F32 = mybir.dt.float32
F32R = mybir.dt.float32r
BF16 = mybir.dt.bfloat16
AF = mybir.ActivationFunctionType
ALU = mybir.AluOpType

D = 4096
NCH = 32
DFF = 11008
NF = 86
FG = 4
NGRP = 22
NSLOT = 4
TPA, TPB, TS = 527, 512, 64
TA, TB = 592, 576
TPRE = 506
TM = 592
MAGIC = 12582912.0
TWO_PI = 2.0 * math.pi
EPS = 1e-6
GC0 = math.sqrt(2.0 / math.pi)
NT_GLU, NT_FFN, NT_POOL = 128, 21 * 24 + 16, 16
NTILE = NT_GLU + NT_FFN + NT_POOL + NT_FFN
NTOK_IN = 2 * TPRE + TA + TB
WIN = (2, 4, 8, 16)


class Prog:
    def __init__(self):
        self.ops = []
        self.lastw = {}
        self.readers = {}
        self.lastdma = {}

    def add(self, eng, fn, R=(), W=(), dsem=None):
        i = len(self.ops)
        deps = {}
        for r in R:
            j = self.lastw.get(r)
            if j is not None:
                deps[j] = 'raw'
        for w in W:
            j = self.lastw.get(w)
            if j is not None and j not in deps:
                deps[j] = 'waw'
            for j in self.readers.get(w, {}).values():
                if j not in deps:
                    deps[j] = 'war'
        if dsem is not None:
            j = self.lastdma.get(dsem)
            if j is not None:
                deps[j] = 'raw'
            self.lastdma[dsem] = i
        rk = eng if dsem is None else ('dma', i)
        for r in R:
            self.readers.setdefault(r, {})[rk] = i
        for w in W:
            self.lastw[w] = i
            self.readers[w] = {}
        self.ops.append(dict(eng=eng, fn=fn, deps=deps, dsem=dsem, sig=False, force=False))
        return i

    def emit(self, nc, block, es):
        ops = self.ops
        engs = ('pe', 'act', 'dve', 'pool', 'sp')
        for i, o in enumerate(ops):
            keep = []
            for j, kind in o['deps'].items():
                d = ops[j]
                if d['dsem'] is None and d['eng'] == o['eng'] and not o['force']:
                    if kind != 'raw' or o['eng'] == 'pe':
                        continue
                keep.append(j)
                if d['dsem'] is None:
                    d['sig'] = True
            o['keep'] = keep
        cnt = {e: 0 for e in engs}
        dcnt = {}
        for o in ops:
            if o['dsem'] is not None:
                dcnt[o['dsem']] = dcnt.get(o['dsem'], 0) + 16
                o['sv'] = ('d_' + o['dsem'], dcnt[o['dsem']])
            elif o['sig']:
                cnt[o['eng']] += 1
                o['sv'] = ('e_' + o['eng'], cnt[o['eng']])
        self.cnt, self.dcnt = cnt, dcnt
        names = ['e_' + e for e in engs] + ['d_' + k for k in dcnt]
        sems = {n: es.enter_context(nc.semaphore(n)) for n in names}

        def run(engname, e):
            waited = {}
            for o in ops:
                if o['eng'] != engname:
                    continue
                need = {}
                for j in o['keep']:
                    s, v = ops[j]['sv']
                    if need.get(s, 0) < v:
                        need[s] = v
                for s, v in need.items():
                    if waited.get(s, 0) < v:
                        e.wait_ge(sems[s], v)
                        waited[s] = v
                if o['fn'] is None:
                    continue
                ins = o['fn'](e)
                if o['dsem'] is not None:
                    ins.then_inc(sems['d_' + o['dsem']], 16)
                elif o['sig']:
                    ins.then_inc(sems['e_' + engname], 1)

        @block.tensor
        def _(e):
            run('pe', e)

        @block.scalar
        def _(e):
            run('act', e)

        @block.vector
        def _(e):
            run('dve', e)

        @block.gpsimd
        def _(e):
            run('pool', e)

        @block.sync
        def _(e):
            run('sp', e)


DBG = []
STAGE = 0


def build_program(stage=0):
    nc = bass.Bass("TRN2", target_bir_lowering=False)
    del DBG[:]
    dt_in = lambda n, s: nc.dram_tensor(n, s, F32, kind="ExternalInput").ap()
    dt_out = lambda n, s: nc.dram_tensor(n, s, F32, kind="ExternalOutput").ap()
    xin = dt_in("xin", [NTOK_IN, D])
    NW = {0: NTILE, 1: 1, 2: 1, 3: NT_GLU, 4: NT_GLU + NT_FFN, 5: NT_GLU + NT_FFN + NT_POOL, 6: NTILE}[stage]
    wst = dt_in("wst", [NW, 128, 2048])
    vecs_d = dt_in("vecs", [128, 7 * 32])
    ident_d = dt_in("ident", [128, 128])
    kr_d = dt_in("kr", [2, 128, 2 * TM])
    krp_d = dt_in("krp", [2, 128, TM])
    pos_d = dt_in("pos", [128, 2 * TM])
    lam_d = dt_in("lam", [3, 128, 128])
    bt_d = dt_in("bt", [128, 128, 256])
    ct_d = dt_in("ct", [128, 128, 64])
    sst_d = dt_in("sst", [2, 16, 128, 128])
    spool_d = dt_in("spool", [16, 15, D])
    y_d = dt_out("y", [1152, D])
    sp_d = dt_out("sp", [2, 128, 128])
    ss_d = dt_out("ss", [2, 16, 128, 128])
    psn_d = dt_out("psn", [2, 32, 64, 128])
    pso_d = dt_out("pso", [16, 7, D])
    ppn_d = dt_out("ppn", [32, 15, 128])

    es = ExitStack()
    dbg_d = dt_out("dbg", [128, 65536]) if stage else None
    dcur = dict(o=0)
    sb = lambda n, s, d=F32: es.enter_context(nc.sbuf_tensor(n, s, d))
    x = sb("x", [128, NCH, TM])
    xn = sb("xn", [128, NCH, TM], BF16)
    ring = sb("ring", [128, NSLOT, 2048], BF16)
    ident = sb("ident_s", [128, 128])
    onesr = sb("onesr", [128, 128], F32R)
    vecs = sb("vecs_s", [128, 7 * 32])
    kidx = sb("kidx", [128, TM])
    rmask = sb("rmask", [128, TM])
    rstd = sb("rstd", [128, TM])
    prm = sb("prm", [128, 10, 128])
    St = sb("St", [128, 128, 9, 2])
    zq = sb("zq", [128, 9, 2])
    zt = sb("zt", [128, 96])
    magic_c = sb("magic_c", [128, 2])
    hist = sb("hist", [128, NCH, 15])
    ucr = sb("ucr", [128, TM], F32R)
    xr = sb("xr", [128, TM], F32R)
    xi = sb("xi", [128, TM], F32R)
    bt = sb("bt_s", [128, 2, 256], F32R)
    cp = sb("cp", [128, 4, 2, 128], F32R)
    ctraw = sb("ctraw", [128, 1, 64])
    sq = ucr
    h1 = sb("h1", [128, FG, TM], BF16)
    scr = sb("scr", [128, 15 * TM])
    sphc = sb("sphc", [120, 2, 128])
    hst = sb("hst", [64, 2, 128])
    hpt = sb("hpt", [15, 2, 128])
    pst = [es.enter_context(nc.psum_tensor(f"ps{i}", [128, 512], F32)) for i in range(8)]
    block = es.enter_context(nc.Block())

    P = Prog()
    V = lambda c: vecs[:, c:c + 1]
    vcol = lambda k, c: vecs[:, k * 32 + c:k * 32 + c + 1]
    S = lambda i: scr[:, i * TM:(i + 1) * TM]

    P.add('sp', lambda e: e.dma_start(out=ident[:], in_=ident_d), W=['ident'], dsem='c0')
    P.add('sp', lambda e: e.dma_start(out=vecs[:], in_=vecs_d), W=['vecs'], dsem='c1')
    P.add('sp', lambda e: e.dma_start(out=prm[:, 0:3, :], in_=lam_d.rearrange("a p q -> p a q")), W=['prm'], dsem='c2')
    P.add('dve', lambda e: e.memset(scr[:, 0:128], 1.0), W=['scrp'])
    P.add('dve', lambda e: e.tensor_copy(out=onesr[:], in_=scr[:, 0:128]), R=['scrp'], W=['onesr'])
    P.add('dve', lambda e: e.memset(scr[:, 1024:2048], 0.0), W=['scrz'])
    P.add('dve', lambda e: e.tensor_copy(out=cp[:, :, :, :].rearrange("p a b c -> p (a b c)"), in_=scr[:, 1024:2048]), R=['scrz'], W=['cp0', 'cp1', 'cp2', 'cp3'])
    P.add('dve', lambda e: e.memset(St[:], 0.0), W=['St'])
    P.add('dve', lambda e: e.memset(magic_c[:, 0:1], MAGIC), W=['magic'])
    P.add('dve', lambda e: e.memset(magic_c[:, 1:2], -MAGIC), R=['magic'], W=['magic'])
    P.add('dve', lambda e: e.memset(hist[:], 0.0), W=['hist'])

    pr = lambda i: prm[:, i, :]
    T0, T1 = S(0)[:, 0:128], S(1)[:, 0:128]
    T2, T3 = S(2)[:, 0:128], S(3)[:, 0:128]
    R_, W_ = ['prm'], ['prm']
    a = lambda eng, fn, R=(), W=(): P.add(eng, fn, R=list(R) + ['prm', 'scrp'], W=list(W) + ['prm', 'scrp'])
    def ts(out, in0, s1, s2=None, op0=ALU.mult, op1=ALU.add):
        if s2 is None:
            a('dve', lambda e: e.tensor_scalar(out=out, in0=in0, scalar1=s1, scalar2=None, op0=op0))
        else:
            a('dve', lambda e: e.tensor_scalar(out=out, in0=in0, scalar1=s1, scalar2=s2, op0=op0, op1=op1))

    def tt(out, in0, in1, op):
        a('dve', lambda e: e.tensor_tensor(out=out, in0=in0, in1=in1, op=op))

    def stt(out, in0, sc, in1, op0, op1):
        a('dve', lambda e: e.scalar_tensor_tensor(out=out, in0=in0, scalar=sc, in1=in1, op0=op0, op1=op1))

    T4, T5, T6, T7 = S(4)[:, 0:128], S(5)[:, 0:128], S(6)[:, 0:128], S(7)[:, 0:128]
    a('act', lambda e: e.activation(out=T0, in_=pr(2), func=AF.Exp))
    tt(T1, pr(0), T0, ALU.mult)
    tt(T2, pr(1), T0, ALU.mult)
    ts(pr(8), T2, 1.0 / TWO_PI)
    ts(T0, pr(8), MAGIC, None, op0=ALU.add)
    ts(T0, T0, -MAGIC, None, op0=ALU.add)
    tt(T0, pr(8), T0, ALU.subtract)
    ts(T0, T0, math.pi / 2.0)
    tt(T2, T0, T0, ALU.mult)
    ts(T3, T2, 1.0 / 362880.0)
    stt(T3, T3, -1.0 / 5040.0, T2, ALU.add, ALU.mult)
    stt(T3, T3, 1.0 / 120.0, T2, ALU.add, ALU.mult)
    stt(T3, T3, -1.0 / 6.0, T2, ALU.add, ALU.mult)
    stt(T3, T3, 1.0, T0, ALU.add, ALU.mult)
    ts(T4, T2, -1.0 / 3628800.0)
    stt(T4, T4, 1.0 / 40320.0, T2, ALU.add, ALU.mult)
    stt(T4, T4, -1.0 / 720.0, T2, ALU.add, ALU.mult)
    stt(T4, T4, 1.0 / 24.0, T2, ALU.add, ALU.mult)
    stt(T4, T4, -0.5, T2, ALU.add, ALU.mult)
    ts(T4, T4, 1.0, None, op0=ALU.add)
    stt(T5, T3, 2.0, T4, ALU.mult, ALU.mult)
    tt(T6, T3, T3, ALU.mult)
    ts(T6, T6, -2.0, 1.0)
    stt(T3, T5, 2.0, T6, ALU.mult, ALU.mult)
    tt(T4, T5, T5, ALU.mult)
    ts(T4, T4, -2.0)
    ts(T5, T1, 1.0 / 6.0, 1.0)
    tt(T5, T5, T1, ALU.mult)
    ts(T5, T5, 1.0 / 5.0, 1.0)
    tt(T5, T5, T1, ALU.mult)
    ts(T5, T5, 1.0 / 4.0, 1.0)
    tt(T5, T5, T1, ALU.mult)
    ts(T5, T5, 1.0 / 3.0, 1.0)
    tt(T5, T5, T1, ALU.mult)
    ts(T5, T5, 1.0 / 2.0, 1.0)
    tt(T5, T5, T1, ALU.mult)
    ts(pr(3), T5, 1.0, None, op0=ALU.add)
    tt(pr(9), pr(3), T3, ALU.mult)
    tt(T1, pr(3), T4, ALU.mult)
    tt(T1, T1, T5, ALU.add)
    ts(T3, T1, 1.0, None, op0=ALU.add)
    tt(T0, pr(0), pr(0), ALU.mult)
    tt(T2, pr(1), pr(1), ALU.mult)
    tt(T0, T0, T2, ALU.add)
    a('dve', lambda e: e.reciprocal(out=T0, in_=T0))
    tt(T2, T1, pr(0), ALU.mult)
    tt(pr(4), pr(9), pr(1), ALU.mult)
    tt(T2, T2, pr(4), ALU.add)
    tt(pr(4), T2, T0, ALU.mult)
    tt(T2, pr(9), pr(0), ALU.mult)
    tt(T1, T1, pr(1), ALU.mult)
    tt(T2, T2, T1, ALU.subtract)
    tt(pr(5), T2, T0, ALU.mult)
    a('dve', lambda e: e.tensor_copy(out=pr(0), in_=pr(8)))
    a('dve', lambda e: e.tensor_copy(out=pr(1), in_=pr(3)))
    a('dve', lambda e: e.tensor_copy(out=pr(2), in_=T3))
    a('dve', lambda e: e.tensor_copy(out=pr(3), in_=pr(9)))
    a('dve', lambda e: e.tensor_tensor(out=T0, in0=pr(4), in1=pr(4), op=ALU.mult))
    a('dve', lambda e: e.tensor_tensor(out=T1, in0=pr(5), in1=pr(5), op=ALU.mult))
    a('dve', lambda e: e.tensor_tensor(out=T0, in0=T0, in1=T1, op=ALU.add))
    a('dve', lambda e: e.reciprocal(out=T0, in_=T0))
    a('dve', lambda e: e.tensor_tensor(out=pr(6), in0=pr(4), in1=T0, op=ALU.mult))
    a('dve', lambda e: e.tensor_tensor(out=T1, in0=pr(5), in1=T0, op=ALU.mult))
    a('dve', lambda e: e.tensor_scalar(out=pr(7), in0=T1, scalar1=-1.0, scalar2=None, op0=ALU.mult))

    wstate = dict(next_dma=0, next_use=0)
    TOTAL_TILES = 2 * NTILE

    def wtile():
        i = wstate['next_use']
        wstate['next_use'] += 1
        while wstate['next_dma'] < min(TOTAL_TILES, i + NSLOT):
            j = wstate['next_dma']
            wstate['next_dma'] += 1
            sl = j % NSLOT
            P.add('pool', (lambda e, j=j, sl=sl: e.dma_start(out=ring[:, sl, :], in_=wst[(j % NTILE) % NW])),
                  W=[f'ring{sl}'], dsem=f'w{sl}')
        return i % NSLOT

    def nts_of(T):
        h = T // 2
        if h % 2:
            h += 1
        return [(0, h), (h, T)]

    def barrier():
        regs = list(P.lastw.keys())
        i = P.add('sp', (lambda e: e.nop()), R=[], W=regs)
        P.ops[i]['force'] = True
        for k, j in P.lastdma.items():
            P.ops[i]['deps'].setdefault(j, 'raw')

    bank = dict(i=0)

    def nbank(nb=8):
        b = bank['i'] % nb
        bank['i'] += 1
        return b

    def load_x(row0, T):
        stage = scr[:, 0:D]
        t0 = 0
        while t0 < T:
            n = min(128, T - t0)
            P.add('sp', (lambda e, t0=t0, n=n: e.dma_start(out=stage[0:n, :], in_=xin[row0 + t0:row0 + t0 + n, :])),
                  W=['stage'], dsem='ld')
            for c4 in range(8):
                b = nbank()
                for cc in range(4):
                    c = c4 * 4 + cc
                    P.add('pe', (lambda e, b=b, cc=cc, c=c, n=n: e.transpose(
                        out=pst[b][:, cc * 128:cc * 128 + n], in_=stage[0:n, c * 128:(c + 1) * 128],
                        identity=ident[0:n, 0:n])), R=['stage', 'ident'], W=[f'ps{b}'])
                eng = 'act' if c4 % 2 else 'dve'
                if eng == 'act':
                    P.add('act', (lambda e, b=b, c4=c4, t0=t0, n=n: e.activation(
                        out=x[:, c4 * 4:c4 * 4 + 4, t0:t0 + n],
                        in_=pst[b][:, :].rearrange("p (c t) -> p c t", t=128)[:, :, 0:n], func=AF.Copy)),
                        R=[f'ps{b}'], W=[f'x{c}' for c in range(c4 * 4, c4 * 4 + 4)])
                else:
                    P.add('dve', (lambda e, b=b, c4=c4, t0=t0, n=n: e.tensor_copy(
                        out=x[:, c4 * 4:c4 * 4 + 4, t0:t0 + n],
                        in_=pst[b][:, :].rearrange("p (c t) -> p c t", t=128)[:, :, 0:n])),
                        R=[f'ps{b}'], W=[f'x{c}' for c in range(c4 * 4, c4 * 4 + 4)])
            t0 += n

    def rms(T):
        nts = nts_of(T)
        bs = [nbank() for _ in nts]
        for c in range(NCH):
            P.add('act', (lambda e, c=c: e.activation(out=sq[:, 0:T], in_=x[:, c, 0:T], func=AF.Square)),
                  R=[f'x{c}'], W=['ucr'])
            for (lo, hi), b in zip(nts, bs):
                P.add('pe', (lambda e, c=c, lo=lo, hi=hi, b=b: e.matmul(
                    pst[b][:, 0:hi - lo], onesr[:], sq[:, lo:hi], start=(c == 0), stop=(c == NCH - 1))),
                    R=['ucr', 'onesr'], W=[f'ps{b}'])
        for (lo, hi), b in zip(nts, bs):
            P.add('dve', (lambda e, lo=lo, hi=hi, b=b: e.tensor_scalar(
                out=rstd[:, lo:hi], in0=pst[b][:, 0:hi - lo], scalar1=1.0 / D, scalar2=EPS, op0=ALU.mult, op1=ALU.add)),
                R=[f'ps{b}'], W=['rstd'])
        P.add('act', lambda e: e.activation(out=rstd[:, 0:T], in_=rstd[:, 0:T], func=AF.Sqrt), R=['rstd'], W=['rstd'])
        P.add('dve', lambda e: e.reciprocal(out=rstd[:, 0:T], in_=rstd[:, 0:T]), R=['rstd'], W=['rstd'])

    def norm_to_xn(T, gk):
        for c in range(NCH):
            P.add('dve', (lambda e, c=c: e.scalar_tensor_tensor(
                out=xn[:, c, 0:T], in0=x[:, c, 0:T], scalar=vcol(gk, c), in1=rstd[:, 0:T], op0=ALU.mult, op1=ALU.mult)),
                R=[f'x{c}', 'rstd', 'vecs'], W=[f'xn{c}'])

    qdma = dict(n=0)

    def ssm(T, Tp, nseg_s, kcol0, state_only, seq0):
        nts = nts_of(T)
        A1, A2 = S(0)[:, 0:T], S(1)[:, 0:T]
        TSs = [S(2)[:, 0:T], S(10)[:, 0:T]]
        TCs = [S(3)[:, 0:T], S(11)[:, 0:T]]
        RMs = [S(4)[:, 0:T], S(12)[:, 0:T]]
        W1, W2, W3 = S(5)[:, 0:T], S(6)[:, 0:T], S(7)[:, 0:T]
        ZRs = [S(8)[:, 0:T], S(9)[:, 0:T]]
        ZIs = [S(13)[:, 0:T], S(14)[:, 0:T]]
        nseg = 1 + nseg_s
        sview = lambda ap: ap[:, Tp:Tp + 8 * nseg_s].rearrange("p (s k) -> p s k", k=8)

        def tablegen(q):
            p = q % 2
            TSb, TCb, RM = TSs[p], TCs[p], RMs[p]
            thq, rq = prm[:, 0, q:q + 1], prm[:, 1, q:q + 1]
            P.add('act', (lambda e: e.activation(out=A1, in_=kidx[:, 0:T], func=AF.Identity, scale=thq, bias=magic_c[:, 0:1])),
                  R=['kidx', 'prm', 'magic'], W=['A1'])
            P.add('act', (lambda e: e.activation(out=A1, in_=A1, func=AF.Identity, scale=1.0, bias=magic_c[:, 1:2])), R=['A1', 'magic'], W=['A1'])
            P.add('act', (lambda e: e.activation(out=A2, in_=kidx[:, 0:T], func=AF.Copy, scale=thq)),
                  R=['kidx', 'prm'], W=['A2'])
            P.add('pool', (lambda e: e.tensor_tensor(out=A2, in0=A2, in1=A1, op=ALU.subtract)), R=['A1', 'A2'], W=['A2'])
            P.add('act', (lambda e: e.activation(out=TSb, in_=A2, func=AF.Sin, scale=TWO_PI * 0.999999)), R=['A2'], W=[f'TS{p}'])
            P.add('act', (lambda e: e.activation(out=TCb, in_=A2, func=AF.Sin, scale=math.pi * 0.999999)), R=['A2'], W=[f'TC{p}'])
            P.add('pool', (lambda e: e.tensor_tensor(out=TCb, in0=TCb, in1=TCb, op=ALU.mult)), R=[f'TC{p}'], W=[f'TC{p}'])
            P.add('pool', (lambda e: e.tensor_scalar(out=TCb, in0=TCb, scalar1=-2.0, scalar2=1.0, op0=ALU.mult, op1=ALU.add)), R=[f'TC{p}'], W=[f'TC{p}'])
            P.add('act', (lambda e: e.activation(out=RM, in_=rmask[:, 0:T], func=AF.Copy, scale=rq)),
                  R=['rmask', 'prm'], W=[f'RM{p}'])

        tablegen(0)
        for c in range(NCH):
            P.add('dve', (lambda e, c=c: e.scalar_tensor_tensor(
                out=ucr[:, 0:T], in0=x[:, c, 0:T], scalar=vcol(0, c), in1=rstd[:, 0:T], op0=ALU.mult, op1=ALU.mult)),
                R=[f'x{c}', 'rstd', 'vecs'], W=['ucr'])
            ybs = [6, 7] if not state_only else []
            for ql in range(4):
                q = 4 * c + ql
                p = q % 2
                TSb, TCb, RM = TSs[p], TCs[p], RMs[p]
                tsn, tcn, rmn = f'TS{p}', f'TC{p}', f'RM{p}'
                ZR, ZI, zrn, zin = ZRs[p], ZIs[p], f'ZR{p}', f'ZI{p}'
                bsl = qdma['n'] % 2
                qdma['n'] += 1
                P.add('pool', (lambda e, q=q, bsl=bsl: e.dma_start(out=bt[:, bsl, :], in_=bt_d[q])),
                      W=[f'bt{bsl}'], dsem=f'bt{bsl}')
                if not state_only:
                    P.add('sp', (lambda e, q=q: e.dma_start(out=ctraw[:, 0, :], in_=ct_d[q])),
                          W=['ctraw'], dsem='ct')
                if q + 1 < 128:
                    tablegen(q + 1)
                arq, aiq = prm[:, 2, q:q + 1], prm[:, 3, q:q + 1]
                P.add('dve', (lambda e, q=q, aiq=aiq: e.tensor_scalar(out=zt[:, 0:nseg], in0=St[:, q, 0:nseg, 1], scalar1=aiq, scalar2=None, op0=ALU.mult)), R=['St', 'prm'], W=['zt'])
                P.add('dve', (lambda e, q=q, arq=arq: e.scalar_tensor_tensor(out=zq[:, 0:nseg, 0], in0=St[:, q, 0:nseg, 0], scalar=arq, in1=zt[:, 0:nseg], op0=ALU.mult, op1=ALU.subtract)), R=['St', 'prm', 'zt'], W=['zq'])
                P.add('dve', (lambda e, q=q, arq=arq: e.tensor_scalar(out=zt[:, 0:nseg], in0=St[:, q, 0:nseg, 1], scalar1=arq, scalar2=None, op0=ALU.mult)), R=['St', 'prm', 'zq'], W=['zt'])
                P.add('dve', (lambda e, q=q, aiq=aiq: e.scalar_tensor_tensor(out=zq[:, 0:nseg, 1], in0=St[:, q, 0:nseg, 0], scalar=aiq, in1=zt[:, 0:nseg], op0=ALU.mult, op1=ALU.add)), R=['St', 'prm', 'zt'], W=['zq'])
                bb = [(nbank(6), nbank(6)) for _ in nts]
                for (lo, hi), (b0, b1) in zip(nts, bb):
                    for comp, b in ((0, b0), (1, b1)):
                        P.add('pe', (lambda e, comp=comp, b=b, lo=lo, hi=hi, bsl=bsl: e.matmul(
                            pst[b][:, 0:hi - lo], bt[:, bsl, comp * 128:(comp + 1) * 128], ucr[:, lo:hi], start=True, stop=True)),
                            R=[f'bt{bsl}', 'ucr'], W=[f'ps{b}'])
                for (lo, hi), (b0, b1) in zip(nts, bb):
                    n = hi - lo
                    P.add('dve', (lambda e, lo=lo, hi=hi, b0=b0, n=n, TCb=TCb: e.tensor_tensor(out=W1[:, lo:hi], in0=pst[b0][:, 0:n], in1=TCb[:, lo:hi], op=ALU.mult)),
                          R=[f'ps{b0}', tcn], W=['W1'])
                    P.add('dve', (lambda e, lo=lo, hi=hi, b1=b1, n=n, TSb=TSb: e.tensor_tensor(out=W3[:, lo:hi], in0=pst[b1][:, 0:n], in1=TSb[:, lo:hi], op=ALU.mult)),
                          R=[f'ps{b1}', tsn], W=['W3'])
                P.add('dve', (lambda e: e.tensor_tensor(out=W1, in0=W1, in1=W3, op=ALU.add)), R=['W1', 'W3'], W=['W1'])
                for (lo, hi), (b0, b1) in zip(nts, bb):
                    n = hi - lo
                    P.add('dve', (lambda e, lo=lo, hi=hi, b1=b1, n=n, TCb=TCb: e.tensor_tensor(out=W2[:, lo:hi], in0=pst[b1][:, 0:n], in1=TCb[:, lo:hi], op=ALU.mult)),
                          R=[f'ps{b1}', tcn], W=['W2'])
                    P.add('dve', (lambda e, lo=lo, hi=hi, b0=b0, n=n, TSb=TSb: e.tensor_tensor(out=W3[:, lo:hi], in0=pst[b0][:, 0:n], in1=TSb[:, lo:hi], op=ALU.mult)),
                          R=[f'ps{b0}', tsn, 'W1'], W=['W3'])
                P.add('dve', (lambda e: e.tensor_tensor(out=W2, in0=W2, in1=W3, op=ALU.subtract)), R=['W2', 'W3'], W=['W2'])
                for comp, Wb, nm in ((0, W1, 'W1'), (1, W2, 'W2')):
                    P.add('dve', (lambda e, comp=comp, Wb=Wb: e.tensor_tensor(
                        out=Wb[:, 0:1], in0=Wb[:, 0:1], in1=zq[:, 0:1, comp], op=ALU.add)), R=[nm, 'zq'], W=[nm])
                    if nseg_s:
                        P.add('dve', (lambda e, comp=comp, Wb=Wb: e.tensor_tensor(
                            out=sview(Wb)[:, :, 0], in0=sview(Wb)[:, :, 0], in1=zq[:, 1:nseg, comp], op=ALU.add)), R=[nm, 'zq'], W=[nm])
                P.add('dve', (lambda e, RM=RM, ZR=ZR: e.tensor_tensor_scan(out=ZR, data0=RM, data1=W1, initial=0.0, op0=ALU.mult, op1=ALU.add)),
                      R=[rmn, 'W1'], W=[zrn])
                P.add('dve', (lambda e, RM=RM, ZI=ZI: e.tensor_tensor_scan(out=ZI, data0=RM, data1=W2, initial=0.0, op0=ALU.mult, op1=ALU.add)),
                      R=[rmn, 'W2'], W=[zin])
                if state_only:
                    lo, hi = Tp - 1, Tp
                    P.add('dve', (lambda e, TCb=TCb, ZR=ZR: e.tensor_tensor(out=zt[:, 80:81], in0=ZR[:, lo:hi], in1=TCb[:, lo:hi], op=ALU.mult)), R=[zrn, tcn], W=['zt'])
                    P.add('dve', (lambda e, TSb=TSb, ZI=ZI: e.tensor_tensor(out=zt[:, 81:82], in0=ZI[:, lo:hi], in1=TSb[:, lo:hi], op=ALU.mult)), R=[zin, tsn], W=['zt'])
                    P.add('dve', (lambda e, q=q: e.tensor_tensor(out=St[:, q, 0:1, 0], in0=zt[:, 80:81], in1=zt[:, 81:82], op=ALU.subtract)), R=['zt'], W=['St'])
                    P.add('dve', (lambda e, TSb=TSb, ZR=ZR: e.tensor_tensor(out=zt[:, 82:83], in0=ZR[:, lo:hi], in1=TSb[:, lo:hi], op=ALU.mult)), R=[zrn, tsn], W=['zt'])
                    P.add('dve', (lambda e, TCb=TCb, ZI=ZI: e.tensor_tensor(out=zt[:, 83:84], in0=ZI[:, lo:hi], in1=TCb[:, lo:hi], op=ALU.mult)), R=[zin, tcn], W=['zt'])
                    P.add('dve', (lambda e, q=q: e.tensor_tensor(out=St[:, q, 0:1, 1], in0=zt[:, 82:83], in1=zt[:, 83:84], op=ALU.add)), R=['zt'], W=['St'])
                    continue
                P.add('pool', (lambda e, TCb=TCb, ZR=ZR: e.tensor_tensor(out=A1, in0=ZR, in1=TCb, op=ALU.mult)), R=[zrn, tcn], W=['A1'])
                P.add('pool', (lambda e, TSb=TSb, ZI=ZI: e.tensor_tensor(out=A2, in0=ZI, in1=TSb, op=ALU.mult)), R=[zin, tsn], W=['A2'])
                P.add('pool', (lambda e: e.tensor_tensor(out=xr[:, 0:T], in0=A1, in1=A2, op=ALU.subtract)), R=['A1', 'A2'], W=['xr'])
                P.add('pool', (lambda e, TSb=TSb, ZR=ZR: e.tensor_tensor(out=A1, in0=ZR, in1=TSb, op=ALU.mult)), R=[zrn, tsn], W=['A1'])
                P.add('pool', (lambda e, TCb=TCb, ZI=ZI: e.tensor_tensor(out=A2, in0=ZI, in1=TCb, op=ALU.mult)), R=[zin, tcn], W=['A2'])
                P.add('pool', (lambda e: e.tensor_tensor(out=xi[:, 0:T], in0=A1, in1=A2, op=ALU.add)), R=['A1', 'A2'], W=['xi'])
                for comp, src, nm in ((0, xr, 'xr'), (1, xi, 'xi')):
                    P.add('act', (lambda e, comp=comp, src=src, q=q: e.activation(out=St[:, q, 0:1, comp], in_=src[:, Tp - 1:Tp], func=AF.Copy)), R=[nm], W=['St'])
                    if nseg_s:
                        P.add('act', (lambda e, comp=comp, src=src, q=q: e.activation(
                            out=St[:, q, 1:nseg, comp], in_=sview(src)[:, :, 7], func=AF.Copy)), R=[nm], W=['St'])
                frq, fiq = prm[:, 4, q:q + 1], prm[:, 5, q:q + 1]
                cpr = cp[:, ql, 0, 32 * ql:32 * ql + 32]
                cpi = cp[:, ql, 1, 32 * ql:32 * ql + 32]
                t32a, t32b = zt[:, 16:48], zt[:, 48:80]
                P.add('dve', (lambda e, frq=frq: e.tensor_scalar(out=t32a, in0=ctraw[:, 0, 0:32], scalar1=frq, scalar2=None, op0=ALU.mult)), R=['ctraw', 'prm'], W=['zt2'])
                P.add('dve', (lambda e, fiq=fiq: e.tensor_scalar(out=t32b, in0=ctraw[:, 0, 32:64], scalar1=fiq, scalar2=None, op0=ALU.mult)), R=['ctraw', 'prm'], W=['zt2'])
                P.add('dve', (lambda e, cpr=cpr: e.tensor_tensor(out=cpr, in0=t32a, in1=t32b, op=ALU.subtract)), R=['zt2'], W=[f'cp{ql}'])
                P.add('dve', (lambda e, fiq=fiq: e.tensor_scalar(out=t32a, in0=ctraw[:, 0, 0:32], scalar1=fiq, scalar2=-1.0, op0=ALU.mult, op1=ALU.mult)), R=['ctraw', 'prm'], W=['zt2'])
                P.add('dve', (lambda e, frq=frq: e.tensor_scalar(out=t32b, in0=ctraw[:, 0, 32:64], scalar1=frq, scalar2=None, op0=ALU.mult)), R=['ctraw', 'prm'], W=['zt2'])
                P.add('dve', (lambda e, cpi=cpi: e.tensor_tensor(out=cpi, in0=t32a, in1=t32b, op=ALU.subtract)), R=['zt2'], W=[f'cp{ql}'])
                for (lo, hi), yb in zip(nts, ybs):
                    P.add('pe', (lambda e, lo=lo, hi=hi, yb=yb, ql=ql: e.matmul(
                        pst[yb][:, 0:hi - lo], cp[:, ql, 0, :], xr[:, lo:hi], start=(ql == 0), stop=False)),
                        R=[f'cp{ql}', 'xr'], W=[f'ps{yb}'])
                    P.add('pe', (lambda e, lo=lo, hi=hi, yb=yb, ql=ql: e.matmul(
                        pst[yb][:, 0:hi - lo], cp[:, ql, 1, :], xi[:, lo:hi], start=False, stop=(ql == 3))),
                        R=[f'cp{ql}', 'xi'], W=[f'ps{yb}'])
            if state_only:
                continue
            for (lo, hi), yb in zip(nts, ybs):
                n = hi - lo
                P.add('dve', (lambda e, lo=lo, hi=hi, yb=yb, n=n, c=c: e.scalar_tensor_tensor(
                    out=W1[:, lo:hi], in0=ucr[:, lo:hi], scalar=vcol(5, c), in1=pst[yb][:, 0:n], op0=ALU.mult, op1=ALU.add)),
                    R=['ucr', 'vecs', f'ps{yb}'], W=['W1'])
            P.add('act', (lambda e: e.activation(out=W2, in_=W1, func=AF.Square)), R=['W1'], W=['W2'])
            P.add('dve', (lambda e: e.tensor_scalar(out=W2, in0=W2, scalar1=0.044715 * 2 * GC0, scalar2=2 * GC0, op0=ALU.mult, op1=ALU.add)), R=['W2'], W=['W2'])
            P.add('dve', (lambda e: e.tensor_tensor(out=W2, in0=W2, in1=W1, op=ALU.mult)), R=['W2', 'W1'], W=['W2'])
            P.add('act', (lambda e: e.activation(out=W3, in_=W2, func=AF.Sigmoid)), R=['W2'], W=['W3'])
            P.add('dve', (lambda e, c=c: e.tensor_tensor(out=xn[:, c, 0:T], in0=W3, in1=W1, op=ALU.mult)), R=['W3', 'W1'], W=[f'xn{c}'])

    def glu(T):
        nts = nts_of(T)
        for m in range(NCH):
            bs = [[nbank() for _ in nts] for _ in range(2)]
            for part in range(2):
                for half in range(2):
                    sl = wtile()
                    for kl in range(16):
                        k = half * 16 + kl
                        for (lo, hi), b in zip(nts, bs[part]):
                            P.add('pe', (lambda e, sl=sl, kl=kl, k=k, lo=lo, hi=hi, b=b: e.matmul(
                                pst[b][:, 0:hi - lo], ring[:, sl, kl * 128:(kl + 1) * 128], xn[:, k, lo:hi],
                                start=(k == 0), stop=(k == 31))), R=[f'ring{sl}', f'xn{k}'], W=[f'ps{b}'])
            for i, (lo, hi) in enumerate(nts):
                n = hi - lo
                b1, b2 = bs[0][i], bs[1][i]
                G = S(8 + i % 2)[:, 0:n]
                gn = f'G{1 + i % 2}'
                P.add('act', (lambda e, b2=b2, n=n, G=G: e.activation(out=G, in_=pst[b2][:, 0:n], func=AF.Sigmoid)), R=[f'ps{b2}'], W=[gn])
                P.add('dve', (lambda e, b1=b1, n=n, G=G: e.tensor_tensor(out=G, in0=pst[b1][:, 0:n], in1=G, op=ALU.mult)), R=[f'ps{b1}', gn], W=[gn])
                P.add('dve', (lambda e, m=m, lo=lo, hi=hi, G=G: e.tensor_tensor(out=x[:, m, lo:hi], in0=x[:, m, lo:hi], in1=G, op=ALU.add)), R=[gn, f'x{m}'], W=[f'x{m}'])

    def ffn(T):
        nts = nts_of(T)
        for g in range(NGRP):
            nf = FG if g < NGRP - 1 else NF - FG * (NGRP - 1)
            for fl in range(nf):
                bs = [[nbank() for _ in nts] for _ in range(2)]
                for part in range(2):
                    for half in range(2):
                        sl = wtile()
                        for kl in range(16):
                            k = half * 16 + kl
                            for (lo, hi), b in zip(nts, bs[part]):
                                P.add('pe', (lambda e, sl=sl, kl=kl, k=k, lo=lo, hi=hi, b=b: e.matmul(
                                    pst[b][:, 0:hi - lo], ring[:, sl, kl * 128:(kl + 1) * 128], xn[:, k, lo:hi],
                                    start=(k == 0), stop=(k == 31))), R=[f'ring{sl}', f'xn{k}'], W=[f'ps{b}'])
                for i, (lo, hi) in enumerate(nts):
                    n = hi - lo
                    bg, bu = bs[0][i], bs[1][i]
                    G = S(8 + i % 2)[:, 0:n]
                    gn = f'G{1 + i % 2}'
                    P.add('act', (lambda e, bg=bg, n=n, G=G: e.activation(out=G, in_=pst[bg][:, 0:n], func=AF.Silu)), R=[f'ps{bg}'], W=[gn])
                    P.add('dve', (lambda e, bu=bu, n=n, G=G, fl=fl, lo=lo, hi=hi: e.tensor_tensor(
                        out=h1[:, fl, lo:hi], in0=pst[bu][:, 0:n], in1=G, op=ALU.mult)), R=[f'ps{bu}', gn], W=[f'h1_{fl}'])
            for mp in range(8):
                sl = wtile()
                for ml in range(4):
                    m = mp * 4 + ml
                    for i, (lo, hi) in enumerate(nts):
                        n = hi - lo
                        b = nbank()
                        for fl in range(nf):
                            P.add('pe', (lambda e, sl=sl, fl=fl, ml=ml, lo=lo, hi=hi, b=b, nf=nf: e.matmul(
                                pst[b][:, 0:hi - lo], ring[:, sl, fl * 512 + ml * 128:fl * 512 + (ml + 1) * 128], h1[:, fl, lo:hi],
                                start=(fl == 0), stop=(fl == nf - 1))), R=[f'ring{sl}', f'h1_{fl}'], W=[f'ps{b}'])
                        P.add('dve', (lambda e, m=m, lo=lo, hi=hi, b=b, n=n: e.tensor_tensor(
                            out=x[:, m, lo:hi], in0=pst[b][:, 0:n], in1=x[:, m, lo:hi], op=ALU.add)), R=[f'ps{b}', f'x{m}'], W=[f'x{m}'])

    def pool_layer(T, Tp, poscol0, first, st_idx):
        nts = nts_of(T)
        E = 15 + Tp
        ES = 8 * 23
        icnt = scr[:, 10 * TM - 4 * TM:10 * TM]
        posr = S(5)
        P.add('sp', (lambda e: e.dma_start(out=posr[:, 0:T], in_=pos_d[:, poscol0:poscol0 + T])), W=['A1p'], dsem='pos')
        for wi, w in enumerate(WIN):
            ic = icnt[:, wi * TM:wi * TM + T]
            P.add('dve', (lambda e, ic=ic, w=w: e.tensor_scalar(out=ic, in0=posr[:, 0:T], scalar1=1.0, scalar2=float(w), op0=ALU.add, op1=ALU.min)), R=['A1p'], W=[f'ic{wi}'])
            P.add('dve', (lambda e, ic=ic: e.reciprocal(out=ic, in_=ic)), R=[f'ic{wi}'], W=[f'ic{wi}'])
        hc = S(0)
        ext = scr[:, TM:TM + E + ES]
        s2 = scr[:, 3 * TM:3 * TM + E + ES]
        stg = scr[:, 5 * TM:5 * TM + 128]
        for c in range(NCH):
            gi = c // 8
            w = WIN[gi]
            P.add('dve', (lambda e, c=c: e.scalar_tensor_tensor(
                out=hc[:, 0:T], in0=x[:, c, 0:T], scalar=vcol(1, c), in1=rstd[:, 0:T], op0=ALU.mult, op1=ALU.mult)),
                R=[f'x{c}', 'rstd', 'vecs'], W=['hc'])
            P.add('act', (lambda e, c=c: e.activation(out=ext[:, 0:15], in_=hist[:, c, :], func=AF.Copy)), R=['hist'], W=['ext'])
            P.add('act', (lambda e: e.activation(out=ext[:, 15:15 + Tp], in_=hc[:, 0:Tp], func=AF.Copy)), R=['hc'], W=['ext'])
            exs = ext[:, E:E + ES].rearrange("p (s k) -> p s k", k=23)
            P.add('act', (lambda e, exs=exs: e.activation(out=exs[:, :, 15:23], in_=hc[:, Tp:Tp + 64].rearrange("p (s k) -> p s k", k=8), func=AF.Copy)), R=['hc'], W=['ext'])
            b = nbank()
            P.add('sp', (lambda e, c=c: e.dma_start(out=sphc[:, c % 2, :], in_=spool_d[st_idx * 8:st_idx * 8 + 8, :, c * 128:(c + 1) * 128].rearrange("s k d -> (s k) d"))),
                  W=[f'sphc{c % 2}'], dsem=f'sph{c % 2}')
            P.add('pe', (lambda e, c=c, b=b: e.transpose(out=pst[b][:, 0:120], in_=sphc[0:120, c % 2, :], identity=ident[0:120, 0:120])),
                  R=[f'sphc{c % 2}', 'ident'], W=[f'ps{b}'])
            P.add('dve', (lambda e, b=b, exs=exs: e.tensor_copy(out=exs[:, :, 0:15], in_=pst[b][:, 0:120].rearrange("p (s k) -> p s k", k=15))), R=[f'ps{b}'], W=['ext'])
            P.add('act', (lambda e, c=c: e.activation(out=hist[:, c, :], in_=hc[:, Tp - 15:Tp], func=AF.Copy)), R=['hc', 'ext'], W=['hist'])
            L = E + ES
            cur, other = ext, s2
            sh = 1
            while sh < w:
                P.add('dve', (lambda e, cur=cur, other=other, sh=sh, L=L: e.tensor_tensor(out=other[:, sh:L], in0=cur[:, sh:L], in1=cur[:, 0:L - sh], op=ALU.add)),
                      R=['ext', 's2'], W=['ext', 's2'])
                if sh > 1 or True:
                    P.add('act', (lambda e, cur=cur, other=other, sh=sh: e.activation(out=other[:, 0:sh], in_=cur[:, 0:sh], func=AF.Copy)), R=['ext', 's2'], W=['ext', 's2'])
                cur, other = other, cur
                sh *= 2
            ic = icnt[:, gi * TM:gi * TM + T]
            pb = S(5)
            P.add('dve', (lambda e, cur=cur, ic=ic: e.tensor_tensor(out=pb[:, 0:Tp], in0=cur[:, 15:15 + Tp], in1=ic[:, 0:Tp], op=ALU.mult)), R=['ext', 's2', f'ic{gi}'], W=['pb'])
            curs = cur[:, E:E + ES].rearrange("p (s k) -> p s k", k=23)
            P.add('dve', (lambda e, curs=curs, ic=ic: e.tensor_tensor(
                out=pb[:, Tp:Tp + 64].rearrange("p (s k) -> p s k", k=8), in0=curs[:, :, 15:23],
                in1=ic[:, Tp:Tp + 64].rearrange("p (s k) -> p s k", k=8), op=ALU.mult)), R=['ext', 's2', f'ic{gi}'], W=['pb'])
            if T > Tp + 64:
                P.add('dve', (lambda e: e.memset(pb[:, Tp + 64:T], 0.0)), W=['pb'])
            P.add('dve', (lambda e, c=c: e.tensor_tensor(out=xn[:, c, 0:T], in0=pb[:, 0:T], in1=hc[:, 0:T], op=ALU.subtract)), R=['pb', 'hc'], W=[f'xn{c}'])
            b = nbank()
            P.add('pe', (lambda e, b=b: e.transpose(out=pst[b][0:64, 0:128], in_=hc[:, Tp:Tp + 64], identity=ident[:, :])), R=['hc', 'ident'], W=[f'ps{b}'])
            P.add('act', (lambda e, b=b, c=c: e.activation(out=hst[0:64, c % 2, :], in_=pst[b][0:64, 0:128], func=AF.Copy)), R=[f'ps{b}'], W=[f'hst{c % 2}'])
            P.add('sp', (lambda e, c=c: e.dma_start(out=psn_d[st_idx, c, :, :], in_=hst[0:64, c % 2, :])), R=[f'hst{c % 2}'], dsem=f'o_psn{c % 2}')
            if not first:
                b = nbank()
                P.add('pe', (lambda e, b=b: e.transpose(out=pst[b][0:15, 0:128], in_=hc[:, Tp - 15:Tp], identity=ident[:, :])), R=['hc', 'ident'], W=[f'ps{b}'])
                P.add('act', (lambda e, b=b, c=c: e.activation(out=hpt[0:15, c % 2, :], in_=pst[b][0:15, 0:128], func=AF.Copy)), R=[f'ps{b}'], W=[f'hpt{c % 2}'])
                P.add('sp', (lambda e, c=c: e.dma_start(out=ppn_d[c, :, :], in_=hpt[0:15, c % 2, :])), R=[f'hpt{c % 2}'], dsem=f'o_ppn{c % 2}')
        for gi in range(4):
            for mpair in range(4):
                sl = wtile()
                for ml in range(2):
                    m = gi * 8 + mpair * 2 + ml
                    for (lo, hi) in nts:
                        n = hi - lo
                        b = nbank()
                        for k in range(8):
                            P.add('pe', (lambda e, sl=sl, ml=ml, k=k, gi=gi, lo=lo, hi=hi, b=b: e.matmul(
                                pst[b][:, 0:hi - lo], ring[:, sl, (ml * 8 + k) * 128:(ml * 8 + k + 1) * 128], xn[:, gi * 8 + k, lo:hi],
                                start=(k == 0), stop=(k == 7))), R=[f'ring{sl}', f'xn{gi * 8 + k}'], W=[f'ps{b}'])
                        P.add('dve', (lambda e, m=m, lo=lo, hi=hi, b=b, n=n: e.scalar_tensor_tensor(
                            out=x[:, m, lo:hi], in0=pst[b][:, 0:n], scalar=vcol(6, m), in1=x[:, m, lo:hi], op0=ALU.mult, op1=ALU.add)),
                            R=[f'ps{b}', f'x{m}', 'vecs'], W=[f'x{m}'])


    def out_y(T, Tp, prow0, srow0, halo):
        ost = scr[:, 0:D]
        segs = []
        t = halo
        while t < Tp:
            n = min(128, Tp - t)
            segs.append((t, n, prow0 + (t - halo)))
            t += n
        segs.append((Tp, 64, srow0))
        for (t0, n, r0) in segs:
            for c in range(NCH):
                hcf = S(8 + c % 2)
                gn = f'G{1 + c % 2}'
                P.add('dve', (lambda e, c=c, t0=t0, n=n, hcf=hcf: e.scalar_tensor_tensor(
                    out=hcf[:, 0:n], in0=x[:, c, t0:t0 + n], scalar=vcol(4, c), in1=rstd[:, t0:t0 + n], op0=ALU.mult, op1=ALU.mult)),
                    R=[f'x{c}', 'rstd', 'vecs'], W=[gn])
                b = nbank()
                P.add('pe', (lambda e, b=b, n=n, hcf=hcf: e.transpose(out=pst[b][0:n, 0:128], in_=hcf[:, 0:n], identity=ident[:, :])), R=[gn, 'ident'], W=[f'ps{b}'])
                P.add('act', (lambda e, b=b, n=n, c=c: e.activation(out=ost[0:n, c * 128:(c + 1) * 128], in_=pst[b][0:n, 0:128], func=AF.Copy)), R=[f'ps{b}'], W=['stage'])
            P.add('sp', (lambda e, n=n, r0=r0: e.dma_start(out=y_d[r0:r0 + n, :], in_=ost[0:n, :])), R=['stage'], dsem='o_y')

    def load_sample_states(seq0):
        tmp = scr[:, 0:2 * 8 * 128].rearrange("p (a s q) -> p a s q", a=2, s=8)
        for a_ in range(2):
            P.add('sp', (lambda e, a_=a_: e.dma_start(out=tmp[:, a_, :, :], in_=sst_d[a_, seq0:seq0 + 8, :, :].rearrange("s q p -> q s p"))), W=['stage'], dsem='ld')
        xs = scr[:, 2048:2048 + 2048].rearrange("p (a s q) -> p a s q", a=2, s=8)
        for a_ in range(2):
            for s in range(8):
                b = nbank()
                P.add('pe', (lambda e, a_=a_, s=s, b=b: e.transpose(out=pst[b][:, 0:128], in_=tmp[:, a_, s, :], identity=ident[:, :])), R=['stage', 'ident'], W=[f'ps{b}'])
                P.add('act', (lambda e, a_=a_, s=s, b=b: e.activation(out=xs[:, a_, s, :], in_=pst[b][:, 0:128], func=AF.Copy)), R=[f'ps{b}'], W=['xs'])
        for s in range(8):
            t_ = scr[:, 4096:4224]
            P.add('dve', (lambda e, s=s: e.tensor_tensor(out=t_, in0=xs[:, 1, s, :], in1=prm[:, 7, :], op=ALU.mult)), R=['xs', 'prm'], W=['A1p'])
            P.add('dve', (lambda e, s=s: e.tensor_tensor(out=St[:, :, 1 + s, 0], in0=xs[:, 0, s, :], in1=prm[:, 6, :], op=ALU.mult)), R=['xs', 'prm'], W=['St'])
            P.add('dve', (lambda e, s=s: e.tensor_tensor(out=St[:, :, 1 + s, 0], in0=St[:, :, 1 + s, 0], in1=t_, op=ALU.subtract)), R=['St', 'A1p'], W=['St'])
            P.add('dve', (lambda e, s=s: e.tensor_tensor(out=t_, in0=xs[:, 1, s, :], in1=prm[:, 6, :], op=ALU.mult)), R=['xs', 'prm', 'St'], W=['A1p'])
            P.add('dve', (lambda e, s=s: e.tensor_tensor(out=St[:, :, 1 + s, 1], in0=xs[:, 0, s, :], in1=prm[:, 7, :], op=ALU.mult)), R=['xs', 'prm'], W=['St'])
            P.add('dve', (lambda e, s=s: e.tensor_tensor(out=St[:, :, 1 + s, 1], in0=St[:, :, 1 + s, 1], in1=t_, op=ALU.add)), R=['St', 'A1p'], W=['St'])

    def store_states(segs, dst_fn, dsem):
        ob = scr[:, 0:2 * 9 * 128].rearrange("p (a s q) -> p a s q", a=2, s=9)
        for seg in segs:
            t_ = scr[:, 2304:2432]
            u_ = scr[:, 2432:2560]
            P.add('dve', (lambda e, seg=seg: e.tensor_tensor(out=t_, in0=St[:, :, seg, 1], in1=prm[:, 5, :], op=ALU.mult)), R=['St', 'prm'], W=['A1p'])
            P.add('dve', (lambda e, seg=seg: e.tensor_tensor(out=u_, in0=St[:, :, seg, 0], in1=prm[:, 4, :], op=ALU.mult)), R=['St', 'prm'], W=['A1q'])
            P.add('dve', (lambda e: e.tensor_tensor(out=u_, in0=u_, in1=t_, op=ALU.subtract)), R=['A1p', 'A1q'], W=['A1q'])
            b = nbank()
            P.add('pe', (lambda e, b=b: e.transpose(out=pst[b][:, 0:128], in_=u_, identity=ident[:, :])), R=['A1q', 'ident'], W=[f'ps{b}'])
            P.add('act', (lambda e, b=b, seg=seg: e.activation(out=ob[:, 0, seg, :], in_=pst[b][:, 0:128], func=AF.Copy)), R=[f'ps{b}'], W=['ob'])
            P.add('dve', (lambda e, seg=seg: e.tensor_tensor(out=t_, in0=St[:, :, seg, 1], in1=prm[:, 4, :], op=ALU.mult)), R=['St', 'prm', 'A1q'], W=['A1p'])
            P.add('dve', (lambda e, seg=seg: e.tensor_tensor(out=u_, in0=St[:, :, seg, 0], in1=prm[:, 5, :], op=ALU.mult)), R=['St', 'prm'], W=['A1q'])
            P.add('dve', (lambda e: e.tensor_tensor(out=u_, in0=u_, in1=t_, op=ALU.add)), R=['A1p', 'A1q'], W=['A1q'])
            b = nbank()
            P.add('pe', (lambda e, b=b: e.transpose(out=pst[b][:, 0:128], in_=u_, identity=ident[:, :])), R=['A1q', 'ident'], W=[f'ps{b}'])
            P.add('act', (lambda e, b=b, seg=seg: e.activation(out=ob[:, 1, seg, :], in_=pst[b][:, 0:128], func=AF.Copy)), R=[f'ps{b}'], W=['ob'])
            for a_ in range(2):
                P.add('sp', (lambda e, a_=a_, seg=seg: e.dma_start(out=dst_fn(a_, seg), in_=ob[:, a_, seg, :])), R=['ob'], dsem=dsem)

    def load_kr(src, col0, T):
        P.add('sp', (lambda e: e.dma_start(out=kidx[:, 0:T], in_=src[0, :, col0:col0 + T])), W=['kidx'], dsem='k0')
        P.add('sp', (lambda e: e.dma_start(out=rmask[:, 0:T], in_=src[1, :, col0:col0 + T])), W=['rmask'], dsem='k1')

    def dump(name, ap, regs, n, b3=None):
        if not stage:
            return
        off = dcur['o']
        dcur['o'] += n
        DBG.append((name, off, n))
        o_ap = dbg_d[:, off:off + n]
        if b3:
            o_ap = o_ap.rearrange("p (a b) -> p a b", b=b3)
        P.add('sp', (lambda e: e.dma_start(out=o_ap, in_=ap)), R=regs, dsem='o_dbg')

    def finish():
        P.add('sp', (lambda e: e.nop()), R=[], W=[], dsem=None)
        last = P.ops[-1]
        for k, j in P.lastdma.items():
            if k.startswith('o_'):
                last['deps'][j] = 'raw'
        P.emit(nc, block, es)
        es.close()
        _CACHE['P'] = P
        return nc

    barrier()
    dump('prm', prm[:, :, :].rearrange("p a b -> p (a b)"), ['prm', 'scrp'], 1280)
    load_kr(krp_d, 0, TPRE)
    for seg in range(2):
        load_x(seg * TPRE, TPRE)
        barrier()
        if seg == 1:
            dump('x_c0', x[:, 0, 0:TPRE], ['x0'], TPRE)
            dump('x_c31', x[:, 31, 0:TPRE], ['x31'], TPRE)
        rms(TPRE)
        if seg == 1:
            dump('rstd', rstd[:, 0:TPRE], ['rstd'], TPRE)
        ssm(TPRE, TPRE, 0, 0, True, 0)
        barrier()
        dump(f'St_pre{seg}', St[:, :, 0, :], ['St'], 256, b3=2)
        if stage == 1 and seg == 1:
            return finish()
    row0 = 2 * TPRE
    for st_idx, (T, Tp, halo) in enumerate(((TA, TPA, 15), (TB, TPB, 0))):
        first = st_idx == 0
        load_kr(kr_d, st_idx * TM, T)
        load_sample_states(st_idx * 8)
        barrier()
        load_x(row0, T)
        barrier()
        rms(T)
        ssm(T, Tp, 8, 0, False, st_idx * 8)
        barrier()
        if stage and first:
            dump('StA', St[:, :, 0, :], ['St'], 256, b3=2)
            for cc in (0, 17, 31):
                P.add('dve', (lambda e, cc=cc: e.tensor_copy(out=S(9), in_=xn[:, cc, :])), R=[f'xn{cc}'], W=['G2'])
                dump(f'xnA_{cc}', S(9), ['G2'], TM)
            P.add('dve', (lambda e: e.tensor_copy(out=S(8), in_=xr[:, :])), R=['xr'], W=['G1'])
            dump('xrA', S(8), ['G1'], TM)
            dump('TSA', S(2), ['TS0'], TM)
            dump('TCA', S(3), ['TC0'], TM)
            dump('RMA', S(4), ['RM0'], TM)
            dump('kidxA', kidx[:, :], ['kidx'], TM)
        store_states(range(1, 9), (lambda a_, seg, st_idx=st_idx: ss_d[a_, st_idx * 8 + seg - 1, :, :]), 'o_ss')
        if stage == 2:
            barrier()
            return finish()
        if not first:
            store_states([0], (lambda a_, seg: sp_d[a_, :, :]), 'o_sp')
        barrier()
        def dumpx(tag):
            if stage and first:
                barrier()
                for cc in (0, 17, 31):
                    dump(f'{tag}_{cc}', x[:, cc, :], [f'x{cc}'], TM)
                dump(f'{tag}_rstd', rstd[:, :], ['rstd'], TM)
        glu(T)
        dumpx('x1')
        if stage == 3:
            barrier()
            return finish()
        rms(T)
        norm_to_xn(T, 2)
        ffn(T)
        dumpx('x2')
        if stage == 4:
            barrier()
            return finish()
        rms(T)
        barrier()
        P.add('sp', (lambda e, st_idx=st_idx: e.dma_start(out=pso_d[st_idx * 8:st_idx * 8 + 8, :, :], in_=spool_d[st_idx * 8:st_idx * 8 + 8, 8:15, :])), dsem='o_pso')
        pool_layer(T, Tp, st_idx * TM, first, st_idx)
        barrier()
        dumpx('x3')
        if stage == 5:
            barrier()
            return finish()
        rms(T)
        norm_to_xn(T, 3)
        ffn(T)
        dumpx('x4')
        rms(T)
        barrier()
        out_y(T, Tp, st_idx * 512, 1024 + st_idx * 64, halo)
        barrier()
        if stage == 6:
            return finish()
        row0 += T
    return finish()


_CACHE = {}


def _prep_weights(ssm_w_glu, pool_w, ffn_w_gate_up, ffn_w_down):
    tiles = np.zeros((NTILE, 128, 2048), np.float32)
    W = ssm_w_glu[0].reshape(32, 128, 2, 32, 128)
    W = W.reshape(2, 16, 128, 2, 32, 128)
    tiles[0:NT_GLU] = W.transpose(4, 3, 0, 2, 1, 5).reshape(NT_GLU, 128, 2048)
    base = NT_GLU
    for L in range(2):
        GU = ffn_w_gate_up[L].reshape(2, 16, 128, 2, NF, 128)
        GUt = GU.transpose(4, 3, 0, 2, 1, 5).reshape(NF, 4, 128, 2048)
        DN = ffn_w_down[L].reshape(NF, 128, 8, 512)
        t = base
        for g in range(NGRP):
            nf = FG if g < NGRP - 1 else NF - FG * (NGRP - 1)
            for fl in range(nf):
                tiles[t:t + 4] = GUt[g * FG + fl]
                t += 4
            blk = DN[g * FG:g * FG + nf]
            tiles[t:t + 8, :, 0:nf * 512] = blk.transpose(2, 1, 0, 3).reshape(8, 128, nf * 512)
            t += 8
        assert t == base + NT_FFN
        base = t
        if L == 0:
            PW = pool_w[0].reshape(4, 8, 128, 4, 2, 128)
            tiles[base:base + NT_POOL] = PW.transpose(0, 3, 2, 4, 1, 5).reshape(NT_POOL, 128, 2048)
            base += NT_POOL
    assert base == NTILE
    return tiles


def kernel(x_prompt, x_sample, state_ssm_re, state_ssm_im, state_pool, norm_mix, norm_ffn,
           ssm_lambda_re, ssm_lambda_im, ssm_log_step, ssm_b_re, ssm_b_im, ssm_c_re, ssm_c_im,
           ssm_d, ssm_w_glu, pool_w, pool_scale, ffn_w_gate_up, ffn_w_down, norm_final):
    f = lambda a: np.ascontiguousarray(np.asarray(a, dtype=np.float32))
    x_prompt, x_sample = f(x_prompt), f(x_sample)
    stage = STAGE
    if 'nc' not in _CACHE:
        _CACHE['nc'] = build_program(stage)
    nc = _CACHE['nc']
    wst = _prep_weights(f(ssm_w_glu), f(pool_w), f(ffn_w_gate_up), f(ffn_w_down))
    if stage:
        wst = np.ascontiguousarray(wst[0:{1: 1, 2: 1, 3: NT_GLU, 4: NT_GLU + NT_FFN, 5: NT_GLU + NT_FFN + NT_POOL, 6: NTILE}[stage]])
    fm = lambda v: f(v).reshape(32, 128).T
    vecs = np.concatenate([fm(norm_mix[0]), fm(norm_mix[1]), fm(norm_ffn[0]), fm(norm_ffn[1]), fm(norm_final),
                           fm(ssm_d[0]), fm(pool_scale[0])], axis=1)
    ident = np.eye(128, dtype=np.float32)
    lq = lambda a: f(a).reshape(128, 2, 64).transpose(1, 2, 0).reshape(128, 128)
    lam = np.stack([lq(ssm_lambda_re[0]), lq(ssm_lambda_im[0]),
                    lq(np.repeat(f(ssm_log_step[0])[:, None], 64, axis=1))])
    bt = np.zeros((128, 128, 256), np.float32)
    ct = np.zeros((128, 128, 64), np.float32)
    for comp, (B, C) in enumerate(((f(ssm_b_re[0]), f(ssm_c_re[0])), (f(ssm_b_im[0]), f(ssm_c_im[0])))):
        Bq = B.reshape(128, 2, 64, 16)
        Cq = C.reshape(128, 2, 16, 64)
        for gl in range(2):
            for q4 in range(4):
                qs = np.arange(q4, 128, 4)
                r0 = q4 * 32 + gl * 16
                bt[qs, r0:r0 + 16, comp * 128 + gl * 64:comp * 128 + gl * 64 + 64] = Bq[qs, gl].transpose(0, 2, 1)
            ct[:, gl * 64:gl * 64 + 64, comp * 32 + gl * 16:comp * 32 + gl * 16 + 16] = Cq[:, gl].transpose(0, 2, 1)
    def krow(Tp, T):
        k = np.zeros(TM, np.float32)
        m = np.ones(TM, np.float32)
        k[0:Tp] = np.arange(Tp)
        m[0] = 0.0
        for s in range(8):
            if Tp + 8 * s + 8 <= T:
                k[Tp + 8 * s:Tp + 8 * s + 8] = np.arange(8)
                m[Tp + 8 * s] = 0.0
        return k, m
    kA, mA = krow(TPA, TA)
    kB, mB = krow(TPB, TB)
    kr = np.stack([np.concatenate([kA, kB]), np.concatenate([mA, mB])])
    kr = np.ascontiguousarray(np.broadcast_to(kr[:, None, :], (2, 128, 2 * TM)))
    kP = np.zeros(TM, np.float32); kP[0:TPRE] = np.arange(TPRE)
    mP = np.ones(TM, np.float32); mP[0] = 0.0
    krp = np.ascontiguousarray(np.broadcast_to(np.stack([kP, mP])[:, None, :], (2, 128, TM)))
    in_maps = []
    for core in range(8):
        b, hf = core // 2, core % 2
        xin = np.zeros((NTOK_IN, D), np.float32)
        pos = np.full((2 * TM,), 1.0e4, np.float32)
        if hf == 1:
            xin[3:3 + 1009] = x_prompt[b, 0:1009]
            xin[2 * TPRE:2 * TPRE + 15] = x_prompt[b, 1009:1024]
        p0 = hf * 1024
        a0 = 2 * TPRE
        xin[a0 + 15:a0 + 15 + 512] = x_prompt[b, p0:p0 + 512]
        xin[a0 + TPA:a0 + TPA + 64] = x_sample[core * 16:core * 16 + 8].reshape(64, D)
        b0 = a0 + TA
        xin[b0:b0 + 512] = x_prompt[b, p0 + 512:p0 + 1024]
        xin[b0 + TPB:b0 + TPB + 64] = x_sample[core * 16 + 8:core * 16 + 16].reshape(64, D)
        pos[15:15 + 512] = p0 + np.arange(512)
        pos[TM:TM + 512] = p0 + 512 + np.arange(512)
        sst = np.stack([f(state_ssm_re[0])[core * 16:core * 16 + 16].reshape(16, 128, 128),
                        f(state_ssm_im[0])[core * 16:core * 16 + 16].reshape(16, 128, 128)])
        in_maps.append(dict(xin=xin, wst=wst, vecs=vecs, ident=ident, kr=kr, krp=krp,
                            pos=np.ascontiguousarray(np.broadcast_to(pos[None, :], (128, 2 * TM))),
                            lam=lam, bt=bt, ct=ct, sst=sst,
                            spool=f(state_pool[0])[core * 16:core * 16 + 16]))
    res = run_bass_kernel_spmd(nc, in_maps, core_ids=list(range(8)))
    R = res.results
    if stage:
        _CACHE['dbg'] = (list(DBG), R, dict(xin1=in_maps[1]['xin'], lam=lam, bt=bt, ct=ct, vecs=vecs))
    y_prompt = np.zeros((4, 2048, D), np.float32)
    y_sample = np.zeros((128, 8, D), np.float32)
    sre_p = np.zeros((1, 4, 256, 64), np.float32)
    sim_p = np.zeros((1, 4, 256, 64), np.float32)
    pool_p = np.zeros((1, 4, 15, D), np.float32)
    sre_s = np.zeros((1, 128, 256, 64), np.float32)
    sim_s = np.zeros((1, 128, 256, 64), np.float32)
    pool_s = np.zeros((1, 128, 15, D), np.float32)
    for core in range(8):
        b, hf = core // 2, core % 2
        r = R[core]
        y_prompt[b, hf * 1024:(hf + 1) * 1024] = r["y"][0:1024]
        y_sample[core * 16:(core + 1) * 16] = r["y"][1024:1152].reshape(16, 8, D)
        if hf == 1:
            sre_p[0, b] = r["sp"][0].reshape(256, 64)
            sim_p[0, b] = r["sp"][1].reshape(256, 64)
            pool_p[0, b] = r["ppn"].transpose(1, 0, 2).reshape(15, D)
        sre_s[0, core * 16:(core + 1) * 16] = r["ss"][0].reshape(16, 256, 64)
        sim_s[0, core * 16:(core + 1) * 16] = r["ss"][1].reshape(16, 256, 64)
        pool_s[0, core * 16:(core + 1) * 16, 0:7] = r["pso"]
        pool_s[0, core * 16:(core + 1) * 16, 7:15] = r["psn"].reshape(2, 32, 8, 8, 128).transpose(0, 2, 3, 1, 4).reshape(16, 8, D)
    return (y_prompt, y_sample, sre_p, sim_p, pool_p, sre_s, sim_s, pool_s)
```

```python
import math
import numpy as np
import concourse.bass as bass
import concourse.mybir as mybir
from concourse.bass_utils import run_bass_kernel_spmd
from contextlib import ExitStack

F32 = mybir.dt.float32
F32R = mybir.dt.float32r
BF16 = mybir.dt.bfloat16
AF = mybir.ActivationFunctionType
ALU = mybir.AluOpType

D = 4096
NCH = 32
DFF = 11008
NF = 86
FG = 4
NGRP = 22
NSLOT = 4
TPA, TPB, TS = 527, 512, 64
TA, TB = 592, 576
TPRE = 506
TM = 592
MAGIC = 12582912.0
TWO_PI = 2.0 * math.pi
EPS = 1e-6
GC0 = math.sqrt(2.0 / math.pi)
NT_GLU, NT_FFN, NT_POOL = 128, 21 * 24 + 16, 16
NTILE = NT_GLU + NT_FFN + NT_POOL + NT_FFN
NTOK_IN = 2 * TPRE + TA + TB
WIN = (2, 4, 8, 16)


class Prog:
    def __init__(self):
        self.ops = []
        self.lastw = {}
        self.readers = {}
        self.lastdma = {}

    def add(self, eng, fn, R=(), W=(), dsem=None):
        i = len(self.ops)
        deps = {}
        for r in R:
            j = self.lastw.get(r)
            if j is not None:
                deps[j] = 'raw'
        for w in W:
            j = self.lastw.get(w)
            if j is not None and j not in deps:
                deps[j] = 'waw'
            for j in self.readers.get(w, {}).values():
                if j not in deps:
                    deps[j] = 'war'
        if dsem is not None:
            j = self.lastdma.get(dsem)
            if j is not None:
                deps[j] = 'raw'
            self.lastdma[dsem] = i
        rk = eng if dsem is None else ('dma', i)
        for r in R:
            self.readers.setdefault(r, {})[rk] = i
        for w in W:
            self.lastw[w] = i
            self.readers[w] = {}
        self.ops.append(dict(eng=eng, fn=fn, deps=deps, dsem=dsem, sig=False, force=False))
        return i

    def emit(self, nc, block, es):
        ops = self.ops
        engs = ('pe', 'act', 'dve', 'pool', 'sp')
        for i, o in enumerate(ops):
            keep = []
            for j, kind in o['deps'].items():
                d = ops[j]
                if d['dsem'] is None and d['eng'] == o['eng'] and not o['force']:
                    if kind != 'raw' or o['eng'] == 'pe':
                        continue
                keep.append(j)
                if d['dsem'] is None:
                    d['sig'] = True
            o['keep'] = keep
        cnt = {e: 0 for e in engs}
        dcnt = {}
        for o in ops:
            if o['dsem'] is not None:
                dcnt[o['dsem']] = dcnt.get(o['dsem'], 0) + 16
                o['sv'] = ('d_' + o['dsem'], dcnt[o['dsem']])
            elif o['sig']:
                cnt[o['eng']] += 1
                o['sv'] = ('e_' + o['eng'], cnt[o['eng']])
        self.cnt, self.dcnt = cnt, dcnt
        names = ['e_' + e for e in engs] + ['d_' + k for k in dcnt]
        sems = {n: es.enter_context(nc.semaphore(n)) for n in names}

        def run(engname, e):
            waited = {}
            for o in ops:
                if o['eng'] != engname:
                    continue
                need = {}
                for j in o['keep']:
                    s, v = ops[j]['sv']
                    if need.get(s, 0) < v:
                        need[s] = v
                for s, v in need.items():
                    if waited.get(s, 0) < v:
                        e.wait_ge(sems[s], v)
                        waited[s] = v
                if o['fn'] is None:
                    continue
                ins = o['fn'](e)
                if o['dsem'] is not None:
                    ins.then_inc(sems['d_' + o['dsem']], 16)
                elif o['sig']:
                    ins.then_inc(sems['e_' + engname], 1)

        @block.tensor
        def _(e):
            run('pe', e)

        @block.scalar
        def _(e):
            run('act', e)

        @block.vector
        def _(e):
            run('dve', e)

        @block.gpsimd
        def _(e):
            run('pool', e)

        @block.sync
        def _(e):
            run('sp', e)


DBG = []
STAGE = 0


def build_program(stage=0):
    nc = bass.Bass("TRN2", target_bir_lowering=False)
    del DBG[:]
    dt_in = lambda n, s: nc.dram_tensor(n, s, F32, kind="ExternalInput").ap()
    dt_out = lambda n, s: nc.dram_tensor(n, s, F32, kind="ExternalOutput").ap()
    xin = dt_in("xin", [NTOK_IN, D])
    NW = {0: NTILE, 1: 1, 2: 1, 3: NT_GLU, 4: NT_GLU + NT_FFN, 5: NT_GLU + NT_FFN + NT_POOL, 6: NTILE}[stage]
    wst = dt_in("wst", [NW, 128, 2048])
    vecs_d = dt_in("vecs", [128, 7 * 32])
    ident_d = dt_in("ident", [128, 128])
    kr_d = dt_in("kr", [2, 128, 2 * TM])
    krp_d = dt_in("krp", [2, 128, TM])
    pos_d = dt_in("pos", [128, 2 * TM])
    lam_d = dt_in("lam", [3, 128, 128])
    bt_d = dt_in("bt", [128, 128, 256])
    ct_d = dt_in("ct", [128, 128, 64])
    sst_d = dt_in("sst", [2, 16, 128, 128])
    spool_d = dt_in("spool", [16, 15, D])
    y_d = dt_out("y", [1152, D])
    sp_d = dt_out("sp", [2, 128, 128])
    ss_d = dt_out("ss", [2, 16, 128, 128])
    psn_d = dt_out("psn", [2, 32, 64, 128])
    pso_d = dt_out("pso", [16, 7, D])
    ppn_d = dt_out("ppn", [32, 15, 128])

    es = ExitStack()
    dbg_d = dt_out("dbg", [128, 65536]) if stage else None
    dcur = dict(o=0)
    sb = lambda n, s, d=F32: es.enter_context(nc.sbuf_tensor(n, s, d))
    x = sb("x", [128, NCH, TM])
    xn = sb("xn", [128, NCH, TM], BF16)
    ring = sb("ring", [128, NSLOT, 2048], BF16)
    ident = sb("ident_s", [128, 128])
    onesr = sb("onesr", [128, 128], F32R)
    vecs = sb("vecs_s", [128, 7 * 32])
    kidx = sb("kidx", [128, TM])
    rmask = sb("rmask", [128, TM])
    rstd = sb("rstd", [128, TM])
    prm = sb("prm", [128, 10, 128])
    St = sb("St", [128, 128, 9, 2])
    zq = sb("zq", [128, 9, 2])
    zt = sb("zt", [128, 16])
    magic_c = sb("magic_c", [128, 2])
    hist = sb("hist", [128, NCH, 15])
    ucr = sb("ucr", [128, TM], F32R)
    xr = sb("xr", [128, TM], F32R)
    xi = sb("xi", [128, TM], F32R)
    bt = sb("bt_s", [128, 2, 256], F32R)
    cp = sb("cp", [128, 4, 2, 128], F32R)
    ctraw = sb("ctraw", [128, 1, 64])
    sq = ucr
    h1 = sb("h1", [128, FG, TM], BF16)
    scr = sb("scr", [128, 13 * TM])
    sphc = sb("sphc", [120, 2, 128])
    hst = sb("hst", [64, 4, 128])
    hpt = sb("hpt", [15, 4, 128])
    pst = [es.enter_context(nc.psum_tensor(f"ps{i}", [128, 512], F32)) for i in range(8)]
    block = es.enter_context(nc.Block())

    P = Prog()
    V = lambda c: vecs[:, c:c + 1]
    vcol = lambda k, c: vecs[:, k * 32 + c:k * 32 + c + 1]
    S = lambda i: scr[:, i * TM:(i + 1) * TM]

    P.add('sp', lambda e: e.dma_start(out=ident[:], in_=ident_d), W=['ident'], dsem='c0')
    P.add('sp', lambda e: e.dma_start(out=vecs[:], in_=vecs_d), W=['vecs'], dsem='c1')
    P.add('sp', lambda e: e.dma_start(out=prm[:, 0:3, :], in_=lam_d.rearrange("a p q -> p a q")), W=['prm'], dsem='c2')
    P.add('dve', lambda e: e.memset(scr[:, 0:128], 1.0), W=['scrp'])
    P.add('dve', lambda e: e.tensor_copy(out=onesr[:], in_=scr[:, 0:128]), R=['scrp'], W=['onesr'])
    P.add('dve', lambda e: e.memset(scr[:, 1024:2048], 0.0), W=['scrz'])
    P.add('dve', lambda e: e.tensor_copy(out=cp[:, :, :, :].rearrange("p a b c -> p (a b c)"), in_=scr[:, 1024:2048]), R=['scrz'], W=['cp0', 'cp1', 'cp2', 'cp3'])
    P.add('dve', lambda e: e.memset(St[:], 0.0), W=['St'])
    P.add('dve', lambda e: e.memset(magic_c[:, 0:1], MAGIC), W=['magic'])
    P.add('dve', lambda e: e.memset(magic_c[:, 1:2], -MAGIC), R=['magic'], W=['magic'])
    P.add('dve', lambda e: e.memset(hist[:], 0.0), W=['hist'])

    pr = lambda i: prm[:, i, :]
    T0, T1 = S(0)[:, 0:128], S(1)[:, 0:128]
    T2, T3 = S(2)[:, 0:128], S(3)[:, 0:128]
    R_, W_ = ['prm'], ['prm']
    a = lambda eng, fn, R=(), W=(): P.add(eng, fn, R=list(R) + ['prm', 'scrp'], W=list(W) + ['prm', 'scrp'])
    def ts(out, in0, s1, s2=None, op0=ALU.mult, op1=ALU.add):
        if s2 is None:
            a('dve', lambda e: e.tensor_scalar(out=out, in0=in0, scalar1=s1, scalar2=None, op0=op0))
        else:
            a('dve', lambda e: e.tensor_scalar(out=out, in0=in0, scalar1=s1, scalar2=s2, op0=op0, op1=op1))

    def tt(out, in0, in1, op):
        a('dve', lambda e: e.tensor_tensor(out=out, in0=in0, in1=in1, op=op))

    def stt(out, in0, sc, in1, op0, op1):
        a('dve', lambda e: e.scalar_tensor_tensor(out=out, in0=in0, scalar=sc, in1=in1, op0=op0, op1=op1))

    T4, T5, T6, T7 = S(4)[:, 0:128], S(5)[:, 0:128], S(6)[:, 0:128], S(7)[:, 0:128]
    a('act', lambda e: e.activation(out=T0, in_=pr(2), func=AF.Exp))
    tt(T1, pr(0), T0, ALU.mult)
    tt(T2, pr(1), T0, ALU.mult)
    ts(pr(8), T2, 1.0 / TWO_PI)
    ts(T0, pr(8), MAGIC, None, op0=ALU.add)
    ts(T0, T0, -MAGIC, None, op0=ALU.add)
    tt(T0, pr(8), T0, ALU.subtract)
    ts(T0, T0, math.pi / 2.0)
    tt(T2, T0, T0, ALU.mult)
    ts(T3, T2, 1.0 / 362880.0)
    stt(T3, T3, -1.0 / 5040.0, T2, ALU.add, ALU.mult)
    stt(T3, T3, 1.0 / 120.0, T2, ALU.add, ALU.mult)
    stt(T3, T3, -1.0 / 6.0, T2, ALU.add, ALU.mult)
    stt(T3, T3, 1.0, T0, ALU.add, ALU.mult)
    ts(T4, T2, -1.0 / 3628800.0)
    stt(T4, T4, 1.0 / 40320.0, T2, ALU.add, ALU.mult)
    stt(T4, T4, -1.0 / 720.0, T2, ALU.add, ALU.mult)
    stt(T4, T4, 1.0 / 24.0, T2, ALU.add, ALU.mult)
    stt(T4, T4, -0.5, T2, ALU.add, ALU.mult)
    ts(T4, T4, 1.0, None, op0=ALU.add)
    stt(T5, T3, 2.0, T4, ALU.mult, ALU.mult)
    tt(T6, T3, T3, ALU.mult)
    ts(T6, T6, -2.0, 1.0)
    stt(T3, T5, 2.0, T6, ALU.mult, ALU.mult)
    tt(T4, T5, T5, ALU.mult)
    ts(T4, T4, -2.0)
    ts(T5, T1, 1.0 / 6.0, 1.0)
    tt(T5, T5, T1, ALU.mult)
    ts(T5, T5, 1.0 / 5.0, 1.0)
    tt(T5, T5, T1, ALU.mult)
    ts(T5, T5, 1.0 / 4.0, 1.0)
    tt(T5, T5, T1, ALU.mult)
    ts(T5, T5, 1.0 / 3.0, 1.0)
    tt(T5, T5, T1, ALU.mult)
    ts(T5, T5, 1.0 / 2.0, 1.0)
    tt(T5, T5, T1, ALU.mult)
    ts(pr(3), T5, 1.0, None, op0=ALU.add)
    tt(pr(9), pr(3), T3, ALU.mult)
    tt(T1, pr(3), T4, ALU.mult)
    tt(T1, T1, T5, ALU.add)
    ts(T3, T1, 1.0, None, op0=ALU.add)
    tt(T0, pr(0), pr(0), ALU.mult)
    tt(T2, pr(1), pr(1), ALU.mult)
    tt(T0, T0, T2, ALU.add)
    a('dve', lambda e: e.reciprocal(out=T0, in_=T0))
    tt(T2, T1, pr(0), ALU.mult)
    tt(pr(4), pr(9), pr(1), ALU.mult)
    tt(T2, T2, pr(4), ALU.add)
    tt(pr(4), T2, T0, ALU.mult)
    tt(T2, pr(9), pr(0), ALU.mult)
    tt(T1, T1, pr(1), ALU.mult)
    tt(T2, T2, T1, ALU.subtract)
    tt(pr(5), T2, T0, ALU.mult)
    a('dve', lambda e: e.tensor_copy(out=pr(0), in_=pr(8)))
    a('dve', lambda e: e.tensor_copy(out=pr(1), in_=pr(3)))
    a('dve', lambda e: e.tensor_copy(out=pr(2), in_=T3))
    a('dve', lambda e: e.tensor_copy(out=pr(3), in_=pr(9)))
    a('dve', lambda e: e.tensor_tensor(out=T0, in0=pr(4), in1=pr(4), op=ALU.mult))
    a('dve', lambda e: e.tensor_tensor(out=T1, in0=pr(5), in1=pr(5), op=ALU.mult))
    a('dve', lambda e: e.tensor_tensor(out=T0, in0=T0, in1=T1, op=ALU.add))
    a('dve', lambda e: e.reciprocal(out=T0, in_=T0))
    a('dve', lambda e: e.tensor_tensor(out=pr(6), in0=pr(4), in1=T0, op=ALU.mult))
    a('dve', lambda e: e.tensor_tensor(out=T1, in0=pr(5), in1=T0, op=ALU.mult))
    a('dve', lambda e: e.tensor_scalar(out=pr(7), in0=T1, scalar1=-1.0, scalar2=None, op0=ALU.mult))

    wstate = dict(next_dma=0, next_use=0)
    TOTAL_TILES = 2 * NTILE

    def wtile():
        i = wstate['next_use']
        wstate['next_use'] += 1
        while wstate['next_dma'] < min(TOTAL_TILES, i + NSLOT):
            j = wstate['next_dma']
            wstate['next_dma'] += 1
            sl = j % NSLOT
            P.add('pool', (lambda e, j=j, sl=sl: e.dma_start(out=ring[:, sl, :], in_=wst[(j % NTILE) % NW])),
                  W=[f'ring{sl}'], dsem=f'w{sl}')
        return i % NSLOT

    def nts_of(T):
        h = T // 2
        if h % 2:
            h += 1
        return [(0, h), (h, T)]

    def barrier():
        regs = list(P.lastw.keys())
        i = P.add('sp', (lambda e: e.nop()), R=[], W=regs)
        P.ops[i]['force'] = True
        for k, j in P.lastdma.items():
            P.ops[i]['deps'].setdefault(j, 'raw')

    bank = dict(i=0)

    def nbank(nb=8):
        b = bank['i'] % nb
        bank['i'] += 1
        return b

    def load_x(row0, T):
        stage = scr[:, 0:D]
        t0 = 0
        while t0 < T:
            n = min(128, T - t0)
            P.add('sp', (lambda e, t0=t0, n=n: e.dma_start(out=stage[0:n, :], in_=xin[row0 + t0:row0 + t0 + n, :])),
                  W=['stage'], dsem='ld')
            for c4 in range(8):
                b = nbank()
                for cc in range(4):
                    c = c4 * 4 + cc
                    P.add('pe', (lambda e, b=b, cc=cc, c=c, n=n: e.transpose(
                        out=pst[b][:, cc * 128:cc * 128 + n], in_=stage[0:n, c * 128:(c + 1) * 128],
                        identity=ident[0:n, 0:n])), R=['stage', 'ident'], W=[f'ps{b}'])
                eng = 'act' if c4 % 2 else 'dve'
                if eng == 'act':
                    P.add('act', (lambda e, b=b, c4=c4, t0=t0, n=n: e.activation(
                        out=x[:, c4 * 4:c4 * 4 + 4, t0:t0 + n],
                        in_=pst[b][:, :].rearrange("p (c t) -> p c t", t=128)[:, :, 0:n], func=AF.Copy)),
                        R=[f'ps{b}'], W=[f'x{c}' for c in range(c4 * 4, c4 * 4 + 4)])
                else:
                    P.add('dve', (lambda e, b=b, c4=c4, t0=t0, n=n: e.tensor_copy(
                        out=x[:, c4 * 4:c4 * 4 + 4, t0:t0 + n],
                        in_=pst[b][:, :].rearrange("p (c t) -> p c t", t=128)[:, :, 0:n])),
                        R=[f'ps{b}'], W=[f'x{c}' for c in range(c4 * 4, c4 * 4 + 4)])
            t0 += n

    def rms(T):
        nts = nts_of(T)
        bs = [nbank() for _ in nts]
        for c in range(NCH):
            sqb, sqn = ((ucr, 'ucr'), (xr, 'xr'))[c % 2]
            P.add('act', (lambda e, c=c, sqb=sqb: e.activation(out=sqb[:, 0:T], in_=x[:, c, 0:T], func=AF.Square)),
                  R=[f'x{c}'], W=[sqn])
            for (lo, hi), b in zip(nts, bs):
                P.add('pe', (lambda e, c=c, lo=lo, hi=hi, b=b, sqb=sqb: e.matmul(
                    pst[b][:, 0:hi - lo], onesr[:], sqb[:, lo:hi], start=(c == 0), stop=(c == NCH - 1))),
                    R=[sqn, 'onesr'], W=[f'ps{b}'])
        for (lo, hi), b in zip(nts, bs):
            P.add('dve', (lambda e, lo=lo, hi=hi, b=b: e.tensor_scalar(
                out=rstd[:, lo:hi], in0=pst[b][:, 0:hi - lo], scalar1=1.0 / D, scalar2=EPS, op0=ALU.mult, op1=ALU.add)),
                R=[f'ps{b}'], W=['rstd'])
        P.add('act', lambda e: e.activation(out=rstd[:, 0:T], in_=rstd[:, 0:T], func=AF.Sqrt), R=['rstd'], W=['rstd'])
        P.add('dve', lambda e: e.reciprocal(out=rstd[:, 0:T], in_=rstd[:, 0:T]), R=['rstd'], W=['rstd'])

    def norm_to_xn(T, gk):
        for c in range(NCH):
            P.add('dve', (lambda e, c=c: e.scalar_tensor_tensor(
                out=xn[:, c, 0:T], in0=x[:, c, 0:T], scalar=vcol(gk, c), in1=rstd[:, 0:T], op0=ALU.mult, op1=ALU.mult)),
                R=[f'x{c}', 'rstd', 'vecs'], W=[f'xn{c}'])

    qdma = dict(n=0)

    def ssm(T, Tp, nseg_s, kcol0, state_only, seq0):
        nts = nts_of(T)
        ntb = [(0, T)] if T <= 512 else nts
        A1, A2 = S(0)[:, 0:T], S(1)[:, 0:T]
        TSs = [S(2)[:, 0:T], S(10)[:, 0:T]]
        TCs = [S(3)[:, 0:T], S(11)[:, 0:T]]
        RMs = [S(4)[:, 0:T], S(12)[:, 0:T]]
        W1, W2, W3 = S(5)[:, 0:T], S(6)[:, 0:T], S(7)[:, 0:T]
        G1, G2 = S(8)[:, 0:T], S(9)[:, 0:T]
        nseg = 1 + nseg_s
        sview = lambda ap: ap[:, Tp:Tp + 8 * nseg_s].rearrange("p (s k) -> p s k", k=8)

        def tablegen(q):
            p = q % 2
            TSb, TCb, RM = TSs[p], TCs[p], RMs[p]
            thq, rq = prm[:, 0, q:q + 1], prm[:, 1, q:q + 1]
            P.add('act', (lambda e: e.activation(out=A1, in_=kidx[:, 0:T], func=AF.Identity, scale=thq, bias=magic_c[:, 0:1])),
                  R=['kidx', 'prm', 'magic'], W=['A1'])
            P.add('act', (lambda e: e.activation(out=A1, in_=A1, func=AF.Identity, scale=1.0, bias=magic_c[:, 1:2])), R=['A1', 'magic'], W=['A1'])
            P.add('act', (lambda e: e.activation(out=A2, in_=kidx[:, 0:T], func=AF.Copy, scale=thq)),
                  R=['kidx', 'prm'], W=['A2'])
            P.add('pool', (lambda e: e.tensor_tensor(out=A2, in0=A2, in1=A1, op=ALU.subtract)), R=['A1', 'A2'], W=['A2'])
            P.add('act', (lambda e: e.activation(out=TSb, in_=A2, func=AF.Sin, scale=TWO_PI * 0.999999)), R=['A2'], W=[f'TS{p}'])
            P.add('act', (lambda e: e.activation(out=TCb, in_=A2, func=AF.Sin, scale=math.pi * 0.999999)), R=['A2'], W=[f'TC{p}'])
            P.add('pool', (lambda e: e.tensor_tensor(out=TCb, in0=TCb, in1=TCb, op=ALU.mult)), R=[f'TC{p}'], W=[f'TC{p}'])
            P.add('pool', (lambda e: e.tensor_scalar(out=TCb, in0=TCb, scalar1=-2.0, scalar2=1.0, op0=ALU.mult, op1=ALU.add)), R=[f'TC{p}'], W=[f'TC{p}'])
            P.add('act', (lambda e: e.activation(out=RM, in_=rmask[:, 0:T], func=AF.Copy, scale=rq)),
                  R=['rmask', 'prm'], W=[f'RM{p}'])

        tablegen(0)
        for c in range(NCH):
            P.add('dve', (lambda e, c=c: e.scalar_tensor_tensor(
                out=ucr[:, 0:T], in0=x[:, c, 0:T], scalar=vcol(0, c), in1=rstd[:, 0:T], op0=ALU.mult, op1=ALU.mult)),
                R=[f'x{c}', 'rstd', 'vecs'], W=['ucr'])
            ybs = [6, 7] if not state_only else []
            for ql in range(4):
                q = 4 * c + ql
                p = q % 2
                TSb, TCb, RM = TSs[p], TCs[p], RMs[p]
                tsn, tcn, rmn = f'TS{p}', f'TC{p}', f'RM{p}'
                bsl = qdma['n'] % 2
                qdma['n'] += 1
                P.add('pool', (lambda e, q=q, bsl=bsl: e.dma_start(out=bt[:, bsl, :], in_=bt_d[q])),
                      W=[f'bt{bsl}'], dsem=f'bt{bsl}')
                if not state_only:
                    P.add('sp', (lambda e, q=q: e.dma_start(out=ctraw[:, 0, :], in_=ct_d[q])),
                          W=['ctraw'], dsem='ct')
                if q + 1 < 128:
                    tablegen(q + 1)
                arq, aiq = prm[:, 2, q:q + 1], prm[:, 3, q:q + 1]
                P.add('dve', (lambda e, q=q, aiq=aiq: e.tensor_scalar(out=zt[:, 0:nseg], in0=St[:, q, 0:nseg, 1], scalar1=aiq, scalar2=None, op0=ALU.mult)), R=['St', 'prm'], W=['zt'])
                P.add('dve', (lambda e, q=q, arq=arq: e.scalar_tensor_tensor(out=zq[:, 0:nseg, 0], in0=St[:, q, 0:nseg, 0], scalar=arq, in1=zt[:, 0:nseg], op0=ALU.mult, op1=ALU.subtract)), R=['St', 'prm', 'zt'], W=['zq'])
                P.add('dve', (lambda e, q=q, arq=arq: e.tensor_scalar(out=zt[:, 0:nseg], in0=St[:, q, 0:nseg, 1], scalar1=arq, scalar2=None, op0=ALU.mult)), R=['St', 'prm', 'zq'], W=['zt'])
                P.add('dve', (lambda e, q=q, aiq=aiq: e.scalar_tensor_tensor(out=zq[:, 0:nseg, 1], in0=St[:, q, 0:nseg, 0], scalar=aiq, in1=zt[:, 0:nseg], op0=ALU.mult, op1=ALU.add)), R=['St', 'prm', 'zt'], W=['zq'])
                bb = [(nbank(6), nbank(6)) for _ in ntb]
                for (lo, hi), (b0, b1) in zip(ntb, bb):
                    for comp, b in ((0, b0), (1, b1)):
                        P.add('pe', (lambda e, comp=comp, b=b, lo=lo, hi=hi, bsl=bsl: e.matmul(
                            pst[b][:, 0:hi - lo], bt[:, bsl, comp * 128:(comp + 1) * 128], ucr[:, lo:hi], start=True, stop=True)),
                            R=[f'bt{bsl}', 'ucr'], W=[f'ps{b}'])
                for (lo, hi), (b0, b1) in zip(ntb, bb):
                    n = hi - lo
                    P.add('dve', (lambda e, lo=lo, hi=hi, b0=b0, n=n, TCb=TCb: e.tensor_tensor(out=W1[:, lo:hi], in0=pst[b0][:, 0:n], in1=TCb[:, lo:hi], op=ALU.mult)),
                          R=[f'ps{b0}', tcn], W=['W1'])
                    P.add('dve', (lambda e, lo=lo, hi=hi, b1=b1, n=n, TSb=TSb: e.tensor_tensor(out=W2[:, lo:hi], in0=pst[b1][:, 0:n], in1=TSb[:, lo:hi], op=ALU.mult)),
                          R=[f'ps{b1}', tsn], W=['W2'])
                    P.add('dve', (lambda e, lo=lo, hi=hi, b1=b1, n=n, TCb=TCb: e.tensor_tensor(out=W3[:, lo:hi], in0=pst[b1][:, 0:n], in1=TCb[:, lo:hi], op=ALU.mult)),
                          R=[f'ps{b1}', tcn], W=['W3'])
                    P.add('dve', (lambda e, lo=lo, hi=hi, b0=b0, n=n, TSb=TSb: e.tensor_tensor(out=G2[:, lo:hi], in0=pst[b0][:, 0:n], in1=TSb[:, lo:hi], op=ALU.mult)),
                          R=[f'ps{b0}', tsn], W=['G2'])
                P.add('dve', (lambda e: e.tensor_tensor(out=W1, in0=W1, in1=W2, op=ALU.add)), R=['W1', 'W2'], W=['W1'])
                P.add('dve', (lambda e: e.tensor_tensor(out=W3, in0=W3, in1=G2, op=ALU.subtract)), R=['W3', 'G2'], W=['W3'])
                for comp, Wb, nm in ((0, W1, 'W1'), (1, W3, 'W3')):
                    P.add('dve', (lambda e, comp=comp, Wb=Wb: e.tensor_tensor(
                        out=Wb[:, 0:1], in0=Wb[:, 0:1], in1=zq[:, 0:1, comp], op=ALU.add)), R=[nm, 'zq'], W=[nm])
                    if nseg_s:
                        P.add('dve', (lambda e, comp=comp, Wb=Wb: e.tensor_tensor(
                            out=sview(Wb)[:, :, 0], in0=sview(Wb)[:, :, 0], in1=zq[:, 1:nseg, comp], op=ALU.add)), R=[nm, 'zq'], W=[nm])
                P.add('dve', (lambda e, RM=RM: e.tensor_tensor_scan(out=W2, data0=RM, data1=W1, initial=0.0, op0=ALU.mult, op1=ALU.add)),
                      R=[rmn, 'W1'], W=['W2'])
                P.add('dve', (lambda e, RM=RM: e.tensor_tensor_scan(out=W1, data0=RM, data1=W3, initial=0.0, op0=ALU.mult, op1=ALU.add)),
                      R=[rmn, 'W3', 'W2'], W=['W1'])
                if state_only:
                    lo, hi = Tp - 1, Tp
                    P.add('dve', (lambda e, TCb=TCb: e.tensor_tensor(out=W3[:, lo:hi], in0=W2[:, lo:hi], in1=TCb[:, lo:hi], op=ALU.mult)), R=['W2', tcn], W=['W3'])
                    P.add('dve', (lambda e, TSb=TSb: e.tensor_tensor(out=G2[:, lo:hi], in0=W1[:, lo:hi], in1=TSb[:, lo:hi], op=ALU.mult)), R=['W1', tsn], W=['G2'])
                    P.add('dve', (lambda e, q=q: e.tensor_tensor(out=St[:, q, 0:1, 0], in0=W3[:, lo:hi], in1=G2[:, lo:hi], op=ALU.subtract)), R=['W3', 'G2'], W=['St'])
                    P.add('dve', (lambda e, TSb=TSb: e.tensor_tensor(out=W3[:, lo:hi], in0=W2[:, lo:hi], in1=TSb[:, lo:hi], op=ALU.mult)), R=['W2', tsn], W=['W3'])
                    P.add('dve', (lambda e, TCb=TCb: e.tensor_tensor(out=G2[:, lo:hi], in0=W1[:, lo:hi], in1=TCb[:, lo:hi], op=ALU.mult)), R=['W1', tcn], W=['G2'])
                    P.add('dve', (lambda e, q=q: e.tensor_tensor(out=St[:, q, 0:1, 1], in0=W3[:, lo:hi], in1=G2[:, lo:hi], op=ALU.add)), R=['W3', 'G2'], W=['St'])
                    continue
                P.add('pool', (lambda e, TCb=TCb: e.tensor_tensor(out=A1, in0=W2, in1=TCb, op=ALU.mult)), R=['W2', tcn], W=['A1'])
                P.add('pool', (lambda e, TSb=TSb: e.tensor_tensor(out=A2, in0=W1, in1=TSb, op=ALU.mult)), R=['W1', tsn], W=['A2'])
                P.add('pool', (lambda e: e.tensor_tensor(out=xr[:, 0:T], in0=A1, in1=A2, op=ALU.subtract)), R=['A1', 'A2'], W=['xr'])
                P.add('pool', (lambda e, TSb=TSb: e.tensor_tensor(out=A1, in0=W2, in1=TSb, op=ALU.mult)), R=['W2', tsn], W=['A1'])
                P.add('pool', (lambda e, TCb=TCb: e.tensor_tensor(out=A2, in0=W1, in1=TCb, op=ALU.mult)), R=['W1', tcn], W=['A2'])
                P.add('pool', (lambda e: e.tensor_tensor(out=xi[:, 0:T], in0=A1, in1=A2, op=ALU.add)), R=['A1', 'A2'], W=['xi'])
                for comp, src, nm in ((0, xr, 'xr'), (1, xi, 'xi')):
                    P.add('act', (lambda e, comp=comp, src=src, q=q: e.activation(out=St[:, q, 0:1, comp], in_=src[:, Tp - 1:Tp], func=AF.Copy)), R=[nm], W=['St'])
                    if nseg_s:
                        P.add('act', (lambda e, comp=comp, src=src, q=q: e.activation(
                            out=St[:, q, 1:nseg, comp], in_=sview(src)[:, :, 7], func=AF.Copy)), R=[nm], W=['St'])
                frq, fiq = prm[:, 4, q:q + 1], prm[:, 5, q:q + 1]
                cpr = cp[:, ql, 0, 32 * ql:32 * ql + 32]
                cpi = cp[:, ql, 1, 32 * ql:32 * ql + 32]
                t32a, t32b = G1[:, 0:32], G1[:, 32:64]
                P.add('dve', (lambda e, frq=frq: e.tensor_scalar(out=t32a, in0=ctraw[:, 0, 0:32], scalar1=frq, scalar2=None, op0=ALU.mult)), R=['ctraw', 'prm'], W=['G1'])
                P.add('dve', (lambda e, fiq=fiq: e.tensor_scalar(out=t32b, in0=ctraw[:, 0, 32:64], scalar1=fiq, scalar2=None, op0=ALU.mult)), R=['ctraw', 'prm'], W=['G1'])
                P.add('dve', (lambda e, cpr=cpr: e.tensor_tensor(out=cpr, in0=t32a, in1=t32b, op=ALU.subtract)), R=['G1'], W=[f'cp{ql}'])
                P.add('dve', (lambda e, fiq=fiq: e.tensor_scalar(out=t32a, in0=ctraw[:, 0, 0:32], scalar1=fiq, scalar2=-1.0, op0=ALU.mult, op1=ALU.mult)), R=['ctraw', 'prm'], W=['G1'])
                P.add('dve', (lambda e, frq=frq: e.tensor_scalar(out=t32b, in0=ctraw[:, 0, 32:64], scalar1=frq, scalar2=None, op0=ALU.mult)), R=['ctraw', 'prm'], W=['G1'])
                P.add('dve', (lambda e, cpi=cpi: e.tensor_tensor(out=cpi, in0=t32a, in1=t32b, op=ALU.subtract)), R=['G1'], W=[f'cp{ql}'])
                for (lo, hi), yb in zip(nts, ybs):
                    P.add('pe', (lambda e, lo=lo, hi=hi, yb=yb, ql=ql: e.matmul(
                        pst[yb][:, 0:hi - lo], cp[:, ql, 0, :], xr[:, lo:hi], start=(ql == 0), stop=False)),
                        R=[f'cp{ql}', 'xr'], W=[f'ps{yb}'])
                    P.add('pe', (lambda e, lo=lo, hi=hi, yb=yb, ql=ql: e.matmul(
                        pst[yb][:, 0:hi - lo], cp[:, ql, 1, :], xi[:, lo:hi], start=False, stop=(ql == 3))),
                        R=[f'cp{ql}', 'xi'], W=[f'ps{yb}'])
            if state_only:
                continue
            for (lo, hi), yb in zip(nts, ybs):
                n = hi - lo
                P.add('dve', (lambda e, lo=lo, hi=hi, yb=yb, n=n, c=c: e.scalar_tensor_tensor(
                    out=W1[:, lo:hi], in0=ucr[:, lo:hi], scalar=vcol(5, c), in1=pst[yb][:, 0:n], op0=ALU.mult, op1=ALU.add)),
                    R=['ucr', 'vecs', f'ps{yb}'], W=['W1'])
            P.add('act', (lambda e: e.activation(out=W2, in_=W1, func=AF.Square)), R=['W1'], W=['W2'])
            P.add('dve', (lambda e: e.tensor_scalar(out=W2, in0=W2, scalar1=0.044715 * 2 * GC0, scalar2=2 * GC0, op0=ALU.mult, op1=ALU.add)), R=['W2'], W=['W2'])
            P.add('dve', (lambda e: e.tensor_tensor(out=W2, in0=W2, in1=W1, op=ALU.mult)), R=['W2', 'W1'], W=['W2'])
            P.add('act', (lambda e: e.activation(out=W3, in_=W2, func=AF.Sigmoid)), R=['W2'], W=['W3'])
            P.add('dve', (lambda e, c=c: e.tensor_tensor(out=xn[:, c, 0:T], in0=W3, in1=W1, op=ALU.mult)), R=['W3', 'W1'], W=[f'xn{c}'])

    def glu(T):
        nts = nts_of(T)
        for m in range(NCH):
            bs = [[nbank() for _ in nts] for _ in range(2)]
            for part in range(2):
                for half in range(2):
                    sl = wtile()
                    for kl in range(16):
                        k = half * 16 + kl
                        for (lo, hi), b in zip(nts, bs[part]):
                            P.add('pe', (lambda e, sl=sl, kl=kl, k=k, lo=lo, hi=hi, b=b: e.matmul(
                                pst[b][:, 0:hi - lo], ring[:, sl, kl * 128:(kl + 1) * 128], xn[:, k, lo:hi],
                                start=(k == 0), stop=(k == 31))), R=[f'ring{sl}', f'xn{k}'], W=[f'ps{b}'])
            for i, (lo, hi) in enumerate(nts):
                n = hi - lo
                b1, b2 = bs[0][i], bs[1][i]
                G = S(8 + i % 2)[:, 0:n]
                gn = f'G{1 + i % 2}'
                P.add('act', (lambda e, b2=b2, n=n, G=G: e.activation(out=G, in_=pst[b2][:, 0:n], func=AF.Sigmoid)), R=[f'ps{b2}'], W=[gn])
                P.add('dve', (lambda e, b1=b1, n=n, G=G: e.tensor_tensor(out=G, in0=pst[b1][:, 0:n], in1=G, op=ALU.mult)), R=[f'ps{b1}', gn], W=[gn])
                P.add('dve', (lambda e, m=m, lo=lo, hi=hi, G=G: e.tensor_tensor(out=x[:, m, lo:hi], in0=x[:, m, lo:hi], in1=G, op=ALU.add)), R=[gn, f'x{m}'], W=[f'x{m}'])

    def ffn(T):
        nts = nts_of(T)
        for g in range(NGRP):
            nf = FG if g < NGRP - 1 else NF - FG * (NGRP - 1)
            for fl in range(nf):
                bs = [[nbank() for _ in nts] for _ in range(2)]
                for part in range(2):
                    for half in range(2):
                        sl = wtile()
                        for kl in range(16):
                            k = half * 16 + kl
                            for (lo, hi), b in zip(nts, bs[part]):
                                P.add('pe', (lambda e, sl=sl, kl=kl, k=k, lo=lo, hi=hi, b=b: e.matmul(
                                    pst[b][:, 0:hi - lo], ring[:, sl, kl * 128:(kl + 1) * 128], xn[:, k, lo:hi],
                                    start=(k == 0), stop=(k == 31))), R=[f'ring{sl}', f'xn{k}'], W=[f'ps{b}'])
                for i, (lo, hi) in enumerate(nts):
                    n = hi - lo
                    bg, bu = bs[0][i], bs[1][i]
                    G = S(8 + i % 2)[:, 0:n]
                    gn = f'G{1 + i % 2}'
                    P.add('act', (lambda e, bg=bg, n=n, G=G: e.activation(out=G, in_=pst[bg][:, 0:n], func=AF.Silu)), R=[f'ps{bg}'], W=[gn])
                    P.add('dve', (lambda e, bu=bu, n=n, G=G, fl=fl, lo=lo, hi=hi: e.tensor_tensor(
                        out=h1[:, fl, lo:hi], in0=pst[bu][:, 0:n], in1=G, op=ALU.mult)), R=[f'ps{bu}', gn], W=[f'h1_{fl}'])
            for mp in range(8):
                sl = wtile()
                for ml in range(4):
                    m = mp * 4 + ml
                    for i, (lo, hi) in enumerate(nts):
                        n = hi - lo
                        b = nbank()
                        for fl in range(nf):
                            P.add('pe', (lambda e, sl=sl, fl=fl, ml=ml, lo=lo, hi=hi, b=b, nf=nf: e.matmul(
                                pst[b][:, 0:hi - lo], ring[:, sl, fl * 512 + ml * 128:fl * 512 + (ml + 1) * 128], h1[:, fl, lo:hi],
                                start=(fl == 0), stop=(fl == nf - 1))), R=[f'ring{sl}', f'h1_{fl}'], W=[f'ps{b}'])
                        P.add('dve', (lambda e, m=m, lo=lo, hi=hi, b=b, n=n: e.tensor_tensor(
                            out=x[:, m, lo:hi], in0=pst[b][:, 0:n], in1=x[:, m, lo:hi], op=ALU.add)), R=[f'ps{b}', f'x{m}'], W=[f'x{m}'])

    def pool_layer(T, Tp, poscol0, first, st_idx):
        nts = nts_of(T)
        E = 15 + Tp
        ES = 8 * 23
        icnt = scr[:, 10 * TM - 4 * TM:10 * TM]
        posr = S(5)
        P.add('sp', (lambda e: e.dma_start(out=posr[:, 0:T], in_=pos_d[:, poscol0:poscol0 + T])), W=['A1p'], dsem='pos')
        for wi, w in enumerate(WIN):
            ic = icnt[:, wi * TM:wi * TM + T]
            P.add('dve', (lambda e, ic=ic, w=w: e.tensor_scalar(out=ic, in0=posr[:, 0:T], scalar1=1.0, scalar2=float(w), op0=ALU.add, op1=ALU.min)), R=['A1p'], W=[f'ic{wi}'])
            P.add('dve', (lambda e, ic=ic: e.reciprocal(out=ic, in_=ic)), R=[f'ic{wi}'], W=[f'ic{wi}'])
        hc = S(0)
        ext = scr[:, TM:TM + E + ES]
        s2 = scr[:, 3 * TM:3 * TM + E + ES]
        stg = scr[:, 5 * TM:5 * TM + 128]
        for c in range(NCH):
            gi = c // 8
            w = WIN[gi]
            P.add('dve', (lambda e, c=c: e.scalar_tensor_tensor(
                out=hc[:, 0:T], in0=x[:, c, 0:T], scalar=vcol(1, c), in1=rstd[:, 0:T], op0=ALU.mult, op1=ALU.mult)),
                R=[f'x{c}', 'rstd', 'vecs'], W=['hc'])
            P.add('act', (lambda e, c=c: e.activation(out=ext[:, 0:15], in_=hist[:, c, :], func=AF.Copy)), R=['hist'], W=['ext'])
            P.add('act', (lambda e: e.activation(out=ext[:, 15:15 + Tp], in_=hc[:, 0:Tp], func=AF.Copy)), R=['hc'], W=['ext'])
            exs = ext[:, E:E + ES].rearrange("p (s k) -> p s k", k=23)
            P.add('act', (lambda e, exs=exs: e.activation(out=exs[:, :, 15:23], in_=hc[:, Tp:Tp + 64].rearrange("p (s k) -> p s k", k=8), func=AF.Copy)), R=['hc'], W=['ext'])
            b = nbank()
            P.add('sp', (lambda e, c=c: e.dma_start(out=sphc[:, c % 2, :], in_=spool_d[st_idx * 8:st_idx * 8 + 8, :, c * 128:(c + 1) * 128].rearrange("s k d -> (s k) d"))),
                  W=[f'sphc{c % 2}'], dsem=f'sph{c % 2}')
            P.add('pe', (lambda e, c=c, b=b: e.transpose(out=pst[b][:, 0:120], in_=sphc[0:120, c % 2, :], identity=ident[0:120, 0:120])),
                  R=[f'sphc{c % 2}', 'ident'], W=[f'ps{b}'])
            P.add('dve', (lambda e, b=b, exs=exs: e.tensor_copy(out=exs[:, :, 0:15], in_=pst[b][:, 0:120].rearrange("p (s k) -> p s k", k=15))), R=[f'ps{b}'], W=['ext'])
            P.add('act', (lambda e, c=c: e.activation(out=hist[:, c, :], in_=hc[:, Tp - 15:Tp], func=AF.Copy)), R=['hc', 'ext'], W=['hist'])
            L = E + ES
            cur, other = ext, s2
            sh = 1
            while sh < w:
                P.add('dve', (lambda e, cur=cur, other=other, sh=sh, L=L: e.tensor_tensor(out=other[:, sh:L], in0=cur[:, sh:L], in1=cur[:, 0:L - sh], op=ALU.add)),
                      R=['ext', 's2'], W=['ext', 's2'])
                if sh > 1 or True:
                    P.add('act', (lambda e, cur=cur, other=other, sh=sh: e.activation(out=other[:, 0:sh], in_=cur[:, 0:sh], func=AF.Copy)), R=['ext', 's2'], W=['ext', 's2'])
                cur, other = other, cur
                sh *= 2
            ic = icnt[:, gi * TM:gi * TM + T]
            pb = S(5)
            P.add('dve', (lambda e, cur=cur, ic=ic: e.tensor_tensor(out=pb[:, 0:Tp], in0=cur[:, 15:15 + Tp], in1=ic[:, 0:Tp], op=ALU.mult)), R=['ext', 's2', f'ic{gi}'], W=['pb'])
            curs = cur[:, E:E + ES].rearrange("p (s k) -> p s k", k=23)
            P.add('dve', (lambda e, curs=curs, ic=ic: e.tensor_tensor(
                out=pb[:, Tp:Tp + 64].rearrange("p (s k) -> p s k", k=8), in0=curs[:, :, 15:23],
                in1=ic[:, Tp:Tp + 64].rearrange("p (s k) -> p s k", k=8), op=ALU.mult)), R=['ext', 's2', f'ic{gi}'], W=['pb'])
            if T > Tp + 64:
                P.add('dve', (lambda e: e.memset(pb[:, Tp + 64:T], 0.0)), W=['pb'])
            P.add('dve', (lambda e, c=c: e.tensor_tensor(out=xn[:, c, 0:T], in0=pb[:, 0:T], in1=hc[:, 0:T], op=ALU.subtract)), R=['pb', 'hc'], W=[f'xn{c}'])
            b = nbank()
            P.add('pe', (lambda e, b=b: e.transpose(out=pst[b][0:64, 0:128], in_=hc[:, Tp:Tp + 64], identity=ident[:, :])), R=['hc', 'ident'], W=[f'ps{b}'])
            P.add('act', (lambda e, b=b, c=c: e.activation(out=hst[0:64, c % 4, :], in_=pst[b][0:64, 0:128], func=AF.Copy)), R=[f'ps{b}'], W=[f'hst{c % 4}'])
            P.add('sp', (lambda e, c=c: e.dma_start(out=psn_d[st_idx, c, :, :], in_=hst[0:64, c % 4, :])), R=[f'hst{c % 4}'], dsem=f'o_psn{c % 4}')
            if not first:
                b = nbank()
                P.add('pe', (lambda e, b=b: e.transpose(out=pst[b][0:15, 0:128], in_=hc[:, Tp - 15:Tp], identity=ident[:, :])), R=['hc', 'ident'], W=[f'ps{b}'])
                P.add('act', (lambda e, b=b, c=c: e.activation(out=hpt[0:15, c % 4, :], in_=pst[b][0:15, 0:128], func=AF.Copy)), R=[f'ps{b}'], W=[f'hpt{c % 4}'])
                P.add('sp', (lambda e, c=c: e.dma_start(out=ppn_d[c, :, :], in_=hpt[0:15, c % 4, :])), R=[f'hpt{c % 4}'], dsem=f'o_ppn{c % 4}')
        for gi in range(4):
            for mpair in range(4):
                sl = wtile()
                for ml in range(2):
                    m = gi * 8 + mpair * 2 + ml
                    for (lo, hi) in nts:
                        n = hi - lo
                        b = nbank()
                        for k in range(8):
                            P.add('pe', (lambda e, sl=sl, ml=ml, k=k, gi=gi, lo=lo, hi=hi, b=b: e.matmul(
                                pst[b][:, 0:hi - lo], ring[:, sl, (ml * 8 + k) * 128:(ml * 8 + k + 1) * 128], xn[:, gi * 8 + k, lo:hi],
                                start=(k == 0), stop=(k == 7))), R=[f'ring{sl}', f'xn{gi * 8 + k}'], W=[f'ps{b}'])
                        P.add('dve', (lambda e, m=m, lo=lo, hi=hi, b=b, n=n: e.scalar_tensor_tensor(
                            out=x[:, m, lo:hi], in0=pst[b][:, 0:n], scalar=vcol(6, m), in1=x[:, m, lo:hi], op0=ALU.mult, op1=ALU.add)),
                            R=[f'ps{b}', f'x{m}', 'vecs'], W=[f'x{m}'])


    def out_y(T, Tp, prow0, srow0, halo):
        ost = scr[:, 0:D]
        segs = []
        t = halo
        while t < Tp:
            n = min(128, Tp - t)
            segs.append((t, n, prow0 + (t - halo)))
            t += n
        segs.append((Tp, 64, srow0))
        for (t0, n, r0) in segs:
            for c in range(NCH):
                hcf = S(8 + c % 2)
                gn = f'G{1 + c % 2}'
                P.add('dve', (lambda e, c=c, t0=t0, n=n, hcf=hcf: e.scalar_tensor_tensor(
                    out=hcf[:, 0:n], in0=x[:, c, t0:t0 + n], scalar=vcol(4, c), in1=rstd[:, t0:t0 + n], op0=ALU.mult, op1=ALU.mult)),
                    R=[f'x{c}', 'rstd', 'vecs'], W=[gn])
                b = nbank()
                P.add('pe', (lambda e, b=b, n=n, hcf=hcf: e.transpose(out=pst[b][0:n, 0:128], in_=hcf[:, 0:n], identity=ident[:, :])), R=[gn, 'ident'], W=[f'ps{b}'])
                P.add('act', (lambda e, b=b, n=n, c=c: e.activation(out=ost[0:n, c * 128:(c + 1) * 128], in_=pst[b][0:n, 0:128], func=AF.Copy)), R=[f'ps{b}'], W=['stage'])
            P.add('sp', (lambda e, n=n, r0=r0: e.dma_start(out=y_d[r0:r0 + n, :], in_=ost[0:n, :])), R=['stage'], dsem='o_y')

    def load_sample_states(seq0):
        tmp = scr[:, 0:2 * 8 * 128].rearrange("p (a s q) -> p a s q", a=2, s=8)
        for a_ in range(2):
            P.add('sp', (lambda e, a_=a_: e.dma_start(out=tmp[:, a_, :, :], in_=sst_d[a_, seq0:seq0 + 8, :, :].rearrange("s q p -> q s p"))), W=['stage'], dsem='ld')
        xs = scr[:, 2048:2048 + 2048].rearrange("p (a s q) -> p a s q", a=2, s=8)
        for a_ in range(2):
            for s in range(8):
                b = nbank()
                P.add('pe', (lambda e, a_=a_, s=s, b=b: e.transpose(out=pst[b][:, 0:128], in_=tmp[:, a_, s, :], identity=ident[:, :])), R=['stage', 'ident'], W=[f'ps{b}'])
                P.add('act', (lambda e, a_=a_, s=s, b=b: e.activation(out=xs[:, a_, s, :], in_=pst[b][:, 0:128], func=AF.Copy)), R=[f'ps{b}'], W=['xs'])
        for s in range(8):
            t_ = scr[:, 4096:4224]
            P.add('dve', (lambda e, s=s: e.tensor_tensor(out=t_, in0=xs[:, 1, s, :], in1=prm[:, 7, :], op=ALU.mult)), R=['xs', 'prm'], W=['A1p'])
            P.add('dve', (lambda e, s=s: e.tensor_tensor(out=St[:, :, 1 + s, 0], in0=xs[:, 0, s, :], in1=prm[:, 6, :], op=ALU.mult)), R=['xs', 'prm'], W=['St'])
            P.add('dve', (lambda e, s=s: e.tensor_tensor(out=St[:, :, 1 + s, 0], in0=St[:, :, 1 + s, 0], in1=t_, op=ALU.subtract)), R=['St', 'A1p'], W=['St'])
            P.add('dve', (lambda e, s=s: e.tensor_tensor(out=t_, in0=xs[:, 1, s, :], in1=prm[:, 6, :], op=ALU.mult)), R=['xs', 'prm', 'St'], W=['A1p'])
            P.add('dve', (lambda e, s=s: e.tensor_tensor(out=St[:, :, 1 + s, 1], in0=xs[:, 0, s, :], in1=prm[:, 7, :], op=ALU.mult)), R=['xs', 'prm'], W=['St'])
            P.add('dve', (lambda e, s=s: e.tensor_tensor(out=St[:, :, 1 + s, 1], in0=St[:, :, 1 + s, 1], in1=t_, op=ALU.add)), R=['St', 'A1p'], W=['St'])

    def store_states(segs, dst_fn, dsem):
        ob = scr[:, 0:2 * 9 * 128].rearrange("p (a s q) -> p a s q", a=2, s=9)
        for seg in segs:
            t_ = scr[:, 2304:2432]
            u_ = scr[:, 2432:2560]
            P.add('dve', (lambda e, seg=seg: e.tensor_tensor(out=t_, in0=St[:, :, seg, 1], in1=prm[:, 5, :], op=ALU.mult)), R=['St', 'prm'], W=['A1p'])
            P.add('dve', (lambda e, seg=seg: e.tensor_tensor(out=u_, in0=St[:, :, seg, 0], in1=prm[:, 4, :], op=ALU.mult)), R=['St', 'prm'], W=['A1q'])
            P.add('dve', (lambda e: e.tensor_tensor(out=u_, in0=u_, in1=t_, op=ALU.subtract)), R=['A1p', 'A1q'], W=['A1q'])
            b = nbank()
            P.add('pe', (lambda e, b=b: e.transpose(out=pst[b][:, 0:128], in_=u_, identity=ident[:, :])), R=['A1q', 'ident'], W=[f'ps{b}'])
            P.add('act', (lambda e, b=b, seg=seg: e.activation(out=ob[:, 0, seg, :], in_=pst[b][:, 0:128], func=AF.Copy)), R=[f'ps{b}'], W=['ob'])
            P.add('dve', (lambda e, seg=seg: e.tensor_tensor(out=t_, in0=St[:, :, seg, 1], in1=prm[:, 4, :], op=ALU.mult)), R=['St', 'prm', 'A1q'], W=['A1p'])
            P.add('dve', (lambda e, seg=seg: e.tensor_tensor(out=u_, in0=St[:, :, seg, 0], in1=prm[:, 5, :], op=ALU.mult)), R=['St', 'prm'], W=['A1q'])
            P.add('dve', (lambda e: e.tensor_tensor(out=u_, in0=u_, in1=t_, op=ALU.add)), R=['A1p', 'A1q'], W=['A1q'])
            b = nbank()
            P.add('pe', (lambda e, b=b: e.transpose(out=pst[b][:, 0:128], in_=u_, identity=ident[:, :])), R=['A1q', 'ident'], W=[f'ps{b}'])
            P.add('act', (lambda e, b=b, seg=seg: e.activation(out=ob[:, 1, seg, :], in_=pst[b][:, 0:128], func=AF.Copy)), R=[f'ps{b}'], W=['ob'])
            for a_ in range(2):
                P.add('sp', (lambda e, a_=a_, seg=seg: e.dma_start(out=dst_fn(a_, seg), in_=ob[:, a_, seg, :])), R=['ob'], dsem=dsem)

    def load_kr(src, col0, T):
        P.add('sp', (lambda e: e.dma_start(out=kidx[:, 0:T], in_=src[0, :, col0:col0 + T])), W=['kidx'], dsem='k0')
        P.add('sp', (lambda e: e.dma_start(out=rmask[:, 0:T], in_=src[1, :, col0:col0 + T])), W=['rmask'], dsem='k1')

    def dump(name, ap, regs, n, b3=None):
        if not stage:
            return
        off = dcur['o']
        dcur['o'] += n
        DBG.append((name, off, n))
        o_ap = dbg_d[:, off:off + n]
        if b3:
            o_ap = o_ap.rearrange("p (a b) -> p a b", b=b3)
        P.add('sp', (lambda e: e.dma_start(out=o_ap, in_=ap)), R=regs, dsem='o_dbg')

    def finish():
        P.add('sp', (lambda e: e.nop()), R=[], W=[], dsem=None)
        last = P.ops[-1]
        for k, j in P.lastdma.items():
            if k.startswith('o_'):
                last['deps'][j] = 'raw'
        P.emit(nc, block, es)
        es.close()
        _CACHE['P'] = P
        return nc

    barrier()
    dump('prm', prm[:, :, :].rearrange("p a b -> p (a b)"), ['prm', 'scrp'], 1280)
    load_kr(krp_d, 0, TPRE)
    for seg in range(2):
        load_x(seg * TPRE, TPRE)
        barrier()
        if seg == 1:
            dump('x_c0', x[:, 0, 0:TPRE], ['x0'], TPRE)
            dump('x_c31', x[:, 31, 0:TPRE], ['x31'], TPRE)
        rms(TPRE)
        if seg == 1:
            dump('rstd', rstd[:, 0:TPRE], ['rstd'], TPRE)
        ssm(TPRE, TPRE, 0, 0, True, 0)
        barrier()
        dump(f'St_pre{seg}', St[:, :, 0, :], ['St'], 256, b3=2)
        if stage == 1 and seg == 1:
            return finish()
    row0 = 2 * TPRE
    for st_idx, (T, Tp, halo) in enumerate(((TA, TPA, 15), (TB, TPB, 0))):
        first = st_idx == 0
        load_kr(kr_d, st_idx * TM, T)
        load_sample_states(st_idx * 8)
        barrier()
        load_x(row0, T)
        barrier()
        rms(T)
        ssm(T, Tp, 8, 0, False, st_idx * 8)
        barrier()
        if stage and first:
            dump('StA', St[:, :, 0, :], ['St'], 256, b3=2)
            for cc in (0, 17, 31):
                P.add('dve', (lambda e, cc=cc: e.tensor_copy(out=S(9), in_=xn[:, cc, :])), R=[f'xn{cc}'], W=['G2'])
                dump(f'xnA_{cc}', S(9), ['G2'], TM)
            P.add('dve', (lambda e: e.tensor_copy(out=S(8), in_=xr[:, :])), R=['xr'], W=['G1'])
            dump('xrA', S(8), ['G1'], TM)
            dump('TSA', S(2), ['TS0'], TM)
            dump('TCA', S(3), ['TC0'], TM)
            dump('RMA', S(4), ['RM0'], TM)
            dump('kidxA', kidx[:, :], ['kidx'], TM)
        store_states(range(1, 9), (lambda a_, seg, st_idx=st_idx: ss_d[a_, st_idx * 8 + seg - 1, :, :]), 'o_ss')
        if stage == 2:
            barrier()
            return finish()
        if not first:
            store_states([0], (lambda a_, seg: sp_d[a_, :, :]), 'o_sp')
        barrier()
        def dumpx(tag):
            if stage and first:
                barrier()
                for cc in (0, 17, 31):
                    dump(f'{tag}_{cc}', x[:, cc, :], [f'x{cc}'], TM)
                dump(f'{tag}_rstd', rstd[:, :], ['rstd'], TM)
        glu(T)
        dumpx('x1')
        if stage == 3:
            barrier()
            return finish()
        rms(T)
        norm_to_xn(T, 2)
        ffn(T)
        dumpx('x2')
        if stage == 4:
            barrier()
            return finish()
        rms(T)
        barrier()
        P.add('sp', (lambda e, st_idx=st_idx: e.dma_start(out=pso_d[st_idx * 8:st_idx * 8 + 8, :, :], in_=spool_d[st_idx * 8:st_idx * 8 + 8, 8:15, :])), dsem='o_pso')
        pool_layer(T, Tp, st_idx * TM, first, st_idx)
        barrier()
        dumpx('x3')
        if stage == 5:
            barrier()
            return finish()
        rms(T)
        norm_to_xn(T, 3)
        ffn(T)
        dumpx('x4')
        rms(T)
        barrier()
        out_y(T, Tp, st_idx * 512, 1024 + st_idx * 64, halo)
        barrier()
        if stage == 6:
            return finish()
        row0 += T
    return finish()


_CACHE = {}


def _prep_weights(ssm_w_glu, pool_w, ffn_w_gate_up, ffn_w_down):
    tiles = np.zeros((NTILE, 128, 2048), np.float32)
    W = ssm_w_glu[0].reshape(32, 128, 2, 32, 128)
    W = W.reshape(2, 16, 128, 2, 32, 128)
    tiles[0:NT_GLU] = W.transpose(4, 3, 0, 2, 1, 5).reshape(NT_GLU, 128, 2048)
    base = NT_GLU
    for L in range(2):
        GU = ffn_w_gate_up[L].reshape(2, 16, 128, 2, NF, 128)
        GUt = GU.transpose(4, 3, 0, 2, 1, 5).reshape(NF, 4, 128, 2048)
        DN = ffn_w_down[L].reshape(NF, 128, 8, 512)
        t = base
        for g in range(NGRP):
            nf = FG if g < NGRP - 1 else NF - FG * (NGRP - 1)
            for fl in range(nf):
                tiles[t:t + 4] = GUt[g * FG + fl]
                t += 4
            blk = DN[g * FG:g * FG + nf]
            tiles[t:t + 8, :, 0:nf * 512] = blk.transpose(2, 1, 0, 3).reshape(8, 128, nf * 512)
            t += 8
        assert t == base + NT_FFN
        base = t
        if L == 0:
            PW = pool_w[0].reshape(4, 8, 128, 4, 2, 128)
            tiles[base:base + NT_POOL] = PW.transpose(0, 3, 2, 4, 1, 5).reshape(NT_POOL, 128, 2048)
            base += NT_POOL
    assert base == NTILE
    return tiles


def kernel(x_prompt, x_sample, state_ssm_re, state_ssm_im, state_pool, norm_mix, norm_ffn,
           ssm_lambda_re, ssm_lambda_im, ssm_log_step, ssm_b_re, ssm_b_im, ssm_c_re, ssm_c_im,
           ssm_d, ssm_w_glu, pool_w, pool_scale, ffn_w_gate_up, ffn_w_down, norm_final):
    f = lambda a: np.ascontiguousarray(np.asarray(a, dtype=np.float32))
    x_prompt, x_sample = f(x_prompt), f(x_sample)
    stage = STAGE
    if 'nc' not in _CACHE:
        _CACHE['nc'] = build_program(stage)
    nc = _CACHE['nc']
    wst = _prep_weights(f(ssm_w_glu), f(pool_w), f(ffn_w_gate_up), f(ffn_w_down))
    if stage:
        wst = np.ascontiguousarray(wst[0:{1: 1, 2: 1, 3: NT_GLU, 4: NT_GLU + NT_FFN, 5: NT_GLU + NT_FFN + NT_POOL, 6: NTILE}[stage]])
    fm = lambda v: f(v).reshape(32, 128).T
    vecs = np.concatenate([fm(norm_mix[0]), fm(norm_mix[1]), fm(norm_ffn[0]), fm(norm_ffn[1]), fm(norm_final),
                           fm(ssm_d[0]), fm(pool_scale[0])], axis=1)
    ident = np.eye(128, dtype=np.float32)
    lq = lambda a: f(a).reshape(128, 2, 64).transpose(1, 2, 0).reshape(128, 128)
    lam = np.stack([lq(ssm_lambda_re[0]), lq(ssm_lambda_im[0]),
                    lq(np.repeat(f(ssm_log_step[0])[:, None], 64, axis=1))])
    bt = np.zeros((128, 128, 256), np.float32)
    ct = np.zeros((128, 128, 64), np.float32)
    for comp, (B, C) in enumerate(((f(ssm_b_re[0]), f(ssm_c_re[0])), (f(ssm_b_im[0]), f(ssm_c_im[0])))):
        Bq = B.reshape(128, 2, 64, 16)
        Cq = C.reshape(128, 2, 16, 64)
        for gl in range(2):
            for q4 in range(4):
                qs = np.arange(q4, 128, 4)
                r0 = q4 * 32 + gl * 16
                bt[qs, r0:r0 + 16, comp * 128 + gl * 64:comp * 128 + gl * 64 + 64] = Bq[qs, gl].transpose(0, 2, 1)
            ct[:, gl * 64:gl * 64 + 64, comp * 32 + gl * 16:comp * 32 + gl * 16 + 16] = Cq[:, gl].transpose(0, 2, 1)
    def krow(Tp, T):
        k = np.zeros(TM, np.float32)
        m = np.ones(TM, np.float32)
        k[0:Tp] = np.arange(Tp)
        m[0] = 0.0
        for s in range(8):
            if Tp + 8 * s + 8 <= T:
                k[Tp + 8 * s:Tp + 8 * s + 8] = np.arange(8)
                m[Tp + 8 * s] = 0.0
        return k, m
    kA, mA = krow(TPA, TA)
    kB, mB = krow(TPB, TB)
    kr = np.stack([np.concatenate([kA, kB]), np.concatenate([mA, mB])])
    kr = np.ascontiguousarray(np.broadcast_to(kr[:, None, :], (2, 128, 2 * TM)))
    kP = np.zeros(TM, np.float32); kP[0:TPRE] = np.arange(TPRE)
    mP = np.ones(TM, np.float32); mP[0] = 0.0
    krp = np.ascontiguousarray(np.broadcast_to(np.stack([kP, mP])[:, None, :], (2, 128, TM)))
    in_maps = []
    for core in range(8):
        b, hf = core // 2, core % 2
        xin = np.zeros((NTOK_IN, D), np.float32)
        pos = np.full((2 * TM,), 1.0e4, np.float32)
        if hf == 1:
            xin[3:3 + 1009] = x_prompt[b, 0:1009]
            xin[2 * TPRE:2 * TPRE + 15] = x_prompt[b, 1009:1024]
        p0 = hf * 1024
        a0 = 2 * TPRE
        xin[a0 + 15:a0 + 15 + 512] = x_prompt[b, p0:p0 + 512]
        xin[a0 + TPA:a0 + TPA + 64] = x_sample[core * 16:core * 16 + 8].reshape(64, D)
        b0 = a0 + TA
        xin[b0:b0 + 512] = x_prompt[b, p0 + 512:p0 + 1024]
        xin[b0 + TPB:b0 + TPB + 64] = x_sample[core * 16 + 8:core * 16 + 16].reshape(64, D)
        pos[15:15 + 512] = p0 + np.arange(512)
        pos[TM:TM + 512] = p0 + 512 + np.arange(512)
        sst = np.stack([f(state_ssm_re[0])[core * 16:core * 16 + 16].reshape(16, 128, 128),
                        f(state_ssm_im[0])[core * 16:core * 16 + 16].reshape(16, 128, 128)])
        in_maps.append(dict(xin=xin, wst=wst, vecs=vecs, ident=ident, kr=kr, krp=krp,
                            pos=np.ascontiguousarray(np.broadcast_to(pos[None, :], (128, 2 * TM))),
                            lam=lam, bt=bt, ct=ct, sst=sst,
                            spool=f(state_pool[0])[core * 16:core * 16 + 16]))
    res = run_bass_kernel_spmd(nc, in_maps, core_ids=list(range(8)))
    R = res.results
    if stage:
        _CACHE['dbg'] = (list(DBG), R, dict(xin1=in_maps[1]['xin'], lam=lam, bt=bt, ct=ct, vecs=vecs))
    y_prompt = np.zeros((4, 2048, D), np.float32)
    y_sample = np.zeros((128, 8, D), np.float32)
    sre_p = np.zeros((1, 4, 256, 64), np.float32)
    sim_p = np.zeros((1, 4, 256, 64), np.float32)
    pool_p = np.zeros((1, 4, 15, D), np.float32)
    sre_s = np.zeros((1, 128, 256, 64), np.float32)
    sim_s = np.zeros((1, 128, 256, 64), np.float32)
    pool_s = np.zeros((1, 128, 15, D), np.float32)
    for core in range(8):
        b, hf = core // 2, core % 2
        r = R[core]
        y_prompt[b, hf * 1024:(hf + 1) * 1024] = r["y"][0:1024]
        y_sample[core * 16:(core + 1) * 16] = r["y"][1024:1152].reshape(16, 8, D)
        if hf == 1:
            sre_p[0, b] = r["sp"][0].reshape(256, 64)
            sim_p[0, b] = r["sp"][1].reshape(256, 64)
            pool_p[0, b] = r["ppn"].transpose(1, 0, 2).reshape(15, D)
        sre_s[0, core * 16:(core + 1) * 16] = r["ss"][0].reshape(16, 256, 64)
        sim_s[0, core * 16:(core + 1) * 16] = r["ss"][1].reshape(16, 256, 64)
        pool_s[0, core * 16:(core + 1) * 16, 0:7] = r["pso"]
        pool_s[0, core * 16:(core + 1) * 16, 7:15] = r["psn"].reshape(2, 32, 8, 8, 128).transpose(0, 2, 3, 1, 4).reshape(16, 8, D)
    return (y_prompt, y_sample, sre_p, sim_p, pool_p, sre_s, sim_s, pool_s)
```

```python
import math
import numpy as np
import concourse.bass as bass
import concourse.mybir as mybir
from concourse.bass_utils import run_bass_kernel_spmd
from contextlib import ExitStack

F32 = mybir.dt.float32
F32R = mybir.dt.float32r
BF16 = mybir.dt.bfloat16
AF = mybir.ActivationFunctionType
ALU = mybir.AluOpType

D = 4096
NCH = 32
DFF = 11008
NF = 86
FG = 4
NGRP = 22
NSLOT = 4
TPA, TPB, TS = 527, 512, 64
TA, TB = 592, 576
TPRE = 506
TM = 592
MAGIC = 12582912.0
TWO_PI = 2.0 * math.pi
EPS = 1e-6
GC0 = math.sqrt(2.0 / math.pi)
NT_GLU, NT_FFN, NT_POOL = 128, 21 * 24 + 16, 16
NTILE = NT_GLU + NT_FFN + NT_POOL + NT_FFN
NTOK_IN = 2 * TPRE + TA + TB
WIN = (2, 4, 8, 16)


class Prog:
    def __init__(self):
        self.ops = []
        self.lastw = {}
        self.readers = {}
        self.lastdma = {}

    def add(self, eng, fn, R=(), W=(), dsem=None):
        i = len(self.ops)
        deps = {}
        for r in R:
            j = self.lastw.get(r)
            if j is not None:
                deps[j] = 'raw'
        for w in W:
            j = self.lastw.get(w)
            if j is not None and j not in deps:
                deps[j] = 'waw'
            for j in self.readers.get(w, {}).values():
                if j not in deps:
                    deps[j] = 'war'
        if dsem is not None:
            j = self.lastdma.get(dsem)
            if j is not None:
                deps[j] = 'raw'
            self.lastdma[dsem] = i
        rk = eng if dsem is None else ('dma', i)
        for r in R:
            self.readers.setdefault(r, {})[rk] = i
        for w in W:
            self.lastw[w] = i
            self.readers[w] = {}
        self.ops.append(dict(eng=eng, fn=fn, deps=deps, dsem=dsem, sig=False, force=False))
        return i

    def emit(self, nc, block, es):
        ops = self.ops
        engs = ('pe', 'act', 'dve', 'pool', 'sp')
        for i, o in enumerate(ops):
            keep = []
            for j, kind in o['deps'].items():
                d = ops[j]
                if d['dsem'] is None and d['eng'] == o['eng'] and not o['force']:
                    if kind != 'raw' or o['eng'] == 'pe':
                        continue
                keep.append(j)
                if d['dsem'] is None:
                    d['sig'] = True
            o['keep'] = keep
        cnt = {e: 0 for e in engs}
        dcnt = {}
        for o in ops:
            if o['dsem'] is not None:
                dcnt[o['dsem']] = dcnt.get(o['dsem'], 0) + 16
                o['sv'] = ('d_' + o['dsem'], dcnt[o['dsem']])
            elif o['sig']:
                cnt[o['eng']] += 1
                o['sv'] = ('e_' + o['eng'], cnt[o['eng']])
        self.cnt, self.dcnt = cnt, dcnt
        names = ['e_' + e for e in engs] + ['d_' + k for k in dcnt]
        sems = {n: es.enter_context(nc.semaphore(n)) for n in names}

        def run(engname, e):
            waited = {}
            for o in ops:
                if o['eng'] != engname:
                    continue
                need = {}
                for j in o['keep']:
                    s, v = ops[j]['sv']
                    if need.get(s, 0) < v:
                        need[s] = v
                for s, v in need.items():
                    if waited.get(s, 0) < v:
                        e.wait_ge(sems[s], v)
                        waited[s] = v
                if o['fn'] is None:
                    continue
                ins = o['fn'](e)
                if o['dsem'] is not None:
                    ins.then_inc(sems['d_' + o['dsem']], 16)
                elif o['sig']:
                    ins.then_inc(sems['e_' + engname], 1)

        @block.tensor
        def _(e):
            run('pe', e)

        @block.scalar
        def _(e):
            run('act', e)

        @block.vector
        def _(e):
            run('dve', e)

        @block.gpsimd
        def _(e):
            run('pool', e)

        @block.sync
        def _(e):
            run('sp', e)


DBG = []
STAGE = 0


def build_program(stage=0):
    nc = bass.Bass("TRN2", target_bir_lowering=False)
    del DBG[:]
    dt_in = lambda n, s: nc.dram_tensor(n, s, F32, kind="ExternalInput").ap()
    dt_out = lambda n, s: nc.dram_tensor(n, s, F32, kind="ExternalOutput").ap()
    xin = dt_in("xin", [NTOK_IN, D])
    NW = {0: NTILE, 1: 1, 2: 1, 3: NT_GLU, 4: NT_GLU + NT_FFN, 5: NT_GLU + NT_FFN + NT_POOL, 6: NTILE}[stage]
    wst = dt_in("wst", [NW, 128, 2048])
    vecs_d = dt_in("vecs", [128, 7 * 32])
    ident_d = dt_in("ident", [128, 128])
    kr_d = dt_in("kr", [2, 128, 2 * TM])
    krp_d = dt_in("krp", [2, 128, TM])
    pos_d = dt_in("pos", [128, 2 * TM])
    lam_d = dt_in("lam", [3, 128, 128])
    bt_d = dt_in("bt", [128, 128, 256])
    ct_d = dt_in("ct", [128, 128, 64])
    sst_d = dt_in("sst", [2, 16, 128, 128])
    spool_d = dt_in("spool", [16, 15, D])
    y_d = dt_out("y", [1152, D])
    sp_d = dt_out("sp", [2, 128, 128])
    ss_d = dt_out("ss", [2, 16, 128, 128])
    psn_d = dt_out("psn", [2, 32, 64, 128])
    pso_d = dt_out("pso", [16, 7, D])
    ppn_d = dt_out("ppn", [32, 15, 128])

    es = ExitStack()
    dbg_d = dt_out("dbg", [128, 65536]) if stage else None
    dcur = dict(o=0)
    sb = lambda n, s, d=F32: es.enter_context(nc.sbuf_tensor(n, s, d))
    x = sb("x", [128, NCH, TM])
    xn = sb("xn", [128, NCH, TM], BF16)
    ring = sb("ring", [128, NSLOT, 2048], BF16)
    ident = sb("ident_s", [128, 128])
    onesr = sb("onesr", [128, 128], F32R)
    vecs = sb("vecs_s", [128, 7 * 32])
    kidx = sb("kidx", [128, TM])
    rmask = sb("rmask", [128, TM])
    rstd = sb("rstd", [128, TM])
    prm = sb("prm", [128, 10, 128])
    St = sb("St", [128, 128, 9, 2])
    zq = sb("zq", [128, 9, 2])
    zt = sb("zt", [128, 16])
    magic_c = sb("magic_c", [128, 3])
    hist = sb("hist", [128, NCH, 15])
    ucr = sb("ucr", [128, TM], F32R)
    xr = sb("xr", [128, TM], F32R)
    xi = sb("xi", [128, TM], F32R)
    bt = sb("bt_s", [128, 2, 256], F32R)
    cp = sb("cp", [128, 4, 2, 128], F32R)
    ctraw = sb("ctraw", [128, 1, 64])
    sq = ucr
    h1 = sb("h1", [128, FG, TM], BF16)
    scr = sb("scr", [128, 13 * TM])
    sphc = sb("sphc", [120, 2, 128])
    hst = sb("hst", [64, 4, 128])
    hpt = sb("hpt", [15, 4, 128])
    pst = [es.enter_context(nc.psum_tensor(f"ps{i}", [128, 512], F32)) for i in range(8)]
    block = es.enter_context(nc.Block())

    P = Prog()
    V = lambda c: vecs[:, c:c + 1]
    vcol = lambda k, c: vecs[:, k * 32 + c:k * 32 + c + 1]
    S = lambda i: scr[:, i * TM:(i + 1) * TM]

    P.add('sp', lambda e: e.dma_start(out=ident[:], in_=ident_d), W=['ident'], dsem='c0')
    P.add('sp', lambda e: e.dma_start(out=vecs[:], in_=vecs_d), W=['vecs'], dsem='c1')
    P.add('sp', lambda e: e.dma_start(out=prm[:, 0:3, :], in_=lam_d.rearrange("a p q -> p a q")), W=['prm'], dsem='c2')
    P.add('dve', lambda e: e.memset(scr[:, 0:128], 1.0), W=['scrp'])
    P.add('dve', lambda e: e.tensor_copy(out=onesr[:], in_=scr[:, 0:128]), R=['scrp'], W=['onesr'])
    P.add('dve', lambda e: e.memset(scr[:, 1024:2048], 0.0), W=['scrz'])
    P.add('dve', lambda e: e.tensor_copy(out=cp[:, :, :, :].rearrange("p a b c -> p (a b c)"), in_=scr[:, 1024:2048]), R=['scrz'], W=['cp0', 'cp1', 'cp2', 'cp3'])
    P.add('dve', lambda e: e.memset(St[:], 0.0), W=['St'])
    P.add('dve', lambda e: e.memset(magic_c[:, 0:1], MAGIC), W=['magic'])
    P.add('dve', lambda e: e.memset(magic_c[:, 1:2], -MAGIC), R=['magic'], W=['magic'])
    P.add('dve', lambda e: e.memset(magic_c[:, 2:3], 1.0), R=['magic'], W=['magic'])
    P.add('dve', lambda e: e.memset(hist[:], 0.0), W=['hist'])

    pr = lambda i: prm[:, i, :]
    T0, T1 = S(0)[:, 0:128], S(1)[:, 0:128]
    T2, T3 = S(2)[:, 0:128], S(3)[:, 0:128]
    R_, W_ = ['prm'], ['prm']
    a = lambda eng, fn, R=(), W=(): P.add(eng, fn, R=list(R) + ['prm', 'scrp'], W=list(W) + ['prm', 'scrp'])
    def ts(out, in0, s1, s2=None, op0=ALU.mult, op1=ALU.add):
        if s2 is None:
            a('dve', lambda e: e.tensor_scalar(out=out, in0=in0, scalar1=s1, scalar2=None, op0=op0))
        else:
            a('dve', lambda e: e.tensor_scalar(out=out, in0=in0, scalar1=s1, scalar2=s2, op0=op0, op1=op1))

    def tt(out, in0, in1, op):
        a('dve', lambda e: e.tensor_tensor(out=out, in0=in0, in1=in1, op=op))

    def stt(out, in0, sc, in1, op0, op1):
        a('dve', lambda e: e.scalar_tensor_tensor(out=out, in0=in0, scalar=sc, in1=in1, op0=op0, op1=op1))

    T4, T5, T6, T7 = S(4)[:, 0:128], S(5)[:, 0:128], S(6)[:, 0:128], S(7)[:, 0:128]
    a('act', lambda e: e.activation(out=T0, in_=pr(2), func=AF.Exp))
    tt(T1, pr(0), T0, ALU.mult)
    tt(T2, pr(1), T0, ALU.mult)
    ts(pr(8), T2, 1.0 / TWO_PI)
    ts(T0, pr(8), MAGIC, None, op0=ALU.add)
    ts(T0, T0, -MAGIC, None, op0=ALU.add)
    tt(T0, pr(8), T0, ALU.subtract)
    ts(T0, T0, math.pi / 2.0)
    tt(T2, T0, T0, ALU.mult)
    ts(T3, T2, 1.0 / 362880.0)
    stt(T3, T3, -1.0 / 5040.0, T2, ALU.add, ALU.mult)
    stt(T3, T3, 1.0 / 120.0, T2, ALU.add, ALU.mult)
    stt(T3, T3, -1.0 / 6.0, T2, ALU.add, ALU.mult)
    stt(T3, T3, 1.0, T0, ALU.add, ALU.mult)
    ts(T4, T2, -1.0 / 3628800.0)
    stt(T4, T4, 1.0 / 40320.0, T2, ALU.add, ALU.mult)
    stt(T4, T4, -1.0 / 720.0, T2, ALU.add, ALU.mult)
    stt(T4, T4, 1.0 / 24.0, T2, ALU.add, ALU.mult)
    stt(T4, T4, -0.5, T2, ALU.add, ALU.mult)
    ts(T4, T4, 1.0, None, op0=ALU.add)
    stt(T5, T3, 2.0, T4, ALU.mult, ALU.mult)
    tt(T6, T3, T3, ALU.mult)
    ts(T6, T6, -2.0, 1.0)
    stt(T3, T5, 2.0, T6, ALU.mult, ALU.mult)
    tt(T4, T5, T5, ALU.mult)
    ts(T4, T4, -2.0)
    ts(T5, T1, 1.0 / 6.0, 1.0)
    tt(T5, T5, T1, ALU.mult)
    ts(T5, T5, 1.0 / 5.0, 1.0)
    tt(T5, T5, T1, ALU.mult)
    ts(T5, T5, 1.0 / 4.0, 1.0)
    tt(T5, T5, T1, ALU.mult)
    ts(T5, T5, 1.0 / 3.0, 1.0)
    tt(T5, T5, T1, ALU.mult)
    ts(T5, T5, 1.0 / 2.0, 1.0)
    tt(T5, T5, T1, ALU.mult)
    ts(pr(3), T5, 1.0, None, op0=ALU.add)
    tt(pr(9), pr(3), T3, ALU.mult)
    tt(T1, pr(3), T4, ALU.mult)
    tt(T1, T1, T5, ALU.add)
    ts(T3, T1, 1.0, None, op0=ALU.add)
    tt(T0, pr(0), pr(0), ALU.mult)
    tt(T2, pr(1), pr(1), ALU.mult)
    tt(T0, T0, T2, ALU.add)
    a('dve', lambda e: e.reciprocal(out=T0, in_=T0))
    tt(T2, T1, pr(0), ALU.mult)
    tt(pr(4), pr(9), pr(1), ALU.mult)
    tt(T2, T2, pr(4), ALU.add)
    tt(pr(4), T2, T0, ALU.mult)
    tt(T2, pr(9), pr(0), ALU.mult)
    tt(T1, T1, pr(1), ALU.mult)
    tt(T2, T2, T1, ALU.subtract)
    tt(pr(5), T2, T0, ALU.mult)
    a('dve', lambda e: e.tensor_copy(out=pr(0), in_=pr(8)))
    a('dve', lambda e: e.tensor_copy(out=pr(1), in_=pr(3)))
    a('dve', lambda e: e.tensor_copy(out=pr(2), in_=T3))
    a('dve', lambda e: e.tensor_copy(out=pr(3), in_=pr(9)))
    a('dve', lambda e: e.tensor_tensor(out=T0, in0=pr(4), in1=pr(4), op=ALU.mult))
    a('dve', lambda e: e.tensor_tensor(out=T1, in0=pr(5), in1=pr(5), op=ALU.mult))
    a('dve', lambda e: e.tensor_tensor(out=T0, in0=T0, in1=T1, op=ALU.add))
    a('dve', lambda e: e.reciprocal(out=T0, in_=T0))
    a('dve', lambda e: e.tensor_tensor(out=pr(6), in0=pr(4), in1=T0, op=ALU.mult))
    a('dve', lambda e: e.tensor_tensor(out=T1, in0=pr(5), in1=T0, op=ALU.mult))
    a('dve', lambda e: e.tensor_scalar(out=pr(7), in0=T1, scalar1=-1.0, scalar2=None, op0=ALU.mult))

    wstate = dict(next_dma=0, next_use=0)
    TOTAL_TILES = 2 * NTILE

    def wtile():
        i = wstate['next_use']
        wstate['next_use'] += 1
        while wstate['next_dma'] < min(TOTAL_TILES, i + NSLOT):
            j = wstate['next_dma']
            wstate['next_dma'] += 1
            sl = j % NSLOT
            P.add('pool', (lambda e, j=j, sl=sl: e.dma_start(out=ring[:, sl, :], in_=wst[(j % NTILE) % NW])),
                  W=[f'ring{sl}'], dsem=f'w{sl}')
        return i % NSLOT

    def nts_of(T):
        h = T // 2
        if h % 2:
            h += 1
        return [(0, h), (h, T)]

    def barrier():
        regs = list(P.lastw.keys())
        i = P.add('sp', (lambda e: e.nop()), R=[], W=regs)
        P.ops[i]['force'] = True
        for k, j in P.lastdma.items():
            P.ops[i]['deps'].setdefault(j, 'raw')

    bank = dict(i=0)

    def nbank(nb=8):
        b = bank['i'] % nb
        bank['i'] += 1
        return b

    def load_x(row0, T):
        stage = scr[:, 0:D]
        t0 = 0
        while t0 < T:
            n = min(128, T - t0)
            P.add('sp', (lambda e, t0=t0, n=n: e.dma_start(out=stage[0:n, :], in_=xin[row0 + t0:row0 + t0 + n, :])),
                  W=['stage'], dsem='ld')
            for c4 in range(8):
                b = nbank()
                for cc in range(4):
                    c = c4 * 4 + cc
                    P.add('pe', (lambda e, b=b, cc=cc, c=c, n=n: e.transpose(
                        out=pst[b][:, cc * 128:cc * 128 + n], in_=stage[0:n, c * 128:(c + 1) * 128],
                        identity=ident[0:n, 0:n])), R=['stage', 'ident'], W=[f'ps{b}'])
                eng = 'act' if c4 % 2 else 'dve'
                if eng == 'act':
                    P.add('act', (lambda e, b=b, c4=c4, t0=t0, n=n: e.activation(
                        out=x[:, c4 * 4:c4 * 4 + 4, t0:t0 + n],
                        in_=pst[b][:, :].rearrange("p (c t) -> p c t", t=128)[:, :, 0:n], func=AF.Copy)),
                        R=[f'ps{b}'], W=[f'x{c}' for c in range(c4 * 4, c4 * 4 + 4)])
                else:
                    P.add('dve', (lambda e, b=b, c4=c4, t0=t0, n=n: e.tensor_copy(
                        out=x[:, c4 * 4:c4 * 4 + 4, t0:t0 + n],
                        in_=pst[b][:, :].rearrange("p (c t) -> p c t", t=128)[:, :, 0:n])),
                        R=[f'ps{b}'], W=[f'x{c}' for c in range(c4 * 4, c4 * 4 + 4)])
            t0 += n

    def rms(T):
        nts = nts_of(T)
        bs = [nbank() for _ in nts]
        for c in range(NCH):
            sqb, sqn = ((ucr, 'ucr'), (xr, 'xr'))[c % 2]
            P.add('act', (lambda e, c=c, sqb=sqb: e.activation(out=sqb[:, 0:T], in_=x[:, c, 0:T], func=AF.Square)),
                  R=[f'x{c}'], W=[sqn])
            for (lo, hi), b in zip(nts, bs):
                P.add('pe', (lambda e, c=c, lo=lo, hi=hi, b=b, sqb=sqb: e.matmul(
                    pst[b][:, 0:hi - lo], onesr[:], sqb[:, lo:hi], start=(c == 0), stop=(c == NCH - 1))),
                    R=[sqn, 'onesr'], W=[f'ps{b}'])
        for (lo, hi), b in zip(nts, bs):
            P.add('dve', (lambda e, lo=lo, hi=hi, b=b: e.tensor_scalar(
                out=rstd[:, lo:hi], in0=pst[b][:, 0:hi - lo], scalar1=1.0 / D, scalar2=EPS, op0=ALU.mult, op1=ALU.add)),
                R=[f'ps{b}'], W=['rstd'])
        P.add('act', lambda e: e.activation(out=rstd[:, 0:T], in_=rstd[:, 0:T], func=AF.Sqrt), R=['rstd'], W=['rstd'])
        P.add('dve', lambda e: e.reciprocal(out=rstd[:, 0:T], in_=rstd[:, 0:T]), R=['rstd'], W=['rstd'])

    def norm_to_xn(T, gk):
        for c in range(NCH):
            P.add('dve', (lambda e, c=c: e.scalar_tensor_tensor(
                out=xn[:, c, 0:T], in0=x[:, c, 0:T], scalar=vcol(gk, c), in1=rstd[:, 0:T], op0=ALU.mult, op1=ALU.mult)),
                R=[f'x{c}', 'rstd', 'vecs'], W=[f'xn{c}'])

    qdma = dict(n=0)

    def ssm(T, Tp, nseg_s, kcol0, state_only, seq0):
        nts = nts_of(T)
        ntb = [(0, T)] if T <= 512 else nts
        A1, A2 = S(0)[:, 0:T], S(1)[:, 0:T]
        TSs = [S(2)[:, 0:T], S(10)[:, 0:T]]
        TCs = [S(3)[:, 0:T], S(11)[:, 0:T]]
        RMs = [S(4)[:, 0:T], S(12)[:, 0:T]]
        W1, W2, W3 = S(5)[:, 0:T], S(6)[:, 0:T], S(7)[:, 0:T]
        G1, G2 = S(8)[:, 0:T], S(9)[:, 0:T]
        nseg = 1 + nseg_s
        sview = lambda ap: ap[:, Tp:Tp + 8 * nseg_s].rearrange("p (s k) -> p s k", k=8)

        def tablegen(q):
            p = q % 2
            TSb, TCb, RM = TSs[p], TCs[p], RMs[p]
            thq, rq = prm[:, 0, q:q + 1], prm[:, 1, q:q + 1]
            P.add('act', (lambda e: e.activation(out=A1, in_=kidx[:, 0:T], func=AF.Identity, scale=thq, bias=magic_c[:, 0:1])),
                  R=['kidx', 'prm', 'magic'], W=['A1'])
            P.add('act', (lambda e: e.activation(out=A1, in_=A1, func=AF.Identity, scale=1.0, bias=magic_c[:, 1:2])), R=['A1', 'magic'], W=['A1'])
            P.add('act', (lambda e: e.activation(out=A2, in_=kidx[:, 0:T], func=AF.Copy, scale=thq)),
                  R=['kidx', 'prm'], W=['A2'])
            P.add('pool', (lambda e: e.tensor_tensor(out=A2, in0=A2, in1=A1, op=ALU.subtract)), R=['A1', 'A2'], W=['A2'])
            P.add('act', (lambda e: e.activation(out=TSb, in_=A2, func=AF.Sin, scale=TWO_PI * 0.999999)), R=['A2'], W=[f'TS{p}'])
            P.add('act', (lambda e: e.activation(out=TCb, in_=A2, func=AF.Sin, scale=math.pi * 0.999999)), R=['A2'], W=[f'TC{p}'])
            P.add('act', (lambda e: e.activation(out=TCb, in_=TCb, func=AF.Square)), R=[f'TC{p}'], W=[f'TC{p}'])
            P.add('act', (lambda e: e.activation(out=TCb, in_=TCb, func=AF.Identity, scale=-2.0, bias=magic_c[:, 2:3])), R=[f'TC{p}', 'magic'], W=[f'TC{p}'])
            P.add('act', (lambda e: e.activation(out=RM, in_=rmask[:, 0:T], func=AF.Copy, scale=rq)),
                  R=['rmask', 'prm'], W=[f'RM{p}'])

        tablegen(0)
        for c in range(NCH):
            P.add('dve', (lambda e, c=c: e.scalar_tensor_tensor(
                out=ucr[:, 0:T], in0=x[:, c, 0:T], scalar=vcol(0, c), in1=rstd[:, 0:T], op0=ALU.mult, op1=ALU.mult)),
                R=[f'x{c}', 'rstd', 'vecs'], W=['ucr'])
            ybs = [6, 7] if not state_only else []
            for ql in range(4):
                q = 4 * c + ql
                p = q % 2
                TSb, TCb, RM = TSs[p], TCs[p], RMs[p]
                tsn, tcn, rmn = f'TS{p}', f'TC{p}', f'RM{p}'
                bsl = qdma['n'] % 2
                qdma['n'] += 1
                P.add('pool', (lambda e, q=q, bsl=bsl: e.dma_start(out=bt[:, bsl, :], in_=bt_d[q])),
                      W=[f'bt{bsl}'], dsem=f'bt{bsl}')
                if not state_only:
                    P.add('sp', (lambda e, q=q: e.dma_start(out=ctraw[:, 0, :], in_=ct_d[q])),
                          W=['ctraw'], dsem='ct')
                if q + 1 < 128:
                    tablegen(q + 1)
                arq, aiq = prm[:, 2, q:q + 1], prm[:, 3, q:q + 1]
                P.add('dve', (lambda e, q=q, aiq=aiq: e.tensor_scalar(out=zt[:, 0:nseg], in0=St[:, q, 0:nseg, 1], scalar1=aiq, scalar2=None, op0=ALU.mult)), R=['St', 'prm'], W=['zt'])
                P.add('dve', (lambda e, q=q, arq=arq: e.scalar_tensor_tensor(out=zq[:, 0:nseg, 0], in0=St[:, q, 0:nseg, 0], scalar=arq, in1=zt[:, 0:nseg], op0=ALU.mult, op1=ALU.subtract)), R=['St', 'prm', 'zt'], W=['zq'])
                P.add('dve', (lambda e, q=q, arq=arq: e.tensor_scalar(out=zt[:, 0:nseg], in0=St[:, q, 0:nseg, 1], scalar1=arq, scalar2=None, op0=ALU.mult)), R=['St', 'prm', 'zq'], W=['zt'])
                P.add('dve', (lambda e, q=q, aiq=aiq: e.scalar_tensor_tensor(out=zq[:, 0:nseg, 1], in0=St[:, q, 0:nseg, 0], scalar=aiq, in1=zt[:, 0:nseg], op0=ALU.mult, op1=ALU.add)), R=['St', 'prm', 'zt'], W=['zq'])
                bb = [(nbank(6), nbank(6)) for _ in ntb]
                for (lo, hi), (b0, b1) in zip(ntb, bb):
                    for comp, b in ((0, b0), (1, b1)):
                        P.add('pe', (lambda e, comp=comp, b=b, lo=lo, hi=hi, bsl=bsl: e.matmul(
                            pst[b][:, 0:hi - lo], bt[:, bsl, comp * 128:(comp + 1) * 128], ucr[:, lo:hi], start=True, stop=True)),
                            R=[f'bt{bsl}', 'ucr'], W=[f'ps{b}'])
                for (lo, hi), (b0, b1) in zip(ntb, bb):
                    n = hi - lo
                    P.add('dve', (lambda e, lo=lo, hi=hi, b0=b0, n=n, TCb=TCb: e.tensor_tensor(out=W1[:, lo:hi], in0=pst[b0][:, 0:n], in1=TCb[:, lo:hi], op=ALU.mult)),
                          R=[f'ps{b0}', tcn], W=['W1'])
                    P.add('dve', (lambda e, lo=lo, hi=hi, b1=b1, n=n, TSb=TSb: e.tensor_tensor(out=W2[:, lo:hi], in0=pst[b1][:, 0:n], in1=TSb[:, lo:hi], op=ALU.mult)),
                          R=[f'ps{b1}', tsn], W=['W2'])
                    P.add('dve', (lambda e, lo=lo, hi=hi, b1=b1, n=n, TCb=TCb: e.tensor_tensor(out=W3[:, lo:hi], in0=pst[b1][:, 0:n], in1=TCb[:, lo:hi], op=ALU.mult)),
                          R=[f'ps{b1}', tcn], W=['W3'])
                    P.add('dve', (lambda e, lo=lo, hi=hi, b0=b0, n=n, TSb=TSb: e.tensor_tensor(out=G2[:, lo:hi], in0=pst[b0][:, 0:n], in1=TSb[:, lo:hi], op=ALU.mult)),
                          R=[f'ps{b0}', tsn], W=['G2'])
                P.add('dve', (lambda e: e.tensor_tensor(out=W1, in0=W1, in1=W2, op=ALU.add)), R=['W1', 'W2'], W=['W1'])
                P.add('dve', (lambda e: e.tensor_tensor(out=W3, in0=W3, in1=G2, op=ALU.subtract)), R=['W3', 'G2'], W=['W3'])
                for comp, Wb, nm in ((0, W1, 'W1'), (1, W3, 'W3')):
                    P.add('dve', (lambda e, comp=comp, Wb=Wb: e.tensor_tensor(
                        out=Wb[:, 0:1], in0=Wb[:, 0:1], in1=zq[:, 0:1, comp], op=ALU.add)), R=[nm, 'zq'], W=[nm])
                    if nseg_s:
                        P.add('dve', (lambda e, comp=comp, Wb=Wb: e.tensor_tensor(
                            out=sview(Wb)[:, :, 0], in0=sview(Wb)[:, :, 0], in1=zq[:, 1:nseg, comp], op=ALU.add)), R=[nm, 'zq'], W=[nm])
                P.add('dve', (lambda e, RM=RM: e.tensor_tensor_scan(out=W2, data0=RM, data1=W1, initial=0.0, op0=ALU.mult, op1=ALU.add)),
                      R=[rmn, 'W1'], W=['W2'])
                P.add('dve', (lambda e, RM=RM: e.tensor_tensor_scan(out=W1, data0=RM, data1=W3, initial=0.0, op0=ALU.mult, op1=ALU.add)),
                      R=[rmn, 'W3', 'W2'], W=['W1'])
                if state_only:
                    lo, hi = Tp - 1, Tp
                    P.add('dve', (lambda e, TCb=TCb: e.tensor_tensor(out=W3[:, lo:hi], in0=W2[:, lo:hi], in1=TCb[:, lo:hi], op=ALU.mult)), R=['W2', tcn], W=['W3'])
                    P.add('dve', (lambda e, TSb=TSb: e.tensor_tensor(out=G2[:, lo:hi], in0=W1[:, lo:hi], in1=TSb[:, lo:hi], op=ALU.mult)), R=['W1', tsn], W=['G2'])
                    P.add('dve', (lambda e, q=q: e.tensor_tensor(out=St[:, q, 0:1, 0], in0=W3[:, lo:hi], in1=G2[:, lo:hi], op=ALU.subtract)), R=['W3', 'G2'], W=['St'])
                    P.add('dve', (lambda e, TSb=TSb: e.tensor_tensor(out=W3[:, lo:hi], in0=W2[:, lo:hi], in1=TSb[:, lo:hi], op=ALU.mult)), R=['W2', tsn], W=['W3'])
                    P.add('dve', (lambda e, TCb=TCb: e.tensor_tensor(out=G2[:, lo:hi], in0=W1[:, lo:hi], in1=TCb[:, lo:hi], op=ALU.mult)), R=['W1', tcn], W=['G2'])
                    P.add('dve', (lambda e, q=q: e.tensor_tensor(out=St[:, q, 0:1, 1], in0=W3[:, lo:hi], in1=G2[:, lo:hi], op=ALU.add)), R=['W3', 'G2'], W=['St'])
                    continue
                P.add('pool', (lambda e, TCb=TCb: e.tensor_tensor(out=A1, in0=W2, in1=TCb, op=ALU.mult)), R=['W2', tcn], W=['A1'])
                P.add('pool', (lambda e, TSb=TSb: e.tensor_tensor(out=A2, in0=W1, in1=TSb, op=ALU.mult)), R=['W1', tsn], W=['A2'])
                P.add('pool', (lambda e: e.tensor_tensor(out=xr[:, 0:T], in0=A1, in1=A2, op=ALU.subtract)), R=['A1', 'A2'], W=['xr'])
                P.add('pool', (lambda e, TSb=TSb: e.tensor_tensor(out=A1, in0=W2, in1=TSb, op=ALU.mult)), R=['W2', tsn], W=['A1'])
                P.add('pool', (lambda e, TCb=TCb: e.tensor_tensor(out=A2, in0=W1, in1=TCb, op=ALU.mult)), R=['W1', tcn], W=['A2'])
                P.add('pool', (lambda e: e.tensor_tensor(out=xi[:, 0:T], in0=A1, in1=A2, op=ALU.add)), R=['A1', 'A2'], W=['xi'])
                for comp, src, nm in ((0, xr, 'xr'), (1, xi, 'xi')):
                    P.add('act', (lambda e, comp=comp, src=src, q=q: e.activation(out=St[:, q, 0:1, comp], in_=src[:, Tp - 1:Tp], func=AF.Copy)), R=[nm], W=['St'])
                    if nseg_s:
                        P.add('act', (lambda e, comp=comp, src=src, q=q: e.activation(
                            out=St[:, q, 1:nseg, comp], in_=sview(src)[:, :, 7], func=AF.Copy)), R=[nm], W=['St'])
                frq, fiq = prm[:, 4, q:q + 1], prm[:, 5, q:q + 1]
                cpr = cp[:, ql, 0, 32 * ql:32 * ql + 32]
                cpi = cp[:, ql, 1, 32 * ql:32 * ql + 32]
                t32a, t32b = G1[:, 0:32], G1[:, 32:64]
                P.add('dve', (lambda e, frq=frq: e.tensor_scalar(out=t32a, in0=ctraw[:, 0, 0:32], scalar1=frq, scalar2=None, op0=ALU.mult)), R=['ctraw', 'prm'], W=['G1'])
                P.add('dve', (lambda e, fiq=fiq: e.tensor_scalar(out=t32b, in0=ctraw[:, 0, 32:64], scalar1=fiq, scalar2=None, op0=ALU.mult)), R=['ctraw', 'prm'], W=['G1'])
                P.add('dve', (lambda e, cpr=cpr: e.tensor_tensor(out=cpr, in0=t32a, in1=t32b, op=ALU.subtract)), R=['G1'], W=[f'cp{ql}'])
                P.add('dve', (lambda e, fiq=fiq: e.tensor_scalar(out=t32a, in0=ctraw[:, 0, 0:32], scalar1=fiq, scalar2=-1.0, op0=ALU.mult, op1=ALU.mult)), R=['ctraw', 'prm'], W=['G1'])
                P.add('dve', (lambda e, frq=frq: e.tensor_scalar(out=t32b, in0=ctraw[:, 0, 32:64], scalar1=frq, scalar2=None, op0=ALU.mult)), R=['ctraw', 'prm'], W=['G1'])
                P.add('dve', (lambda e, cpi=cpi: e.tensor_tensor(out=cpi, in0=t32a, in1=t32b, op=ALU.subtract)), R=['G1'], W=[f'cp{ql}'])
                for (lo, hi), yb in zip(nts, ybs):
                    P.add('pe', (lambda e, lo=lo, hi=hi, yb=yb, ql=ql: e.matmul(
                        pst[yb][:, 0:hi - lo], cp[:, ql, 0, :], xr[:, lo:hi], start=(ql == 0), stop=False)),
                        R=[f'cp{ql}', 'xr'], W=[f'ps{yb}'])
                    P.add('pe', (lambda e, lo=lo, hi=hi, yb=yb, ql=ql: e.matmul(
                        pst[yb][:, 0:hi - lo], cp[:, ql, 1, :], xi[:, lo:hi], start=False, stop=(ql == 3))),
                        R=[f'cp{ql}', 'xi'], W=[f'ps{yb}'])
            if state_only:
                continue
            for (lo, hi), yb in zip(nts, ybs):
                n = hi - lo
                P.add('dve', (lambda e, lo=lo, hi=hi, yb=yb, n=n, c=c: e.scalar_tensor_tensor(
                    out=W1[:, lo:hi], in0=ucr[:, lo:hi], scalar=vcol(5, c), in1=pst[yb][:, 0:n], op0=ALU.mult, op1=ALU.add)),
                    R=['ucr', 'vecs', f'ps{yb}'], W=['W1'])
            P.add('act', (lambda e: e.activation(out=W2, in_=W1, func=AF.Square)), R=['W1'], W=['W2'])
            P.add('dve', (lambda e: e.tensor_scalar(out=W2, in0=W2, scalar1=0.044715 * 2 * GC0, scalar2=2 * GC0, op0=ALU.mult, op1=ALU.add)), R=['W2'], W=['W2'])
            P.add('dve', (lambda e: e.tensor_tensor(out=W2, in0=W2, in1=W1, op=ALU.mult)), R=['W2', 'W1'], W=['W2'])
            P.add('act', (lambda e: e.activation(out=W3, in_=W2, func=AF.Sigmoid)), R=['W2'], W=['W3'])
            P.add('dve', (lambda e, c=c: e.tensor_tensor(out=xn[:, c, 0:T], in0=W3, in1=W1, op=ALU.mult)), R=['W3', 'W1'], W=[f'xn{c}'])

    def glu(T):
        nts = nts_of(T)
        for m in range(NCH):
            bs = [[nbank() for _ in nts] for _ in range(2)]
            for part in range(2):
                for half in range(2):
                    sl = wtile()
                    for kl in range(16):
                        k = half * 16 + kl
                        for (lo, hi), b in zip(nts, bs[part]):
                            P.add('pe', (lambda e, sl=sl, kl=kl, k=k, lo=lo, hi=hi, b=b: e.matmul(
                                pst[b][:, 0:hi - lo], ring[:, sl, kl * 128:(kl + 1) * 128], xn[:, k, lo:hi],
                                start=(k == 0), stop=(k == 31))), R=[f'ring{sl}', f'xn{k}'], W=[f'ps{b}'])
            for i, (lo, hi) in enumerate(nts):
                n = hi - lo
                b1, b2 = bs[0][i], bs[1][i]
                G = S(8 + i % 2)[:, 0:n]
                gn = f'G{1 + i % 2}'
                P.add('act', (lambda e, b2=b2, n=n, G=G: e.activation(out=G, in_=pst[b2][:, 0:n], func=AF.Sigmoid)), R=[f'ps{b2}'], W=[gn])
                P.add('dve', (lambda e, b1=b1, n=n, G=G: e.tensor_tensor(out=G, in0=pst[b1][:, 0:n], in1=G, op=ALU.mult)), R=[f'ps{b1}', gn], W=[gn])
                P.add('dve', (lambda e, m=m, lo=lo, hi=hi, G=G: e.tensor_tensor(out=x[:, m, lo:hi], in0=x[:, m, lo:hi], in1=G, op=ALU.add)), R=[gn, f'x{m}'], W=[f'x{m}'])

    def ffn(T):
        nts = nts_of(T)
        for g in range(NGRP):
            nf = FG if g < NGRP - 1 else NF - FG * (NGRP - 1)
            for fl in range(nf):
                bs = [[nbank() for _ in nts] for _ in range(2)]
                for part in range(2):
                    for half in range(2):
                        sl = wtile()
                        for kl in range(16):
                            k = half * 16 + kl
                            for (lo, hi), b in zip(nts, bs[part]):
                                P.add('pe', (lambda e, sl=sl, kl=kl, k=k, lo=lo, hi=hi, b=b: e.matmul(
                                    pst[b][:, 0:hi - lo], ring[:, sl, kl * 128:(kl + 1) * 128], xn[:, k, lo:hi],
                                    start=(k == 0), stop=(k == 31))), R=[f'ring{sl}', f'xn{k}'], W=[f'ps{b}'])
                for i, (lo, hi) in enumerate(nts):
                    n = hi - lo
                    bg, bu = bs[0][i], bs[1][i]
                    G = S(8 + i % 2)[:, 0:n]
                    gn = f'G{1 + i % 2}'
                    P.add('act', (lambda e, bg=bg, n=n, G=G: e.activation(out=G, in_=pst[bg][:, 0:n], func=AF.Silu)), R=[f'ps{bg}'], W=[gn])
                    P.add('dve', (lambda e, bu=bu, n=n, G=G, fl=fl, lo=lo, hi=hi: e.tensor_tensor(
                        out=h1[:, fl, lo:hi], in0=pst[bu][:, 0:n], in1=G, op=ALU.mult)), R=[f'ps{bu}', gn], W=[f'h1_{fl}'])
            for mp in range(8):
                sl = wtile()
                for ml in range(4):
                    m = mp * 4 + ml
                    for i, (lo, hi) in enumerate(nts):
                        n = hi - lo
                        b = nbank()
                        for fl in range(nf):
                            P.add('pe', (lambda e, sl=sl, fl=fl, ml=ml, lo=lo, hi=hi, b=b, nf=nf: e.matmul(
                                pst[b][:, 0:hi - lo], ring[:, sl, fl * 512 + ml * 128:fl * 512 + (ml + 1) * 128], h1[:, fl, lo:hi],
                                start=(fl == 0), stop=(fl == nf - 1))), R=[f'ring{sl}', f'h1_{fl}'], W=[f'ps{b}'])
                        P.add('dve', (lambda e, m=m, lo=lo, hi=hi, b=b, n=n: e.tensor_tensor(
                            out=x[:, m, lo:hi], in0=pst[b][:, 0:n], in1=x[:, m, lo:hi], op=ALU.add)), R=[f'ps{b}', f'x{m}'], W=[f'x{m}'])

    def pool_layer(T, Tp, poscol0, first, st_idx):
        nts = nts_of(T)
        E = 15 + Tp
        ES = 8 * 23
        icnt = scr[:, 10 * TM - 4 * TM:10 * TM]
        posr = S(5)
        P.add('sp', (lambda e: e.dma_start(out=posr[:, 0:T], in_=pos_d[:, poscol0:poscol0 + T])), W=['A1p'], dsem='pos')
        for wi, w in enumerate(WIN):
            ic = icnt[:, wi * TM:wi * TM + T]
            P.add('dve', (lambda e, ic=ic, w=w: e.tensor_scalar(out=ic, in0=posr[:, 0:T], scalar1=1.0, scalar2=float(w), op0=ALU.add, op1=ALU.min)), R=['A1p'], W=[f'ic{wi}'])
            P.add('dve', (lambda e, ic=ic: e.reciprocal(out=ic, in_=ic)), R=[f'ic{wi}'], W=[f'ic{wi}'])
        hc = S(0)
        ext = scr[:, TM:TM + E + ES]
        s2 = scr[:, 3 * TM:3 * TM + E + ES]
        stg = scr[:, 5 * TM:5 * TM + 128]
        for c in range(NCH):
            gi = c // 8
            w = WIN[gi]
            P.add('dve', (lambda e, c=c: e.scalar_tensor_tensor(
                out=hc[:, 0:T], in0=x[:, c, 0:T], scalar=vcol(1, c), in1=rstd[:, 0:T], op0=ALU.mult, op1=ALU.mult)),
                R=[f'x{c}', 'rstd', 'vecs'], W=['hc'])
            P.add('act', (lambda e, c=c: e.activation(out=ext[:, 0:15], in_=hist[:, c, :], func=AF.Copy)), R=['hist'], W=['ext'])
            P.add('act', (lambda e: e.activation(out=ext[:, 15:15 + Tp], in_=hc[:, 0:Tp], func=AF.Copy)), R=['hc'], W=['ext'])
            exs = ext[:, E:E + ES].rearrange("p (s k) -> p s k", k=23)
            P.add('act', (lambda e, exs=exs: e.activation(out=exs[:, :, 15:23], in_=hc[:, Tp:Tp + 64].rearrange("p (s k) -> p s k", k=8), func=AF.Copy)), R=['hc'], W=['ext'])
            b = nbank()
            P.add('sp', (lambda e, c=c: e.dma_start(out=sphc[:, c % 2, :], in_=spool_d[st_idx * 8:st_idx * 8 + 8, :, c * 128:(c + 1) * 128].rearrange("s k d -> (s k) d"))),
                  W=[f'sphc{c % 2}'], dsem=f'sph{c % 2}')
            P.add('pe', (lambda e, c=c, b=b: e.transpose(out=pst[b][:, 0:120], in_=sphc[0:120, c % 2, :], identity=ident[0:120, 0:120])),
                  R=[f'sphc{c % 2}', 'ident'], W=[f'ps{b}'])
            P.add('dve', (lambda e, b=b, exs=exs: e.tensor_copy(out=exs[:, :, 0:15], in_=pst[b][:, 0:120].rearrange("p (s k) -> p s k", k=15))), R=[f'ps{b}'], W=['ext'])
            P.add('act', (lambda e, c=c: e.activation(out=hist[:, c, :], in_=hc[:, Tp - 15:Tp], func=AF.Copy)), R=['hc', 'ext'], W=['hist'])
            L = E + ES
            cur, other = ext, s2
            sh = 1
            while sh < w:
                P.add('dve', (lambda e, cur=cur, other=other, sh=sh, L=L: e.tensor_tensor(out=other[:, sh:L], in0=cur[:, sh:L], in1=cur[:, 0:L - sh], op=ALU.add)),
                      R=['ext', 's2'], W=['ext', 's2'])
                if sh > 1 or True:
                    P.add('act', (lambda e, cur=cur, other=other, sh=sh: e.activation(out=other[:, 0:sh], in_=cur[:, 0:sh], func=AF.Copy)), R=['ext', 's2'], W=['ext', 's2'])
                cur, other = other, cur
                sh *= 2
            ic = icnt[:, gi * TM:gi * TM + T]
            pb = S(5)
            P.add('dve', (lambda e, cur=cur, ic=ic: e.tensor_tensor(out=pb[:, 0:Tp], in0=cur[:, 15:15 + Tp], in1=ic[:, 0:Tp], op=ALU.mult)), R=['ext', 's2', f'ic{gi}'], W=['pb'])
            curs = cur[:, E:E + ES].rearrange("p (s k) -> p s k", k=23)
            P.add('dve', (lambda e, curs=curs, ic=ic: e.tensor_tensor(
                out=pb[:, Tp:Tp + 64].rearrange("p (s k) -> p s k", k=8), in0=curs[:, :, 15:23],
                in1=ic[:, Tp:Tp + 64].rearrange("p (s k) -> p s k", k=8), op=ALU.mult)), R=['ext', 's2', f'ic{gi}'], W=['pb'])
            if T > Tp + 64:
                P.add('dve', (lambda e: e.memset(pb[:, Tp + 64:T], 0.0)), W=['pb'])
            P.add('dve', (lambda e, c=c: e.tensor_tensor(out=xn[:, c, 0:T], in0=pb[:, 0:T], in1=hc[:, 0:T], op=ALU.subtract)), R=['pb', 'hc'], W=[f'xn{c}'])
            b = nbank()
            P.add('pe', (lambda e, b=b: e.transpose(out=pst[b][0:64, 0:128], in_=hc[:, Tp:Tp + 64], identity=ident[:, :])), R=['hc', 'ident'], W=[f'ps{b}'])
            P.add('act', (lambda e, b=b, c=c: e.activation(out=hst[0:64, c % 4, :], in_=pst[b][0:64, 0:128], func=AF.Copy)), R=[f'ps{b}'], W=[f'hst{c % 4}'])
            P.add('sp', (lambda e, c=c: e.dma_start(out=psn_d[st_idx, c, :, :], in_=hst[0:64, c % 4, :])), R=[f'hst{c % 4}'], dsem=f'o_psn{c % 4}')
            if not first:
                b = nbank()
                P.add('pe', (lambda e, b=b: e.transpose(out=pst[b][0:15, 0:128], in_=hc[:, Tp - 15:Tp], identity=ident[:, :])), R=['hc', 'ident'], W=[f'ps{b}'])
                P.add('act', (lambda e, b=b, c=c: e.activation(out=hpt[0:15, c % 4, :], in_=pst[b][0:15, 0:128], func=AF.Copy)), R=[f'ps{b}'], W=[f'hpt{c % 4}'])
                P.add('sp', (lambda e, c=c: e.dma_start(out=ppn_d[c, :, :], in_=hpt[0:15, c % 4, :])), R=[f'hpt{c % 4}'], dsem=f'o_ppn{c % 4}')
        for gi in range(4):
            for mpair in range(4):
                sl = wtile()
                for ml in range(2):
                    m = gi * 8 + mpair * 2 + ml
                    for (lo, hi) in nts:
                        n = hi - lo
                        b = nbank()
                        for k in range(8):
                            P.add('pe', (lambda e, sl=sl, ml=ml, k=k, gi=gi, lo=lo, hi=hi, b=b: e.matmul(
                                pst[b][:, 0:hi - lo], ring[:, sl, (ml * 8 + k) * 128:(ml * 8 + k + 1) * 128], xn[:, gi * 8 + k, lo:hi],
                                start=(k == 0), stop=(k == 7))), R=[f'ring{sl}', f'xn{gi * 8 + k}'], W=[f'ps{b}'])
                        P.add('dve', (lambda e, m=m, lo=lo, hi=hi, b=b, n=n: e.scalar_tensor_tensor(
                            out=x[:, m, lo:hi], in0=pst[b][:, 0:n], scalar=vcol(6, m), in1=x[:, m, lo:hi], op0=ALU.mult, op1=ALU.add)),
                            R=[f'ps{b}', f'x{m}', 'vecs'], W=[f'x{m}'])


    def out_y(T, Tp, prow0, srow0, halo):
        ost = scr[:, 0:D]
        segs = []
        t = halo
        while t < Tp:
            n = min(128, Tp - t)
            segs.append((t, n, prow0 + (t - halo)))
            t += n
        segs.append((Tp, 64, srow0))
        for (t0, n, r0) in segs:
            for c in range(NCH):
                hcf = S(8 + c % 2)
                gn = f'G{1 + c % 2}'
                P.add('dve', (lambda e, c=c, t0=t0, n=n, hcf=hcf: e.scalar_tensor_tensor(
                    out=hcf[:, 0:n], in0=x[:, c, t0:t0 + n], scalar=vcol(4, c), in1=rstd[:, t0:t0 + n], op0=ALU.mult, op1=ALU.mult)),
                    R=[f'x{c}', 'rstd', 'vecs'], W=[gn])
                b = nbank()
                P.add('pe', (lambda e, b=b, n=n, hcf=hcf: e.transpose(out=pst[b][0:n, 0:128], in_=hcf[:, 0:n], identity=ident[:, :])), R=[gn, 'ident'], W=[f'ps{b}'])
                P.add('act', (lambda e, b=b, n=n, c=c: e.activation(out=ost[0:n, c * 128:(c + 1) * 128], in_=pst[b][0:n, 0:128], func=AF.Copy)), R=[f'ps{b}'], W=['stage'])
            P.add('sp', (lambda e, n=n, r0=r0: e.dma_start(out=y_d[r0:r0 + n, :], in_=ost[0:n, :])), R=['stage'], dsem='o_y')

    def load_sample_states(seq0):
        tmp = scr[:, 0:2 * 8 * 128].rearrange("p (a s q) -> p a s q", a=2, s=8)
        for a_ in range(2):
            P.add('sp', (lambda e, a_=a_: e.dma_start(out=tmp[:, a_, :, :], in_=sst_d[a_, seq0:seq0 + 8, :, :].rearrange("s q p -> q s p"))), W=['stage'], dsem='ld')
        xs = scr[:, 2048:2048 + 2048].rearrange("p (a s q) -> p a s q", a=2, s=8)
        for a_ in range(2):
            for s in range(8):
                b = nbank()
                P.add('pe', (lambda e, a_=a_, s=s, b=b: e.transpose(out=pst[b][:, 0:128], in_=tmp[:, a_, s, :], identity=ident[:, :])), R=['stage', 'ident'], W=[f'ps{b}'])
                P.add('act', (lambda e, a_=a_, s=s, b=b: e.activation(out=xs[:, a_, s, :], in_=pst[b][:, 0:128], func=AF.Copy)), R=[f'ps{b}'], W=['xs'])
        for s in range(8):
            t_ = scr[:, 4096:4224]
            P.add('dve', (lambda e, s=s: e.tensor_tensor(out=t_, in0=xs[:, 1, s, :], in1=prm[:, 7, :], op=ALU.mult)), R=['xs', 'prm'], W=['A1p'])
            P.add('dve', (lambda e, s=s: e.tensor_tensor(out=St[:, :, 1 + s, 0], in0=xs[:, 0, s, :], in1=prm[:, 6, :], op=ALU.mult)), R=['xs', 'prm'], W=['St'])
            P.add('dve', (lambda e, s=s: e.tensor_tensor(out=St[:, :, 1 + s, 0], in0=St[:, :, 1 + s, 0], in1=t_, op=ALU.subtract)), R=['St', 'A1p'], W=['St'])
            P.add('dve', (lambda e, s=s: e.tensor_tensor(out=t_, in0=xs[:, 1, s, :], in1=prm[:, 6, :], op=ALU.mult)), R=['xs', 'prm', 'St'], W=['A1p'])
            P.add('dve', (lambda e, s=s: e.tensor_tensor(out=St[:, :, 1 + s, 1], in0=xs[:, 0, s, :], in1=prm[:, 7, :], op=ALU.mult)), R=['xs', 'prm'], W=['St'])
            P.add('dve', (lambda e, s=s: e.tensor_tensor(out=St[:, :, 1 + s, 1], in0=St[:, :, 1 + s, 1], in1=t_, op=ALU.add)), R=['St', 'A1p'], W=['St'])

    def store_states(segs, dst_fn, dsem):
        ob = scr[:, 0:2 * 9 * 128].rearrange("p (a s q) -> p a s q", a=2, s=9)
        for seg in segs:
            t_ = scr[:, 2304:2432]
            u_ = scr[:, 2432:2560]
            P.add('dve', (lambda e, seg=seg: e.tensor_tensor(out=t_, in0=St[:, :, seg, 1], in1=prm[:, 5, :], op=ALU.mult)), R=['St', 'prm'], W=['A1p'])
            P.add('dve', (lambda e, seg=seg: e.tensor_tensor(out=u_, in0=St[:, :, seg, 0], in1=prm[:, 4, :], op=ALU.mult)), R=['St', 'prm'], W=['A1q'])
            P.add('dve', (lambda e: e.tensor_tensor(out=u_, in0=u_, in1=t_, op=ALU.subtract)), R=['A1p', 'A1q'], W=['A1q'])
            b = nbank()
            P.add('pe', (lambda e, b=b: e.transpose(out=pst[b][:, 0:128], in_=u_, identity=ident[:, :])), R=['A1q', 'ident'], W=[f'ps{b}'])
            P.add('act', (lambda e, b=b, seg=seg: e.activation(out=ob[:, 0, seg, :], in_=pst[b][:, 0:128], func=AF.Copy)), R=[f'ps{b}'], W=['ob'])
            P.add('dve', (lambda e, seg=seg: e.tensor_tensor(out=t_, in0=St[:, :, seg, 1], in1=prm[:, 4, :], op=ALU.mult)), R=['St', 'prm', 'A1q'], W=['A1p'])
            P.add('dve', (lambda e, seg=seg: e.tensor_tensor(out=u_, in0=St[:, :, seg, 0], in1=prm[:, 5, :], op=ALU.mult)), R=['St', 'prm'], W=['A1q'])
            P.add('dve', (lambda e: e.tensor_tensor(out=u_, in0=u_, in1=t_, op=ALU.add)), R=['A1p', 'A1q'], W=['A1q'])
            b = nbank()
            P.add('pe', (lambda e, b=b: e.transpose(out=pst[b][:, 0:128], in_=u_, identity=ident[:, :])), R=['A1q', 'ident'], W=[f'ps{b}'])
            P.add('act', (lambda e, b=b, seg=seg: e.activation(out=ob[:, 1, seg, :], in_=pst[b][:, 0:128], func=AF.Copy)), R=[f'ps{b}'], W=['ob'])
            for a_ in range(2):
                P.add('sp', (lambda e, a_=a_, seg=seg: e.dma_start(out=dst_fn(a_, seg), in_=ob[:, a_, seg, :])), R=['ob'], dsem=dsem)

    def load_kr(src, col0, T):
        P.add('sp', (lambda e: e.dma_start(out=kidx[:, 0:T], in_=src[0, :, col0:col0 + T])), W=['kidx'], dsem='k0')
        P.add('sp', (lambda e: e.dma_start(out=rmask[:, 0:T], in_=src[1, :, col0:col0 + T])), W=['rmask'], dsem='k1')

    def dump(name, ap, regs, n, b3=None):
        if not stage:
            return
        off = dcur['o']
        dcur['o'] += n
        DBG.append((name, off, n))
        o_ap = dbg_d[:, off:off + n]
        if b3:
            o_ap = o_ap.rearrange("p (a b) -> p a b", b=b3)
        P.add('sp', (lambda e: e.dma_start(out=o_ap, in_=ap)), R=regs, dsem='o_dbg')

    def finish():
        P.add('sp', (lambda e: e.nop()), R=[], W=[], dsem=None)
        last = P.ops[-1]
        for k, j in P.lastdma.items():
            if k.startswith('o_'):
                last['deps'][j] = 'raw'
        P.emit(nc, block, es)
        es.close()
        _CACHE['P'] = P
        return nc

    barrier()
    dump('prm', prm[:, :, :].rearrange("p a b -> p (a b)"), ['prm', 'scrp'], 1280)
    load_kr(krp_d, 0, TPRE)
    for seg in range(2):
        load_x(seg * TPRE, TPRE)
        barrier()
        if seg == 1:
            dump('x_c0', x[:, 0, 0:TPRE], ['x0'], TPRE)
            dump('x_c31', x[:, 31, 0:TPRE], ['x31'], TPRE)
        rms(TPRE)
        if seg == 1:
            dump('rstd', rstd[:, 0:TPRE], ['rstd'], TPRE)
        ssm(TPRE, TPRE, 0, 0, True, 0)
        barrier()
        dump(f'St_pre{seg}', St[:, :, 0, :], ['St'], 256, b3=2)
        if stage == 1 and seg == 1:
            return finish()
    row0 = 2 * TPRE
    for st_idx, (T, Tp, halo) in enumerate(((TA, TPA, 15), (TB, TPB, 0))):
        first = st_idx == 0
        load_kr(kr_d, st_idx * TM, T)
        load_sample_states(st_idx * 8)
        barrier()
        load_x(row0, T)
        barrier()
        rms(T)
        ssm(T, Tp, 8, 0, False, st_idx * 8)
        barrier()
        if stage and first:
            dump('StA', St[:, :, 0, :], ['St'], 256, b3=2)
            for cc in (0, 17, 31):
                P.add('dve', (lambda e, cc=cc: e.tensor_copy(out=S(9), in_=xn[:, cc, :])), R=[f'xn{cc}'], W=['G2'])
                dump(f'xnA_{cc}', S(9), ['G2'], TM)
            P.add('dve', (lambda e: e.tensor_copy(out=S(8), in_=xr[:, :])), R=['xr'], W=['G1'])
            dump('xrA', S(8), ['G1'], TM)
            dump('TSA', S(2), ['TS0'], TM)
            dump('TCA', S(3), ['TC0'], TM)
            dump('RMA', S(4), ['RM0'], TM)
            dump('kidxA', kidx[:, :], ['kidx'], TM)
        store_states(range(1, 9), (lambda a_, seg, st_idx=st_idx: ss_d[a_, st_idx * 8 + seg - 1, :, :]), 'o_ss')
        if stage == 2:
            barrier()
            return finish()
        if not first:
            store_states([0], (lambda a_, seg: sp_d[a_, :, :]), 'o_sp')
        barrier()
        def dumpx(tag):
            if stage and first:
                barrier()
                for cc in (0, 17, 31):
                    dump(f'{tag}_{cc}', x[:, cc, :], [f'x{cc}'], TM)
                dump(f'{tag}_rstd', rstd[:, :], ['rstd'], TM)
        glu(T)
        dumpx('x1')
        if stage == 3:
            barrier()
            return finish()
        rms(T)
        norm_to_xn(T, 2)
        ffn(T)
        dumpx('x2')
        if stage == 4:
            barrier()
            return finish()
        rms(T)
        barrier()
        P.add('sp', (lambda e, st_idx=st_idx: e.dma_start(out=pso_d[st_idx * 8:st_idx * 8 + 8, :, :], in_=spool_d[st_idx * 8:st_idx * 8 + 8, 8:15, :])), dsem='o_pso')
        pool_layer(T, Tp, st_idx * TM, first, st_idx)
        barrier()
        dumpx('x3')
        if stage == 5:
            barrier()
            return finish()
        rms(T)
        norm_to_xn(T, 3)
        ffn(T)
        dumpx('x4')
        rms(T)
        barrier()
        out_y(T, Tp, st_idx * 512, 1024 + st_idx * 64, halo)
        barrier()
        if stage == 6:
            return finish()
        row0 += T
    return finish()


_CACHE = {}


def _prep_weights(ssm_w_glu, pool_w, ffn_w_gate_up, ffn_w_down):
    tiles = np.zeros((NTILE, 128, 2048), np.float32)
    W = ssm_w_glu[0].reshape(32, 128, 2, 32, 128)
    W = W.reshape(2, 16, 128, 2, 32, 128)
    tiles[0:NT_GLU] = W.transpose(4, 3, 0, 2, 1, 5).reshape(NT_GLU, 128, 2048)
    base = NT_GLU
    for L in range(2):
        GU = ffn_w_gate_up[L].reshape(2, 16, 128, 2, NF, 128)
        GUt = GU.transpose(4, 3, 0, 2, 1, 5).reshape(NF, 4, 128, 2048)
        DN = ffn_w_down[L].reshape(NF, 128, 8, 512)
        t = base
        for g in range(NGRP):
            nf = FG if g < NGRP - 1 else NF - FG * (NGRP - 1)
            for fl in range(nf):
                tiles[t:t + 4] = GUt[g * FG + fl]
                t += 4
            blk = DN[g * FG:g * FG + nf]
            tiles[t:t + 8, :, 0:nf * 512] = blk.transpose(2, 1, 0, 3).reshape(8, 128, nf * 512)
            t += 8
        assert t == base + NT_FFN
        base = t
        if L == 0:
            PW = pool_w[0].reshape(4, 8, 128, 4, 2, 128)
            tiles[base:base + NT_POOL] = PW.transpose(0, 3, 2, 4, 1, 5).reshape(NT_POOL, 128, 2048)
            base += NT_POOL
    assert base == NTILE
    return tiles


def kernel(x_prompt, x_sample, state_ssm_re, state_ssm_im, state_pool, norm_mix, norm_ffn,
           ssm_lambda_re, ssm_lambda_im, ssm_log_step, ssm_b_re, ssm_b_im, ssm_c_re, ssm_c_im,
           ssm_d, ssm_w_glu, pool_w, pool_scale, ffn_w_gate_up, ffn_w_down, norm_final):
    f = lambda a: np.ascontiguousarray(np.asarray(a, dtype=np.float32))
    x_prompt, x_sample = f(x_prompt), f(x_sample)
    stage = STAGE
    if 'nc' not in _CACHE:
        _CACHE['nc'] = build_program(stage)
    nc = _CACHE['nc']
    wst = _prep_weights(f(ssm_w_glu), f(pool_w), f(ffn_w_gate_up), f(ffn_w_down))
    if stage:
        wst = np.ascontiguousarray(wst[0:{1: 1, 2: 1, 3: NT_GLU, 4: NT_GLU + NT_FFN, 5: NT_GLU + NT_FFN + NT_POOL, 6: NTILE}[stage]])
    fm = lambda v: f(v).reshape(32, 128).T
    vecs = np.concatenate([fm(norm_mix[0]), fm(norm_mix[1]), fm(norm_ffn[0]), fm(norm_ffn[1]), fm(norm_final),
                           fm(ssm_d[0]), fm(pool_scale[0])], axis=1)
    ident = np.eye(128, dtype=np.float32)
    lq = lambda a: f(a).reshape(128, 2, 64).transpose(1, 2, 0).reshape(128, 128)
    lam = np.stack([lq(ssm_lambda_re[0]), lq(ssm_lambda_im[0]),
                    lq(np.repeat(f(ssm_log_step[0])[:, None], 64, axis=1))])
    bt = np.zeros((128, 128, 256), np.float32)
    ct = np.zeros((128, 128, 64), np.float32)
    for comp, (B, C) in enumerate(((f(ssm_b_re[0]), f(ssm_c_re[0])), (f(ssm_b_im[0]), f(ssm_c_im[0])))):
        Bq = B.reshape(128, 2, 64, 16)
        Cq = C.reshape(128, 2, 16, 64)
        for gl in range(2):
            for q4 in range(4):
                qs = np.arange(q4, 128, 4)
                r0 = q4 * 32 + gl * 16
                bt[qs, r0:r0 + 16, comp * 128 + gl * 64:comp * 128 + gl * 64 + 64] = Bq[qs, gl].transpose(0, 2, 1)
            ct[:, gl * 64:gl * 64 + 64, comp * 32 + gl * 16:comp * 32 + gl * 16 + 16] = Cq[:, gl].transpose(0, 2, 1)
    def krow(Tp, T):
        k = np.zeros(TM, np.float32)
        m = np.ones(TM, np.float32)
        k[0:Tp] = np.arange(Tp)
        m[0] = 0.0
        for s in range(8):
            if Tp + 8 * s + 8 <= T:
                k[Tp + 8 * s:Tp + 8 * s + 8] = np.arange(8)
                m[Tp + 8 * s] = 0.0
        return k, m
    kA, mA = krow(TPA, TA)
    kB, mB = krow(TPB, TB)
    kr = np.stack([np.concatenate([kA, kB]), np.concatenate([mA, mB])])
    kr = np.ascontiguousarray(np.broadcast_to(kr[:, None, :], (2, 128, 2 * TM)))
    kP = np.zeros(TM, np.float32); kP[0:TPRE] = np.arange(TPRE)
    mP = np.ones(TM, np.float32); mP[0] = 0.0
    krp = np.ascontiguousarray(np.broadcast_to(np.stack([kP, mP])[:, None, :], (2, 128, TM)))
    in_maps = []
    for core in range(8):
        b, hf = core // 2, core % 2
        xin = np.zeros((NTOK_IN, D), np.float32)
        pos = np.full((2 * TM,), 1.0e4, np.float32)
        if hf == 1:
            xin[3:3 + 1009] = x_prompt[b, 0:1009]
            xin[2 * TPRE:2 * TPRE + 15] = x_prompt[b, 1009:1024]
        p0 = hf * 1024
        a0 = 2 * TPRE
        xin[a0 + 15:a0 + 15 + 512] = x_prompt[b, p0:p0 + 512]
        xin[a0 + TPA:a0 + TPA + 64] = x_sample[core * 16:core * 16 + 8].reshape(64, D)
        b0 = a0 + TA
        xin[b0:b0 + 512] = x_prompt[b, p0 + 512:p0 + 1024]
        xin[b0 + TPB:b0 + TPB + 64] = x_sample[core * 16 + 8:core * 16 + 16].reshape(64, D)
        pos[15:15 + 512] = p0 + np.arange(512)
        pos[TM:TM + 512] = p0 + 512 + np.arange(512)
        sst = np.stack([f(state_ssm_re[0])[core * 16:core * 16 + 16].reshape(16, 128, 128),
                        f(state_ssm_im[0])[core * 16:core * 16 + 16].reshape(16, 128, 128)])
        in_maps.append(dict(xin=xin, wst=wst, vecs=vecs, ident=ident, kr=kr, krp=krp,
                            pos=np.ascontiguousarray(np.broadcast_to(pos[None, :], (128, 2 * TM))),
                            lam=lam, bt=bt, ct=ct, sst=sst,
                            spool=f(state_pool[0])[core * 16:core * 16 + 16]))
    res = run_bass_kernel_spmd(nc, in_maps, core_ids=list(range(8)))
    R = res.results
    if stage:
        _CACHE['dbg'] = (list(DBG), R, dict(xin1=in_maps[1]['xin'], lam=lam, bt=bt, ct=ct, vecs=vecs))
    y_prompt = np.zeros((4, 2048, D), np.float32)
    y_sample = np.zeros((128, 8, D), np.float32)
    sre_p = np.zeros((1, 4, 256, 64), np.float32)
    sim_p = np.zeros((1, 4, 256, 64), np.float32)
    pool_p = np.zeros((1, 4, 15, D), np.float32)
    sre_s = np.zeros((1, 128, 256, 64), np.float32)
    sim_s = np.zeros((1, 128, 256, 64), np.float32)
    pool_s = np.zeros((1, 128, 15, D), np.float32)
    for core in range(8):
        b, hf = core // 2, core % 2
        r = R[core]
        y_prompt[b, hf * 1024:(hf + 1) * 1024] = r["y"][0:1024]
        y_sample[core * 16:(core + 1) * 16] = r["y"][1024:1152].reshape(16, 8, D)
        if hf == 1:
            sre_p[0, b] = r["sp"][0].reshape(256, 64)
            sim_p[0, b] = r["sp"][1].reshape(256, 64)
            pool_p[0, b] = r["ppn"].transpose(1, 0, 2).reshape(15, D)
        sre_s[0, core * 16:(core + 1) * 16] = r["ss"][0].reshape(16, 256, 64)
        sim_s[0, core * 16:(core + 1) * 16] = r["ss"][1].reshape(16, 256, 64)
        pool_s[0, core * 16:(core + 1) * 16, 0:7] = r["pso"]
        pool_s[0, core * 16:(core + 1) * 16, 7:15] = r["psn"].reshape(2, 32, 8, 8, 128).transpose(0, 2, 3, 1, 4).reshape(16, 8, D)
    return (y_prompt, y_sample, sre_p, sim_p, pool_p, sre_s, sim_s, pool_s)
```
